# Optimizing a Trainium2 kernel written in Bass

```python
import math
import jax, jax.numpy as jnp
from jax import lax
import numpy as np

D_MODEL = 1024
BATCH = 4
SEQ = 4096
DEPTH = 1

DN_HEADS = 8
DN_HEAD_DIM = 128
DN_CONV = 4
DN_CHUNK = 64
NSA_HEADS = 8
NSA_KV_HEADS = 2
NSA_HEAD_DIM = 128
NSA_CMP_LEN = 32
NSA_CMP_STRIDE = 16
NSA_CMP_HIDDEN = 256
NSA_SEL_LEN = 64
NSA_SEL_TOPK = 16
NSA_SEL_QBLOCK = 64
NSA_WINDOW = 512
NSA_WIN_QBLOCK = 128
ROPE_THETA = 500000.0
ROPE_DIM = NSA_HEAD_DIM // 4
D_FF = 2816
FFN_CONV = 3
LN_EPS = 1e-5
RMS_EPS = 1e-6
DEEPNORM_ALPHA = (2.0 * DEPTH) ** 0.25
DEEPNORM_BETA = (8.0 * DEPTH) ** -0.25

DN_WIDTH = DN_HEADS * DN_HEAD_DIM
NSA_WIDTH = NSA_HEADS * NSA_HEAD_DIM
NSA_KV_WIDTH = NSA_KV_HEADS * NSA_HEAD_DIM
IN_SIZES = (DN_WIDTH, DN_WIDTH, DN_WIDTH, DN_WIDTH, DN_HEADS, DN_HEADS,
            NSA_WIDTH, NSA_KV_WIDTH, NSA_KV_WIDTH, NSA_KV_WIDTH, NSA_KV_WIDTH,
            NSA_KV_WIDTH, NSA_KV_WIDTH, 3 * NSA_HEADS, D_MODEL, D_MODEL)
D_IN = sum(IN_SIZES)

kernel_name = "hybrid_gdn_nsa_convglu_deepnorm"


def layer_norm(x, g, b):
    xf = x.astype(jnp.float32)
    mu = jnp.mean(xf, -1, keepdims=True)
    var = jnp.mean(jnp.square(xf - mu), -1, keepdims=True)
    return ((xf - mu) * lax.rsqrt(var + LN_EPS) * g + b).astype(x.dtype)


def l2norm(t):
    t = t.astype(jnp.float32)
    return t * lax.rsqrt(jnp.sum(t * t, -1, keepdims=True) + RMS_EPS)


def causal_dwconv(x, w):
    K = w.shape[0]
    S = x.shape[1]
    xp = jnp.pad(x, ((0, 0), (K - 1, 0), (0, 0)))
    return sum(xp[:, i:i + S] * w[i] for i in range(K))


def partial_rope(t, pos):
    half = ROPE_DIM // 2
    inv_freq = ROPE_THETA ** (-jnp.arange(half, dtype=jnp.float32) / half)
    ang = pos.astype(jnp.float32)[:, None] * inv_freq
    cos = jnp.cos(ang)[:, None, :].astype(t.dtype)
    sin = jnp.sin(ang)[:, None, :].astype(t.dtype)
    t1, t2, rest = t[..., :half], t[..., half:ROPE_DIM], t[..., ROPE_DIM:]
    return jnp.concatenate([t1 * cos - t2 * sin, t2 * cos + t1 * sin, rest], -1)


def masked_softmax(s, mask):
    s = jnp.where(mask, s.astype(jnp.float32), -jnp.inf)
    m = jnp.max(s, -1, keepdims=True)
    m = jnp.where(jnp.isfinite(m), m, 0.0)
    p = jnp.exp(s - m)
    return p / jnp.maximum(jnp.sum(p, -1, keepdims=True), jnp.finfo(jnp.float32).tiny)


def gated_delta_rule(q, k, v, g, beta):
    B, S, H, dk = q.shape
    dv = v.shape[-1]
    C = DN_CHUNK
    N = S // C

    def to_chunks(t):
        t = t.reshape((B, N, C, H) + t.shape[3:])
        return jnp.moveaxis(t, (1, 3), (0, 2))

    q, k, v, g, beta = (to_chunks(t) for t in (q, k, v, g, beta))
    g = jnp.cumsum(g, axis=-1)
    incl = jnp.tril(jnp.ones((C, C), bool))
    strict = jnp.tril(jnp.ones((C, C), bool), -1)
    decay = jnp.exp(jnp.where(incl, g[..., :, None] - g[..., None, :], -jnp.inf))
    k_beta = k * beta[..., None]
    v_beta = v * beta[..., None]
    lower = jnp.where(strict, jnp.einsum('nbhid,nbhjd->nbhij', k_beta, k) * decay, 0.0)
    a_mat = lower + jnp.eye(C, dtype=jnp.float32)
    rhs = jnp.concatenate([v_beta, k_beta * jnp.exp(g)[..., None]], -1)
    sol = lax.linalg.triangular_solve(a_mat, rhs, left_side=True, lower=True, unit_diagonal=True)
    u, w = sol[..., :dv], sol[..., dv:]
    qk = jnp.einsum('nbhid,nbhjd->nbhij', q, k) * decay
    q_dec = q * jnp.exp(g)[..., None]
    k_dec = k * jnp.exp(g[..., -1:] - g)[..., None]
    g_last = jnp.exp(g[..., -1])

    def step(state, inp):
        qd, kd, u_c, w_c, qk_c, gl = inp
        v_new = u_c - jnp.einsum('bhck,bhkv->bhcv', w_c, state)
        out = (jnp.einsum('bhck,bhkv->bhcv', qd, state)
               + jnp.einsum('bhij,bhjv->bhiv', qk_c, v_new))
        state = state * gl[..., None, None] + jnp.einsum('bhck,bhcv->bhkv', kd, v_new)
        return state, out

    state0 = jnp.zeros((B, H, dk, dv), jnp.float32)
    _, o = lax.scan(step, state0, (q_dec, k_dec, u, w, qk, g_last))
    return jnp.moveaxis(o, (0, 2), (1, 3)).reshape(B, S, H, dv)


def nsa_attention(q, k_cmp, v_cmp, k_sel, v_sel, k_win, v_win, gates,
                  cmp_pos_k, cmp_pos_v, cmp_k_w1, cmp_k_w2, cmp_v_w1, cmp_v_w2):
    B, S, H, dh = q.shape
    G = k_cmp.shape[2]
    hpg = H // G
    scale = dh ** -0.5
    pos = jnp.arange(S)
    q = partial_rope(q, pos)
    k_sel = partial_rope(k_sel, pos)
    k_win = partial_rope(k_win, pos)
    qg = q.reshape(B, S, G, hpg, dh).transpose(0, 2, 3, 1, 4)

    def heads_first(t):
        return t.transpose(0, 2, 1, 3)

    n_cmp = (S - NSA_CMP_LEN) // NSA_CMP_STRIDE + 1
    tok = np.arange(n_cmp)[:, None] * NSA_CMP_STRIDE + np.arange(NSA_CMP_LEN)[None]
    cmp_end = tok[:, -1]

    def compress(t, pe, w1, w2):
        blk = t[:, tok] + pe[:, None, :]
        blk = blk.transpose(0, 1, 3, 2, 4).reshape(B, n_cmp, G, NSA_CMP_LEN * dh)
        return jax.nn.silu(blk @ w1) @ w2

    kc = partial_rope(compress(k_cmp, cmp_pos_k, cmp_k_w1, cmp_k_w2), jnp.asarray(cmp_end))
    vc = compress(v_cmp, cmp_pos_v, cmp_v_w1, cmp_v_w2)
    s_cmp = jnp.einsum('bghtd,bngd->bghtn', qg, kc) * scale
    cmp_mask = cmp_end[None, :] <= np.arange(S)[:, None]
    p_cmp = masked_softmax(s_cmp, cmp_mask)
    o_cmp = jnp.einsum('bghtn,bngd->bghtd', p_cmp.astype(vc.dtype), vc)

    n_sel = S // NSA_SEL_LEN
    n_top = min(NSA_SEL_TOPK, n_sel)
    sel_start = np.arange(n_sel) * NSA_SEL_LEN
    overlap = ((tok[:, 0][:, None] <= (sel_start + NSA_SEL_LEN - 1)[None])
               & (cmp_end[:, None] >= sel_start[None])).astype(np.float32)
    p_sel = jnp.einsum('bghtn,nj->bgtj', p_cmp, overlap)
    cur = np.arange(S) // NSA_SEL_LEN
    jb = np.arange(n_sel)
    valid = jb[None] <= cur[:, None]
    forced = valid & ((jb[None] == 0) | (jb[None] == cur[:, None]) | (jb[None] == cur[:, None] - 1))
    score = jnp.where(forced, jnp.inf, jnp.where(valid, p_sel, -jnp.inf))
    top_score, top_idx = lax.top_k(score, n_top)
    top_ok = top_score > -jnp.inf

    QB = NSA_SEL_QBLOCK
    nq = S // QB
    kb = heads_first(k_sel).reshape(B, G, n_sel, NSA_SEL_LEN, dh)
    vb = heads_first(v_sel).reshape(B, G, n_sel, NSA_SEL_LEN, dh)
    q_blocks = qg.reshape(B, G, hpg, nq, QB, dh).transpose(3, 0, 1, 2, 4, 5)
    idx_blocks = top_idx.reshape(B, G, nq, QB, n_top).transpose(2, 0, 1, 3, 4)
    ok_blocks = top_ok.reshape(B, G, nq, QB, n_top).transpose(2, 0, 1, 3, 4)
    q_start = jnp.arange(nq) * QB
    bi = jnp.arange(B)[:, None, None, None]
    gi = jnp.arange(G)[None, :, None, None]
    offs = jnp.arange(NSA_SEL_LEN)
    n_keys = n_top * NSA_SEL_LEN

    def sel_block(args):
        qb_, ib, okb, t0 = args
        kg = kb[bi, gi, ib].reshape(B, G, QB, n_keys, dh)
        vg = vb[bi, gi, ib].reshape(B, G, QB, n_keys, dh)
        kpos = ib[..., None] * NSA_SEL_LEN + offs
        qpos = t0 + jnp.arange(QB)
        mask = (okb[..., None] & (kpos <= qpos[:, None, None])).reshape(B, G, 1, QB, n_keys)
        s = jnp.einsum('bghqd,bgqkd->bghqk', qb_, kg) * scale
        p = masked_softmax(s, mask)
        return jnp.einsum('bghqk,bgqkd->bghqd', p.astype(vg.dtype), vg)

    o_sel = lax.map(sel_block, (q_blocks, idx_blocks, ok_blocks, q_start))
    o_sel = o_sel.transpose(1, 2, 3, 0, 4, 5).reshape(B, G, hpg, S, dh)

    W = NSA_WINDOW
    WB = NSA_WIN_QBLOCK
    nb = S // WB
    nw = W // WB

    def band(t):
        tp = jnp.pad(heads_first(t), ((0, 0), (0, 0), (W, 0), (0, 0))).reshape(B, G, nb + nw, WB, dh)
        return jnp.concatenate([tp[:, :, i:i + nb] for i in range(nw + 1)], axis=3)

    kw_b = band(k_win)
    vw_b = band(v_win)
    qw = qg.reshape(B, G, hpg, nb, WB, dh)
    qpos = np.arange(nb)[:, None] * WB + np.arange(WB)[None]
    kpos = np.arange(nb)[:, None] * WB - W + np.arange((nw + 1) * WB)[None]
    rel = qpos[:, :, None] - kpos[:, None, :]
    win_mask = (rel >= 0) & (rel < W) & (kpos[:, None, :] >= 0)
    s_win = jnp.einsum('bghnqd,bgnkd->bghnqk', qw, kw_b) * scale
    p_win = masked_softmax(s_win, win_mask)
    o_win = jnp.einsum('bghnqk,bgnkd->bghnqd', p_win.astype(vw_b.dtype), vw_b).reshape(B, G, hpg, S, dh)

    gt = jax.nn.sigmoid(gates).reshape(B, S, 3, G, hpg).transpose(2, 0, 3, 4, 1)[..., None]
    o = gt[0] * o_cmp + gt[1] * o_sel + gt[2] * o_win
    return o.transpose(0, 3, 1, 2, 4).reshape(B, S, H * dh)


def setup_inputs(seed: int = 0) -> dict:
    key = jax.random.key(seed)
    ks = jax.random.split(key, 24)
    f32 = jnp.float32
    L = DEPTH

    def nrm(k, shape, scale):
        return jax.random.normal(k, shape, f32) * scale

    cmp_in = NSA_CMP_LEN * NSA_HEAD_DIM
    dt = jnp.exp(jax.random.uniform(ks[4], (L, DN_HEADS), f32, math.log(1e-3), math.log(1e-1)))
    return {
        "x": nrm(ks[0], (BATCH, SEQ, D_MODEL), 1.0),
        "w_in": nrm(ks[1], (L, D_MODEL, D_IN), D_MODEL ** -0.5),
        "dn_conv_w": nrm(ks[2], (L, DN_CONV, 3 * DN_WIDTH), DN_CONV ** -0.5),
        "dn_a_log": jnp.log(jax.random.uniform(ks[3], (L, DN_HEADS), f32, 1.0, 16.0)),
        "dn_dt_bias": dt + jnp.log(-jnp.expm1(-dt)),
        "dn_norm_w": 1.0 + nrm(ks[5], (L, DN_HEAD_DIM), 0.02),
        "cmp_pos_k": nrm(ks[6], (L, NSA_CMP_LEN, NSA_HEAD_DIM), 0.1),
        "cmp_pos_v": nrm(ks[7], (L, NSA_CMP_LEN, NSA_HEAD_DIM), 0.1),
        "cmp_k_w1": nrm(ks[8], (L, cmp_in, NSA_CMP_HIDDEN), cmp_in ** -0.5),
        "cmp_k_w2": nrm(ks[9], (L, NSA_CMP_HIDDEN, NSA_HEAD_DIM), NSA_CMP_HIDDEN ** -0.5),
        "cmp_v_w1": nrm(ks[10], (L, cmp_in, NSA_CMP_HIDDEN), cmp_in ** -0.5),
        "cmp_v_w2": nrm(ks[11], (L, NSA_CMP_HIDDEN, NSA_HEAD_DIM), NSA_CMP_HIDDEN ** -0.5),
        "w_branch_dn": nrm(ks[12], (L, DN_WIDTH, D_MODEL), DN_WIDTH ** -0.5),
        "w_branch_nsa": nrm(ks[13], (L, NSA_WIDTH, D_MODEL), NSA_WIDTH ** -0.5),
        "w_out": nrm(ks[14], (L, D_MODEL, D_MODEL), D_MODEL ** -0.5 * DEEPNORM_BETA),
        "ln1_g": 1.0 + nrm(ks[15], (L, D_MODEL), 0.02),
        "ln1_b": nrm(ks[16], (L, D_MODEL), 0.02),
        "ffn_w_up": nrm(ks[17], (L, D_MODEL, 2 * D_FF), D_MODEL ** -0.5),
        "ffn_conv_w": nrm(ks[18], (L, FFN_CONV, 2 * D_FF), FFN_CONV ** -0.5),
        "ffn_conv_b": nrm(ks[19], (L, 2 * D_FF), 0.02),
        "ffn_w_down": nrm(ks[20], (L, D_FF, D_MODEL), D_FF ** -0.5 * DEEPNORM_BETA),
        "ln2_g": 1.0 + nrm(ks[21], (L, D_MODEL), 0.02),
        "ln2_b": nrm(ks[22], (L, D_MODEL), 0.02),
    }


def reference(x, w_in, dn_conv_w, dn_a_log, dn_dt_bias, dn_norm_w,
              cmp_pos_k, cmp_pos_v, cmp_k_w1, cmp_k_w2, cmp_v_w1, cmp_v_w2,
              w_branch_dn, w_branch_nsa, w_out, ln1_g, ln1_b,
              ffn_w_up, ffn_conv_w, ffn_conv_b, ffn_w_down, ln2_g, ln2_b):
    B, S, _ = x.shape
    f32 = jnp.float32
    split_at = np.cumsum(IN_SIZES)[:-1].tolist()
    for layer in range(DEPTH):
        h = x @ w_in[layer]
        (dq, dk, dv, dz, db, da, nq, kc, vc, ksl, vsl, kwn, vwn, ngate, mg_dn, mg_nsa) = \
            jnp.split(h, split_at, axis=-1)

        qkv = jax.nn.silu(causal_dwconv(jnp.concatenate([dq, dk, dv], -1), dn_conv_w[layer]))
        q, k, v = jnp.split(qkv, 3, axis=-1)

        def dn_heads(t):
            return t.reshape(B, S, DN_HEADS, DN_HEAD_DIM)

        q = l2norm(dn_heads(q)) * DN_HEAD_DIM ** -0.5
        k = l2norm(dn_heads(k))
        v = dn_heads(v).astype(f32)
        beta = jax.nn.sigmoid(db.astype(f32))
        g = -jnp.exp(dn_a_log[layer].astype(f32)) * jax.nn.softplus(
            da.astype(f32) + dn_dt_bias[layer].astype(f32))
        o = gated_delta_rule(q, k, v, g, beta)
        o = o * lax.rsqrt(jnp.mean(o * o, -1, keepdims=True) + RMS_EPS)
        o = o * dn_norm_w[layer] * jax.nn.silu(dn_heads(dz).astype(f32))
        y_dn = o.reshape(B, S, DN_WIDTH).astype(x.dtype)

        def kv_heads(t):
            return t.reshape(B, S, NSA_KV_HEADS, NSA_HEAD_DIM)

        y_nsa = nsa_attention(nq.reshape(B, S, NSA_HEADS, NSA_HEAD_DIM),
                              kv_heads(kc), kv_heads(vc), kv_heads(ksl), kv_heads(vsl),
                              kv_heads(kwn), kv_heads(vwn), ngate,
                              cmp_pos_k[layer], cmp_pos_v[layer], cmp_k_w1[layer], cmp_k_w2[layer],
                              cmp_v_w1[layer], cmp_v_w2[layer]).astype(x.dtype)

        y = (jax.nn.sigmoid(mg_dn) * (y_dn @ w_branch_dn[layer])
             + jax.nn.sigmoid(mg_nsa) * (y_nsa @ w_branch_nsa[layer]))
        x = layer_norm(DEEPNORM_ALPHA * x + y @ w_out[layer], ln1_g[layer], ln1_b[layer])

        u = causal_dwconv(x @ ffn_w_up[layer], ffn_conv_w[layer]) + ffn_conv_b[layer]
        u_gate, u_val = jnp.split(u, 2, axis=-1)
        f = (jax.nn.silu(u_gate) * u_val) @ ffn_w_down[layer]
        x = layer_norm(DEEPNORM_ALPHA * x + f, ln2_g[layer], ln2_b[layer])
    return x
```

```python
import numpy as np
import concourse.bass as bass
import concourse.mybir as mybir
from concourse.bass_utils import run_bass_kernel_spmd
from contextlib import ExitStack

F32 = mybir.dt.float32
BF16 = mybir.dt.bfloat16
AF = mybir.ActivationFunctionType
ALU = mybir.AluOpType
AX = mybir.AxisListType

ENGS = ["sync", "scalar", "vector", "gpsimd", "tensor"]
DMA_K = 12

SEQ = 4096
NT = 32
D = 1024
HALF = 2048
RMS_EPS = 1e-6
LN_EPS = 1e-5
ALPHA = 2.0 ** 0.25


class Buf:
    __slots__ = ("name", "last_w", "rd_c", "rd_d")

    def __init__(self, name):
        self.name = name
        self.last_w = None
        self.rd_c = {}
        self.rd_d = []


class Op:
    __slots__ = ("eng", "fn", "deps", "needed", "count", "dma", "sem", "k")


class Sched:
    def __init__(self, nc, stack):
        self.nc = nc
        self.ops = {e: [] for e in ENGS}
        self.ndma = {e: 0 for e in ENGS}
        self.dma_ops = {e: [] for e in ENGS}
        self.csem = {e: stack.enter_context(nc.semaphore("c_" + e)) for e in ENGS}
        self.dsem = {e: [stack.enter_context(nc.semaphore("d_%s_%d" % (e, i))) for i in range(DMA_K)]
                     for e in ("sync", "scalar", "gpsimd")}
        self.nbuf = 0

    def buf(self, name=None):
        self.nbuf += 1
        return Buf(name or "b%d" % self.nbuf)

    def add(self, eng, fn, reads=(), writes=(), dma=False):
        op = Op()
        op.eng = eng
        op.fn = fn
        op.dma = dma
        op.needed = False
        op.count = None
        op.sem = None
        deps = []

        def dep(d, raw):
            if d is None:
                return
            if (not raw) and (not dma) and (not d.dma) and d.eng == eng:
                return
            if eng == "tensor" and d.eng == "tensor" and not d.dma and not dma:
                return
            deps.append(d)

        for b in reads:
            dep(b.last_w, True)
        for b in writes:
            dep(b.last_w, False)
            for r in b.rd_c.values():
                dep(r, False)
            for r in b.rd_d:
                dep(r, False)
        if dma:
            j = self.ndma[eng]
            self.ndma[eng] += 1
            op.k = j
            if j >= DMA_K:
                deps.append(self.dma_ops[eng][j - DMA_K])
            self.dma_ops[eng].append(op)
        seen = set()
        dd = []
        for d in deps:
            if id(d) not in seen:
                seen.add(id(d))
                dd.append(d)
                d.needed = True
        op.deps = dd
        for b in reads:
            if dma:
                b.rd_d.append(op)
            else:
                b.rd_c[eng] = op
        for b in writes:
            b.last_w = op
            b.rd_c = {}
            b.rd_d = []
        self.ops[eng].append(op)
        return op

    def barrier(self):
        lasts = []
        for e in ENGS:
            for op in reversed(self.ops[e]):
                if (not op.dma) and op.fn is not None:
                    lasts.append(op)
                    break
            lasts.extend(self.dma_ops[e][-DMA_K:])
        for d in lasts:
            d.needed = True
        for e in ENGS:
            op = Op()
            op.eng = e
            op.fn = None
            op.dma = False
            op.needed = False
            op.count = None
            op.sem = None
            op.deps = list(lasts)
            self.ops[e].append(op)

    def dma(self, q, out, in_, reads=(), writes=(), **kw):
        return self.add(q, lambda e: e.dma_start(out=out, in_=in_, **kw), reads, writes, dma=True)

    def finalize(self, final_deps=()):
        nc = self.nc
        for e in ENGS:
            c = 0
            for op in self.ops[e]:
                if op.dma:
                    op.sem = self.dsem[e][op.k % DMA_K]
                    op.count = 16 * (op.k // DMA_K + 1)
                    op.needed = True
                elif op.needed:
                    c += 1
                    op.sem = self.csem[e]
                    op.count = c
        ops = self.ops

        def emit(e, eng):
            waited = {}
            for op in ops[e]:
                for d in op.deps:
                    key = id(d.sem)
                    if waited.get(key, 0) < d.count:
                        eng.wait_ge(d.sem, d.count)
                        waited[key] = d.count
                if op.fn is None:
                    continue
                inst = op.fn(eng)
                if op.needed:
                    inst.then_inc(op.sem, 16 if op.dma else 1)
            if e == "sync":
                for d in final_deps:
                    eng.wait_ge(d.sem, d.count)

        with nc.Block() as block:
            @block.sync
            def _(eng):
                emit("sync", eng)

            @block.scalar
            def _(eng):
                emit("scalar", eng)

            @block.vector
            def _(eng):
                emit("vector", eng)

            @block.gpsimd
            def _(eng):
                emit("gpsimd", eng)

            @block.tensor
            def _(eng):
                emit("tensor", eng)


class TT:
    def __init__(self, C, st, name, shape, dt=F32):
        self.t = st.enter_context(C.nc.sbuf_tensor("s_" + name, shape, dt))
        self.b = C.S.buf(name)


class Ctx:
    pass


CONST_COLS = {}
NC2 = {}


def _build_consts():
    cols = []
    off = [0]

    def put(name, arr):
        arr = np.asarray(arr, np.float32).reshape(128, -1)
        CONST_COLS[name] = (off[0], arr.shape[1])
        off[0] += arr.shape[1]
        cols.append(arr)

    p = np.arange(128)[:, None]
    f = np.arange(128)[None, :]
    same = (p // 64) == (f // 64)
    put("ident", (p == f))
    put("ones", np.ones((128, 128)))
    put("tri", (p <= f) & same)
    put("blk", same)
    put("strict", (f < p) & same)
    put("upper", (f >= p) & same)
    NEG = -30000.0
    pp = np.arange(128)[:, None]
    ff = np.arange(512)[None, :]
    cm = [np.where(pp + 128 * d <= ff, 0.0, NEG) for d in range(4)]
    NC2.__setitem__("cmask", np.concatenate(cm, axis=1).astype(np.float32))
    wm = []
    for k in range(8):
        rel = ff - pp - 128 * (k - 4)
        wm.append(np.where((rel >= 0) & (rel < 512), 0.0, NEG))
    NC2.__setitem__("wmask", np.concatenate(wm, axis=1).astype(np.float32))
    pm = [np.where(16 * pp + 31 - ff <= 512 * d, 0.0, NEG) for d in range(5)]
    NC2.__setitem__("pm_list", [p_.astype(np.float32) for p_ in pm])
    tok0 = np.arange(256) * 16
    cend = tok0 + 31
    sst = np.arange(64) * 64
    ov = ((tok0[:, None] <= (sst + 63)[None]) & (cend[:, None] >= sst[None])).astype(np.float32)
    ov[255] = 0.0
    NC2.__setitem__("ovl", np.ascontiguousarray(np.concatenate([ov[0:128], ov[128:256]], axis=1)))
    eb = np.zeros((128, 4096), np.float32)
    eb[np.arange(4096) // 64, np.arange(4096)] = 1.0
    NC2.__setitem__("ebig", eb)
    gs = np.zeros((128, 12 * 128), np.float32)
    for r in range(12):
        gs[r, r * 128:(r + 1) * 128] = 1.0
    NC2.__setitem__("gsel", gs)
    return np.concatenate(cols, axis=1)


CONSTS = _build_consts()


def _rope_tables(off=0):
    half = 16
    inv = 500000.0 ** (-np.arange(half, dtype=np.float32) / half)
    def tab(pos):
        ang = pos.astype(np.float32)[None, :] * inv[:, None]
        c = np.cos(ang).astype(np.float32)
        sn = np.sin(ang).astype(np.float32)
        P = ang.shape[1]
        return (np.concatenate([c, c, np.ones((96, P), np.float32)], 0),
                np.concatenate([-sn, sn, np.zeros((96, P), np.float32)], 0))
    c1, s1 = tab(np.maximum(np.arange(4096) - off, 0))
    pk = np.maximum(np.arange(256) * 16 + 31 - off, 0)
    c2, s2 = tab(pk)
    return np.ascontiguousarray(c1), np.ascontiguousarray(s1), np.ascontiguousarray(c2), np.ascontiguousarray(s2)


def _sel_tables(off=0):
    t = np.arange(4096)
    cur = t // 64
    jb = np.arange(64)
    j0 = off // 64
    valid = (jb[None] <= cur[:, None]) & (jb[None] >= j0)
    forced = valid & ((jb[None] == j0) | (jb[None] == cur[:, None]) | (jb[None] == cur[:, None] - 1))
    bias = np.where(forced, 10.0, np.where(valid, 0.0, -1e9)).astype(np.float32)
    return bias, valid.astype(np.float32)


def _rot_perm(ncols_heads):
    idx = []
    for h in range(ncols_heads):
        for d in range(128):
            idx.append(h * 128 + (d + 16 if d < 16 else (d - 16 if d < 32 else d)))
    return np.array(idx)


def cst(C, name, dt="f"):
    o, n = CONST_COLS[name]
    t = C.cf if dt == "f" else C.cb
    return t.t[:, o:o + n]


def build(debug=None):
    nc = bass.Bass("TRN2", target_bir_lowering=False)
    C = Ctx()
    C.nc = nc
    I = {}

    def inp(name, shape):
        I[name] = nc.dram_tensor(name, list(shape), F32, kind="ExternalInput").ap()
        return I[name]

    inp("xT", [D, SEQ])
    inp("consts", [128, CONSTS.shape[1]])
    inp("selm", [128, 3])
    inp("wg_qkv", [D, 3072])
    inp("wg_z", [D, 1024])
    inp("wg_ba", [D, 16])
    inp("convw", [128, 24 * 4])
    inp("a_log", [1, 8])
    inp("dt_bias", [1, 8])
    inp("dn_norm_w", [1, 128])
    for (nm, shp) in NSA_INPUTS:
        inp(nm, shp)
    for (nm, shp) in P2_INPUTS:
        inp(nm, shp)
    C.xTb = nc.dram_tensor("xTb_scratch", [D, SEQ], BF16).ap()
    outs = {}
    if debug == "nsa":
        outs["ynsa"] = nc.dram_tensor("ynsa", [128, 8 * HALF], F32, kind="ExternalOutput").ap()
    if debug == "gdn":
        outs["ydn"] = nc.dram_tensor("ydn", [128, 8 * HALF], F32, kind="ExternalOutput").ap()

    with ExitStack() as st:
        S = Sched(nc, st)
        C.S = S
        C.st = st
        C.xTb_b = S.buf("xTb")
        C.wsc_b = S.buf("wscratch")
        C.wsc = {
            "dn": nc.dram_tensor("sc_dn", [8, 128, 8, 128], BF16).ap(),
            "ns": nc.dram_tensor("sc_ns", [8, 128, 8, 128], BF16).ap(),
            "m": nc.dram_tensor("sc_m", [16, 128, 8, 128], BF16).ap(),
            "out": nc.dram_tensor("sc_out", [128, 8, 1024], BF16).ap(),
            "up": nc.dram_tensor("sc_up", [44, 128, 8, 128], BF16).ap(),
            "down": nc.dram_tensor("sc_down", [2, 128, 22, 512], BF16).ap(),
        }
        C.ps = [st.enter_context(nc.psum_tensor("ps%d" % i, [128, 512], F32)) for i in range(8)]
        C.psb = [S.buf("ps%d" % i) for i in range(8)]
        C.pi = 0
        C.nrot = 8

        def pb():
            i = C.pi % C.nrot
            C.pi = (i + 1) % C.nrot
            return C.ps[i], C.psb[i]

        C.pb = pb
        C.cf = TT(C, st, "cf", [128, CONSTS.shape[1]])
        C.cb = TT(C, st, "cb", [128, CONSTS.shape[1]], BF16)
        S.dma("sync", C.cf.t[:], I["consts"], writes=[C.cf.b])
        S.add("vector", lambda e: e.tensor_copy(out=C.cb.t[:], in_=C.cf.t[:]), [C.cf.b], [C.cb.b])
        C.selm = TT(C, st, "selm", [128, 3])
        S.dma("sync", C.selm.t[:], I["selm"], writes=[C.selm.b])
        C.eps = TT(C, st, "eps", [128, 3])
        S.add("gpsimd", lambda e: e.memset(C.eps.t[:, 2:3], 128.0 * RMS_EPS), [], [C.eps.b])
        S.add("gpsimd", lambda e: e.memset(C.eps.t[:, 0:1], RMS_EPS), [], [C.eps.b])
        S.add("gpsimd", lambda e: e.memset(C.eps.t[:, 1:2], LN_EPS), [], [C.eps.b])
        C.ymdn = TT(C, st, "ymdn", [128, 8, HALF], BF16)
        C.yhdn = TT(C, st, "yhdn", [128, 8, 128], BF16)
        S.add("gpsimd", lambda e: e.memset(C.ymdn.t[:], 0.0), [], [C.ymdn.b])

        fin = []
        C.have_xTb = debug not in ("nsa", "p2")
        if debug not in ("nsa", "p2"):
            stage_gdn(C, I)
        S.barrier()
        C.ymnsa = TT(C, st, "ymnsa", [128, 8, HALF], BF16)
        C.yhnsa = TT(C, st, "yhnsa", [128, 8, 128], BF16)
        S.add("gpsimd", lambda e: e.memset(C.ymnsa.t[:], 0.0), [], [C.ymnsa.b])
        if debug == "p2":
            S.add("gpsimd", lambda e: e.memset(C.yhdn.t[:], 0.0), [], [C.yhdn.b])
            S.add("gpsimd", lambda e: e.memset(C.yhnsa.t[:], 0.0), [], [C.yhnsa.b])
        if debug not in ("gdn", "p2"):
            stage_nsa(C, I)
        if debug == "nsa":
            with ExitStack() as s2:
                for h in range(8):
                    tmp = TT(C, s2, "dbgn%d" % h, [128, HALF])
                    S.add("vector", lambda e, tmp=tmp, h=h: e.tensor_copy(out=tmp.t[:], in_=C.ymnsa.t[:, h, :]), [C.ymnsa.b], [tmp.b])
                    fin.append(S.dma("sync", outs["ynsa"][:, h * HALF:(h + 1) * HALF], tmp.t[:], reads=[tmp.b]))
                S.finalize(final_deps=fin)
            return nc
        if debug == "gdn":
            with ExitStack() as s2:
                for h in range(8):
                    tmp = TT(C, s2, "dbg%d" % h, [128, HALF])
                    S.add("vector", lambda e, tmp=tmp, h=h: e.tensor_copy(out=tmp.t[:], in_=C.ymdn.t[:, h, :]), [C.ymdn.b], [tmp.b])
                    fin.append(S.dma("sync", outs["ydn"][:, h * HALF:(h + 1) * HALF], tmp.t[:], reads=[tmp.b]))
                S.finalize(final_deps=fin)
            return nc
        out_ap = nc.dram_tensor("out", [HALF, D], F32, kind="ExternalOutput").ap()
        if debug == "all":
            dd = {"ydn": nc.dram_tensor("ydn", [128, 8 * HALF], F32, kind="ExternalOutput").ap(),
                  "ynsa": nc.dram_tensor("ynsa", [128, 8 * HALF], F32, kind="ExternalOutput").ap()}
            with ExitStack() as s2:
                tmps = [TT(C, s2, "dbga%d" % i, [128, HALF]) for i in range(2)]
                k = 0
                for nm, src in (("ydn", C.ymdn), ("ynsa", C.ymnsa)):
                    for h in range(8):
                        tmp = tmps[k % 2]
                        k += 1
                        S.add("vector", lambda e, tmp=tmp, h=h, src=src: e.tensor_copy(out=tmp.t[:], in_=src.t[:, h, :]), [src.b], [tmp.b])
                        fin.append(S.dma("sync", dd[nm][:, h * HALF:(h + 1) * HALF], tmp.t[:], reads=[tmp.b]))
        stage_p2(C, I, out_ap, fin)
        S.finalize(final_deps=fin)
    return nc


def stage_gdn(C, I):
    nc, S, pb = C.nc, C.S, C.pb
    C.nrot = 6
    C.pi = 0
    with ExitStack() as st:
        A = lambda name, shape, dt=F32: TT(C, st, name, shape, dt)
        ident_b = cst(C, "ident", "b")
        ones_b = cst(C, "ones", "b")
        ones_f = cst(C, "ones")
        cbf = [C.cf.b]
        cbb = [C.cb.b]
        Wqkv = A("Wqkv", [128, 8, 3072], BF16)
        Wz = A("Wz", [128, 8, 1024], BF16)
        Wba = A("Wba", [128, 8, 16], BF16)
        stg = [A("stg0", [128, 8, 128])] * 2
        xf = [stg[0]]
        k = 0
        for (src, dst, ncol) in ((I["wg_qkv"], Wqkv, 3072), (I["wg_z"], Wz, 1024)):
            for c0 in range(0, ncol, 128):
                sg = stg[k % 2]
                S.dma("sync", sg.t[:], src[:, c0:c0 + 128].rearrange("(kc p) c -> p kc c", p=128), writes=[sg.b])
                eng = "gpsimd" if k % 2 else "vector"
                S.add(eng, lambda e, sg=sg, dst=dst, c0=c0: e.tensor_copy(out=dst.t[:, :, c0:c0 + 128], in_=sg.t[:]), [sg.b], [dst.b])
                k += 1
        sg = stg[0]
        S.dma("sync", sg.t[:, :, 0:16], I["wg_ba"].rearrange("(kc p) c -> p kc c", p=128), writes=[sg.b])
        S.add("vector", lambda e: e.tensor_copy(out=Wba.t[:], in_=sg.t[:, :, 0:16]), [sg.b], [Wba.b])
        cw = A("cw", [128, 96])
        S.dma("sync", cw.t[:], I["convw"], writes=[cw.b])
        Dg = A("Dg", [128, 96, 128], BF16)
        for j in range(96):
            eng = "gpsimd" if j % 2 else "vector"
            S.add(eng, lambda e, j=j: e.tensor_scalar(out=Dg.t[:, j, :], in0=cst(C, "ident"), scalar1=cw.t[:, j:j + 1], scalar2=None, op0=ALU.mult),
                  [cw.b] + cbf, [Dg.b])
        sp = A("sp", [128, 16 + 128])
        S.dma("sync", sp.t[:, 0:8], I["a_log"].partition_broadcast(128), writes=[sp.b])
        S.dma("sync", sp.t[:, 8:16], I["dt_bias"].partition_broadcast(128), writes=[sp.b])
        S.dma("sync", sp.t[:, 16:144], I["dn_norm_w"].partition_broadcast(128), writes=[sp.b])
        negA = A("negA", [128, 8])
        S.add("scalar", lambda e: e.activation(out=negA.t[:], in_=sp.t[:, 0:8], func=AF.Exp), [sp.b], [negA.b])
        S.add("vector", lambda e: e.tensor_scalar(out=negA.t[:], in0=negA.t[:], scalar1=-1.0, scalar2=None, op0=ALU.mult), [negA.b], [negA.b])
        nw4 = A("nw4", [128, 4, 128])
        for h in range(4):
            S.add("vector", lambda e, h=h: e.tensor_copy(out=nw4.t[:, h, :], in_=sp.t[:, 16:144]), [sp.b], [nw4.b])

        xf = stg
        xb = [A("xb%d" % i, [128, 8, 128], BF16) for i in range(1)]
        pre = [A("pre%d" % i, [128, 24, 131], BF16) for i in range(1)]
        S.add("gpsimd", lambda e: e.memset(pre[0].t[:], 0.0), [], [pre[0].b])
        actl = [A("act%d" % i, [128, 4, 128]) for i in range(2)]
        sql = [A("sq%d" % i, [128, 4, 128], BF16) for i in range(2)]
        rinv = A("rinv", [128, 4, 128])
        qT = A("qT", [128, 8, 128], BF16)
        kT = A("kT", [128, 8, 128], BF16)
        vT = A("vT", [128, 8, 128], BF16)
        kbeg = A("kbeg", [128, 8, 128], BF16)
        vb = A("vb", [128, 8, 128], BF16)
        kd = A("kd", [128, 8, 128], BF16)
        sm = A("sm", [128, 16 * 8])
        SM = lambda i: sm.t[:, i * 8:(i + 1) * 8]
        triG = A("triG", [128, 8, 128])
        Sf = A("Sf", [128, 8, 128])
        Sb = A("Sb", [128, 8, 128], BF16)
        Sfb = [S.buf("Sf0"), S.buf("Sf1")]
        Sbb = [S.buf("Sb0"), S.buf("Sb1")]
        S.add("gpsimd", lambda e: e.memset(Sf.t[:], 0.0), [], Sfb)
        S.add("gpsimd", lambda e: e.memset(Sb.t[:], 0.0), [], Sbb)
        hbT = []
        for hb in range(2):
            d = {}
            for (nm, shp, dt) in (("d1", [128, 4, 128], F32), ("Ds", [128, 4, 128], BF16), ("DT", [128, 4, 128], BF16),
                                  ("egbc", [128, 4, 128], F32),
                                  ("XA", [128, 4, 128], BF16), ("XB", [128, 4, 128], BF16),
                                  ("TXA", [128, 2, 4, 128], BF16), ("TXB", [128, 2, 4, 128], BF16),
                                  ("u", [128, 4, 128], BF16), ("wT", [128, 4, 128], BF16), ("qkT", [128, 4, 128], BF16),
                                  ("qdT", [128, 4, 128], BF16), ("vnew", [128, 4, 128], BF16),
                                  ("ss", [128, 4], F32),
                                  ):
                d[nm] = A("%s_%d" % (nm, hb), shp, dt)
            d["a1"] = d["d1"]
            d["X0"] = d["XB"]
            d["zs"] = d["Ds"]
            d["ytok"] = d["vnew"]
            d["osq"] = d["d1"]
            d["y1"] = d["egbc"]
            hbT.append(d)

        xT3 = I["xT"].rearrange("(kc p) t -> p kc t", p=128)

        for t in range(NT):
            xfi, xbi, pr = xf[t % 2], xb[0], pre[0]
            if t > 0:
                S.add("gpsimd", lambda e, pr=pr: e.tensor_copy(out=pr.t[:, :, 0:3], in_=pr.t[:, :, 128:131]), [pr.b], [pr.b])
            S.dma("sync", xfi.t[:], xT3[:, :, t * 128:(t + 1) * 128], writes=[xfi.b])
            S.add("gpsimd", lambda e, xfi=xfi, xbi=xbi: e.tensor_copy(out=xbi.t[:], in_=xfi.t[:]), [xfi.b], [xbi.b])
            S.dma("sync", C.xTb.rearrange("(kc p) t -> p kc t", p=128)[:, :, t * 128:(t + 1) * 128], xbi.t[:], reads=[xbi.b], writes=[C.xTb_b])
            for cg in range(6):
                ps, psb = pb()
                for j in range(4):
                    ct = cg * 4 + j
                    for kc in range(8):
                        S.add("tensor", lambda e, ps=ps, j=j, ct=ct, kc=kc, xbi=xbi: e.matmul(
                            ps[:, j * 128:(j + 1) * 128], lhsT=Wqkv.t[:, kc, ct * 128:(ct + 1) * 128], rhs=xbi.t[:, kc, :],
                            start=(kc == 0), stop=(kc == 7)), [Wqkv.b, xbi.b], [psb])
                S.add("scalar", lambda e, ps=ps, cg=cg, pr=pr: e.activation(
                    out=pr.t[:, cg * 4:(cg + 1) * 4, 3:131], in_=ps[:].rearrange("p (a b) -> p a b", a=4), func=AF.Copy), [psb], [pr.b])
            for cg in range(6):
                ps, psb = pb()
                for j in range(4):
                    ct = cg * 4 + j
                    for i in range(4):
                        S.add("tensor", lambda e, ps=ps, j=j, ct=ct, i=i, pr=pr: e.matmul(
                            ps[:, j * 128:(j + 1) * 128], lhsT=Dg.t[:, ct * 4 + i, :], rhs=pr.t[:, ct, i:i + 128],
                            start=(i == 0), stop=(i == 3)), [Dg.b, pr.b], [psb])
                if cg < 4:
                    act, sq = actl[cg % 2], sql[cg % 2]
                    S.add("scalar", lambda e, ps=ps, act=act: e.activation(
                        out=act.t[:], in_=ps[:].rearrange("p (a b) -> p a b", a=4), func=AF.Silu), [psb], [act.b])
                    S.add("gpsimd", lambda e, act=act, sq=sq: e.tensor_tensor(out=sq.t[:], in0=act.t[:], in1=act.t[:], op=ALU.mult), [act.b], [sq.b])
                    ps, psb = pb()
                    S.add("tensor", lambda e, ps=ps, sq=sq: e.matmul(ps[:], lhsT=ones_b, rhs=sq.t[:].rearrange("p a b -> p (a b)"),
                                                                     start=True, stop=True), [sq.b] + cbb, [psb])
                    S.add("scalar", lambda e, ps=ps: e.activation(out=rinv.t[:].rearrange("p a b -> p (a b)"), in_=ps[:],
                                                                  func=AF.Sqrt, bias=C.eps.t[:, 0:1], scale=1.0), [psb, C.eps.b], [rinv.b])
                    S.add("vector", lambda e: e.reciprocal(out=rinv.t[:], in_=rinv.t[:]), [rinv.b], [rinv.b])
                    if cg < 2:
                        S.add("vector", lambda e, cg=cg, act=act: e.scalar_tensor_tensor(out=qT.t[:, cg * 4:(cg + 1) * 4, :], in0=act.t[:], scalar=128.0 ** -0.5, in1=rinv.t[:],
                                                                                         op0=ALU.mult, op1=ALU.mult), [act.b, rinv.b], [qT.b])
                    else:
                        S.add("vector", lambda e, cg=cg, act=act: e.tensor_tensor(out=kT.t[:, (cg - 2) * 4:(cg - 1) * 4, :], in0=act.t[:], in1=rinv.t[:], op=ALU.mult), [act.b, rinv.b], [kT.b])
                else:
                    S.add("scalar", lambda e, ps=ps, cg=cg: e.activation(
                        out=vT.t[:, (cg - 4) * 4:(cg - 3) * 4, :], in_=ps[:].rearrange("p (a b) -> p a b", a=4), func=AF.Silu), [psb], [vT.b])
            ps, psb = pb()
            for kc in range(8):
                S.add("tensor", lambda e, ps=ps, kc=kc, xbi=xbi: e.matmul(ps[:, 0:16], lhsT=xbi.t[:, kc, :], rhs=Wba.t[:, kc, :],
                                                                          start=(kc == 0), stop=(kc == 7)), [Wba.b, xbi.b], [psb])
            smb = [sm.b]
            S.add("scalar", lambda e, ps=ps: e.activation(out=SM(0), in_=ps[:, 0:8], func=AF.Sigmoid), [psb], smb)
            S.add("vector", lambda e, ps=ps: e.tensor_tensor(out=SM(1), in0=ps[:, 8:16], in1=sp.t[:, 8:16], op=ALU.add), [psb, sp.b], smb)
            S.add("vector", lambda e: e.tensor_scalar(out=SM(2), in0=SM(1), scalar1=-1.0, scalar2=None, op0=ALU.mult), smb, smb)
            S.add("vector", lambda e: e.tensor_tensor(out=SM(2), in0=SM(2), in1=SM(1), op=ALU.max), smb, smb)
            S.add("scalar", lambda e: e.activation(out=SM(3), in_=SM(2), func=AF.Exp, scale=-1.0), smb, smb)
            S.add("scalar", lambda e: e.activation(out=SM(4), in_=SM(3), func=AF.Ln, bias=ones_f[:, 0:1], scale=1.0), smb + cbf, smb)
            S.add("vector", lambda e: e.tensor_scalar(out=SM(1), in0=SM(1), scalar1=0.0, scalar2=None, op0=ALU.max), smb, smb)
            S.add("vector", lambda e: e.tensor_tensor(out=SM(4), in0=SM(4), in1=SM(1), op=ALU.add), smb, smb)
            S.add("vector", lambda e: e.tensor_tensor(out=SM(5), in0=SM(4), in1=negA.t[:], op=ALU.mult), smb + [negA.b], smb)
            ps, psb = pb()
            S.add("tensor", lambda e, ps=ps: e.matmul(ps[:, 0:8], lhsT=cst(C, "tri"), rhs=SM(5), start=True, stop=True), smb + cbf, [psb])
            S.add("tensor", lambda e, ps=ps: e.matmul(ps[:, 8:16], lhsT=cst(C, "blk"), rhs=SM(5), start=True, stop=True), smb + cbf, [psb])
            S.add("vector", lambda e, ps=ps: e.tensor_copy(out=sm.t[:, 48:64], in_=ps[:, 0:16]), [psb], smb)
            S.add("scalar", lambda e: e.activation(out=SM(8), in_=SM(6), func=AF.Exp), smb, smb)
            S.add("vector", lambda e: e.tensor_tensor(out=SM(9), in0=SM(8), in1=SM(0), op=ALU.mult), smb, smb)
            S.add("vector", lambda e: e.tensor_tensor(out=SM(10), in0=SM(7), in1=SM(6), op=ALU.subtract), smb, smb)
            S.add("scalar", lambda e: e.activation(out=SM(10), in_=SM(10), func=AF.Exp), smb, smb)
            S.add("vector", lambda e: e.tensor_scalar(out=SM(11), in0=SM(0), scalar1=-1.0, scalar2=None, op0=ALU.mult), smb, smb)
            S.add("vector", lambda e: e.tensor_tensor(out=triG.t[:], in0=cst(C, "tri").unsqueeze(1).to_broadcast([128, 8, 128]),
                                                      in1=SM(5).unsqueeze(2).to_broadcast([128, 8, 128]), op=ALU.mult), smb + cbf, [triG.b])
            ps, psb = pb()
            pk = ps[:].bitcast(BF16)
            for h in range(8):
                S.add("tensor", lambda e, pk=pk, h=h: e.transpose(out=pk[:, h * 128:(h + 1) * 128], in_=kT.t[:, h, :], identity=ident_b), [kT.b] + cbb, [psb])
            pk3 = pk.rearrange("p (a b) -> p a b", a=8)
            S.add("vector", lambda e, pk3=pk3: e.tensor_tensor(out=kbeg.t[:], in0=pk3, in1=SM(9).unsqueeze(2).to_broadcast([128, 8, 128]), op=ALU.mult), [psb] + smb, [kbeg.b])
            S.add("vector", lambda e, pk3=pk3: e.tensor_tensor(out=kd.t[:], in0=pk3, in1=SM(10).unsqueeze(2).to_broadcast([128, 8, 128]), op=ALU.mult), [psb] + smb, [kd.b])
            ps, psb = pb()
            pv = ps[:].bitcast(BF16)
            for h in range(8):
                S.add("tensor", lambda e, pv=pv, h=h: e.transpose(out=pv[:, h * 128:(h + 1) * 128], in_=vT.t[:, h, :], identity=ident_b), [vT.b] + cbb, [psb])
            pv3 = pv.rearrange("p (a b) -> p a b", a=8)
            S.add("vector", lambda e, pv3=pv3: e.tensor_tensor(out=vb.t[:], in0=pv3, in1=SM(0).unsqueeze(2).to_broadcast([128, 8, 128]), op=ALU.mult), [psb] + smb, [vb.b])

            def hb_gen(hb, t=t, xbi=xbi):
                W = hbT[hb]
                h0 = hb * 4
                r3 = lambda ap: ap.rearrange("p (a b) -> p a b", a=4)
                psg, psgb = pb()
                S.add("tensor", lambda e, psg=psg, h0=h0: e.matmul(psg[:], lhsT=ones_f, rhs=triG.t[:, h0:h0 + 4, :].rearrange("p a b -> p (a b)"), start=True, stop=True),
                      [triG.b] + cbf, [psgb])
                S.add("scalar", lambda e, psg=psg, W=W: e.activation(out=W["egbc"].t[:], in_=r3(psg[:]), func=AF.Exp), [psgb], [W["egbc"].b])
                S.add("vector", lambda e, psg=psg, W=W, h0=h0: e.tensor_tensor(out=W["d1"].t[:], in0=r3(psg[:]),
                                                                               in1=sm.t[:, 48 + h0:48 + h0 + 4].unsqueeze(2).to_broadcast([128, 4, 128]), op=ALU.subtract),
                      [psgb] + smb, [W["d1"].b])
                S.add("scalar", lambda e, W=W: e.activation(out=W["Ds"].t[:], in_=W["d1"].t[:], func=AF.Exp, scale=-1.0), [W["d1"].b], [W["Ds"].b])
                S.add("scalar", lambda e, W=W: e.activation(out=W["DT"].t[:], in_=W["d1"].t[:], func=AF.Exp), [W["d1"].b], [W["DT"].b])
                S.add("vector", lambda e, W=W: e.scalar_tensor_tensor(out=W["Ds"].t[:], in0=W["Ds"].t[:], scalar=1.0, in1=cst(C, "strict", "b").unsqueeze(1).to_broadcast([128, 4, 128]),
                                                                     op0=ALU.min, op1=ALU.mult), [W["Ds"].b] + cbb, [W["Ds"].b])
                S.add("vector", lambda e, W=W: e.scalar_tensor_tensor(out=W["DT"].t[:], in0=W["DT"].t[:], scalar=1.0, in1=cst(C, "upper", "b").unsqueeze(1).to_broadcast([128, 4, 128]),
                                                                     op0=ALU.min, op1=ALU.mult), [W["DT"].b] + cbb, [W["DT"].b])
                yield
                ps, psb = pb()
                for h in range(4):
                    S.add("tensor", lambda e, ps=ps, h=h, h0=h0: e.matmul(ps[:, h * 128:(h + 1) * 128], lhsT=kT.t[:, h0 + h, :], rhs=kT.t[:, h0 + h, :], start=True, stop=True),
                          [kT.b], [psb])
                S.add("vector", lambda e, ps=ps, W=W: e.tensor_tensor(out=W["a1"].t[:], in0=r3(ps[:]), in1=W["Ds"].t[:], op=ALU.mult), [psb, W["Ds"].b], [W["a1"].b])
                S.add("vector", lambda e, W=W, h0=h0: e.tensor_tensor(out=W["X0"].t[:], in0=W["a1"].t[:],
                                                                      in1=sm.t[:, 88 + h0:88 + h0 + 4].unsqueeze(2).to_broadcast([128, 4, 128]), op=ALU.mult),
                      [W["a1"].b] + smb, [W["X0"].b])
                yield
                ps, psb = pb()
                pt = ps[:].bitcast(BF16)
                for h in range(4):
                    S.add("tensor", lambda e, pt=pt, h=h, W=W: e.transpose(out=pt[:, h * 128:(h + 1) * 128], in_=W["X0"].t[:, h, :], identity=ident_b), [W["X0"].b] + cbb, [psb])
                S.add("scalar", lambda e, pt=pt, W=W: e.activation(out=W["TXA"].t[:, 1, :, :], in_=pt[:, 0:512].rearrange("p (a b) -> p a b", a=4), func=AF.Copy), [psb], [W["TXA"].b])
                S.add("gpsimd", lambda e, W=W: e.tensor_copy(out=W["TXA"].t[:, 0, :, :], in_=ident_b.unsqueeze(1).to_broadcast([128, 4, 128])), cbb, [W["TXA"].b])
                yield
                Xc, TXc = W["X0"], W["TXA"]
                Xn_l = [W["XA"], W["XB"]]
                TXn_l = [W["TXB"], W["TXA"]]
                for lv in range(6):
                    Xn, TXn = Xn_l[lv % 2], TXn_l[lv % 2]
                    pa = [pb(), pb()]
                    for h in range(4):
                        ps, psb = pa[h // 2]
                        o4 = ps[:].rearrange("p (s a b) -> p s a b", s=2, a=2)
                        S.add("tensor", lambda e, o4=o4, h=h, Xc=Xc, TXc=TXc: e.matmul(o4[:, :, h % 2, :], lhsT=Xc.t[:, h, :], rhs=TXc.t[:, :, h, :], start=True, stop=True),
                              [Xc.b, TXc.b], [psb])
                    last = (lv == 5)
                    if not last:
                        psx, psxb = pb()
                        for h in range(4):
                            S.add("tensor", lambda e, psx=psx, h=h, Xc=Xc, TXc=TXc: e.matmul(psx[:, h * 128:(h + 1) * 128], lhsT=TXc.t[:, 1, h, :], rhs=Xc.t[:, h, :], start=True, stop=True),
                                  [Xc.b, TXc.b], [psxb])
                    for q in range(2):
                        ps, psb = pa[q]
                        o4 = ps[:].rearrange("p (s a b) -> p s a b", s=2, a=2)
                        S.add("vector", lambda e, o4=o4, q=q, TXc=TXc, TXn=TXn: e.tensor_tensor(out=TXn.t[:, 0, 2 * q:2 * q + 2, :], in0=o4[:, 0, :, :], in1=TXc.t[:, 0, 2 * q:2 * q + 2, :], op=ALU.add),
                              [psb, TXc.b], [TXn.b])
                        if not last:
                            S.add("scalar", lambda e, o4=o4, q=q, TXn=TXn: e.activation(out=TXn.t[:, 1, 2 * q:2 * q + 2, :], in_=o4[:, 1, :, :], func=AF.Copy), [psb], [TXn.b])
                    if not last:
                        S.add("scalar", lambda e, psx=psx, Xn=Xn: e.activation(out=Xn.t[:], in_=r3(psx[:]), func=AF.Copy), [psxb], [Xn.b])
                    Xc, TXc = Xn, TXn
                    yield
                Tt = TXc
                ps, psb = pb()
                for h in range(4):
                    S.add("tensor", lambda e, ps=ps, h=h, h0=h0, Tt=Tt: e.matmul(ps[:, h * 128:(h + 1) * 128], lhsT=Tt.t[:, 0, h, :], rhs=vb.t[:, h0 + h, :], start=True, stop=True),
                          [Tt.b, vb.b], [psb])
                S.add("scalar", lambda e, ps=ps, W=W: e.activation(out=W["u"].t[:], in_=r3(ps[:]), func=AF.Copy), [psb], [W["u"].b])
                ps, psb = pb()
                for h in range(4):
                    S.add("tensor", lambda e, ps=ps, h=h, h0=h0, Tt=Tt: e.matmul(ps[:, h * 128:(h + 1) * 128], lhsT=kbeg.t[:, h0 + h, :], rhs=Tt.t[:, 0, h, :], start=True, stop=True),
                          [Tt.b, kbeg.b], [psb])
                S.add("scalar", lambda e, ps=ps, W=W: e.activation(out=W["wT"].t[:], in_=r3(ps[:]), func=AF.Copy), [psb], [W["wT"].b])
                ps, psb = pb()
                for h in range(4):
                    S.add("tensor", lambda e, ps=ps, h=h, h0=h0: e.matmul(ps[:, h * 128:(h + 1) * 128], lhsT=kT.t[:, h0 + h, :], rhs=qT.t[:, h0 + h, :], start=True, stop=True),
                          [kT.b, qT.b], [psb])
                S.add("vector", lambda e, ps=ps, W=W: e.tensor_tensor(out=W["qkT"].t[:], in0=r3(ps[:]), in1=W["DT"].t[:], op=ALU.mult), [psb, W["DT"].b], [W["qkT"].b])
                S.add("gpsimd", lambda e, W=W, h0=h0: e.tensor_tensor(out=W["qdT"].t[:], in0=qT.t[:, h0:h0 + 4, :], in1=W["egbc"].t[:], op=ALU.mult), [qT.b, W["egbc"].b], [W["qdT"].b])
                need_out = t >= 15
                yield
                pso, psob = C.ps[6 + hb], C.psb[6 + hb]
                for c in range(2):
                    rs = slice(c * 64, (c + 1) * 64)
                    ps1, ps1b = pb()
                    for h in range(4):
                        S.add("tensor", lambda e, ps1=ps1, h=h, h0=h0, rs=rs, W=W: e.matmul(ps1[rs, h * 128:(h + 1) * 128], lhsT=W["wT"].t[:, h, rs], rhs=Sb.t[:, h0 + h, :], start=True, stop=True),
                              [W["wT"].b, Sbb[hb]], [ps1b])
                    S.add("vector", lambda e, ps1=ps1, rs=rs, W=W: e.tensor_tensor(out=W["vnew"].t[rs, :, :], in0=W["u"].t[rs, :, :], in1=r3(ps1[rs, :]), op=ALU.subtract),
                          [ps1b, W["u"].b], [W["vnew"].b])
                    yield
                    for h in (range(4) if need_out else ()):
                        S.add("tensor", lambda e, h=h, h0=h0, rs=rs, W=W: e.matmul(pso[rs, h * 128:(h + 1) * 128], lhsT=W["qdT"].t[:, h, rs], rhs=Sb.t[:, h0 + h, :], start=True, stop=False),
                              [W["qdT"].b, Sbb[hb]], [psob])
                        S.add("tensor", lambda e, h=h, rs=rs, W=W: e.matmul(pso[rs, h * 128:(h + 1) * 128], lhsT=W["qkT"].t[rs, h, rs], rhs=W["vnew"].t[rs, h, :], start=False, stop=True),
                              [W["qkT"].b, W["vnew"].b], [psob])
                    ps3, ps3b = pb()
                    for h in range(4):
                        S.add("tensor", lambda e, ps3=ps3, h=h, h0=h0, rs=rs, W=W: e.matmul(ps3[:, h * 128:(h + 1) * 128], lhsT=kd.t[rs, h0 + h, :], rhs=W["vnew"].t[rs, h, :], start=True, stop=True),
                              [kd.b, W["vnew"].b], [ps3b])
                    col = c * 64 + 63
                    S.add("vector", lambda e, h0=h0, col=col, W=W: e.tensor_tensor(out=Sf.t[:, h0:h0 + 4, :], in0=Sf.t[:, h0:h0 + 4, :],
                                                                                   in1=W["egbc"].t[:, :, col:col + 1].to_broadcast([128, 4, 128]), op=ALU.mult),
                          [Sfb[hb], W["egbc"].b], [Sfb[hb]])
                    S.add("vector", lambda e, ps3=ps3, h0=h0: e.tensor_tensor(out=Sf.t[:, h0:h0 + 4, :], in0=Sf.t[:, h0:h0 + 4, :], in1=r3(ps3[:]), op=ALU.add),
                          [Sfb[hb], ps3b], [Sfb[hb]])
                    S.add("scalar", lambda e, h0=h0: e.activation(out=Sb.t[:, h0:h0 + 4, :], in_=Sf.t[:, h0:h0 + 4, :], func=AF.Copy), [Sfb[hb]], [Sbb[hb]])
                    yield
                if not need_out:
                    return
                zc = hb
                ps, psb = pb()
                for kc in range(8):
                    S.add("tensor", lambda e, ps=ps, zc=zc, kc=kc, xbi=xbi: e.matmul(ps[:], lhsT=xbi.t[:, kc, :], rhs=Wz.t[:, kc, zc * 512:(zc + 1) * 512],
                                                                                      start=(kc == 0), stop=(kc == 7)), [Wz.b, xbi.b], [psb])
                S.add("scalar", lambda e, ps=ps, W=W: e.activation(out=W["zs"].t[:], in_=ps[:].rearrange("p (a b) -> p a b", a=4), func=AF.Silu), [psb], [W["zs"].b])
                S.add("gpsimd", lambda e, W=W: e.tensor_tensor(out=W["zs"].t[:], in0=W["zs"].t[:], in1=nw4.t[:], op=ALU.mult), [W["zs"].b, nw4.b], [W["zs"].b])
                S.add("scalar", lambda e, W=W: e.activation(out=W["osq"].t[:], in_=r3(pso[:]), func=AF.Square), [psob], [W["osq"].b])
                S.add("vector", lambda e, W=W: e.tensor_reduce(out=W["ss"].t[:], in_=W["osq"].t[:], axis=AX.X, op=ALU.add), [W["osq"].b], [W["ss"].b])
                S.add("scalar", lambda e, W=W: e.activation(out=W["ss"].t[:], in_=W["ss"].t[:], func=AF.Sqrt, bias=C.eps.t[:, 0:1], scale=1.0 / 128.0), [W["ss"].b, C.eps.b], [W["ss"].b])
                S.add("vector", lambda e, W=W: e.reciprocal(out=W["ss"].t[:], in_=W["ss"].t[:]), [W["ss"].b], [W["ss"].b])
                S.add("vector", lambda e, W=W: e.tensor_tensor(out=W["y1"].t[:], in0=r3(pso[:]), in1=W["ss"].t[:].unsqueeze(2).to_broadcast([128, 4, 128]), op=ALU.mult),
                      [psob, W["ss"].b], [W["y1"].b])
                S.add("vector", lambda e, W=W: e.tensor_tensor(out=W["ytok"].t[:], in0=W["y1"].t[:], in1=W["zs"].t[:], op=ALU.mult), [W["y1"].b, W["zs"].b], [W["ytok"].b])
                ps, psb = pb()
                py = ps[:].bitcast(BF16)
                for h in range(4):
                    S.add("tensor", lambda e, py=py, h=h, W=W: e.transpose(out=py[:, h * 128:(h + 1) * 128], in_=W["ytok"].t[:, h, :], identity=ident_b), [W["ytok"].b] + cbb, [psb])
                slot = (t % 16) * 128
                half = t // 16
                S.add("vector", lambda e, py=py, h0=h0, slot=slot, half=half: e.scalar_tensor_tensor(
                    out=C.ymdn.t[:, h0:h0 + 4, slot:slot + 128], in0=py[:, 0:512].rearrange("p (a b) -> p a b", a=4), scalar=C.selm.t[:, half:half + 1],
                    in1=C.ymdn.t[:, h0:h0 + 4, slot:slot + 128], op0=ALU.mult, op1=ALU.add), [psb, C.ymdn.b, C.selm.b], [C.ymdn.b])
                if t == 15:
                    S.add("vector", lambda e, py=py, h0=h0: e.tensor_scalar(out=C.yhdn.t[:, h0:h0 + 4, :], in0=py[:, 0:512].rearrange("p (a b) -> p a b", a=4),
                                                                           scalar1=C.selm.t[:, 2:3], scalar2=None, op0=ALU.mult), [psb, C.selm.b], [C.yhdn.b])

            gens = [hb_gen(0), hb_gen(1)]
            alive = [True, True]
            while any(alive):
                for gi in range(2):
                    if alive[gi]:
                        try:
                            next(gens[gi])
                        except StopIteration:
                            alive[gi] = False


def make_in_maps(inp):
    x = np.asarray(inp["x"], np.float32)
    w_in = np.asarray(inp["w_in"], np.float32)[0]
    maps = []
    cw = np.asarray(inp["dn_conv_w"], np.float32)[0]
    convw = np.ascontiguousarray(cw.reshape(4, 24, 128).transpose(2, 1, 0).reshape(128, 96))
    off = np.cumsum((0,) + (1024, 1024, 1024, 1024, 8, 8, 1024, 256, 256, 256, 256, 256, 256, 24, 1024, 1024))
    o_nq, o_kc, o_vc, o_ksl, o_vsl, o_kwn, o_vwn, o_gate, o_mgd, o_mgn = [int(off[i]) for i in (6, 7, 8, 9, 10, 11, 12, 13, 14, 15)]
    nsa_host = {k_: v_ for k_, v_ in NC2.items() if k_ != "pm_list"}
    pml = NC2["pm_list"]
    NEGT = np.full((128, 512), -30000.0, np.float32)
    ZT = np.zeros((128, 512), np.float32)
    percore = []
    for gg_ in range(2):
        off_ = 2048 * (1 - gg_)
        c1, s1, c2, s2 = _rope_tables(off_)
        sbias, svalid = _sel_tables(off_)
        if gg_ == 1:
            pm0 = [pml[3], pml[4], ZT]
            kneg = np.zeros((128, 1024), np.float32)
        else:
            pm0 = [NEGT, NEGT, NEGT]
            kneg = np.full((128, 1024), -30000.0, np.float32)
        percore.append({"ropeC": c1, "ropeS": s1, "ropeCk": c2, "ropeSk": s2, "selbias": sbias, "selvalid": svalid,
                        "pmask": np.ascontiguousarray(np.concatenate(pml[0:4] + pm0, axis=1)), "keyneg": kneg})
    f32c = lambda a: np.ascontiguousarray(np.asarray(a, np.float32))
    nsa_host["cmp_k_w1"] = f32c(inp["cmp_k_w1"][0])
    nsa_host["cmp_v_w1"] = f32c(inp["cmp_v_w1"][0])
    nsa_host["cmp_k_w2"] = f32c(inp["cmp_k_w2"][0])
    nsa_host["cmp_k_w2r"] = f32c(np.asarray(inp["cmp_k_w2"][0])[:, _rot_perm(1)])
    nsa_host["cmp_v_w2"] = f32c(inp["cmp_v_w2"][0])
    nsa_host["peT_k"] = f32c(np.asarray(inp["cmp_pos_k"][0]).T)
    nsa_host["peT_v"] = f32c(np.asarray(inp["cmp_pos_v"][0]).T)
    for gg in range(2):
        sl = lambda o: w_in[:, o + gg * 128:o + (gg + 1) * 128]
        nsa_host["wn_kf%d" % gg] = f32c(np.concatenate([sl(o_kc), sl(o_vc), sl(o_ksl), sl(o_kwn)], axis=1))
        rp = _rot_perm(1)
        nsa_host["wn_kr%d" % gg] = f32c(np.concatenate([sl(o_ksl)[:, rp], sl(o_kwn)[:, rp]], axis=1))
        nsa_host["wn_v%d" % gg] = f32c(np.concatenate([sl(o_vsl), sl(o_vwn)], axis=1))
        wq = w_in[:, o_nq + gg * 512:o_nq + (gg + 1) * 512]
        nsa_host["wn_q%d" % gg] = f32c(wq)
        nsa_host["wn_qr%d" % gg] = f32c(wq[:, _rot_perm(4)])
        gcols = [o_gate + br * 8 + gg * 4 + hh for br in range(3) for hh in range(4)]
        nsa_host["wn_g%d" % gg] = f32c(w_in[:, gcols])
    fcw_ = np.asarray(inp["ffn_conv_w"], np.float32)[0]
    p2_host = {
        "wm": f32c(w_in[:, o_mgd:o_mgd + 2048]),
        "w_dn": f32c(inp["w_branch_dn"][0]), "w_nsa": f32c(inp["w_branch_nsa"][0]), "w_out": f32c(inp["w_out"][0]),
        "ln1_g": f32c(inp["ln1_g"]).reshape(1, D), "ln1_b": f32c(inp["ln1_b"]).reshape(1, D),
        "ln2_g": f32c(inp["ln2_g"]).reshape(1, D), "ln2_b": f32c(inp["ln2_b"]).reshape(1, D),
        "w_up": f32c(inp["ffn_w_up"][0]), "w_down": f32c(inp["ffn_w_down"][0]),
        "fconvw": f32c(fcw_.reshape(3, 44, 128).transpose(2, 1, 0).reshape(128, 132)),
        "fconvb": f32c(np.asarray(inp["ffn_conv_b"], np.float32)[0].reshape(44, 128).T),
    }
    for c in range(8):
        b, g = c // 2, c % 2
        selm = np.zeros((128, 3), np.float32)
        selm[:, 1] = 1.0
        selm[:, 2] = float(g)
        m = {
            "xT": (np.ascontiguousarray(x[b].T) if g == 1 else
                   np.ascontiguousarray(np.concatenate([np.zeros((D, HALF), np.float32), x[b, 0:HALF].T], axis=1))),
            "consts": CONSTS,
            "selm": selm,
            "wg_qkv": np.ascontiguousarray(w_in[:, 0:3072]),
            "wg_z": np.ascontiguousarray(w_in[:, 3072:4096]),
            "wg_ba": np.ascontiguousarray(w_in[:, 4096:4112]),
            "convw": convw,
            "a_log": np.asarray(inp["dn_a_log"], np.float32).reshape(1, 8),
            "dt_bias": np.asarray(inp["dn_dt_bias"], np.float32).reshape(1, 8),
            "dn_norm_w": np.asarray(inp["dn_norm_w"], np.float32).reshape(1, 128),
        }
        m.update(nsa_host)
        m.update(percore[g])
        m.update(p2_host)
        xh = np.zeros((HALF + 128, D), np.float32)
        if g == 0:
            xh[128:] = x[b, 0:HALF]
        else:
            xh[:] = x[b, HALF - 128:SEQ]
        m["xh"] = xh
        m["xTh"] = np.ascontiguousarray(xh.T)
        maps.append(m)
    return maps


def kernel(**inp):
    nc = build()
    maps = make_in_maps(inp)
    res = run_bass_kernel_spmd(nc, maps, core_ids=list(range(8)))
    out = np.zeros((4, SEQ, D), np.float32)
    for c in range(8):
        b, g = c // 2, c % 2
        out[b, g * HALF:(g + 1) * HALF] = res.results[c]["out"]
    return out


NSA_INPUTS = [("ropeC", [128, 4096]), ("ropeS", [128, 4096]), ("ropeCk", [128, 256]), ("ropeSk", [128, 256]),
              ("selbias", [4096, 64]), ("selvalid", [4096, 64]),
              ("cmask", [128, 2048]), ("wmask", [128, 4096]), ("pmask", [128, 3584]), ("keyneg", [128, 1024]), ("ovl", [128, 128]),
              ("ebig", [128, 4096]), ("gsel", [128, 1536]),
              ("cmp_k_w1", [4096, 256]), ("cmp_v_w1", [4096, 256]), ("cmp_k_w2", [256, 128]), ("cmp_k_w2r", [256, 128]),
              ("cmp_v_w2", [256, 128]), ("peT_k", [128, 32]), ("peT_v", [128, 32])]
for _g in range(2):
    NSA_INPUTS += [("wn_kf%d" % _g, [D, 512]), ("wn_kr%d" % _g, [D, 256]), ("wn_v%d" % _g, [D, 256]),
                   ("wn_q%d" % _g, [D, 512]), ("wn_qr%d" % _g, [D, 512]), ("wn_g%d" % _g, [D, 12])]


def stage_nsa(C, I):
    nc, S, pb = C.nc, C.S, C.pb
    C.nrot = 3
    C.pi = 0
    pselp = (C.ps[3], C.psb[3])
    accs = [((C.ps[4], C.psb[4]), (C.ps[5], C.psb[5])), ((C.ps[6], C.psb[6]), (C.ps[7], C.psb[7]))]
    acci = [0]
    SC = 128.0 ** -0.5
    with ExitStack() as st:
        A = lambda name, shape, dt=F32: TT(C, st, name, shape, dt)
        ident_b = cst(C, "ident", "b")
        ones_b = cst(C, "ones", "b")
        cbb = [C.cb.b]
        stg = A("nstg", [128, 8, 256])
        stg2 = stg.t[:].rearrange("p a b -> p (a b)")
        xb = A("nxb", [128, 8, 512], BF16)
        xT3 = I["xT"].rearrange("(kc p) t -> p kc t", p=128)

        def load_bf(name, ncols, dst, rows=128):
            for c0 in range(0, ncols, 2048):
                n = min(2048, ncols - c0)
                S.dma("sync", stg2[0:rows, 0:n], I[name][0:rows, c0:c0 + n], writes=[stg.b])
                S.add("vector", lambda e, n=n, c0=c0: e.tensor_copy(out=dst.t[0:rows, c0:c0 + n], in_=stg2[0:rows, 0:n]), [stg.b], [dst.b])

        def load_w(name, ncols, dst):
            src3 = I[name].rearrange("(kc p) c -> p kc c", p=128)
            for c0 in range(0, ncols, 256):
                n = min(256, ncols - c0)
                S.dma("sync", stg.t[:, :, 0:n], src3[:, :, c0:c0 + n], writes=[stg.b])
                S.add("vector", lambda e, c0=c0, n=n: e.tensor_copy(out=dst.t[:, :, c0:c0 + n], in_=stg.t[:, :, 0:n]), [stg.b], [dst.b])

        xTb3 = C.xTb.rearrange("(kc p) t -> p kc t", p=128)

        def load_x(tg):
            if C.have_xTb:
                S.dma("sync", xb.t[:], xTb3[:, :, tg * 512:(tg + 1) * 512], reads=[C.xTb_b], writes=[xb.b])
            else:
                for hh_ in range(2):
                    S.dma("sync", stg.t[:], xT3[:, :, tg * 512 + hh_ * 256:tg * 512 + (hh_ + 1) * 256], writes=[stg.b])
                    S.add("gpsimd", lambda e, hh_=hh_: e.tensor_copy(out=xb.t[:, :, hh_ * 256:(hh_ + 1) * 256], in_=stg.t[:]), [stg.b], [xb.b])

        ropeC = A("ropeC", [128, 512])
        ropeS = A("ropeS", [128, 512])
        r1 = A("r1", [128, 512])
        r2 = A("r2", [128, 512])

        def load_rope(tg):
            S.dma("sync", ropeC.t[:], I["ropeC"][:, tg * 512:(tg + 1) * 512], writes=[ropeC.b])
            S.dma("sync", ropeS.t[:], I["ropeS"][:, tg * 512:(tg + 1) * 512], writes=[ropeS.b])

        def proj_fm(W, c0, m, n):
            ps, psb = pb()
            for kc in range(8):
                S.add("tensor", lambda e, ps=ps, kc=kc: e.matmul(ps[0:m, 0:n], lhsT=W.t[:, kc, c0:c0 + m], rhs=xb.t[:, kc, 0:n],
                                                                 start=(kc == 0), stop=(kc == 7)), [W.b, xb.b], [psb])
            return ps, psb

        def rope_into(dst_ap, dstb, ps, psb, psr, psrb, n, scale=1.0):
            S.add("vector", lambda e: e.tensor_tensor(out=r1.t[:, 0:n], in0=ps[:, 0:n], in1=ropeC.t[:, 0:n], op=ALU.mult), [psb, ropeC.b], [r1.b])
            S.add("vector", lambda e: e.tensor_tensor(out=r2.t[:, 0:n], in0=psr[:, 0:n], in1=ropeS.t[:, 0:n], op=ALU.mult), [psrb, ropeS.b], [r2.b])
            S.add("gpsimd", lambda e: e.tensor_tensor(out=r1.t[:, 0:n], in0=r1.t[:, 0:n], in1=r2.t[:, 0:n], op=ALU.add), [r1.b, r2.b], [r1.b])
            S.add("scalar", lambda e: e.activation(out=dst_ap, in_=r1.t[:, 0:n], func=AF.Copy, scale=scale), [r1.b], [dstb])

        import os as _os
        _lim = _os.environ.get("NSA_LIM", "")
        pc_bufs = {}

        def precast_gen():
            jobs = []
            for key, nm, ncol in (("m", "wm", 2048), ("dn", "w_dn", 1024), ("ns", "w_nsa", 1024), ("out", "w_out", 1024), ("up", "w_up", 5632)):
                src3 = I[nm].rearrange("(kc p) c -> p kc c", p=128)
                for c0 in range(0, ncol, 256):
                    jobs.append((key, src3[:, :, c0:c0 + 256], c0, 8, 256))
            srcd = I["w_down"].rearrange("(k p) c -> p k c", p=128)
            for k0 in range(0, 22, 2):
                jobs.append(("down", srcd[:, k0:k0 + 2, :], k0, 2, 1024))
            def views(ji):
                key, src, a0, K_, n_ = jobs[ji]
                pf = pc_bufs["f"][ji % 2]
                pbf = pc_bufs["b"][ji % 2]
                return pf, pbf, pf.t[:].rearrange("p (k c) -> p k c", k=K_), pbf.t[:].rearrange("p (k c) -> p k c", k=K_)

            in_ver = {}

            def issue_in(ji):
                pf, pbf, vf, vb_ = views(ji)
                S.dma("gpsimd", vf, jobs[ji][1], writes=[pf.b])
                in_ver[ji] = pc_bufs["ver"]

            issue_in(0)
            for ji, (key, src, a0, K_, n_) in enumerate(jobs):
                pf, pbf, vf, vb_ = views(ji)
                if in_ver.get(ji) != pc_bufs["ver"]:
                    issue_in(ji)
                if ji + 1 < len(jobs):
                    issue_in(ji + 1)
                S.add("gpsimd", lambda e, vf=vf, vb_=vb_: e.tensor_copy(out=vb_, in_=vf), [pf.b], [pbf.b])
                if key == "down":
                    for hf in range(2):
                        S.dma("gpsimd", C.wsc["down"][hf][:, a0:a0 + 2, :], vb_[:, :, hf * 512:(hf + 1) * 512], reads=[pbf.b], writes=[C.wsc_b])
                elif key == "out":
                    S.dma("gpsimd", C.wsc["out"][:, :, a0:a0 + 256], vb_, reads=[pbf.b], writes=[C.wsc_b])
                else:
                    for j_ in range(2):
                        S.dma("gpsimd", C.wsc[key][a0 // 128 + j_], vb_[:, :, j_ * 128:(j_ + 1) * 128], reads=[pbf.b], writes=[C.wsc_b])
                yield

        pc_gen = precast_gen()
        pc_done = [False]

        def precast_step(n):
            for _ in range(n):
                if pc_done[0]:
                    return
                try:
                    next(pc_gen)
                except StopIteration:
                    pc_done[0] = True

        for g in range(2):
            if _lim and g == 1:
                break
            S.barrier()
            with ExitStack() as sg:
                G = lambda name, shape, dt=F32: TT(C, sg, "%s_g%d" % (name, g), shape, dt)
                kselT = G("kselT", [128, 4096], BF16)
                kwinT = G("kwinT", [128, 4096], BF16)
                vsw = G("vsw", [128, 32, 256], BF16)
                kcT = G("kcT", [128, 256], BF16)
                vc = G("vc", [128, 2, 128], BF16)
                S.add("gpsimd", lambda e: e.memset(kcT.t[:], 0.0), [], [kcT.b])
                with ExitStack() as sa:
                    B = lambda name, shape, dt=F32: TT(C, sa, "%s_a%d" % (name, g), shape, dt)
                    kcmpb = B("kcmpb", [128, 4096], BF16)
                    vcmpb = B("vcmpb", [128, 4096], BF16)
                    pc_bufs["ver"] = g
                    pc_bufs["f"] = [B("pcf%d" % i, [128, 2048]) for i in range(2)]
                    pc_bufs["b"] = [B("pcb%d" % i, [128, 2048], BF16) for i in range(2)]
                    sa1 = ExitStack()
                    B1 = lambda name, shape, dt=F32: TT(C, sa1, "%s_a%d" % (name, g), shape, dt)
                    Wkf = B1("Wkf", [128, 8, 512], BF16)
                    Wkr = B1("Wkr", [128, 8, 256], BF16)
                    Wv = B1("Wv", [128, 8, 256], BF16)
                    load_w("wn_kf%d" % g, 512, Wkf)
                    load_w("wn_kr%d" % g, 256, Wkr)
                    load_w("wn_v%d" % g, 256, Wv)
                    for tg in range(8):
                        precast_step(4 if (g == 0 and tg < 3) else 3)
                        load_x(tg)
                        load_rope(tg)
                        cs = slice(tg * 512, (tg + 1) * 512)
                        if _lim == "A00":
                            continue
                        for j, dst in enumerate((kcmpb, vcmpb, kselT, kwinT)):
                            ps, psb = proj_fm(Wkf, j * 128, 128, 512)
                            if j < 2 or _lim == "A01":
                                S.add("scalar", lambda e, ps=ps, dst=dst, cs=cs: e.activation(out=dst.t[:, cs], in_=ps[:], func=AF.Copy), [psb], [dst.b])
                            else:
                                psr, psrb = proj_fm(Wkr, (j - 2) * 128, 128, 512)
                                rope_into(dst.t[:, cs], dst.b, ps, psb, psr, psrb, 512)
                        for s4 in range(4):
                            if _lim in ("A01", "A02"):
                                break
                            ps, psb = pb()
                            for kc in range(8):
                                S.add("tensor", lambda e, ps=ps, kc=kc, s4=s4: e.matmul(ps[:, 0:256], lhsT=xb.t[:, kc, s4 * 128:(s4 + 1) * 128], rhs=Wv.t[:, kc, :],
                                                                                       start=(kc == 0), stop=(kc == 7)), [Wv.b, xb.b], [psb])
                            S.add("scalar", lambda e, ps=ps, s4=s4, tg=tg: e.activation(out=vsw.t[:, tg * 4 + s4, :], in_=ps[:, 0:256], func=AF.Copy), [psb], [vsw.b])
                    if g == 1:
                        precast_step(100)
                    sa1.close()
                    S.barrier()
                    if _lim.startswith("A0"):
                        continue
                    w1 = B("w1", [128, 16, 256], BF16)
                    w2 = B("w2", [128, 2, 128], BF16)
                    w2r = B("w2r", [128, 2, 128], BF16)
                    pe = B("pe", [128, 32])
                    hid = B("hid", [128, 2, 256], BF16)
                    S.add("gpsimd", lambda e: e.memset(hid.t[:], 0.0), [], [hid.b])
                    kpe = [B("kpe%d" % i, [128, 255], BF16) for i in range(3)]
                    for is_k, src, nm in ((True, kcmpb, "k"), (False, vcmpb, "v")):
                        w1v = I["cmp_%s_w1" % nm].rearrange("(l d) h -> d l h", d=128)
                        def load_w1(hf):
                            for q_ in range(2):
                                S.dma("sync", stg2.rearrange("p (l h) -> p l h", l=8), w1v[:, hf * 16 + q_ * 8:hf * 16 + (q_ + 1) * 8, :], writes=[stg.b])
                                S.add("gpsimd", lambda e, q_=q_: e.tensor_copy(out=w1.t[:, q_ * 8:(q_ + 1) * 8, :], in_=stg2.rearrange("p (l h) -> p l h", l=8)), [stg.b], [w1.b])
                        S.dma("sync", stg.t[:, 0:2, 0:128], I["cmp_%s_w2" % nm].rearrange("(hc p) d -> p hc d", p=128), writes=[stg.b])
                        S.add("vector", lambda e: e.tensor_copy(out=w2.t[:], in_=stg.t[:, 0:2, 0:128]), [stg.b], [w2.b])
                        if is_k:
                            S.dma("sync", stg.t[:, 0:2, 0:128], I["cmp_k_w2r"].rearrange("(hc p) d -> p hc d", p=128), writes=[stg.b])
                            S.add("vector", lambda e: e.tensor_copy(out=w2r.t[:], in_=stg.t[:, 0:2, 0:128]), [stg.b], [w2r.b])
                        S.dma("sync", pe.t[:], I["peT_%s" % nm], writes=[pe.b])
                        hps = [pb(), pb()]
                        for l in range(32):
                            if l % 16 == 0:
                                load_w1(l // 16)
                            kp = kpe[l % 3]
                            S.add("vector", lambda e, kp=kp, l=l, src=src: e.tensor_scalar(out=kp.t[:], in0=src.t[:, l:l + 4065:16], scalar1=pe.t[:, l:l + 1], scalar2=None, op0=ALU.add),
                                  [src.b, pe.b], [kp.b])
                            for hc in range(2):
                                S.add("tensor", lambda e, kp=kp, l=l, hc=hc, hps=hps: e.matmul(hps[hc][0][:, 0:255], lhsT=w1.t[:, l % 16, hc * 128:(hc + 1) * 128], rhs=kp.t[:],
                                                                                     start=(l == 0), stop=(l == 31)), [w1.b, kp.b], [hps[hc][1]])
                        for hc in range(2):
                            S.add("scalar", lambda e, hc=hc, hps=hps: e.activation(out=hid.t[:, hc, 0:255], in_=hps[hc][0][:, 0:255], func=AF.Silu), [hps[hc][1]], [hid.b])
                        if is_k:
                            ps, psb = pb()
                            psr, psrb = pb()
                            for hc in range(2):
                                S.add("tensor", lambda e, ps=ps, hc=hc: e.matmul(ps[:, 0:255], lhsT=w2.t[:, hc, :], rhs=hid.t[:, hc, 0:255], start=(hc == 0), stop=(hc == 1)), [w2.b, hid.b], [psb])
                                S.add("tensor", lambda e, psr=psr, hc=hc: e.matmul(psr[:, 0:255], lhsT=w2r.t[:, hc, :], rhs=hid.t[:, hc, 0:255], start=(hc == 0), stop=(hc == 1)), [w2r.b, hid.b], [psrb])
                            S.dma("sync", ropeC.t[:, 0:256], I["ropeCk"], writes=[ropeC.b])
                            S.dma("sync", ropeS.t[:, 0:256], I["ropeSk"], writes=[ropeS.b])
                            rope_into(kcT.t[:, 0:255], kcT.b, ps, psb, psr, psrb, 255)
                        else:
                            for ncn in range(2):
                                ps, psb = pb()
                                for hc in range(2):
                                    S.add("tensor", lambda e, ps=ps, hc=hc, ncn=ncn: e.matmul(ps[:, 0:128], lhsT=hid.t[:, hc, ncn * 128:(ncn + 1) * 128], rhs=w2.t[:, hc, :],
                                                                                             start=(hc == 0), stop=(hc == 1)), [w2.b, hid.b], [psb])
                                S.add("scalar", lambda e, ps=ps, ncn=ncn: e.activation(out=vc.t[:, ncn, :], in_=ps[:, 0:128], func=AF.Copy), [psb], [vc.b])
                if _lim.startswith("A"):
                    continue
                S.barrier()
                with ExitStack() as sc:
                    B = lambda name, shape, dt=F32: TT(C, sc, "%s_c%d" % (name, g), shape, dt)
                    wmask = B("wmask", [128, 4096], BF16)
                    pmask = B("pmask", [128, 3584], BF16)
                    keyneg = B("keyneg", [128, 1024], BF16)
                    onesrow = B("onesrow", [128, 512], BF16)
                    S.add("gpsimd", lambda e: e.memset(onesrow.t[:], 1.0), [], [onesrow.b])
                    ebig = B("ebig", [128, 4096], BF16)
                    ovl = B("ovl", [128, 128], BF16)
                    gsel = B("gsel", [128, 1536])
                    load_bf("wmask", 4096, wmask)
                    load_bf("pmask", 3584, pmask)
                    load_bf("keyneg", 1024, keyneg)
                    load_bf("ebig", 4096, ebig, rows=64)
                    load_bf("ovl", 128, ovl)
                    S.dma("sync", gsel.t[0:12, :], I["gsel"][0:12, :], writes=[gsel.b])
                    Wq = B("Wq", [128, 8, 512], BF16)
                    Wqr = B("Wqr", [128, 8, 512], BF16)
                    Wg = B("Wg", [128, 8, 12], BF16)
                    load_w("wn_q%d" % g, 512, Wq)
                    load_w("wn_qr%d" % g, 512, Wqr)
                    load_w("wn_g%d" % g, 12, Wg)
                    qT = [B("qT%d" % h, [128, 512], BF16) for h in range(4)]
                    gs = B("gs", [128, 512])
                    PT = [B("PT%d" % i, [128, 512], BF16) for i in range(4)]
                    pti = [0]
                    PTcl = [B("PTc%d" % i, [128, 2, 512], BF16) for i in range(2)]
                    acc = B("acc", [128, 4, 512])
                    rr = B("rr", [128, 512])
                    rg = r2
                    tmpc = rr
                    negselT = B("negselT", [128, 512], BF16)
                    selb = B("selb", [128, 64])
                    selv = B("selv", [128, 64])
                    sc1 = B("sc1", [128, 64])
                    sc2 = B("sc2", [128, 64])
                    slt = B("slt", [128, 64])
                    m8 = B("m8", [128, 16])
                    nsb = B("nsb", [128, 64], BF16)
                    zb = B("zb", [128, 256], BF16)
                    S.add("gpsimd", lambda e: e.memset(zb.t[:], 0.0), [], [zb.b])

                    def combine(h, br, ao, asum):
                        S.add("vector", lambda e: e.tensor_scalar(out=rr.t[:], in0=asum[0][:], scalar1=1e-30, scalar2=None, op0=ALU.max), [asum[1]], [rr.b])
                        S.add("vector", lambda e: e.reciprocal(out=rr.t[:], in_=rr.t[:]), [rr.b], [rr.b])
                        psg, psgb = pb()
                        r = br * 4 + h
                        S.add("tensor", lambda e: e.matmul(psg[:], lhsT=gsel.t[0:12, r * 128:(r + 1) * 128], rhs=gs.t[0:12, :], start=True, stop=True), [gsel.b, gs.b], [psgb])
                        S.add("vector", lambda e: e.tensor_tensor(out=rg.t[:], in0=rr.t[:], in1=psg[:], op=ALU.mult), [rr.b, psgb], [rg.b])
                        if br == 0:
                            S.add("vector", lambda e: e.tensor_tensor(out=acc.t[:, h, :], in0=ao[0][:], in1=rg.t[:], op=ALU.mult), [ao[1], rg.b], [acc.b])
                        else:
                            S.add("vector", lambda e: e.tensor_tensor(out=tmpc.t[:], in0=ao[0][:], in1=rg.t[:], op=ALU.mult), [ao[1], rg.b], [tmpc.b])
                            S.add("gpsimd", lambda e: e.tensor_tensor(out=acc.t[:, h, :], in0=acc.t[:, h, :], in1=tmpc.t[:], op=ALU.add), [acc.b, tmpc.b], [acc.b])

                    for tg in range(3, 8):
                        load_x(tg)
                        load_rope(tg)
                        for h in range(4):
                            ps, psb = proj_fm(Wq, h * 128, 128, 512)
                            psr, psrb = proj_fm(Wqr, h * 128, 128, 512)
                            rope_into(qT[h].t[:], qT[h].b, ps, psb, psr, psrb, 512, scale=SC)
                        ps, psb = proj_fm(Wg, 0, 12, 512)
                        S.add("scalar", lambda e, ps=ps: e.activation(out=gs.t[0:12, :], in_=ps[0:12, :], func=AF.Sigmoid), [psb], [gs.b])
                        ncs = [0] if tg < 4 else [0, 1]
                        S.add("tensor", lambda e: e.matmul(pselp[0][:, 0:256], lhsT=zb.t[:, 0:128], rhs=zb.t[:, 0:256], start=True, stop=False), [zb.b], [pselp[1]])
                        for h in range(4):
                            ao, asum = accs[acci[0] % 2]
                            acci[0] += 1
                            PTc = PTcl[h % 2]
                            for ii, nci in enumerate(ncs):
                                ps, psb = pb()
                                dlt = (tg - 4) if nci == 1 else 4 + min(tg - 3, 2)
                                msk = True
                                S.add("tensor", lambda e, ps=ps, nci=nci, h=h, msk=msk: e.matmul(ps[:], lhsT=kcT.t[:, nci * 128:(nci + 1) * 128], rhs=qT[h].t[:], start=True, stop=(not msk)),
                                      [kcT.b, qT[h].b], [psb])
                                if msk:
                                    S.add("tensor", lambda e, ps=ps, dlt=dlt: e.matmul(ps[:], lhsT=ident_b, rhs=pmask.t[:, dlt * 512:(dlt + 1) * 512], start=False, stop=True),
                                          [pmask.b] + cbb, [psb])
                                S.add("scalar", lambda e, ps=ps, PTc=PTc, nci=nci: e.activation(out=PTc.t[:, nci, :], in_=ps[:], func=AF.Exp), [psb], [PTc.b])
                                S.add("tensor", lambda e, PTc=PTc, nci=nci, ii=ii, ao=ao, n=len(ncs): e.matmul(ao[0][:], lhsT=vc.t[:, nci, :], rhs=PTc.t[:, nci, :], start=(ii == 0), stop=(ii == n - 1)),
                                      [vc.b, PTc.b], [ao[1]])
                                S.add("tensor", lambda e, PTc=PTc, nci=nci, ii=ii, asum=asum, n=len(ncs): e.matmul(asum[0][:], lhsT=ones_b, rhs=PTc.t[:, nci, :], start=(ii == 0), stop=(ii == n - 1)),
                                      [PTc.b] + cbb, [asum[1]])
                            combine(h, 0, ao, asum)
                            for ii, nci in enumerate(ncs):
                                S.add("vector", lambda e, PTc=PTc, nci=nci: e.tensor_tensor(out=PTc.t[:, nci, :], in0=PTc.t[:, nci, :], in1=rr.t[:], op=ALU.mult), [PTc.b, rr.b], [PTc.b])
                                for s4 in range(4):
                                    first = (h == 0 and ii == 0)
                                    last = (h == 3 and ii == len(ncs) - 1)
                                    S.add("tensor", lambda e, PTc=PTc, nci=nci, s4=s4, first=first, last=last: e.matmul(
                                        pselp[0][:, s4 * 64:(s4 + 1) * 64], lhsT=PTc.t[:, nci, s4 * 128:(s4 + 1) * 128], rhs=ovl.t[:, nci * 64:(nci + 1) * 64],
                                        start=False, stop=last), [PTc.b, ovl.b], [pselp[1]])
                        for s4 in range(4):
                            tile_i = tg * 4 + s4
                            S.dma("sync", selb.t[:], I["selbias"][tile_i * 128:(tile_i + 1) * 128, :], writes=[selb.b])
                            S.dma("sync", selv.t[:], I["selvalid"][tile_i * 128:(tile_i + 1) * 128, :], writes=[selv.b])
                            ps, psb = pselp
                            S.add("vector", lambda e, ps=ps, s4=s4: e.tensor_tensor(out=sc1.t[:], in0=ps[:, s4 * 64:(s4 + 1) * 64], in1=selb.t[:], op=ALU.add), [psb, selb.b], [sc1.b])
                            S.add("vector", lambda e: e.max(out=m8.t[:, 0:8], in_=sc1.t[:]), [sc1.b], [m8.b])
                            S.add("vector", lambda e: e.match_replace(out=sc2.t[:], in_to_replace=m8.t[:, 0:8], in_values=sc1.t[:], imm_value=-2e9), [sc1.b, m8.b], [sc2.b])
                            S.add("vector", lambda e: e.max(out=m8.t[:, 8:16], in_=sc2.t[:]), [sc2.b], [m8.b])
                            S.add("vector", lambda e: e.tensor_scalar(out=slt.t[:], in0=sc1.t[:], scalar1=m8.t[:, 15:16], scalar2=None, op0=ALU.is_ge), [sc1.b, m8.b], [slt.b])
                            S.add("vector", lambda e: e.tensor_tensor(out=slt.t[:], in0=slt.t[:], in1=selv.t[:], op=ALU.mult), [slt.b, selv.b], [slt.b])
                            S.add("vector", lambda e: e.tensor_scalar(out=nsb.t[:], in0=slt.t[:], scalar1=30000.0, scalar2=-30000.0, op0=ALU.mult, op1=ALU.add), [slt.b], [nsb.b])
                            pst, pstb = pb()
                            ptb = pst[:].bitcast(BF16)
                            S.add("tensor", lambda e, ptb=ptb: e.transpose(out=ptb[0:64, 0:128], in_=nsb.t[:], identity=ident_b), [nsb.b] + cbb, [pstb])
                            S.add("scalar", lambda e, ptb=ptb, s4=s4: e.activation(out=negselT.t[0:64, s4 * 128:(s4 + 1) * 128], in_=ptb[0:64, 0:128], func=AF.Copy), [pstb], [negselT.b])
                        tiles = []
                        for br in (1, 2):
                            for h in range(4):
                                pair = accs[acci[0] % 2]
                                acci[0] += 1
                                kts = list(range(0, 4 * tg + 4)) if br == 1 else list(range(max(0, 4 * tg - 4), 4 * tg + 4))
                                for ii, kt in enumerate(kts):
                                    tiles.append((br, h, kt, ii, len(kts), pair))

                        def stage1(tl):
                            br, h, kt, ii, n, pair = tl
                            ps, psb = pb()
                            ksrc = kselT if br == 1 else kwinT
                            S.add("tensor", lambda e: e.matmul(ps[:], lhsT=ksrc.t[:, kt * 128:(kt + 1) * 128], rhs=qT[h].t[:], start=True, stop=False), [ksrc.b, qT[h].b], [psb])
                            if br == 1:
                                diag = kt >= 4 * tg
                                S.add("tensor", lambda e: e.matmul(ps[:], lhsT=ebig.t[0:64, kt * 128:(kt + 1) * 128], rhs=negselT.t[0:64, :], start=False, stop=(not diag)), [ebig.b, negselT.b], [psb])
                                if diag:
                                    dd = kt - 4 * tg
                                    S.add("tensor", lambda e: e.matmul(ps[:], lhsT=ident_b, rhs=wmask.t[:, (4 + dd) * 512:(5 + dd) * 512], start=False, stop=True), [wmask.b] + cbb, [psb])
                            else:
                                dd = kt - (4 * tg - 4)
                                if kt < 16:
                                    S.add("tensor", lambda e: e.matmul(ps[:], lhsT=keyneg.t[0:1, (kt - 8) * 128:(kt - 7) * 128], rhs=onesrow.t[0:1, :], start=False, stop=False), [keyneg.b, onesrow.b], [psb])
                                S.add("tensor", lambda e: e.matmul(ps[:], lhsT=ident_b, rhs=wmask.t[:, dd * 512:(dd + 1) * 512], start=False, stop=True), [wmask.b] + cbb, [psb])
                            pt = PT[pti[0] % len(PT)]
                            pti[0] += 1
                            S.add("scalar", lambda e: e.activation(out=pt.t[:], in_=ps[:], func=AF.Exp), [psb], [pt.b])
                            return pt

                        def stage2(tl, pt):
                            br, h, kt, ii, n, pair = tl
                            ao, asum = pair
                            vo = 0 if br == 1 else 128
                            S.add("tensor", lambda e: e.matmul(ao[0][:], lhsT=vsw.t[:, kt, vo:vo + 128], rhs=pt.t[:], start=(ii == 0), stop=(ii == n - 1)), [vsw.b, pt.b], [ao[1]])
                            sacc = stg.t[:, 0:2, :].rearrange("p a b -> p (a b)")
                            if ii == 0:
                                S.add("vector", lambda e: e.tensor_copy(out=sacc, in_=pt.t[:]), [pt.b], [stg.b])
                            else:
                                S.add("vector", lambda e: e.tensor_tensor(out=sacc, in0=sacc, in1=pt.t[:], op=ALU.add), [pt.b, stg.b], [stg.b])
                            if ii == n - 1:
                                S.add("tensor", lambda e: e.matmul(asum[0][:], lhsT=cst(C, "ones"), rhs=sacc, start=True, stop=True), [stg.b, C.cf.b], [asum[1]])
                                combine(h, br, ao, asum)

                        LA = 3
                        C.nrot = 4
                        pend = []
                        for idx, tl in enumerate(tiles):
                            pend.append((tl, stage1(tl)))
                            if len(pend) > LA:
                                t0_, p0_ = pend.pop(0)
                                stage2(t0_, p0_)
                        while pend:
                            t0_, p0_ = pend.pop(0)
                            stage2(t0_, p0_)
                        C.nrot = 3
                        C.pi = 0
                        half = tg // 4
                        slot = (tg % 4) * 512
                        for h in range(4):
                            if tg >= 4:
                                S.add("vector", lambda e, h=h, g=g, slot=slot, half=half: e.scalar_tensor_tensor(out=C.ymnsa.t[:, g * 4 + h, slot:slot + 512], in0=acc.t[:, h, :], scalar=C.selm.t[:, half:half + 1],
                                    in1=C.ymnsa.t[:, g * 4 + h, slot:slot + 512], op0=ALU.mult, op1=ALU.add), [acc.b, C.ymnsa.b, C.selm.b], [C.ymnsa.b])
                            if tg == 3:
                                S.add("vector", lambda e, h=h, g=g: e.tensor_scalar(out=C.yhnsa.t[:, g * 4 + h, :], in0=acc.t[:, h, 384:512], scalar1=C.selm.t[:, 2:3], scalar2=None, op0=ALU.mult),
                                      [acc.b, C.selm.b], [C.yhnsa.b])
    C.nrot = 8
    C.pi = 0


P2_INPUTS = [("xh", [HALF + 128, D]), ("xTh", [D, HALF + 128]), ("wm", [D, 2048]), ("w_dn", [D, D]), ("w_nsa", [D, D]), ("w_out", [D, D]),
             ("ln1_g", [1, D]), ("ln1_b", [1, D]), ("w_up", [D, 5632]), ("fconvw", [128, 132]), ("fconvb", [128, 44]),
             ("w_down", [2816, D]), ("ln2_g", [1, D]), ("ln2_b", [1, D])]


def stage_p2(C, I, out_ap, fin):
    nc, S, pb = C.nc, C.S, C.pb
    C.nrot = 8
    C.pi = 0
    S.barrier()
    with ExitStack() as st:
        A = lambda name, shape, dt=F32: TT(C, st, name, shape, dt)
        ident_b = cst(C, "ident", "b")
        cbb = [C.cb.b]
        stgs = [A("pstg%d" % i, [128, 1024]) for i in range(2)]
        sti = [0]

        def load_chunk(src, dst_ap, dstb, K, ncol):
            sg = stgs[sti[0] % 2]
            sti[0] += 1
            v = sg.t[:, 0:K * ncol].rearrange("p (k c) -> p k c", k=K)
            S.dma("sync", v, src, writes=[sg.b])
            if sti[0] % 2:
                S.add("scalar", lambda e: e.activation(out=dst_ap, in_=v, func=AF.Copy), [sg.b], [dstb])
            else:
                S.add("gpsimd", lambda e: e.tensor_copy(out=dst_ap, in_=v), [sg.b], [dstb])

        def load_sc(src, dst_ap, dstb):
            S.dma("sync", dst_ap, src, reads=[C.wsc_b], writes=[dstb])

        wdn3 = I["w_dn"].rearrange("(kc p) c -> p kc c", p=128)
        wns3 = I["w_nsa"].rearrange("(kc p) c -> p kc c", p=128)
        wm3 = I["wm"].rearrange("(kc p) c -> p kc c", p=128)
        wo3 = I["w_out"].rearrange("(kc p) c -> p kc c", p=128)
        wup3 = I["w_up"].rearrange("(kc p) c -> p kc c", p=128)
        wdw3 = I["w_down"].rearrange("(k p) c -> p k c", p=128)
        xTh3 = I["xTh"].rearrange("(kc p) t -> p kc t", p=128)

        x1 = A("x1", [128, 4, D])
        x1T = A("x1T", [128, 8, 512], BF16)
        hist = A("hist", [128, 44, 2])
        lnp = A("lnp", [128, 2, D])
        fcw = A("fcw", [128, 132])
        fcb = A("fcb", [128, 44])
        st1 = A("st1", [128, 8])
        junk = A("junk", [128, D], BF16)
        S.dma("sync", fcw.t[:], I["fconvw"], writes=[fcw.b])
        S.dma("sync", fcb.t[:], I["fconvb"], writes=[fcb.b])

        def load_ln(which):
            S.dma("sync", lnp.t[:, 0, :], I["ln%d_g" % which].partition_broadcast(128), writes=[lnp.b])
            S.dma("sync", lnp.t[:, 1, :], I["ln%d_b" % which].partition_broadcast(128), writes=[lnp.b])

        def layernorm(r_ap, rb, out_ap_, outb):
            s1 = [st1.b]
            S.add("scalar", lambda e: e.activation(out=junk.t[:], in_=r_ap, func=AF.Copy, accum_out=st1.t[:, 0:1]), [rb], [junk.b, st1.b])
            S.add("scalar", lambda e: e.activation(out=junk.t[:], in_=r_ap, func=AF.Square, accum_out=st1.t[:, 1:2]), [rb], [junk.b, st1.b])
            S.add("vector", lambda e: e.tensor_scalar(out=st1.t[:, 2:3], in0=st1.t[:, 0:1], scalar1=1.0 / D, scalar2=None, op0=ALU.mult), s1, s1)
            S.add("vector", lambda e: e.tensor_tensor(out=st1.t[:, 3:4], in0=st1.t[:, 2:3], in1=st1.t[:, 2:3], op=ALU.mult), s1, s1)
            S.add("vector", lambda e: e.scalar_tensor_tensor(out=st1.t[:, 4:5], in0=st1.t[:, 1:2], scalar=1.0 / D, in1=st1.t[:, 3:4], op0=ALU.mult, op1=ALU.subtract), s1, s1)
            S.add("scalar", lambda e: e.activation(out=st1.t[:, 5:6], in_=st1.t[:, 4:5], func=AF.Sqrt, bias=C.eps.t[:, 1:2], scale=1.0), s1 + [C.eps.b], s1)
            S.add("vector", lambda e: e.reciprocal(out=st1.t[:, 6:7], in_=st1.t[:, 5:6]), s1, s1)
            S.add("vector", lambda e: e.tensor_scalar(out=out_ap_, in0=r_ap, scalar1=st1.t[:, 2:3], scalar2=st1.t[:, 6:7], op0=ALU.subtract, op1=ALU.mult), [rb] + s1, [outb])
            S.add("gpsimd", lambda e: e.tensor_tensor(out=out_ap_, in0=out_ap_, in1=lnp.t[:, 0, :], op=ALU.mult), [outb, lnp.b], [outb])
            S.add("gpsimd", lambda e: e.tensor_tensor(out=out_ap_, in0=out_ap_, in1=lnp.t[:, 1, :], op=ALU.add), [outb, lnp.b], [outb])

        def do_block(blk):
            N = 128 if blk < 0 else 512
            nt = N // 128
            tok0 = 0 if blk < 0 else 128 + blk * 512
            if blk < 0:
                ysrc = lambda T_, kc: T_[1].t[:, kc, :]
            else:
                ysrc = lambda T_, kc, blk=blk: T_[0].t[:, kc, blk * 512:(blk + 1) * 512]
            YD = (C.ymdn, C.yhdn)
            YN = (C.ymnsa, C.yhnsa)
            ydb = [C.ymdn.b, C.yhdn.b]
            ynb = [C.ymnsa.b, C.yhnsa.b]
            S.barrier()
            with ExitStack() as sa:
                B = lambda name, shape, dt=F32: TT(C, sa, "%s_b%d" % (name, blk + 1), shape, dt)
                yT = B("yT", [128, 8, 512], BF16)
                xgb = B("xgb", [128, 8, 512], BF16)
                wch = {k: [B("wch_%s%d" % (k, i), [128, 8, 128], BF16) for i in range(3)] for k in ("dn", "ns", "md", "mn")}
                gdt = B("gdt", [128, 512])
                gnt = B("gnt", [128, 512])
                t1 = B("t1", [128, 512])
                t2 = B("t2", [128, 512])
                wo = B("wo", [128, 8, D], BF16)
                xt = [B("xt%d" % i, [128, D]) for i in range(2)]
                rbuf = B("rbuf", [128, D])
                x1b = B("x1b", [128, D], BF16)
                for k0 in range(0, 8, 2):
                    load_chunk(xTh3[:, k0:k0 + 2, tok0:tok0 + N], xgb.t[:, k0:k0 + 2, 0:N], xgb.b, 2, N)
                for c in range(8):
                    cs_ = slice(c * 128, (c + 1) * 128)
                    w = {k: wch[k][c % 3] for k in wch}
                    load_sc(C.wsc["dn"][c], w["dn"].t[:], w["dn"].b)
                    load_sc(C.wsc["ns"][c], w["ns"].t[:], w["ns"].b)
                    load_sc(C.wsc["m"][c], w["md"].t[:], w["md"].b)
                    load_sc(C.wsc["m"][8 + c], w["mn"].t[:], w["mn"].b)
                    pd, pdb = pb()
                    pn, pnb = pb()
                    pgd, pgdb = pb()
                    pgn, pgnb = pb()
                    for kc in range(8):
                        S.add("tensor", lambda e, pd=pd, kc=kc, w=w: e.matmul(pd[:, 0:N], lhsT=w["dn"].t[:, kc, :], rhs=ysrc(YD, kc), start=(kc == 0), stop=(kc == 7)), [w["dn"].b] + ydb, [pdb])
                    for kc in range(8):
                        S.add("tensor", lambda e, pn=pn, kc=kc, w=w: e.matmul(pn[:, 0:N], lhsT=w["ns"].t[:, kc, :], rhs=ysrc(YN, kc), start=(kc == 0), stop=(kc == 7)), [w["ns"].b] + ynb, [pnb])
                    for kc in range(8):
                        S.add("tensor", lambda e, pgd=pgd, kc=kc, w=w: e.matmul(pgd[:, 0:N], lhsT=w["md"].t[:, kc, :], rhs=xgb.t[:, kc, 0:N], start=(kc == 0), stop=(kc == 7)), [w["md"].b, xgb.b], [pgdb])
                    for kc in range(8):
                        S.add("tensor", lambda e, pgn=pgn, kc=kc, w=w: e.matmul(pgn[:, 0:N], lhsT=w["mn"].t[:, kc, :], rhs=xgb.t[:, kc, 0:N], start=(kc == 0), stop=(kc == 7)), [w["mn"].b, xgb.b], [pgnb])
                    S.add("scalar", lambda e, pgd=pgd: e.activation(out=gdt.t[:, 0:N], in_=pgd[:, 0:N], func=AF.Sigmoid), [pgdb], [gdt.b])
                    S.add("scalar", lambda e, pgn=pgn: e.activation(out=gnt.t[:, 0:N], in_=pgn[:, 0:N], func=AF.Sigmoid), [pgnb], [gnt.b])
                    S.add("vector", lambda e, pd=pd: e.tensor_tensor(out=t1.t[:, 0:N], in0=gdt.t[:, 0:N], in1=pd[:, 0:N], op=ALU.mult), [gdt.b, pdb], [t1.b])
                    S.add("vector", lambda e, pn=pn: e.tensor_tensor(out=t2.t[:, 0:N], in0=gnt.t[:, 0:N], in1=pn[:, 0:N], op=ALU.mult), [gnt.b, pnb], [t2.b])
                    S.add("vector", lambda e, c=c: e.tensor_tensor(out=yT.t[:, c, 0:N], in0=t1.t[:, 0:N], in1=t2.t[:, 0:N], op=ALU.add), [t1.b, t2.b], [yT.b])
                load_sc(C.wsc["out"], wo.t[:], wo.b)
                load_ln(1)
                for s in range(nt):
                    xti = xt[s % 2]
                    S.dma("sync", xti.t[:], I["xh"][tok0 + s * 128:tok0 + (s + 1) * 128, :], writes=[xti.b])
                    for hf in range(2):
                        pz, pzb = pb()
                        for kc in range(8):
                            S.add("tensor", lambda e, pz=pz, kc=kc, s=s, hf=hf: e.matmul(pz[:], lhsT=yT.t[:, kc, s * 128:(s + 1) * 128], rhs=wo.t[:, kc, hf * 512:(hf + 1) * 512],
                                                                                      start=(kc == 0), stop=(kc == 7)), [yT.b, wo.b], [pzb])
                        S.add("vector", lambda e, pz=pz, hf=hf, xti=xti: e.scalar_tensor_tensor(out=rbuf.t[:, hf * 512:(hf + 1) * 512], in0=xti.t[:, hf * 512:(hf + 1) * 512], scalar=ALPHA, in1=pz[:],
                                                                                               op0=ALU.mult, op1=ALU.add), [xti.b, pzb], [rbuf.b])
                    layernorm(rbuf.t[:], rbuf.b, x1.t[:, s, :], x1.b)
                    S.add("scalar", lambda e, s=s: e.activation(out=x1b.t[:], in_=x1.t[:, s, :], func=AF.Copy), [x1.b], [x1b.b])
                    pt, ptb_ = pb()
                    ptv = pt[:].bitcast(BF16)
                    for kc in range(8):
                        S.add("tensor", lambda e, ptv=ptv, kc=kc: e.transpose(out=ptv[:, kc * 128:(kc + 1) * 128], in_=x1b.t[:, kc * 128:(kc + 1) * 128], identity=ident_b), [x1b.b] + cbb, [ptb_])
                    S.add("scalar", lambda e, ptv=ptv, s=s: e.activation(out=x1T.t[:, :, s * 128:(s + 1) * 128], in_=ptv.rearrange("p (a b) -> p a b", a=8), func=AF.Copy), [ptb_], [x1T.b])
            S.barrier()
            with ExitStack() as sb:
                B = lambda name, shape, dt=F32: TT(C, sb, "%s_f%d" % (name, blk + 1), shape, dt)
                wup = [B("wup%d" % i, [128, 8, 128], BF16) for i in range(6)]
                wi = [0]
                if blk < 0:
                    for cti in range(44):
                        w = wup[wi[0] % 6]
                        wi[0] += 1
                        load_sc(C.wsc["up"][cti], w.t[:], w.b)
                        ps, psb = pb()
                        for kc in range(8):
                            S.add("tensor", lambda e, ps=ps, kc=kc, w=w: e.matmul(ps[:, 0:128], lhsT=w.t[:, kc, :], rhs=x1T.t[:, kc, 0:128], start=(kc == 0), stop=(kc == 7)), [w.b, x1T.b], [psb])
                        S.add("vector", lambda e, ps=ps, cti=cti: e.tensor_scalar(out=hist.t[:, cti, :], in0=ps[:, 126:128], scalar1=C.selm.t[:, 2:3], scalar2=None, op0=ALU.mult), [psb, C.selm.b], [hist.b])
                    return
                hT = B("hT", [128, 22, 512], BF16)
                ub = [B("ub%d" % i, [128, 514]) for i in range(2)]
                cc = [B("cc%d" % i, [128, 512]) for i in range(2)]
                sgt = B("sgt", [128, 512])
                wdh = B("wdh", [128, 22, 512], BF16)
                r2 = B("r2", [128, 4, D])
                ot = [B("ot%d" % i, [128, D]) for i in range(2)]
                for ct in range(22):
                    for gv in range(2):
                        cti = ct + 22 * gv
                        w = wup[wi[0] % 6]
                        wi[0] += 1
                        load_sc(C.wsc["up"][cti], w.t[:], w.b)
                        ps, psb = pb()
                        for kc in range(8):
                            S.add("tensor", lambda e, ps=ps, kc=kc, w=w: e.matmul(ps[:], lhsT=w.t[:, kc, :], rhs=x1T.t[:, kc, :], start=(kc == 0), stop=(kc == 7)), [w.b, x1T.b], [psb])
                        u = ub[gv]
                        S.add("gpsimd", lambda e, u=u, cti=cti: e.tensor_copy(out=u.t[:, 0:2], in_=hist.t[:, cti, :]), [hist.b], [u.b])
                        S.add("scalar", lambda e, u=u, ps=ps: e.activation(out=u.t[:, 2:514], in_=ps[:], func=AF.Copy), [psb], [u.b])
                        S.add("gpsimd", lambda e, u=u, cti=cti: e.tensor_copy(out=hist.t[:, cti, :], in_=u.t[:, 512:514]), [u.b], [hist.b])
                        cv = cc[gv]
                        S.add("vector", lambda e, u=u, cv=cv, cti=cti: e.tensor_scalar(out=cv.t[:], in0=u.t[:, 2:514], scalar1=fcw.t[:, cti * 3 + 2:cti * 3 + 3], scalar2=fcb.t[:, cti:cti + 1],
                                                                                    op0=ALU.mult, op1=ALU.add), [u.b, fcw.b, fcb.b], [cv.b])
                        S.add("vector", lambda e, u=u, cv=cv, cti=cti: e.scalar_tensor_tensor(out=cv.t[:], in0=u.t[:, 1:513], scalar=fcw.t[:, cti * 3 + 1:cti * 3 + 2], in1=cv.t[:],
                                                                                           op0=ALU.mult, op1=ALU.add), [u.b, fcw.b, cv.b], [cv.b])
                        S.add("vector", lambda e, u=u, cv=cv, cti=cti: e.scalar_tensor_tensor(out=cv.t[:], in0=u.t[:, 0:512], scalar=fcw.t[:, cti * 3:cti * 3 + 1], in1=cv.t[:],
                                                                                           op0=ALU.mult, op1=ALU.add), [u.b, fcw.b, cv.b], [cv.b])
                    S.add("scalar", lambda e: e.activation(out=sgt.t[:], in_=cc[0].t[:], func=AF.Silu), [cc[0].b], [sgt.b])
                    S.add("vector", lambda e, ct=ct: e.tensor_tensor(out=hT.t[:, ct, :], in0=sgt.t[:], in1=cc[1].t[:], op=ALU.mult), [sgt.b, cc[1].b], [hT.b])
                for hf in range(2):
                    load_sc(C.wsc["down"][hf], wdh.t[:], wdh.b)
                    for s in range(4):
                        pf, pfb = pb()
                        for k in range(22):
                            S.add("tensor", lambda e, pf=pf, k=k, s=s: e.matmul(pf[:], lhsT=hT.t[:, k, s * 128:(s + 1) * 128], rhs=wdh.t[:, k, :], start=(k == 0), stop=(k == 21)), [hT.b, wdh.b], [pfb])
                        S.add("vector", lambda e, pf=pf, s=s, hf=hf: e.scalar_tensor_tensor(out=r2.t[:, s, hf * 512:(hf + 1) * 512], in0=x1.t[:, s, hf * 512:(hf + 1) * 512], scalar=ALPHA, in1=pf[:],
                                                                                         op0=ALU.mult, op1=ALU.add), [x1.b, pfb], [r2.b])
                load_ln(2)
                for s in range(4):
                    o = ot[s % 2]
                    layernorm(r2.t[:, s, :], r2.b, o.t[:], o.b)
                    row0 = blk * 512 + s * 128
                    fin.append(S.dma("sync", out_ap[row0:row0 + 128, :], o.t[:], reads=[o.b]))

        for blk in range(-1, 4):
            do_block(blk)
```

```python
import numpy as np
import concourse.bass as bass
import concourse.mybir as mybir
from concourse.bass_utils import run_bass_kernel_spmd
from contextlib import ExitStack

F32 = mybir.dt.float32
BF16 = mybir.dt.bfloat16
AF = mybir.ActivationFunctionType
ALU = mybir.AluOpType
AX = mybir.AxisListType

ENGS = ["sync", "scalar", "vector", "gpsimd", "tensor"]
DMA_K = 12

SEQ = 4096
NT = 32
D = 1024
HALF = 2048
RMS_EPS = 1e-6
LN_EPS = 1e-5
ALPHA = 2.0 ** 0.25


class Buf:
    __slots__ = ("name", "last_w", "rd_c", "rd_d")

    def __init__(self, name):
        self.name = name
        self.last_w = None
        self.rd_c = {}
        self.rd_d = []


class Op:
    __slots__ = ("eng", "fn", "deps", "needed", "count", "dma", "sem", "k")


class Sched:
    def __init__(self, nc, stack):
        self.nc = nc
        self.ops = {e: [] for e in ENGS}
        self.ndma = {e: 0 for e in ENGS}
        self.dma_ops = {e: [] for e in ENGS}
        self.csem = {e: stack.enter_context(nc.semaphore("c_" + e)) for e in ENGS}
        self.dsem = {e: [stack.enter_context(nc.semaphore("d_%s_%d" % (e, i))) for i in range(DMA_K)]
                     for e in ("sync", "scalar", "gpsimd")}
        self.nbuf = 0

    def buf(self, name=None):
        self.nbuf += 1
        return Buf(name or "b%d" % self.nbuf)

    def add(self, eng, fn, reads=(), writes=(), dma=False):
        op = Op()
        op.eng = eng
        op.fn = fn
        op.dma = dma
        op.needed = False
        op.count = None
        op.sem = None
        deps = []

        def dep(d, raw):
            if d is None:
                return
            if (not raw) and (not dma) and (not d.dma) and d.eng == eng:
                return
            if eng == "tensor" and d.eng == "tensor" and not d.dma and not dma:
                return
            deps.append(d)

        for b in reads:
            dep(b.last_w, True)
        for b in writes:
            dep(b.last_w, False)
            for r in b.rd_c.values():
                dep(r, False)
            for r in b.rd_d:
                dep(r, False)
        if dma:
            j = self.ndma[eng]
            self.ndma[eng] += 1
            op.k = j
            if j >= DMA_K:
                deps.append(self.dma_ops[eng][j - DMA_K])
            self.dma_ops[eng].append(op)
        seen = set()
        dd = []
        for d in deps:
            if id(d) not in seen:
                seen.add(id(d))
                dd.append(d)
                d.needed = True
        op.deps = dd
        for b in reads:
            if dma:
                b.rd_d.append(op)
            else:
                b.rd_c[eng] = op
        for b in writes:
            b.last_w = op
            b.rd_c = {}
            b.rd_d = []
        self.ops[eng].append(op)
        return op

    def barrier(self):
        lasts = []
        for e in ENGS:
            for op in reversed(self.ops[e]):
                if (not op.dma) and op.fn is not None:
                    lasts.append(op)
                    break
            lasts.extend(self.dma_ops[e][-DMA_K:])
        for d in lasts:
            d.needed = True
        for e in ENGS:
            op = Op()
            op.eng = e
            op.fn = None
            op.dma = False
            op.needed = False
            op.count = None
            op.sem = None
            op.deps = list(lasts)
            self.ops[e].append(op)

    def dma(self, q, out, in_, reads=(), writes=(), **kw):
        return self.add(q, lambda e: e.dma_start(out=out, in_=in_, **kw), reads, writes, dma=True)

    def finalize(self, final_deps=()):
        nc = self.nc
        for e in ENGS:
            c = 0
            for op in self.ops[e]:
                if op.dma:
                    op.sem = self.dsem[e][op.k % DMA_K]
                    op.count = 16 * (op.k // DMA_K + 1)
                    op.needed = True
                elif op.needed:
                    c += 1
                    op.sem = self.csem[e]
                    op.count = c
        ops = self.ops

        def emit(e, eng):
            waited = {}
            for op in ops[e]:
                for d in op.deps:
                    key = id(d.sem)
                    if waited.get(key, 0) < d.count:
                        eng.wait_ge(d.sem, d.count)
                        waited[key] = d.count
                if op.fn is None:
                    continue
                inst = op.fn(eng)
                if op.needed:
                    inst.then_inc(op.sem, 16 if op.dma else 1)
            if e == "sync":
                for d in final_deps:
                    eng.wait_ge(d.sem, d.count)

        with nc.Block() as block:
            @block.sync
            def _(eng):
                emit("sync", eng)

            @block.scalar
            def _(eng):
                emit("scalar", eng)

            @block.vector
            def _(eng):
                emit("vector", eng)

            @block.gpsimd
            def _(eng):
                emit("gpsimd", eng)

            @block.tensor
            def _(eng):
                emit("tensor", eng)


class TT:
    def __init__(self, C, st, name, shape, dt=F32):
        self.t = st.enter_context(C.nc.sbuf_tensor("s_" + name, shape, dt))
        self.b = C.S.buf(name)


class Ctx:
    pass


CONST_COLS = {}
NC2 = {}


def _build_consts():
    cols = []
    off = [0]

    def put(name, arr):
        arr = np.asarray(arr, np.float32).reshape(128, -1)
        CONST_COLS[name] = (off[0], arr.shape[1])
        off[0] += arr.shape[1]
        cols.append(arr)

    p = np.arange(128)[:, None]
    f = np.arange(128)[None, :]
    same = (p // 64) == (f // 64)
    put("ident", (p == f))
    put("ones", np.ones((128, 128)))
    put("tri", (p <= f) & same)
    put("blk", same)
    put("strict", (f < p) & same)
    put("upper", (f >= p) & same)
    NEG = -30000.0
    pp = np.arange(128)[:, None]
    ff = np.arange(512)[None, :]
    cm = [np.where(pp + 128 * d <= ff, 0.0, NEG) for d in range(4)]
    NC2.__setitem__("cmask", np.concatenate(cm, axis=1).astype(np.float32))
    wm = []
    for k in range(8):
        rel = ff - pp - 128 * (k - 4)
        wm.append(np.where((rel >= 0) & (rel < 512), 0.0, NEG))
    NC2.__setitem__("wmask", np.concatenate(wm, axis=1).astype(np.float32))
    pm = [np.where(16 * pp + 31 - ff <= 512 * d, 0.0, NEG) for d in range(5)]
    NC2.__setitem__("pm_list", [p_.astype(np.float32) for p_ in pm])
    tok0 = np.arange(256) * 16
    cend = tok0 + 31
    sst = np.arange(64) * 64
    ov = ((tok0[:, None] <= (sst + 63)[None]) & (cend[:, None] >= sst[None])).astype(np.float32)
    ov[255] = 0.0
    NC2.__setitem__("ovl", np.ascontiguousarray(np.concatenate([ov[0:128], ov[128:256]], axis=1)))
    eb = np.zeros((128, 4096), np.float32)
    eb[np.arange(4096) // 64, np.arange(4096)] = 1.0
    NC2.__setitem__("ebig", eb)
    gs = np.zeros((128, 12 * 128), np.float32)
    for r in range(12):
        gs[r, r * 128:(r + 1) * 128] = 1.0
    NC2.__setitem__("gsel", gs)
    return np.concatenate(cols, axis=1)


CONSTS = _build_consts()


def _rope_tables(off=0):
    half = 16
    inv = 500000.0 ** (-np.arange(half, dtype=np.float32) / half)
    def tab(pos):
        ang = pos.astype(np.float32)[None, :] * inv[:, None]
        c = np.cos(ang).astype(np.float32)
        sn = np.sin(ang).astype(np.float32)
        P = ang.shape[1]
        return (np.concatenate([c, c, np.ones((96, P), np.float32)], 0),
                np.concatenate([-sn, sn, np.zeros((96, P), np.float32)], 0))
    c1, s1 = tab(np.maximum(np.arange(4096) - off, 0))
    pk = np.maximum(np.arange(256) * 16 + 31 - off, 0)
    c2, s2 = tab(pk)
    return np.ascontiguousarray(c1), np.ascontiguousarray(s1), np.ascontiguousarray(c2), np.ascontiguousarray(s2)


def _sel_tables(off=0):
    t = np.arange(4096)
    cur = t // 64
    jb = np.arange(64)
    j0 = off // 64
    valid = (jb[None] <= cur[:, None]) & (jb[None] >= j0)
    forced = valid & ((jb[None] == j0) | (jb[None] == cur[:, None]) | (jb[None] == cur[:, None] - 1))
    bias = np.where(forced, 10.0, np.where(valid, 0.0, -1e9)).astype(np.float32)
    return bias, valid.astype(np.float32)


def _rot_perm(ncols_heads):
    idx = []
    for h in range(ncols_heads):
        for d in range(128):
            idx.append(h * 128 + (d + 16 if d < 16 else (d - 16 if d < 32 else d)))
    return np.array(idx)


def cst(C, name, dt="f"):
    o, n = CONST_COLS[name]
    t = C.cf if dt == "f" else C.cb
    return t.t[:, o:o + n]


def build(debug=None):
    nc = bass.Bass("TRN2", target_bir_lowering=False)
    C = Ctx()
    C.nc = nc
    I = {}

    def inp(name, shape):
        I[name] = nc.dram_tensor(name, list(shape), F32, kind="ExternalInput").ap()
        return I[name]

    inp("xT", [D, SEQ])
    inp("consts", [128, CONSTS.shape[1]])
    inp("selm", [128, 3])
    inp("wg_qkv", [D, 3072])
    inp("wg_z", [D, 1024])
    inp("wg_ba", [D, 16])
    inp("convw", [128, 24 * 4])
    inp("a_log", [1, 8])
    inp("dt_bias", [1, 8])
    inp("dn_norm_w", [1, 128])
    for (nm, shp) in NSA_INPUTS:
        inp(nm, shp)
    for (nm, shp) in P2_INPUTS:
        inp(nm, shp)
    C.xTb = nc.dram_tensor("xTb_scratch", [D, SEQ], BF16).ap()
    outs = {}
    if debug == "nsa":
        outs["ynsa"] = nc.dram_tensor("ynsa", [128, 8 * HALF], F32, kind="ExternalOutput").ap()
    if debug == "gdn":
        outs["ydn"] = nc.dram_tensor("ydn", [128, 8 * HALF], F32, kind="ExternalOutput").ap()

    with ExitStack() as st:
        S = Sched(nc, st)
        C.S = S
        C.st = st
        C.xTb_b = S.buf("xTb")
        C.wsc_b = S.buf("wscratch")
        C.wsc = {
            "dn": nc.dram_tensor("sc_dn", [8, 128, 8, 128], BF16).ap(),
            "ns": nc.dram_tensor("sc_ns", [8, 128, 8, 128], BF16).ap(),
            "m": nc.dram_tensor("sc_m", [16, 128, 8, 128], BF16).ap(),
            "out": nc.dram_tensor("sc_out", [128, 8, 1024], BF16).ap(),
            "up": nc.dram_tensor("sc_up", [44, 128, 8, 128], BF16).ap(),
            "down": nc.dram_tensor("sc_down", [2, 128, 22, 512], BF16).ap(),
        }
        C.ps = [st.enter_context(nc.psum_tensor("ps%d" % i, [128, 512], F32)) for i in range(8)]
        C.psb = [S.buf("ps%d" % i) for i in range(8)]
        C.pi = 0
        C.nrot = 8

        def pb():
            i = C.pi % C.nrot
            C.pi = (i + 1) % C.nrot
            return C.ps[i], C.psb[i]

        C.pb = pb
        C.cf = TT(C, st, "cf", [128, CONSTS.shape[1]])
        C.cb = TT(C, st, "cb", [128, CONSTS.shape[1]], BF16)
        S.dma("sync", C.cf.t[:], I["consts"], writes=[C.cf.b])
        S.add("vector", lambda e: e.tensor_copy(out=C.cb.t[:], in_=C.cf.t[:]), [C.cf.b], [C.cb.b])
        C.selm = TT(C, st, "selm", [128, 3])
        S.dma("sync", C.selm.t[:], I["selm"], writes=[C.selm.b])
        C.eps = TT(C, st, "eps", [128, 2])
        S.add("gpsimd", lambda e: e.memset(C.eps.t[:, 0:1], RMS_EPS), [], [C.eps.b])
        S.add("gpsimd", lambda e: e.memset(C.eps.t[:, 1:2], LN_EPS), [], [C.eps.b])
        C.ymdn = TT(C, st, "ymdn", [128, 8, HALF], BF16)
        C.yhdn = TT(C, st, "yhdn", [128, 8, 128], BF16)
        S.add("gpsimd", lambda e: e.memset(C.ymdn.t[:], 0.0), [], [C.ymdn.b])

        fin = []
        C.have_xTb = debug not in ("nsa", "p2")
        if debug not in ("nsa", "p2"):
            stage_gdn(C, I)
        S.barrier()
        C.ymnsa = TT(C, st, "ymnsa", [128, 8, HALF], BF16)
        C.yhnsa = TT(C, st, "yhnsa", [128, 8, 128], BF16)
        S.add("gpsimd", lambda e: e.memset(C.ymnsa.t[:], 0.0), [], [C.ymnsa.b])
        if debug == "p2":
            S.add("gpsimd", lambda e: e.memset(C.yhdn.t[:], 0.0), [], [C.yhdn.b])
            S.add("gpsimd", lambda e: e.memset(C.yhnsa.t[:], 0.0), [], [C.yhnsa.b])
        if debug not in ("gdn", "p2"):
            stage_nsa(C, I)
        if debug == "nsa":
            with ExitStack() as s2:
                for h in range(8):
                    tmp = TT(C, s2, "dbgn%d" % h, [128, HALF])
                    S.add("vector", lambda e, tmp=tmp, h=h: e.tensor_copy(out=tmp.t[:], in_=C.ymnsa.t[:, h, :]), [C.ymnsa.b], [tmp.b])
                    fin.append(S.dma("sync", outs["ynsa"][:, h * HALF:(h + 1) * HALF], tmp.t[:], reads=[tmp.b]))
                S.finalize(final_deps=fin)
            return nc
        if debug == "gdn":
            with ExitStack() as s2:
                for h in range(8):
                    tmp = TT(C, s2, "dbg%d" % h, [128, HALF])
                    S.add("vector", lambda e, tmp=tmp, h=h: e.tensor_copy(out=tmp.t[:], in_=C.ymdn.t[:, h, :]), [C.ymdn.b], [tmp.b])
                    fin.append(S.dma("sync", outs["ydn"][:, h * HALF:(h + 1) * HALF], tmp.t[:], reads=[tmp.b]))
                S.finalize(final_deps=fin)
            return nc
        out_ap = nc.dram_tensor("out", [HALF, D], F32, kind="ExternalOutput").ap()
        if debug == "all":
            dd = {"ydn": nc.dram_tensor("ydn", [128, 8 * HALF], F32, kind="ExternalOutput").ap(),
                  "ynsa": nc.dram_tensor("ynsa", [128, 8 * HALF], F32, kind="ExternalOutput").ap()}
            with ExitStack() as s2:
                tmps = [TT(C, s2, "dbga%d" % i, [128, HALF]) for i in range(2)]
                k = 0
                for nm, src in (("ydn", C.ymdn), ("ynsa", C.ymnsa)):
                    for h in range(8):
                        tmp = tmps[k % 2]
                        k += 1
                        S.add("vector", lambda e, tmp=tmp, h=h, src=src: e.tensor_copy(out=tmp.t[:], in_=src.t[:, h, :]), [src.b], [tmp.b])
                        fin.append(S.dma("sync", dd[nm][:, h * HALF:(h + 1) * HALF], tmp.t[:], reads=[tmp.b]))
        stage_p2(C, I, out_ap, fin)
        S.finalize(final_deps=fin)
    return nc


def stage_gdn(C, I):
    nc, S, pb = C.nc, C.S, C.pb
    C.nrot = 6
    C.pi = 0
    with ExitStack() as st:
        A = lambda name, shape, dt=F32: TT(C, st, name, shape, dt)
        ident_b = cst(C, "ident", "b")
        ones_b = cst(C, "ones", "b")
        ones_f = cst(C, "ones")
        cbf = [C.cf.b]
        cbb = [C.cb.b]
        Wqkv = A("Wqkv", [128, 8, 3072], BF16)
        Wz = A("Wz", [128, 8, 1024], BF16)
        Wba = A("Wba", [128, 8, 16], BF16)
        stg = [A("stg0", [128, 8, 128])] * 2
        xf = [stg[0]]
        k = 0
        for (src, dst, ncol) in ((I["wg_qkv"], Wqkv, 3072), (I["wg_z"], Wz, 1024)):
            for c0 in range(0, ncol, 128):
                sg = stg[k % 2]
                S.dma("sync", sg.t[:], src[:, c0:c0 + 128].rearrange("(kc p) c -> p kc c", p=128), writes=[sg.b])
                eng = "gpsimd" if k % 2 else "vector"
                S.add(eng, lambda e, sg=sg, dst=dst, c0=c0: e.tensor_copy(out=dst.t[:, :, c0:c0 + 128], in_=sg.t[:]), [sg.b], [dst.b])
                k += 1
        sg = stg[0]
        S.dma("sync", sg.t[:, :, 0:16], I["wg_ba"].rearrange("(kc p) c -> p kc c", p=128), writes=[sg.b])
        S.add("vector", lambda e: e.tensor_copy(out=Wba.t[:], in_=sg.t[:, :, 0:16]), [sg.b], [Wba.b])
        cw = A("cw", [128, 96])
        S.dma("sync", cw.t[:], I["convw"], writes=[cw.b])
        Dg = A("Dg", [128, 96, 128], BF16)
        for j in range(96):
            eng = "gpsimd" if j % 2 else "vector"
            S.add(eng, lambda e, j=j: e.tensor_scalar(out=Dg.t[:, j, :], in0=cst(C, "ident"), scalar1=cw.t[:, j:j + 1], scalar2=None, op0=ALU.mult),
                  [cw.b] + cbf, [Dg.b])
        sp = A("sp", [128, 16 + 128])
        S.dma("sync", sp.t[:, 0:8], I["a_log"].partition_broadcast(128), writes=[sp.b])
        S.dma("sync", sp.t[:, 8:16], I["dt_bias"].partition_broadcast(128), writes=[sp.b])
        S.dma("sync", sp.t[:, 16:144], I["dn_norm_w"].partition_broadcast(128), writes=[sp.b])
        negA = A("negA", [128, 8])
        S.add("scalar", lambda e: e.activation(out=negA.t[:], in_=sp.t[:, 0:8], func=AF.Exp), [sp.b], [negA.b])
        S.add("vector", lambda e: e.tensor_scalar(out=negA.t[:], in0=negA.t[:], scalar1=-1.0, scalar2=None, op0=ALU.mult), [negA.b], [negA.b])
        nw4 = A("nw4", [128, 4, 128])
        for h in range(4):
            S.add("vector", lambda e, h=h: e.tensor_copy(out=nw4.t[:, h, :], in_=sp.t[:, 16:144]), [sp.b], [nw4.b])

        xf = stg
        xb = [A("xb%d" % i, [128, 8, 128], BF16) for i in range(1)]
        pre = [A("pre%d" % i, [128, 24, 131], BF16) for i in range(1)]
        S.add("gpsimd", lambda e: e.memset(pre[0].t[:], 0.0), [], [pre[0].b])
        actl = [A("act%d" % i, [128, 4, 128]) for i in range(2)]
        sql = [A("sq%d" % i, [128, 4, 128], BF16) for i in range(2)]
        rinv = A("rinv", [128, 4, 128])
        qT = A("qT", [128, 8, 128], BF16)
        kT = A("kT", [128, 8, 128], BF16)
        vT = A("vT", [128, 8, 128], BF16)
        kbeg = A("kbeg", [128, 8, 128], BF16)
        vb = A("vb", [128, 8, 128], BF16)
        kd = A("kd", [128, 8, 128], BF16)
        sm = A("sm", [128, 16 * 8])
        SM = lambda i: sm.t[:, i * 8:(i + 1) * 8]
        triG = A("triG", [128, 8, 128])
        Sf = A("Sf", [128, 8, 128])
        Sb = A("Sb", [128, 8, 128], BF16)
        Sfb = [S.buf("Sf0"), S.buf("Sf1")]
        Sbb = [S.buf("Sb0"), S.buf("Sb1")]
        S.add("gpsimd", lambda e: e.memset(Sf.t[:], 0.0), [], Sfb)
        S.add("gpsimd", lambda e: e.memset(Sb.t[:], 0.0), [], Sbb)
        hbT = []
        for hb in range(2):
            d = {}
            for (nm, shp, dt) in (("d1", [128, 4, 128], F32), ("Ds", [128, 4, 128], BF16), ("DT", [128, 4, 128], BF16),
                                  ("egbc", [128, 4, 128], F32),
                                  ("XA", [128, 4, 128], BF16), ("XB", [128, 4, 128], BF16),
                                  ("TXA", [128, 2, 4, 128], BF16), ("TXB", [128, 2, 4, 128], BF16),
                                  ("u", [128, 4, 128], BF16), ("wT", [128, 4, 128], BF16), ("qkT", [128, 4, 128], BF16),
                                  ("qdT", [128, 4, 128], BF16), ("vnew", [128, 4, 128], BF16),
                                  ("ss", [128, 4], F32),
                                  ):
                d[nm] = A("%s_%d" % (nm, hb), shp, dt)
            d["a1"] = d["d1"]
            d["X0"] = d["XB"]
            d["zs"] = d["Ds"]
            d["ytok"] = d["vnew"]
            d["osq"] = d["d1"]
            d["y1"] = d["egbc"]
            hbT.append(d)

        xT3 = I["xT"].rearrange("(kc p) t -> p kc t", p=128)

        for t in range(NT):
            xfi, xbi, pr = xf[t % 2], xb[0], pre[0]
            if t > 0:
                S.add("gpsimd", lambda e, pr=pr: e.tensor_copy(out=pr.t[:, :, 0:3], in_=pr.t[:, :, 128:131]), [pr.b], [pr.b])
            S.dma("sync", xfi.t[:], xT3[:, :, t * 128:(t + 1) * 128], writes=[xfi.b])
            S.add("gpsimd", lambda e, xfi=xfi, xbi=xbi: e.tensor_copy(out=xbi.t[:], in_=xfi.t[:]), [xfi.b], [xbi.b])
            S.dma("sync", C.xTb.rearrange("(kc p) t -> p kc t", p=128)[:, :, t * 128:(t + 1) * 128], xbi.t[:], reads=[xbi.b], writes=[C.xTb_b])
            for cg in range(6):
                ps, psb = pb()
                for j in range(4):
                    ct = cg * 4 + j
                    for kc in range(8):
                        S.add("tensor", lambda e, ps=ps, j=j, ct=ct, kc=kc, xbi=xbi: e.matmul(
                            ps[:, j * 128:(j + 1) * 128], lhsT=Wqkv.t[:, kc, ct * 128:(ct + 1) * 128], rhs=xbi.t[:, kc, :],
                            start=(kc == 0), stop=(kc == 7)), [Wqkv.b, xbi.b], [psb])
                S.add("scalar", lambda e, ps=ps, cg=cg, pr=pr: e.activation(
                    out=pr.t[:, cg * 4:(cg + 1) * 4, 3:131], in_=ps[:].rearrange("p (a b) -> p a b", a=4), func=AF.Copy), [psb], [pr.b])
            pend_norm = []
            for cg in range(6):
                ps, psb = pb()
                for j in range(4):
                    ct = cg * 4 + j
                    for i in range(4):
                        S.add("tensor", lambda e, ps=ps, j=j, ct=ct, i=i, pr=pr: e.matmul(
                            ps[:, j * 128:(j + 1) * 128], lhsT=Dg.t[:, ct * 4 + i, :], rhs=pr.t[:, ct, i:i + 128],
                            start=(i == 0), stop=(i == 3)), [Dg.b, pr.b], [psb])
                if cg < 4:
                    act, sq = actl[cg % 2], sql[cg % 2]
                    S.add("scalar", lambda e, ps=ps, act=act: e.activation(
                        out=act.t[:], in_=ps[:].rearrange("p (a b) -> p a b", a=4), func=AF.Silu), [psb], [act.b])
                    S.add("gpsimd", lambda e, act=act, sq=sq: e.tensor_tensor(out=sq.t[:], in0=act.t[:], in1=act.t[:], op=ALU.mult), [act.b], [sq.b])

                    def norm_part(cg=cg, act=act, sq=sq):
                        ps, psb = pb()
                        S.add("tensor", lambda e, ps=ps, sq=sq: e.matmul(ps[:], lhsT=ones_b, rhs=sq.t[:].rearrange("p a b -> p (a b)"),
                                                                         start=True, stop=True), [sq.b] + cbb, [psb])
                        S.add("scalar", lambda e, ps=ps: e.activation(out=rinv.t[:].rearrange("p a b -> p (a b)"), in_=ps[:],
                                                                      func=AF.Sqrt, bias=C.eps.t[:, 0:1], scale=1.0), [psb, C.eps.b], [rinv.b])
                        S.add("vector", lambda e: e.reciprocal(out=rinv.t[:], in_=rinv.t[:]), [rinv.b], [rinv.b])
                        if cg < 2:
                            S.add("vector", lambda e, cg=cg, act=act: e.scalar_tensor_tensor(out=qT.t[:, cg * 4:(cg + 1) * 4, :], in0=act.t[:], scalar=128.0 ** -0.5, in1=rinv.t[:],
                                                                                             op0=ALU.mult, op1=ALU.mult), [act.b, rinv.b], [qT.b])
                        else:
                            S.add("vector", lambda e, cg=cg, act=act: e.tensor_tensor(out=kT.t[:, (cg - 2) * 4:(cg - 1) * 4, :], in0=act.t[:], in1=rinv.t[:], op=ALU.mult), [act.b, rinv.b], [kT.b])
                    pend_norm.append(norm_part)
                    if cg % 2 == 1:
                        for f_ in pend_norm:
                            f_()
                        del pend_norm[:]
                else:
                    S.add("scalar", lambda e, ps=ps, cg=cg: e.activation(
                        out=vT.t[:, (cg - 4) * 4:(cg - 3) * 4, :], in_=ps[:].rearrange("p (a b) -> p a b", a=4), func=AF.Silu), [psb], [vT.b])
            ps, psb = pb()
            for kc in range(8):
                S.add("tensor", lambda e, ps=ps, kc=kc, xbi=xbi: e.matmul(ps[:, 0:16], lhsT=xbi.t[:, kc, :], rhs=Wba.t[:, kc, :],
                                                                          start=(kc == 0), stop=(kc == 7)), [Wba.b, xbi.b], [psb])
            smb = [sm.b]
            S.add("scalar", lambda e, ps=ps: e.activation(out=SM(0), in_=ps[:, 0:8], func=AF.Sigmoid), [psb], smb)
            S.add("vector", lambda e, ps=ps: e.tensor_tensor(out=SM(1), in0=ps[:, 8:16], in1=sp.t[:, 8:16], op=ALU.add), [psb, sp.b], smb)
            S.add("vector", lambda e: e.tensor_scalar(out=SM(2), in0=SM(1), scalar1=-1.0, scalar2=None, op0=ALU.mult), smb, smb)
            S.add("vector", lambda e: e.tensor_tensor(out=SM(2), in0=SM(2), in1=SM(1), op=ALU.max), smb, smb)
            S.add("scalar", lambda e: e.activation(out=SM(3), in_=SM(2), func=AF.Exp, scale=-1.0), smb, smb)
            S.add("scalar", lambda e: e.activation(out=SM(4), in_=SM(3), func=AF.Ln, bias=ones_f[:, 0:1], scale=1.0), smb + cbf, smb)
            S.add("vector", lambda e: e.tensor_scalar(out=SM(1), in0=SM(1), scalar1=0.0, scalar2=None, op0=ALU.max), smb, smb)
            S.add("vector", lambda e: e.tensor_tensor(out=SM(4), in0=SM(4), in1=SM(1), op=ALU.add), smb, smb)
            S.add("vector", lambda e: e.tensor_tensor(out=SM(5), in0=SM(4), in1=negA.t[:], op=ALU.mult), smb + [negA.b], smb)
            ps, psb = pb()
            S.add("tensor", lambda e, ps=ps: e.matmul(ps[:, 0:8], lhsT=cst(C, "tri"), rhs=SM(5), start=True, stop=True), smb + cbf, [psb])
            S.add("tensor", lambda e, ps=ps: e.matmul(ps[:, 8:16], lhsT=cst(C, "blk"), rhs=SM(5), start=True, stop=True), smb + cbf, [psb])
            S.add("vector", lambda e, ps=ps: e.tensor_copy(out=sm.t[:, 48:64], in_=ps[:, 0:16]), [psb], smb)
            S.add("scalar", lambda e: e.activation(out=SM(8), in_=SM(6), func=AF.Exp), smb, smb)
            S.add("vector", lambda e: e.tensor_tensor(out=SM(9), in0=SM(8), in1=SM(0), op=ALU.mult), smb, smb)
            S.add("vector", lambda e: e.tensor_tensor(out=SM(10), in0=SM(7), in1=SM(6), op=ALU.subtract), smb, smb)
            S.add("scalar", lambda e: e.activation(out=SM(10), in_=SM(10), func=AF.Exp), smb, smb)
            S.add("vector", lambda e: e.tensor_scalar(out=SM(11), in0=SM(0), scalar1=-1.0, scalar2=None, op0=ALU.mult), smb, smb)
            S.add("vector", lambda e: e.tensor_tensor(out=triG.t[:], in0=cst(C, "tri").unsqueeze(1).to_broadcast([128, 8, 128]),
                                                      in1=SM(5).unsqueeze(2).to_broadcast([128, 8, 128]), op=ALU.mult), smb + cbf, [triG.b])
            ps, psb = pb()
            pk = ps[:].bitcast(BF16)
            for h in range(8):
                S.add("tensor", lambda e, pk=pk, h=h: e.transpose(out=pk[:, h * 128:(h + 1) * 128], in_=kT.t[:, h, :], identity=ident_b), [kT.b] + cbb, [psb])
            pk3 = pk.rearrange("p (a b) -> p a b", a=8)
            S.add("vector", lambda e, pk3=pk3: e.tensor_tensor(out=kbeg.t[:], in0=pk3, in1=SM(9).unsqueeze(2).to_broadcast([128, 8, 128]), op=ALU.mult), [psb] + smb, [kbeg.b])
            S.add("vector", lambda e, pk3=pk3: e.tensor_tensor(out=kd.t[:], in0=pk3, in1=SM(10).unsqueeze(2).to_broadcast([128, 8, 128]), op=ALU.mult), [psb] + smb, [kd.b])
            ps, psb = pb()
            pv = ps[:].bitcast(BF16)
            for h in range(8):
                S.add("tensor", lambda e, pv=pv, h=h: e.transpose(out=pv[:, h * 128:(h + 1) * 128], in_=vT.t[:, h, :], identity=ident_b), [vT.b] + cbb, [psb])
            pv3 = pv.rearrange("p (a b) -> p a b", a=8)
            S.add("vector", lambda e, pv3=pv3: e.tensor_tensor(out=vb.t[:], in0=pv3, in1=SM(0).unsqueeze(2).to_broadcast([128, 8, 128]), op=ALU.mult), [psb] + smb, [vb.b])

            def hb_gen(hb, t=t, xbi=xbi):
                W = hbT[hb]
                h0 = hb * 4
                r3 = lambda ap: ap.rearrange("p (a b) -> p a b", a=4)
                psg, psgb = pb()
                S.add("tensor", lambda e, psg=psg, h0=h0: e.matmul(psg[:], lhsT=ones_f, rhs=triG.t[:, h0:h0 + 4, :].rearrange("p a b -> p (a b)"), start=True, stop=True),
                      [triG.b] + cbf, [psgb])
                S.add("scalar", lambda e, psg=psg, W=W: e.activation(out=W["egbc"].t[:], in_=r3(psg[:]), func=AF.Exp), [psgb], [W["egbc"].b])
                S.add("vector", lambda e, psg=psg, W=W, h0=h0: e.tensor_tensor(out=W["d1"].t[:], in0=r3(psg[:]),
                                                                               in1=sm.t[:, 48 + h0:48 + h0 + 4].unsqueeze(2).to_broadcast([128, 4, 128]), op=ALU.subtract),
                      [psgb] + smb, [W["d1"].b])
                S.add("scalar", lambda e, W=W: e.activation(out=W["Ds"].t[:], in_=W["d1"].t[:], func=AF.Exp, scale=-1.0), [W["d1"].b], [W["Ds"].b])
                S.add("scalar", lambda e, W=W: e.activation(out=W["DT"].t[:], in_=W["d1"].t[:], func=AF.Exp), [W["d1"].b], [W["DT"].b])
                S.add("vector", lambda e, W=W: e.scalar_tensor_tensor(out=W["Ds"].t[:], in0=W["Ds"].t[:], scalar=1.0, in1=cst(C, "strict", "b").unsqueeze(1).to_broadcast([128, 4, 128]),
                                                                     op0=ALU.min, op1=ALU.mult), [W["Ds"].b] + cbb, [W["Ds"].b])
                S.add("vector", lambda e, W=W: e.scalar_tensor_tensor(out=W["DT"].t[:], in0=W["DT"].t[:], scalar=1.0, in1=cst(C, "upper", "b").unsqueeze(1).to_broadcast([128, 4, 128]),
                                                                     op0=ALU.min, op1=ALU.mult), [W["DT"].b] + cbb, [W["DT"].b])
                yield
                ps, psb = pb()
                for h in range(4):
                    S.add("tensor", lambda e, ps=ps, h=h, h0=h0: e.matmul(ps[:, h * 128:(h + 1) * 128], lhsT=kT.t[:, h0 + h, :], rhs=kT.t[:, h0 + h, :], start=True, stop=True),
                          [kT.b], [psb])
                S.add("vector", lambda e, ps=ps, W=W: e.tensor_tensor(out=W["a1"].t[:], in0=r3(ps[:]), in1=W["Ds"].t[:], op=ALU.mult), [psb, W["Ds"].b], [W["a1"].b])
                S.add("gpsimd", lambda e, W=W, h0=h0: e.tensor_tensor(out=W["X0"].t[:], in0=W["a1"].t[:],
                                                                      in1=sm.t[:, 88 + h0:88 + h0 + 4].unsqueeze(2).to_broadcast([128, 4, 128]), op=ALU.mult),
                      [W["a1"].b] + smb, [W["X0"].b])
                yield
                ps, psb = pb()
                pt = ps[:].bitcast(BF16)
                for h in range(4):
                    S.add("tensor", lambda e, pt=pt, h=h, W=W: e.transpose(out=pt[:, h * 128:(h + 1) * 128], in_=W["X0"].t[:, h, :], identity=ident_b), [W["X0"].b] + cbb, [psb])
                S.add("scalar", lambda e, pt=pt, W=W: e.activation(out=W["TXA"].t[:, 1, :, :], in_=pt[:, 0:512].rearrange("p (a b) -> p a b", a=4), func=AF.Copy), [psb], [W["TXA"].b])
                S.add("gpsimd", lambda e, W=W: e.tensor_copy(out=W["TXA"].t[:, 0, :, :], in_=ident_b.unsqueeze(1).to_broadcast([128, 4, 128])), cbb, [W["TXA"].b])
                yield
                Xc, TXc = W["X0"], W["TXA"]
                Xn_l = [W["XA"], W["XB"]]
                TXn_l = [W["TXB"], W["TXA"]]
                for lv in range(6):
                    Xn, TXn = Xn_l[lv % 2], TXn_l[lv % 2]
                    pa = [pb(), pb()]
                    for h in range(4):
                        ps, psb = pa[h // 2]
                        o4 = ps[:].rearrange("p (s a b) -> p s a b", s=2, a=2)
                        S.add("tensor", lambda e, o4=o4, h=h, Xc=Xc, TXc=TXc: e.matmul(o4[:, :, h % 2, :], lhsT=Xc.t[:, h, :], rhs=TXc.t[:, :, h, :], start=True, stop=True),
                              [Xc.b, TXc.b], [psb])
                    last = (lv == 5)
                    if not last:
                        psx, psxb = pb()
                        for h in range(4):
                            S.add("tensor", lambda e, psx=psx, h=h, Xc=Xc, TXc=TXc: e.matmul(psx[:, h * 128:(h + 1) * 128], lhsT=TXc.t[:, 1, h, :], rhs=Xc.t[:, h, :], start=True, stop=True),
                                  [Xc.b, TXc.b], [psxb])
                    for q in range(2):
                        ps, psb = pa[q]
                        o4 = ps[:].rearrange("p (s a b) -> p s a b", s=2, a=2)
                        S.add("vector", lambda e, o4=o4, q=q, TXc=TXc, TXn=TXn: e.tensor_tensor(out=TXn.t[:, 0, 2 * q:2 * q + 2, :], in0=o4[:, 0, :, :], in1=TXc.t[:, 0, 2 * q:2 * q + 2, :], op=ALU.add),
                              [psb, TXc.b], [TXn.b])
                        if not last:
                            S.add("scalar", lambda e, o4=o4, q=q, TXn=TXn: e.activation(out=TXn.t[:, 1, 2 * q:2 * q + 2, :], in_=o4[:, 1, :, :], func=AF.Copy), [psb], [TXn.b])
                    if not last:
                        S.add("scalar", lambda e, psx=psx, Xn=Xn: e.activation(out=Xn.t[:], in_=r3(psx[:]), func=AF.Copy), [psxb], [Xn.b])
                    Xc, TXc = Xn, TXn
                    yield
                Tt = TXc
                ps, psb = pb()
                for h in range(4):
                    S.add("tensor", lambda e, ps=ps, h=h, h0=h0, Tt=Tt: e.matmul(ps[:, h * 128:(h + 1) * 128], lhsT=Tt.t[:, 0, h, :], rhs=vb.t[:, h0 + h, :], start=True, stop=True),
                          [Tt.b, vb.b], [psb])
                S.add("scalar", lambda e, ps=ps, W=W: e.activation(out=W["u"].t[:], in_=r3(ps[:]), func=AF.Copy), [psb], [W["u"].b])
                ps, psb = pb()
                for h in range(4):
                    S.add("tensor", lambda e, ps=ps, h=h, h0=h0, Tt=Tt: e.matmul(ps[:, h * 128:(h + 1) * 128], lhsT=kbeg.t[:, h0 + h, :], rhs=Tt.t[:, 0, h, :], start=True, stop=True),
                          [Tt.b, kbeg.b], [psb])
                S.add("scalar", lambda e, ps=ps, W=W: e.activation(out=W["wT"].t[:], in_=r3(ps[:]), func=AF.Copy), [psb], [W["wT"].b])
                ps, psb = pb()
                for h in range(4):
                    S.add("tensor", lambda e, ps=ps, h=h, h0=h0: e.matmul(ps[:, h * 128:(h + 1) * 128], lhsT=kT.t[:, h0 + h, :], rhs=qT.t[:, h0 + h, :], start=True, stop=True),
                          [kT.b, qT.b], [psb])
                S.add("vector", lambda e, ps=ps, W=W: e.tensor_tensor(out=W["qkT"].t[:], in0=r3(ps[:]), in1=W["DT"].t[:], op=ALU.mult), [psb, W["DT"].b], [W["qkT"].b])
                S.add("gpsimd", lambda e, W=W, h0=h0: e.tensor_tensor(out=W["qdT"].t[:], in0=qT.t[:, h0:h0 + 4, :], in1=W["egbc"].t[:], op=ALU.mult), [qT.b, W["egbc"].b], [W["qdT"].b])
                need_out = t >= 15
                yield
                pso, psob = C.ps[6 + hb], C.psb[6 + hb]
                for c in range(2):
                    rs = slice(c * 64, (c + 1) * 64)
                    ps1, ps1b = pb()
                    for h in range(4):
                        S.add("tensor", lambda e, ps1=ps1, h=h, h0=h0, rs=rs, W=W: e.matmul(ps1[rs, h * 128:(h + 1) * 128], lhsT=W["wT"].t[:, h, rs], rhs=Sb.t[:, h0 + h, :], start=True, stop=True),
                              [W["wT"].b, Sbb[hb]], [ps1b])
                    S.add("vector", lambda e, ps1=ps1, rs=rs, W=W: e.tensor_tensor(out=W["vnew"].t[rs, :, :], in0=W["u"].t[rs, :, :], in1=r3(ps1[rs, :]), op=ALU.subtract),
                          [ps1b, W["u"].b], [W["vnew"].b])
                    yield
                    for h in (range(4) if need_out else ()):
                        S.add("tensor", lambda e, h=h, h0=h0, rs=rs, W=W: e.matmul(pso[rs, h * 128:(h + 1) * 128], lhsT=W["qdT"].t[:, h, rs], rhs=Sb.t[:, h0 + h, :], start=True, stop=False),
                              [W["qdT"].b, Sbb[hb]], [psob])
                        S.add("tensor", lambda e, h=h, rs=rs, W=W: e.matmul(pso[rs, h * 128:(h + 1) * 128], lhsT=W["qkT"].t[rs, h, rs], rhs=W["vnew"].t[rs, h, :], start=False, stop=True),
                              [W["qkT"].b, W["vnew"].b], [psob])
                    ps3, ps3b = pb()
                    for h in range(4):
                        S.add("tensor", lambda e, ps3=ps3, h=h, h0=h0, rs=rs, W=W: e.matmul(ps3[:, h * 128:(h + 1) * 128], lhsT=kd.t[rs, h0 + h, :], rhs=W["vnew"].t[rs, h, :], start=True, stop=True),
                              [kd.b, W["vnew"].b], [ps3b])
                    col = c * 64 + 63
                    S.add("vector", lambda e, h0=h0, col=col, W=W: e.tensor_tensor(out=Sf.t[:, h0:h0 + 4, :], in0=Sf.t[:, h0:h0 + 4, :],
                                                                                   in1=W["egbc"].t[:, :, col:col + 1].to_broadcast([128, 4, 128]), op=ALU.mult),
                          [Sfb[hb], W["egbc"].b], [Sfb[hb]])
                    S.add("vector", lambda e, ps3=ps3, h0=h0: e.tensor_tensor(out=Sf.t[:, h0:h0 + 4, :], in0=Sf.t[:, h0:h0 + 4, :], in1=r3(ps3[:]), op=ALU.add),
                          [Sfb[hb], ps3b], [Sfb[hb]])
                    S.add("scalar", lambda e, h0=h0: e.activation(out=Sb.t[:, h0:h0 + 4, :], in_=Sf.t[:, h0:h0 + 4, :], func=AF.Copy), [Sfb[hb]], [Sbb[hb]])
                    yield
                if not need_out:
                    return
                zc = hb
                ps, psb = pb()
                for kc in range(8):
                    S.add("tensor", lambda e, ps=ps, zc=zc, kc=kc, xbi=xbi: e.matmul(ps[:], lhsT=xbi.t[:, kc, :], rhs=Wz.t[:, kc, zc * 512:(zc + 1) * 512],
                                                                                      start=(kc == 0), stop=(kc == 7)), [Wz.b, xbi.b], [psb])
                S.add("scalar", lambda e, ps=ps, W=W: e.activation(out=W["zs"].t[:], in_=ps[:].rearrange("p (a b) -> p a b", a=4), func=AF.Silu), [psb], [W["zs"].b])
                S.add("gpsimd", lambda e, W=W: e.tensor_tensor(out=W["zs"].t[:], in0=W["zs"].t[:], in1=nw4.t[:], op=ALU.mult), [W["zs"].b, nw4.b], [W["zs"].b])
                S.add("scalar", lambda e, W=W: e.activation(out=W["osq"].t[:], in_=r3(pso[:]), func=AF.Square), [psob], [W["osq"].b])
                S.add("vector", lambda e, W=W: e.tensor_reduce(out=W["ss"].t[:], in_=W["osq"].t[:], axis=AX.X, op=ALU.add), [W["osq"].b], [W["ss"].b])
                S.add("scalar", lambda e, W=W: e.activation(out=W["ss"].t[:], in_=W["ss"].t[:], func=AF.Sqrt, bias=C.eps.t[:, 0:1], scale=1.0 / 128.0), [W["ss"].b, C.eps.b], [W["ss"].b])
                S.add("vector", lambda e, W=W: e.reciprocal(out=W["ss"].t[:], in_=W["ss"].t[:]), [W["ss"].b], [W["ss"].b])
                S.add("vector", lambda e, W=W: e.tensor_tensor(out=W["y1"].t[:], in0=r3(pso[:]), in1=W["ss"].t[:].unsqueeze(2).to_broadcast([128, 4, 128]), op=ALU.mult),
                      [psob, W["ss"].b], [W["y1"].b])
                S.add("gpsimd", lambda e, W=W: e.tensor_tensor(out=W["ytok"].t[:], in0=W["y1"].t[:], in1=W["zs"].t[:], op=ALU.mult), [W["y1"].b, W["zs"].b], [W["ytok"].b])
                ps, psb = pb()
                py = ps[:].bitcast(BF16)
                for h in range(4):
                    S.add("tensor", lambda e, py=py, h=h, W=W: e.transpose(out=py[:, h * 128:(h + 1) * 128], in_=W["ytok"].t[:, h, :], identity=ident_b), [W["ytok"].b] + cbb, [psb])
                slot = (t % 16) * 128
                half = t // 16
                S.add("vector", lambda e, py=py, h0=h0, slot=slot, half=half: e.scalar_tensor_tensor(
                    out=C.ymdn.t[:, h0:h0 + 4, slot:slot + 128], in0=py[:, 0:512].rearrange("p (a b) -> p a b", a=4), scalar=C.selm.t[:, half:half + 1],
                    in1=C.ymdn.t[:, h0:h0 + 4, slot:slot + 128], op0=ALU.mult, op1=ALU.add), [psb, C.ymdn.b, C.selm.b], [C.ymdn.b])
                if t == 15:
                    S.add("vector", lambda e, py=py, h0=h0: e.tensor_scalar(out=C.yhdn.t[:, h0:h0 + 4, :], in0=py[:, 0:512].rearrange("p (a b) -> p a b", a=4),
                                                                           scalar1=C.selm.t[:, 2:3], scalar2=None, op0=ALU.mult), [psb, C.selm.b], [C.yhdn.b])

            gens = [hb_gen(0), hb_gen(1)]
            alive = [True, True]
            while any(alive):
                for gi in range(2):
                    if alive[gi]:
                        try:
                            next(gens[gi])
                        except StopIteration:
                            alive[gi] = False


def make_in_maps(inp):
    x = np.asarray(inp["x"], np.float32)
    w_in = np.asarray(inp["w_in"], np.float32)[0]
    maps = []
    cw = np.asarray(inp["dn_conv_w"], np.float32)[0]
    convw = np.ascontiguousarray(cw.reshape(4, 24, 128).transpose(2, 1, 0).reshape(128, 96))
    off = np.cumsum((0,) + (1024, 1024, 1024, 1024, 8, 8, 1024, 256, 256, 256, 256, 256, 256, 24, 1024, 1024))
    o_nq, o_kc, o_vc, o_ksl, o_vsl, o_kwn, o_vwn, o_gate, o_mgd, o_mgn = [int(off[i]) for i in (6, 7, 8, 9, 10, 11, 12, 13, 14, 15)]
    nsa_host = {k_: v_ for k_, v_ in NC2.items() if k_ != "pm_list"}
    pml = NC2["pm_list"]
    NEGT = np.full((128, 512), -30000.0, np.float32)
    ZT = np.zeros((128, 512), np.float32)
    percore = []
    for gg_ in range(2):
        off_ = 2048 * (1 - gg_)
        c1, s1, c2, s2 = _rope_tables(off_)
        sbias, svalid = _sel_tables(off_)
        if gg_ == 1:
            pm0 = [pml[3], pml[4], ZT]
            kneg = np.zeros((128, 1024), np.float32)
        else:
            pm0 = [NEGT, NEGT, NEGT]
            kneg = np.full((128, 1024), -30000.0, np.float32)
        percore.append({"ropeC": c1, "ropeS": s1, "ropeCk": c2, "ropeSk": s2, "selbias": sbias, "selvalid": svalid,
                        "pmask": np.ascontiguousarray(np.concatenate(pml[0:4] + pm0, axis=1)), "keyneg": kneg})
    f32c = lambda a: np.ascontiguousarray(np.asarray(a, np.float32))
    nsa_host["cmp_k_w1"] = f32c(inp["cmp_k_w1"][0])
    nsa_host["cmp_v_w1"] = f32c(inp["cmp_v_w1"][0])
    nsa_host["cmp_k_w2"] = f32c(inp["cmp_k_w2"][0])
    nsa_host["cmp_k_w2r"] = f32c(np.asarray(inp["cmp_k_w2"][0])[:, _rot_perm(1)])
    nsa_host["cmp_v_w2"] = f32c(inp["cmp_v_w2"][0])
    nsa_host["peT_k"] = f32c(np.asarray(inp["cmp_pos_k"][0]).T)
    nsa_host["peT_v"] = f32c(np.asarray(inp["cmp_pos_v"][0]).T)
    for gg in range(2):
        sl = lambda o: w_in[:, o + gg * 128:o + (gg + 1) * 128]
        nsa_host["wn_kf%d" % gg] = f32c(np.concatenate([sl(o_kc), sl(o_vc), sl(o_ksl), sl(o_kwn)], axis=1))
        rp = _rot_perm(1)
        nsa_host["wn_kr%d" % gg] = f32c(np.concatenate([sl(o_ksl)[:, rp], sl(o_kwn)[:, rp]], axis=1))
        nsa_host["wn_v%d" % gg] = f32c(np.concatenate([sl(o_vsl), sl(o_vwn)], axis=1))
        wq = w_in[:, o_nq + gg * 512:o_nq + (gg + 1) * 512]
        nsa_host["wn_q%d" % gg] = f32c(wq)
        nsa_host["wn_qr%d" % gg] = f32c(wq[:, _rot_perm(4)])
        gcols = [o_gate + br * 8 + gg * 4 + hh for br in range(3) for hh in range(4)]
        nsa_host["wn_g%d" % gg] = f32c(w_in[:, gcols])
    fcw_ = np.asarray(inp["ffn_conv_w"], np.float32)[0]
    p2_host = {
        "wm": f32c(w_in[:, o_mgd:o_mgd + 2048]),
        "w_dn": f32c(inp["w_branch_dn"][0]), "w_nsa": f32c(inp["w_branch_nsa"][0]), "w_out": f32c(inp["w_out"][0]),
        "ln1_g": f32c(inp["ln1_g"]).reshape(1, D), "ln1_b": f32c(inp["ln1_b"]).reshape(1, D),
        "ln2_g": f32c(inp["ln2_g"]).reshape(1, D), "ln2_b": f32c(inp["ln2_b"]).reshape(1, D),
        "w_up": f32c(inp["ffn_w_up"][0]), "w_down": f32c(inp["ffn_w_down"][0]),
        "fconvw": f32c(fcw_.reshape(3, 44, 128).transpose(2, 1, 0).reshape(128, 132)),
        "fconvb": f32c(np.asarray(inp["ffn_conv_b"], np.float32)[0].reshape(44, 128).T),
    }
    for c in range(8):
        b, g = c // 2, c % 2
        selm = np.zeros((128, 3), np.float32)
        selm[:, 1] = 1.0
        selm[:, 2] = float(g)
        m = {
            "xT": (np.ascontiguousarray(x[b].T) if g == 1 else
                   np.ascontiguousarray(np.concatenate([np.zeros((D, HALF), np.float32), x[b, 0:HALF].T], axis=1))),
            "consts": CONSTS,
            "selm": selm,
            "wg_qkv": np.ascontiguousarray(w_in[:, 0:3072]),
            "wg_z": np.ascontiguousarray(w_in[:, 3072:4096]),
            "wg_ba": np.ascontiguousarray(w_in[:, 4096:4112]),
            "convw": convw,
            "a_log": np.asarray(inp["dn_a_log"], np.float32).reshape(1, 8),
            "dt_bias": np.asarray(inp["dn_dt_bias"], np.float32).reshape(1, 8),
            "dn_norm_w": np.asarray(inp["dn_norm_w"], np.float32).reshape(1, 128),
        }
        m.update(nsa_host)
        m.update(percore[g])
        m.update(p2_host)
        xh = np.zeros((HALF + 128, D), np.float32)
        if g == 0:
            xh[128:] = x[b, 0:HALF]
        else:
            xh[:] = x[b, HALF - 128:SEQ]
        m["xh"] = xh
        m["xTh"] = np.ascontiguousarray(xh.T)
        maps.append(m)
    return maps


def kernel(**inp):
    nc = build()
    maps = make_in_maps(inp)
    res = run_bass_kernel_spmd(nc, maps, core_ids=list(range(8)))
    out = np.zeros((4, SEQ, D), np.float32)
    for c in range(8):
        b, g = c // 2, c % 2
        out[b, g * HALF:(g + 1) * HALF] = res.results[c]["out"]
    return out


NSA_INPUTS = [("ropeC", [128, 4096]), ("ropeS", [128, 4096]), ("ropeCk", [128, 256]), ("ropeSk", [128, 256]),
              ("selbias", [4096, 64]), ("selvalid", [4096, 64]),
              ("cmask", [128, 2048]), ("wmask", [128, 4096]), ("pmask", [128, 3584]), ("keyneg", [128, 1024]), ("ovl", [128, 128]),
              ("ebig", [128, 4096]), ("gsel", [128, 1536]),
              ("cmp_k_w1", [4096, 256]), ("cmp_v_w1", [4096, 256]), ("cmp_k_w2", [256, 128]), ("cmp_k_w2r", [256, 128]),
              ("cmp_v_w2", [256, 128]), ("peT_k", [128, 32]), ("peT_v", [128, 32])]
for _g in range(2):
    NSA_INPUTS += [("wn_kf%d" % _g, [D, 512]), ("wn_kr%d" % _g, [D, 256]), ("wn_v%d" % _g, [D, 256]),
                   ("wn_q%d" % _g, [D, 512]), ("wn_qr%d" % _g, [D, 512]), ("wn_g%d" % _g, [D, 12])]


def stage_nsa(C, I):
    nc, S, pb = C.nc, C.S, C.pb
    C.nrot = 3
    C.pi = 0
    pselp = (C.ps[3], C.psb[3])
    accs = [((C.ps[4], C.psb[4]), (C.ps[5], C.psb[5])), ((C.ps[6], C.psb[6]), (C.ps[7], C.psb[7]))]
    acci = [0]
    SC = 128.0 ** -0.5
    with ExitStack() as st:
        A = lambda name, shape, dt=F32: TT(C, st, name, shape, dt)
        ident_b = cst(C, "ident", "b")
        ones_b = cst(C, "ones", "b")
        cbb = [C.cb.b]
        stg = A("nstg", [128, 8, 256])
        stg2 = stg.t[:].rearrange("p a b -> p (a b)")
        xb = A("nxb", [128, 8, 512], BF16)
        xT3 = I["xT"].rearrange("(kc p) t -> p kc t", p=128)

        def load_bf(name, ncols, dst, rows=128):
            for c0 in range(0, ncols, 2048):
                n = min(2048, ncols - c0)
                S.dma("sync", stg2[0:rows, 0:n], I[name][0:rows, c0:c0 + n], writes=[stg.b])
                S.add("vector", lambda e, n=n, c0=c0: e.tensor_copy(out=dst.t[0:rows, c0:c0 + n], in_=stg2[0:rows, 0:n]), [stg.b], [dst.b])

        def load_w(name, ncols, dst):
            src3 = I[name].rearrange("(kc p) c -> p kc c", p=128)
            for c0 in range(0, ncols, 256):
                n = min(256, ncols - c0)
                S.dma("sync", stg.t[:, :, 0:n], src3[:, :, c0:c0 + n], writes=[stg.b])
                S.add("vector", lambda e, c0=c0, n=n: e.tensor_copy(out=dst.t[:, :, c0:c0 + n], in_=stg.t[:, :, 0:n]), [stg.b], [dst.b])

        xTb3 = C.xTb.rearrange("(kc p) t -> p kc t", p=128)

        def load_x(tg):
            if C.have_xTb:
                S.dma("sync", xb.t[:], xTb3[:, :, tg * 512:(tg + 1) * 512], reads=[C.xTb_b], writes=[xb.b])
            else:
                for hh_ in range(2):
                    S.dma("sync", stg.t[:], xT3[:, :, tg * 512 + hh_ * 256:tg * 512 + (hh_ + 1) * 256], writes=[stg.b])
                    S.add("gpsimd", lambda e, hh_=hh_: e.tensor_copy(out=xb.t[:, :, hh_ * 256:(hh_ + 1) * 256], in_=stg.t[:]), [stg.b], [xb.b])

        ropeC = A("ropeC", [128, 512])
        ropeS = A("ropeS", [128, 512])
        r1 = A("r1", [128, 512])
        r2 = A("r2", [128, 512])

        def load_rope(tg):
            S.dma("sync", ropeC.t[:], I["ropeC"][:, tg * 512:(tg + 1) * 512], writes=[ropeC.b])
            S.dma("sync", ropeS.t[:], I["ropeS"][:, tg * 512:(tg + 1) * 512], writes=[ropeS.b])

        def proj_fm(W, c0, m, n):
            ps, psb = pb()
            for kc in range(8):
                S.add("tensor", lambda e, ps=ps, kc=kc: e.matmul(ps[0:m, 0:n], lhsT=W.t[:, kc, c0:c0 + m], rhs=xb.t[:, kc, 0:n],
                                                                 start=(kc == 0), stop=(kc == 7)), [W.b, xb.b], [psb])
            return ps, psb

        def rope_into(dst_ap, dstb, ps, psb, psr, psrb, n, scale=1.0):
            S.add("vector", lambda e: e.tensor_tensor(out=r1.t[:, 0:n], in0=ps[:, 0:n], in1=ropeC.t[:, 0:n], op=ALU.mult), [psb, ropeC.b], [r1.b])
            S.add("vector", lambda e: e.tensor_tensor(out=r2.t[:, 0:n], in0=psr[:, 0:n], in1=ropeS.t[:, 0:n], op=ALU.mult), [psrb, ropeS.b], [r2.b])
            S.add("gpsimd", lambda e: e.tensor_tensor(out=r1.t[:, 0:n], in0=r1.t[:, 0:n], in1=r2.t[:, 0:n], op=ALU.add), [r1.b, r2.b], [r1.b])
            S.add("scalar", lambda e: e.activation(out=dst_ap, in_=r1.t[:, 0:n], func=AF.Copy, scale=scale), [r1.b], [dstb])

        import os as _os
        _lim = _os.environ.get("NSA_LIM", "")
        pc_bufs = {}

        def precast_gen():
            jobs = []
            for key, nm, ncol in (("m", "wm", 2048), ("dn", "w_dn", 1024), ("ns", "w_nsa", 1024), ("out", "w_out", 1024), ("up", "w_up", 5632)):
                src3 = I[nm].rearrange("(kc p) c -> p kc c", p=128)
                for c0 in range(0, ncol, 256):
                    jobs.append((key, src3[:, :, c0:c0 + 256], c0, 8, 256))
            srcd = I["w_down"].rearrange("(k p) c -> p k c", p=128)
            for k0 in range(0, 22, 2):
                jobs.append(("down", srcd[:, k0:k0 + 2, :], k0, 2, 1024))
            def views(ji):
                key, src, a0, K_, n_ = jobs[ji]
                pf = pc_bufs["f"][ji % 2]
                pbf = pc_bufs["b"][ji % 2]
                return pf, pbf, pf.t[:].rearrange("p (k c) -> p k c", k=K_), pbf.t[:].rearrange("p (k c) -> p k c", k=K_)

            in_ver = {}

            def issue_in(ji):
                pf, pbf, vf, vb_ = views(ji)
                S.dma("gpsimd", vf, jobs[ji][1], writes=[pf.b])
                in_ver[ji] = pc_bufs["ver"]

            issue_in(0)
            for ji, (key, src, a0, K_, n_) in enumerate(jobs):
                pf, pbf, vf, vb_ = views(ji)
                if in_ver.get(ji) != pc_bufs["ver"]:
                    issue_in(ji)
                if ji + 1 < len(jobs):
                    issue_in(ji + 1)
                S.add("gpsimd", lambda e, vf=vf, vb_=vb_: e.tensor_copy(out=vb_, in_=vf), [pf.b], [pbf.b])
                if key == "down":
                    for hf in range(2):
                        S.dma("gpsimd", C.wsc["down"][hf][:, a0:a0 + 2, :], vb_[:, :, hf * 512:(hf + 1) * 512], reads=[pbf.b], writes=[C.wsc_b])
                elif key == "out":
                    S.dma("gpsimd", C.wsc["out"][:, :, a0:a0 + 256], vb_, reads=[pbf.b], writes=[C.wsc_b])
                else:
                    for j_ in range(2):
                        S.dma("gpsimd", C.wsc[key][a0 // 128 + j_], vb_[:, :, j_ * 128:(j_ + 1) * 128], reads=[pbf.b], writes=[C.wsc_b])
                yield

        pc_gen = precast_gen()
        pc_done = [False]

        def precast_step(n):
            for _ in range(n):
                if pc_done[0]:
                    return
                try:
                    next(pc_gen)
                except StopIteration:
                    pc_done[0] = True

        for g in range(2):
            if _lim and g == 1:
                break
            S.barrier()
            with ExitStack() as sg:
                G = lambda name, shape, dt=F32: TT(C, sg, "%s_g%d" % (name, g), shape, dt)
                kselT = G("kselT", [128, 4096], BF16)
                kwinT = G("kwinT", [128, 4096], BF16)
                vsw = G("vsw", [128, 32, 256], BF16)
                kcT = G("kcT", [128, 256], BF16)
                vc = G("vc", [128, 2, 128], BF16)
                S.add("gpsimd", lambda e: e.memset(kcT.t[:], 0.0), [], [kcT.b])
                with ExitStack() as sa:
                    B = lambda name, shape, dt=F32: TT(C, sa, "%s_a%d" % (name, g), shape, dt)
                    kcmpb = B("kcmpb", [128, 4096], BF16)
                    vcmpb = B("vcmpb", [128, 4096], BF16)
                    pc_bufs["ver"] = g
                    pc_bufs["f"] = [B("pcf%d" % i, [128, 2048]) for i in range(2)]
                    pc_bufs["b"] = [B("pcb%d" % i, [128, 2048], BF16) for i in range(2)]
                    sa1 = ExitStack()
                    B1 = lambda name, shape, dt=F32: TT(C, sa1, "%s_a%d" % (name, g), shape, dt)
                    Wkf = B1("Wkf", [128, 8, 512], BF16)
                    Wkr = B1("Wkr", [128, 8, 256], BF16)
                    Wv = B1("Wv", [128, 8, 256], BF16)
                    load_w("wn_kf%d" % g, 512, Wkf)
                    load_w("wn_kr%d" % g, 256, Wkr)
                    load_w("wn_v%d" % g, 256, Wv)
                    for tg in range(8):
                        precast_step(4 if (g == 0 and tg < 3) else 3)
                        load_x(tg)
                        load_rope(tg)
                        cs = slice(tg * 512, (tg + 1) * 512)
                        if _lim == "A00":
                            continue
                        for j, dst in enumerate((kcmpb, vcmpb, kselT, kwinT)):
                            ps, psb = proj_fm(Wkf, j * 128, 128, 512)
                            if j < 2 or _lim == "A01":
                                S.add("scalar", lambda e, ps=ps, dst=dst, cs=cs: e.activation(out=dst.t[:, cs], in_=ps[:], func=AF.Copy), [psb], [dst.b])
                            else:
                                psr, psrb = proj_fm(Wkr, (j - 2) * 128, 128, 512)
                                rope_into(dst.t[:, cs], dst.b, ps, psb, psr, psrb, 512)
                        for s4 in range(4):
                            if _lim in ("A01", "A02"):
                                break
                            ps, psb = pb()
                            for kc in range(8):
                                S.add("tensor", lambda e, ps=ps, kc=kc, s4=s4: e.matmul(ps[:, 0:256], lhsT=xb.t[:, kc, s4 * 128:(s4 + 1) * 128], rhs=Wv.t[:, kc, :],
                                                                                       start=(kc == 0), stop=(kc == 7)), [Wv.b, xb.b], [psb])
                            S.add("scalar", lambda e, ps=ps, s4=s4, tg=tg: e.activation(out=vsw.t[:, tg * 4 + s4, :], in_=ps[:, 0:256], func=AF.Copy), [psb], [vsw.b])
                    if g == 1:
                        precast_step(100)
                    sa1.close()
                    S.barrier()
                    if _lim.startswith("A0"):
                        continue
                    w1 = B("w1", [128, 16, 256], BF16)
                    w2 = B("w2", [128, 2, 128], BF16)
                    w2r = B("w2r", [128, 2, 128], BF16)
                    pe = B("pe", [128, 32])
                    hid = B("hid", [128, 2, 256], BF16)
                    S.add("gpsimd", lambda e: e.memset(hid.t[:], 0.0), [], [hid.b])
                    kpe = [B("kpe%d" % i, [128, 255], BF16) for i in range(3)]
                    for is_k, src, nm in ((True, kcmpb, "k"), (False, vcmpb, "v")):
                        w1v = I["cmp_%s_w1" % nm].rearrange("(l d) h -> d l h", d=128)
                        def load_w1(hf):
                            for q_ in range(2):
                                S.dma("sync", stg2.rearrange("p (l h) -> p l h", l=8), w1v[:, hf * 16 + q_ * 8:hf * 16 + (q_ + 1) * 8, :], writes=[stg.b])
                                S.add("gpsimd", lambda e, q_=q_: e.tensor_copy(out=w1.t[:, q_ * 8:(q_ + 1) * 8, :], in_=stg2.rearrange("p (l h) -> p l h", l=8)), [stg.b], [w1.b])
                        S.dma("sync", stg.t[:, 0:2, 0:128], I["cmp_%s_w2" % nm].rearrange("(hc p) d -> p hc d", p=128), writes=[stg.b])
                        S.add("vector", lambda e: e.tensor_copy(out=w2.t[:], in_=stg.t[:, 0:2, 0:128]), [stg.b], [w2.b])
                        if is_k:
                            S.dma("sync", stg.t[:, 0:2, 0:128], I["cmp_k_w2r"].rearrange("(hc p) d -> p hc d", p=128), writes=[stg.b])
                            S.add("vector", lambda e: e.tensor_copy(out=w2r.t[:], in_=stg.t[:, 0:2, 0:128]), [stg.b], [w2r.b])
                        S.dma("sync", pe.t[:], I["peT_%s" % nm], writes=[pe.b])
                        hps = [pb(), pb()]
                        for l in range(32):
                            if l % 16 == 0:
                                load_w1(l // 16)
                            kp = kpe[l % 3]
                            S.add("vector", lambda e, kp=kp, l=l, src=src: e.tensor_scalar(out=kp.t[:], in0=src.t[:, l:l + 4065:16], scalar1=pe.t[:, l:l + 1], scalar2=None, op0=ALU.add),
                                  [src.b, pe.b], [kp.b])
                            for hc in range(2):
                                S.add("tensor", lambda e, kp=kp, l=l, hc=hc, hps=hps: e.matmul(hps[hc][0][:, 0:255], lhsT=w1.t[:, l % 16, hc * 128:(hc + 1) * 128], rhs=kp.t[:],
                                                                                     start=(l == 0), stop=(l == 31)), [w1.b, kp.b], [hps[hc][1]])
                        for hc in range(2):
                            S.add("scalar", lambda e, hc=hc, hps=hps: e.activation(out=hid.t[:, hc, 0:255], in_=hps[hc][0][:, 0:255], func=AF.Silu), [hps[hc][1]], [hid.b])
                        if is_k:
                            ps, psb = pb()
                            psr, psrb = pb()
                            for hc in range(2):
                                S.add("tensor", lambda e, ps=ps, hc=hc: e.matmul(ps[:, 0:255], lhsT=w2.t[:, hc, :], rhs=hid.t[:, hc, 0:255], start=(hc == 0), stop=(hc == 1)), [w2.b, hid.b], [psb])
                                S.add("tensor", lambda e, psr=psr, hc=hc: e.matmul(psr[:, 0:255], lhsT=w2r.t[:, hc, :], rhs=hid.t[:, hc, 0:255], start=(hc == 0), stop=(hc == 1)), [w2r.b, hid.b], [psrb])
                            S.dma("sync", ropeC.t[:, 0:256], I["ropeCk"], writes=[ropeC.b])
                            S.dma("sync", ropeS.t[:, 0:256], I["ropeSk"], writes=[ropeS.b])
                            rope_into(kcT.t[:, 0:255], kcT.b, ps, psb, psr, psrb, 255)
                        else:
                            for ncn in range(2):
                                ps, psb = pb()
                                for hc in range(2):
                                    S.add("tensor", lambda e, ps=ps, hc=hc, ncn=ncn: e.matmul(ps[:, 0:128], lhsT=hid.t[:, hc, ncn * 128:(ncn + 1) * 128], rhs=w2.t[:, hc, :],
                                                                                             start=(hc == 0), stop=(hc == 1)), [w2.b, hid.b], [psb])
                                S.add("scalar", lambda e, ps=ps, ncn=ncn: e.activation(out=vc.t[:, ncn, :], in_=ps[:, 0:128], func=AF.Copy), [psb], [vc.b])
                if _lim.startswith("A"):
                    continue
                S.barrier()
                with ExitStack() as sc:
                    B = lambda name, shape, dt=F32: TT(C, sc, "%s_c%d" % (name, g), shape, dt)
                    wmask = B("wmask", [128, 4096], BF16)
                    pmask = B("pmask", [128, 3584], BF16)
                    keyneg = B("keyneg", [128, 1024], BF16)
                    onesrow = B("onesrow", [128, 512], BF16)
                    S.add("gpsimd", lambda e: e.memset(onesrow.t[:], 1.0), [], [onesrow.b])
                    ebig = B("ebig", [128, 4096], BF16)
                    ovl = B("ovl", [128, 128], BF16)
                    gsel = B("gsel", [128, 1536])
                    load_bf("wmask", 4096, wmask)
                    load_bf("pmask", 3584, pmask)
                    load_bf("keyneg", 1024, keyneg)
                    load_bf("ebig", 4096, ebig, rows=64)
                    load_bf("ovl", 128, ovl)
                    S.dma("sync", gsel.t[0:12, :], I["gsel"][0:12, :], writes=[gsel.b])
                    Wq = B("Wq", [128, 8, 512], BF16)
                    Wqr = B("Wqr", [128, 8, 512], BF16)
                    Wg = B("Wg", [128, 8, 12], BF16)
                    load_w("wn_q%d" % g, 512, Wq)
                    load_w("wn_qr%d" % g, 512, Wqr)
                    load_w("wn_g%d" % g, 12, Wg)
                    qT = [B("qT%d" % h, [128, 512], BF16) for h in range(4)]
                    gs = B("gs", [128, 512])
                    PT = [B("PT%d" % i, [128, 512], BF16) for i in range(4)]
                    pti = [0]
                    PTcl = [B("PTc%d" % i, [128, 2, 512], BF16) for i in range(2)]
                    acc = B("acc", [128, 4, 512])
                    rr = B("rr", [128, 512])
                    rg = r2
                    tmpc = rr
                    negselT = B("negselT", [128, 512], BF16)
                    selb = B("selb", [128, 64])
                    selv = B("selv", [128, 64])
                    sc1 = B("sc1", [128, 64])
                    sc2 = B("sc2", [128, 64])
                    slt = B("slt", [128, 64])
                    m8 = B("m8", [128, 16])
                    nsb = B("nsb", [128, 64], BF16)
                    zb = B("zb", [128, 256], BF16)
                    S.add("gpsimd", lambda e: e.memset(zb.t[:], 0.0), [], [zb.b])

                    def combine(h, br, ao, asum):
                        S.add("vector", lambda e: e.tensor_scalar(out=rr.t[:], in0=asum[0][:], scalar1=1e-30, scalar2=None, op0=ALU.max), [asum[1]], [rr.b])
                        S.add("vector", lambda e: e.reciprocal(out=rr.t[:], in_=rr.t[:]), [rr.b], [rr.b])
                        psg, psgb = pb()
                        r = br * 4 + h
                        S.add("tensor", lambda e: e.matmul(psg[:], lhsT=gsel.t[0:12, r * 128:(r + 1) * 128], rhs=gs.t[0:12, :], start=True, stop=True), [gsel.b, gs.b], [psgb])
                        S.add("vector", lambda e: e.tensor_tensor(out=rg.t[:], in0=rr.t[:], in1=psg[:], op=ALU.mult), [rr.b, psgb], [rg.b])
                        if br == 0:
                            S.add("vector", lambda e: e.tensor_tensor(out=acc.t[:, h, :], in0=ao[0][:], in1=rg.t[:], op=ALU.mult), [ao[1], rg.b], [acc.b])
                        else:
                            S.add("vector", lambda e: e.tensor_tensor(out=tmpc.t[:], in0=ao[0][:], in1=rg.t[:], op=ALU.mult), [ao[1], rg.b], [tmpc.b])
                            S.add("gpsimd", lambda e: e.tensor_tensor(out=acc.t[:, h, :], in0=acc.t[:, h, :], in1=tmpc.t[:], op=ALU.add), [acc.b, tmpc.b], [acc.b])

                    for tg in range(3, 8):
                        load_x(tg)
                        load_rope(tg)
                        for h in range(4):
                            ps, psb = proj_fm(Wq, h * 128, 128, 512)
                            psr, psrb = proj_fm(Wqr, h * 128, 128, 512)
                            rope_into(qT[h].t[:], qT[h].b, ps, psb, psr, psrb, 512, scale=SC)
                        ps, psb = proj_fm(Wg, 0, 12, 512)
                        S.add("scalar", lambda e, ps=ps: e.activation(out=gs.t[0:12, :], in_=ps[0:12, :], func=AF.Sigmoid), [psb], [gs.b])
                        ncs = [0] if tg < 4 else [0, 1]
                        S.add("tensor", lambda e: e.matmul(pselp[0][:, 0:256], lhsT=zb.t[:, 0:128], rhs=zb.t[:, 0:256], start=True, stop=False), [zb.b], [pselp[1]])
                        for h in range(4):
                            ao, asum = accs[acci[0] % 2]
                            acci[0] += 1
                            PTc = PTcl[h % 2]
                            for ii, nci in enumerate(ncs):
                                ps, psb = pb()
                                dlt = (tg - 4) if nci == 1 else 4 + min(tg - 3, 2)
                                msk = True
                                S.add("tensor", lambda e, ps=ps, nci=nci, h=h, msk=msk: e.matmul(ps[:], lhsT=kcT.t[:, nci * 128:(nci + 1) * 128], rhs=qT[h].t[:], start=True, stop=(not msk)),
                                      [kcT.b, qT[h].b], [psb])
                                if msk:
                                    S.add("tensor", lambda e, ps=ps, dlt=dlt: e.matmul(ps[:], lhsT=ident_b, rhs=pmask.t[:, dlt * 512:(dlt + 1) * 512], start=False, stop=True),
                                          [pmask.b] + cbb, [psb])
                                S.add("scalar", lambda e, ps=ps, PTc=PTc, nci=nci: e.activation(out=PTc.t[:, nci, :], in_=ps[:], func=AF.Exp), [psb], [PTc.b])
                                S.add("tensor", lambda e, PTc=PTc, nci=nci, ii=ii, ao=ao, n=len(ncs): e.matmul(ao[0][:], lhsT=vc.t[:, nci, :], rhs=PTc.t[:, nci, :], start=(ii == 0), stop=(ii == n - 1)),
                                      [vc.b, PTc.b], [ao[1]])
                                S.add("tensor", lambda e, PTc=PTc, nci=nci, ii=ii, asum=asum, n=len(ncs): e.matmul(asum[0][:], lhsT=ones_b, rhs=PTc.t[:, nci, :], start=(ii == 0), stop=(ii == n - 1)),
                                      [PTc.b] + cbb, [asum[1]])
                            combine(h, 0, ao, asum)
                            for ii, nci in enumerate(ncs):
                                S.add("vector", lambda e, PTc=PTc, nci=nci: e.tensor_tensor(out=PTc.t[:, nci, :], in0=PTc.t[:, nci, :], in1=rr.t[:], op=ALU.mult), [PTc.b, rr.b], [PTc.b])
                                for s4 in range(4):
                                    first = (h == 0 and ii == 0)
                                    last = (h == 3 and ii == len(ncs) - 1)
                                    S.add("tensor", lambda e, PTc=PTc, nci=nci, s4=s4, first=first, last=last: e.matmul(
                                        pselp[0][:, s4 * 64:(s4 + 1) * 64], lhsT=PTc.t[:, nci, s4 * 128:(s4 + 1) * 128], rhs=ovl.t[:, nci * 64:(nci + 1) * 64],
                                        start=False, stop=last), [PTc.b, ovl.b], [pselp[1]])
                        for s4 in range(4):
                            tile_i = tg * 4 + s4
                            S.dma("sync", selb.t[:], I["selbias"][tile_i * 128:(tile_i + 1) * 128, :], writes=[selb.b])
                            S.dma("sync", selv.t[:], I["selvalid"][tile_i * 128:(tile_i + 1) * 128, :], writes=[selv.b])
                            ps, psb = pselp
                            S.add("vector", lambda e, ps=ps, s4=s4: e.tensor_tensor(out=sc1.t[:], in0=ps[:, s4 * 64:(s4 + 1) * 64], in1=selb.t[:], op=ALU.add), [psb, selb.b], [sc1.b])
                            S.add("vector", lambda e: e.max(out=m8.t[:, 0:8], in_=sc1.t[:]), [sc1.b], [m8.b])
                            S.add("vector", lambda e: e.match_replace(out=sc2.t[:], in_to_replace=m8.t[:, 0:8], in_values=sc1.t[:], imm_value=-2e9), [sc1.b, m8.b], [sc2.b])
                            S.add("vector", lambda e: e.max(out=m8.t[:, 8:16], in_=sc2.t[:]), [sc2.b], [m8.b])
                            S.add("vector", lambda e: e.tensor_scalar(out=slt.t[:], in0=sc1.t[:], scalar1=m8.t[:, 15:16], scalar2=None, op0=ALU.is_ge), [sc1.b, m8.b], [slt.b])
                            S.add("vector", lambda e: e.tensor_tensor(out=slt.t[:], in0=slt.t[:], in1=selv.t[:], op=ALU.mult), [slt.b, selv.b], [slt.b])
                            S.add("vector", lambda e: e.tensor_scalar(out=nsb.t[:], in0=slt.t[:], scalar1=30000.0, scalar2=-30000.0, op0=ALU.mult, op1=ALU.add), [slt.b], [nsb.b])
                            pst, pstb = pb()
                            ptb = pst[:].bitcast(BF16)
                            S.add("tensor", lambda e, ptb=ptb: e.transpose(out=ptb[0:64, 0:128], in_=nsb.t[:], identity=ident_b), [nsb.b] + cbb, [pstb])
                            S.add("scalar", lambda e, ptb=ptb, s4=s4: e.activation(out=negselT.t[0:64, s4 * 128:(s4 + 1) * 128], in_=ptb[0:64, 0:128], func=AF.Copy), [pstb], [negselT.b])
                        tiles = []
                        for br in (1, 2):
                            for h in range(4):
                                pair = accs[acci[0] % 2]
                                acci[0] += 1
                                kts = list(range(0, 4 * tg + 4)) if br == 1 else list(range(max(0, 4 * tg - 4), 4 * tg + 4))
                                for ii, kt in enumerate(kts):
                                    tiles.append((br, h, kt, ii, len(kts), pair))

                        def stage1(tl):
                            br, h, kt, ii, n, pair = tl
                            ps, psb = pb()
                            ksrc = kselT if br == 1 else kwinT
                            S.add("tensor", lambda e: e.matmul(ps[:], lhsT=ksrc.t[:, kt * 128:(kt + 1) * 128], rhs=qT[h].t[:], start=True, stop=False), [ksrc.b, qT[h].b], [psb])
                            if br == 1:
                                diag = kt >= 4 * tg
                                S.add("tensor", lambda e: e.matmul(ps[:], lhsT=ebig.t[0:64, kt * 128:(kt + 1) * 128], rhs=negselT.t[0:64, :], start=False, stop=(not diag)), [ebig.b, negselT.b], [psb])
                                if diag:
                                    dd = kt - 4 * tg
                                    S.add("tensor", lambda e: e.matmul(ps[:], lhsT=ident_b, rhs=wmask.t[:, (4 + dd) * 512:(5 + dd) * 512], start=False, stop=True), [wmask.b] + cbb, [psb])
                            else:
                                dd = kt - (4 * tg - 4)
                                if kt < 16:
                                    S.add("tensor", lambda e: e.matmul(ps[:], lhsT=keyneg.t[0:1, (kt - 8) * 128:(kt - 7) * 128], rhs=onesrow.t[0:1, :], start=False, stop=False), [keyneg.b, onesrow.b], [psb])
                                S.add("tensor", lambda e: e.matmul(ps[:], lhsT=ident_b, rhs=wmask.t[:, dd * 512:(dd + 1) * 512], start=False, stop=True), [wmask.b] + cbb, [psb])
                            pt = PT[pti[0] % len(PT)]
                            pti[0] += 1
                            S.add("scalar", lambda e: e.activation(out=pt.t[:], in_=ps[:], func=AF.Exp), [psb], [pt.b])
                            return pt

                        def stage2(tl, pt):
                            br, h, kt, ii, n, pair = tl
                            ao, asum = pair
                            vo = 0 if br == 1 else 128
                            S.add("tensor", lambda e: e.matmul(ao[0][:], lhsT=vsw.t[:, kt, vo:vo + 128], rhs=pt.t[:], start=(ii == 0), stop=(ii == n - 1)), [vsw.b, pt.b], [ao[1]])
                            S.add("tensor", lambda e: e.matmul(asum[0][:], lhsT=ones_b, rhs=pt.t[:], start=(ii == 0), stop=(ii == n - 1)), [pt.b] + cbb, [asum[1]])
                            if ii == n - 1:
                                combine(h, br, ao, asum)

                        LA = 3
                        C.nrot = 4
                        pend = []
                        for idx, tl in enumerate(tiles):
                            pend.append((tl, stage1(tl)))
                            if len(pend) > LA:
                                t0_, p0_ = pend.pop(0)
                                stage2(t0_, p0_)
                        while pend:
                            t0_, p0_ = pend.pop(0)
                            stage2(t0_, p0_)
                        C.nrot = 3
                        C.pi = 0
                        half = tg // 4
                        slot = (tg % 4) * 512
                        for h in range(4):
                            if tg >= 4:
                                S.add("vector", lambda e, h=h, g=g, slot=slot, half=half: e.scalar_tensor_tensor(out=C.ymnsa.t[:, g * 4 + h, slot:slot + 512], in0=acc.t[:, h, :], scalar=C.selm.t[:, half:half + 1],
                                    in1=C.ymnsa.t[:, g * 4 + h, slot:slot + 512], op0=ALU.mult, op1=ALU.add), [acc.b, C.ymnsa.b, C.selm.b], [C.ymnsa.b])
                            if tg == 3:
                                S.add("vector", lambda e, h=h, g=g: e.tensor_scalar(out=C.yhnsa.t[:, g * 4 + h, :], in0=acc.t[:, h, 384:512], scalar1=C.selm.t[:, 2:3], scalar2=None, op0=ALU.mult),
                                      [acc.b, C.selm.b], [C.yhnsa.b])
    C.nrot = 8
    C.pi = 0


P2_INPUTS = [("xh", [HALF + 128, D]), ("xTh", [D, HALF + 128]), ("wm", [D, 2048]), ("w_dn", [D, D]), ("w_nsa", [D, D]), ("w_out", [D, D]),
             ("ln1_g", [1, D]), ("ln1_b", [1, D]), ("w_up", [D, 5632]), ("fconvw", [128, 132]), ("fconvb", [128, 44]),
             ("w_down", [2816, D]), ("ln2_g", [1, D]), ("ln2_b", [1, D])]


def stage_p2(C, I, out_ap, fin):
    nc, S, pb = C.nc, C.S, C.pb
    C.nrot = 8
    C.pi = 0
    S.barrier()
    with ExitStack() as st:
        A = lambda name, shape, dt=F32: TT(C, st, name, shape, dt)
        ident_b = cst(C, "ident", "b")
        cbb = [C.cb.b]
        stgs = [A("pstg%d" % i, [128, 1024]) for i in range(2)]
        sti = [0]

        def load_chunk(src, dst_ap, dstb, K, ncol):
            sg = stgs[sti[0] % 2]
            sti[0] += 1
            v = sg.t[:, 0:K * ncol].rearrange("p (k c) -> p k c", k=K)
            S.dma("sync", v, src, writes=[sg.b])
            if sti[0] % 2:
                S.add("scalar", lambda e: e.activation(out=dst_ap, in_=v, func=AF.Copy), [sg.b], [dstb])
            else:
                S.add("gpsimd", lambda e: e.tensor_copy(out=dst_ap, in_=v), [sg.b], [dstb])

        def load_sc(src, dst_ap, dstb):
            S.dma("sync", dst_ap, src, reads=[C.wsc_b], writes=[dstb])

        wdn3 = I["w_dn"].rearrange("(kc p) c -> p kc c", p=128)
        wns3 = I["w_nsa"].rearrange("(kc p) c -> p kc c", p=128)
        wm3 = I["wm"].rearrange("(kc p) c -> p kc c", p=128)
        wo3 = I["w_out"].rearrange("(kc p) c -> p kc c", p=128)
        wup3 = I["w_up"].rearrange("(kc p) c -> p kc c", p=128)
        wdw3 = I["w_down"].rearrange("(k p) c -> p k c", p=128)
        xTh3 = I["xTh"].rearrange("(kc p) t -> p kc t", p=128)

        x1 = A("x1", [128, 4, D])
        x1T = A("x1T", [128, 8, 512], BF16)
        hist = A("hist", [128, 44, 2])
        lnp = A("lnp", [128, 2, D])
        fcw = A("fcw", [128, 132])
        fcb = A("fcb", [128, 44])
        st1 = A("st1", [128, 8])
        junk = A("junk", [128, D], BF16)
        S.dma("sync", fcw.t[:], I["fconvw"], writes=[fcw.b])
        S.dma("sync", fcb.t[:], I["fconvb"], writes=[fcb.b])

        def load_ln(which):
            S.dma("sync", lnp.t[:, 0, :], I["ln%d_g" % which].partition_broadcast(128), writes=[lnp.b])
            S.dma("sync", lnp.t[:, 1, :], I["ln%d_b" % which].partition_broadcast(128), writes=[lnp.b])

        def layernorm(r_ap, rb, out_ap_, outb):
            s1 = [st1.b]
            S.add("scalar", lambda e: e.activation(out=junk.t[:], in_=r_ap, func=AF.Copy, accum_out=st1.t[:, 0:1]), [rb], [junk.b, st1.b])
            S.add("scalar", lambda e: e.activation(out=junk.t[:], in_=r_ap, func=AF.Square, accum_out=st1.t[:, 1:2]), [rb], [junk.b, st1.b])
            S.add("vector", lambda e: e.tensor_scalar(out=st1.t[:, 2:3], in0=st1.t[:, 0:1], scalar1=1.0 / D, scalar2=None, op0=ALU.mult), s1, s1)
            S.add("vector", lambda e: e.tensor_tensor(out=st1.t[:, 3:4], in0=st1.t[:, 2:3], in1=st1.t[:, 2:3], op=ALU.mult), s1, s1)
            S.add("vector", lambda e: e.scalar_tensor_tensor(out=st1.t[:, 4:5], in0=st1.t[:, 1:2], scalar=1.0 / D, in1=st1.t[:, 3:4], op0=ALU.mult, op1=ALU.subtract), s1, s1)
            S.add("scalar", lambda e: e.activation(out=st1.t[:, 5:6], in_=st1.t[:, 4:5], func=AF.Sqrt, bias=C.eps.t[:, 1:2], scale=1.0), s1 + [C.eps.b], s1)
            S.add("vector", lambda e: e.reciprocal(out=st1.t[:, 6:7], in_=st1.t[:, 5:6]), s1, s1)
            S.add("vector", lambda e: e.tensor_scalar(out=out_ap_, in0=r_ap, scalar1=st1.t[:, 2:3], scalar2=st1.t[:, 6:7], op0=ALU.subtract, op1=ALU.mult), [rb] + s1, [outb])
            S.add("gpsimd", lambda e: e.tensor_tensor(out=out_ap_, in0=out_ap_, in1=lnp.t[:, 0, :], op=ALU.mult), [outb, lnp.b], [outb])
            S.add("gpsimd", lambda e: e.tensor_tensor(out=out_ap_, in0=out_ap_, in1=lnp.t[:, 1, :], op=ALU.add), [outb, lnp.b], [outb])

        def do_block(blk):
            N = 128 if blk < 0 else 512
            nt = N // 128
            tok0 = 0 if blk < 0 else 128 + blk * 512
            if blk < 0:
                ysrc = lambda T_, kc: T_[1].t[:, kc, :]
            else:
                ysrc = lambda T_, kc, blk=blk: T_[0].t[:, kc, blk * 512:(blk + 1) * 512]
            YD = (C.ymdn, C.yhdn)
            YN = (C.ymnsa, C.yhnsa)
            ydb = [C.ymdn.b, C.yhdn.b]
            ynb = [C.ymnsa.b, C.yhnsa.b]
            S.barrier()
            with ExitStack() as sa:
                B = lambda name, shape, dt=F32: TT(C, sa, "%s_b%d" % (name, blk + 1), shape, dt)
                yT = B("yT", [128, 8, 512], BF16)
                xgb = B("xgb", [128, 8, 512], BF16)
                wch = {k: [B("wch_%s%d" % (k, i), [128, 8, 128], BF16) for i in range(3)] for k in ("dn", "ns", "md", "mn")}
                gdt = B("gdt", [128, 512])
                gnt = B("gnt", [128, 512])
                t1 = B("t1", [128, 512])
                t2 = B("t2", [128, 512])
                wo = B("wo", [128, 8, D], BF16)
                xt = [B("xt%d" % i, [128, D]) for i in range(2)]
                rbuf = B("rbuf", [128, D])
                x1b = B("x1b", [128, D], BF16)
                for k0 in range(0, 8, 2):
                    load_chunk(xTh3[:, k0:k0 + 2, tok0:tok0 + N], xgb.t[:, k0:k0 + 2, 0:N], xgb.b, 2, N)
                for c in range(8):
                    cs_ = slice(c * 128, (c + 1) * 128)
                    w = {k: wch[k][c % 3] for k in wch}
                    load_sc(C.wsc["dn"][c], w["dn"].t[:], w["dn"].b)
                    load_sc(C.wsc["ns"][c], w["ns"].t[:], w["ns"].b)
                    load_sc(C.wsc["m"][c], w["md"].t[:], w["md"].b)
                    load_sc(C.wsc["m"][8 + c], w["mn"].t[:], w["mn"].b)
                    pd, pdb = pb()
                    pn, pnb = pb()
                    pgd, pgdb = pb()
                    pgn, pgnb = pb()
                    for kc in range(8):
                        S.add("tensor", lambda e, pd=pd, kc=kc, w=w: e.matmul(pd[:, 0:N], lhsT=w["dn"].t[:, kc, :], rhs=ysrc(YD, kc), start=(kc == 0), stop=(kc == 7)), [w["dn"].b] + ydb, [pdb])
                    for kc in range(8):
                        S.add("tensor", lambda e, pn=pn, kc=kc, w=w: e.matmul(pn[:, 0:N], lhsT=w["ns"].t[:, kc, :], rhs=ysrc(YN, kc), start=(kc == 0), stop=(kc == 7)), [w["ns"].b] + ynb, [pnb])
                    for kc in range(8):
                        S.add("tensor", lambda e, pgd=pgd, kc=kc, w=w: e.matmul(pgd[:, 0:N], lhsT=w["md"].t[:, kc, :], rhs=xgb.t[:, kc, 0:N], start=(kc == 0), stop=(kc == 7)), [w["md"].b, xgb.b], [pgdb])
                    for kc in range(8):
                        S.add("tensor", lambda e, pgn=pgn, kc=kc, w=w: e.matmul(pgn[:, 0:N], lhsT=w["mn"].t[:, kc, :], rhs=xgb.t[:, kc, 0:N], start=(kc == 0), stop=(kc == 7)), [w["mn"].b, xgb.b], [pgnb])
                    S.add("scalar", lambda e, pgd=pgd: e.activation(out=gdt.t[:, 0:N], in_=pgd[:, 0:N], func=AF.Sigmoid), [pgdb], [gdt.b])
                    S.add("scalar", lambda e, pgn=pgn: e.activation(out=gnt.t[:, 0:N], in_=pgn[:, 0:N], func=AF.Sigmoid), [pgnb], [gnt.b])
                    S.add("vector", lambda e, pd=pd: e.tensor_tensor(out=t1.t[:, 0:N], in0=gdt.t[:, 0:N], in1=pd[:, 0:N], op=ALU.mult), [gdt.b, pdb], [t1.b])
                    S.add("vector", lambda e, pn=pn: e.tensor_tensor(out=t2.t[:, 0:N], in0=gnt.t[:, 0:N], in1=pn[:, 0:N], op=ALU.mult), [gnt.b, pnb], [t2.b])
                    S.add("vector", lambda e, c=c: e.tensor_tensor(out=yT.t[:, c, 0:N], in0=t1.t[:, 0:N], in1=t2.t[:, 0:N], op=ALU.add), [t1.b, t2.b], [yT.b])
                load_sc(C.wsc["out"], wo.t[:], wo.b)
                load_ln(1)
                for s in range(nt):
                    xti = xt[s % 2]
                    S.dma("sync", xti.t[:], I["xh"][tok0 + s * 128:tok0 + (s + 1) * 128, :], writes=[xti.b])
                    for hf in range(2):
                        pz, pzb = pb()
                        for kc in range(8):
                            S.add("tensor", lambda e, pz=pz, kc=kc, s=s, hf=hf: e.matmul(pz[:], lhsT=yT.t[:, kc, s * 128:(s + 1) * 128], rhs=wo.t[:, kc, hf * 512:(hf + 1) * 512],
                                                                                      start=(kc == 0), stop=(kc == 7)), [yT.b, wo.b], [pzb])
                        S.add("vector", lambda e, pz=pz, hf=hf, xti=xti: e.scalar_tensor_tensor(out=rbuf.t[:, hf * 512:(hf + 1) * 512], in0=xti.t[:, hf * 512:(hf + 1) * 512], scalar=ALPHA, in1=pz[:],
                                                                                               op0=ALU.mult, op1=ALU.add), [xti.b, pzb], [rbuf.b])
                    layernorm(rbuf.t[:], rbuf.b, x1.t[:, s, :], x1.b)
                    S.add("scalar", lambda e, s=s: e.activation(out=x1b.t[:], in_=x1.t[:, s, :], func=AF.Copy), [x1.b], [x1b.b])
                    pt, ptb_ = pb()
                    ptv = pt[:].bitcast(BF16)
                    for kc in range(8):
                        S.add("tensor", lambda e, ptv=ptv, kc=kc: e.transpose(out=ptv[:, kc * 128:(kc + 1) * 128], in_=x1b.t[:, kc * 128:(kc + 1) * 128], identity=ident_b), [x1b.b] + cbb, [ptb_])
                    S.add("scalar", lambda e, ptv=ptv, s=s: e.activation(out=x1T.t[:, :, s * 128:(s + 1) * 128], in_=ptv.rearrange("p (a b) -> p a b", a=8), func=AF.Copy), [ptb_], [x1T.b])
            S.barrier()
            with ExitStack() as sb:
                B = lambda name, shape, dt=F32: TT(C, sb, "%s_f%d" % (name, blk + 1), shape, dt)
                wup = [B("wup%d" % i, [128, 8, 128], BF16) for i in range(6)]
                wi = [0]
                if blk < 0:
                    for cti in range(44):
                        w = wup[wi[0] % 6]
                        wi[0] += 1
                        load_sc(C.wsc["up"][cti], w.t[:], w.b)
                        ps, psb = pb()
                        for kc in range(8):
                            S.add("tensor", lambda e, ps=ps, kc=kc, w=w: e.matmul(ps[:, 0:128], lhsT=w.t[:, kc, :], rhs=x1T.t[:, kc, 0:128], start=(kc == 0), stop=(kc == 7)), [w.b, x1T.b], [psb])
                        S.add("vector", lambda e, ps=ps, cti=cti: e.tensor_scalar(out=hist.t[:, cti, :], in0=ps[:, 126:128], scalar1=C.selm.t[:, 2:3], scalar2=None, op0=ALU.mult), [psb, C.selm.b], [hist.b])
                    return
                hT = B("hT", [128, 22, 512], BF16)
                ub = [B("ub%d" % i, [128, 514]) for i in range(2)]
                cc = [B("cc%d" % i, [128, 512]) for i in range(2)]
                sgt = B("sgt", [128, 512])
                wdh = B("wdh", [128, 22, 512], BF16)
                r2 = B("r2", [128, 4, D])
                ot = [B("ot%d" % i, [128, D]) for i in range(2)]
                for ct in range(22):
                    for gv in range(2):
                        cti = ct + 22 * gv
                        w = wup[wi[0] % 6]
                        wi[0] += 1
                        load_sc(C.wsc["up"][cti], w.t[:], w.b)
                        ps, psb = pb()
                        for kc in range(8):
                            S.add("tensor", lambda e, ps=ps, kc=kc, w=w: e.matmul(ps[:], lhsT=w.t[:, kc, :], rhs=x1T.t[:, kc, :], start=(kc == 0), stop=(kc == 7)), [w.b, x1T.b], [psb])
                        u = ub[gv]
                        S.add("gpsimd", lambda e, u=u, cti=cti: e.tensor_copy(out=u.t[:, 0:2], in_=hist.t[:, cti, :]), [hist.b], [u.b])
                        S.add("scalar", lambda e, u=u, ps=ps: e.activation(out=u.t[:, 2:514], in_=ps[:], func=AF.Copy), [psb], [u.b])
                        S.add("gpsimd", lambda e, u=u, cti=cti: e.tensor_copy(out=hist.t[:, cti, :], in_=u.t[:, 512:514]), [u.b], [hist.b])
                        cv = cc[gv]
                        S.add("vector", lambda e, u=u, cv=cv, cti=cti: e.tensor_scalar(out=cv.t[:], in0=u.t[:, 2:514], scalar1=fcw.t[:, cti * 3 + 2:cti * 3 + 3], scalar2=fcb.t[:, cti:cti + 1],
                                                                                    op0=ALU.mult, op1=ALU.add), [u.b, fcw.b, fcb.b], [cv.b])
                        S.add("vector", lambda e, u=u, cv=cv, cti=cti: e.scalar_tensor_tensor(out=cv.t[:], in0=u.t[:, 1:513], scalar=fcw.t[:, cti * 3 + 1:cti * 3 + 2], in1=cv.t[:],
                                                                                           op0=ALU.mult, op1=ALU.add), [u.b, fcw.b, cv.b], [cv.b])
                        S.add("vector", lambda e, u=u, cv=cv, cti=cti: e.scalar_tensor_tensor(out=cv.t[:], in0=u.t[:, 0:512], scalar=fcw.t[:, cti * 3:cti * 3 + 1], in1=cv.t[:],
                                                                                           op0=ALU.mult, op1=ALU.add), [u.b, fcw.b, cv.b], [cv.b])
                    S.add("scalar", lambda e: e.activation(out=sgt.t[:], in_=cc[0].t[:], func=AF.Silu), [cc[0].b], [sgt.b])
                    S.add("vector", lambda e, ct=ct: e.tensor_tensor(out=hT.t[:, ct, :], in0=sgt.t[:], in1=cc[1].t[:], op=ALU.mult), [sgt.b, cc[1].b], [hT.b])
                for hf in range(2):
                    load_sc(C.wsc["down"][hf], wdh.t[:], wdh.b)
                    for s in range(4):
                        pf, pfb = pb()
                        for k in range(22):
                            S.add("tensor", lambda e, pf=pf, k=k, s=s: e.matmul(pf[:], lhsT=hT.t[:, k, s * 128:(s + 1) * 128], rhs=wdh.t[:, k, :], start=(k == 0), stop=(k == 21)), [hT.b, wdh.b], [pfb])
                        S.add("vector", lambda e, pf=pf, s=s, hf=hf: e.scalar_tensor_tensor(out=r2.t[:, s, hf * 512:(hf + 1) * 512], in0=x1.t[:, s, hf * 512:(hf + 1) * 512], scalar=ALPHA, in1=pf[:],
                                                                                         op0=ALU.mult, op1=ALU.add), [x1.b, pfb], [r2.b])
                load_ln(2)
                for s in range(4):
                    o = ot[s % 2]
                    layernorm(r2.t[:, s, :], r2.b, o.t[:], o.b)
                    row0 = blk * 512 + s * 128
                    fin.append(S.dma("sync", out_ap[row0:row0 + 128, :], o.t[:], reads=[o.b]))

        for blk in range(-1, 4):
            do_block(blk)
```

```python
import numpy as np
import concourse.bass as bass
import concourse.mybir as mybir
from concourse.bass_utils import run_bass_kernel_spmd
from contextlib import ExitStack

F32 = mybir.dt.float32
BF16 = mybir.dt.bfloat16
AF = mybir.ActivationFunctionType
ALU = mybir.AluOpType
AX = mybir.AxisListType

ENGS = ["sync", "scalar", "vector", "gpsimd", "tensor"]
DMA_K = 12

SEQ = 4096
NT = 32
D = 1024
HALF = 2048
RMS_EPS = 1e-6
LN_EPS = 1e-5
ALPHA = 2.0 ** 0.25


class Buf:
    __slots__ = ("name", "last_w", "rd_c", "rd_d")

    def __init__(self, name):
        self.name = name
        self.last_w = None
        self.rd_c = {}
        self.rd_d = []


class Op:
    __slots__ = ("eng", "fn", "deps", "needed", "count", "dma", "sem", "k")


class Sched:
    def __init__(self, nc, stack):
        self.nc = nc
        self.ops = {e: [] for e in ENGS}
        self.ndma = {e: 0 for e in ENGS}
        self.dma_ops = {e: [] for e in ENGS}
        self.csem = {e: stack.enter_context(nc.semaphore("c_" + e)) for e in ENGS}
        self.dsem = {e: [stack.enter_context(nc.semaphore("d_%s_%d" % (e, i))) for i in range(DMA_K)]
                     for e in ("sync", "scalar", "gpsimd")}
        self.nbuf = 0

    def buf(self, name=None):
        self.nbuf += 1
        return Buf(name or "b%d" % self.nbuf)

    def add(self, eng, fn, reads=(), writes=(), dma=False):
        op = Op()
        op.eng = eng
        op.fn = fn
        op.dma = dma
        op.needed = False
        op.count = None
        op.sem = None
        deps = []

        def dep(d, raw):
            if d is None:
                return
            if (not raw) and (not dma) and (not d.dma) and d.eng == eng:
                return
            if eng == "tensor" and d.eng == "tensor" and not d.dma and not dma:
                return
            deps.append(d)

        for b in reads:
            dep(b.last_w, True)
        for b in writes:
            dep(b.last_w, False)
            for r in b.rd_c.values():
                dep(r, False)
            for r in b.rd_d:
                dep(r, False)
        if dma:
            j = self.ndma[eng]
            self.ndma[eng] += 1
            op.k = j
            if j >= DMA_K:
                deps.append(self.dma_ops[eng][j - DMA_K])
            self.dma_ops[eng].append(op)
        seen = set()
        dd = []
        for d in deps:
            if id(d) not in seen:
                seen.add(id(d))
                dd.append(d)
                d.needed = True
        op.deps = dd
        for b in reads:
            if dma:
                b.rd_d.append(op)
            else:
                b.rd_c[eng] = op
        for b in writes:
            b.last_w = op
            b.rd_c = {}
            b.rd_d = []
        self.ops[eng].append(op)
        return op

    def barrier(self):
        lasts = []
        for e in ENGS:
            for op in reversed(self.ops[e]):
                if (not op.dma) and op.fn is not None:
                    lasts.append(op)
                    break
            lasts.extend(self.dma_ops[e][-DMA_K:])
        for d in lasts:
            d.needed = True
        for e in ENGS:
            op = Op()
            op.eng = e
            op.fn = None
            op.dma = False
            op.needed = False
            op.count = None
            op.sem = None
            op.deps = list(lasts)
            self.ops[e].append(op)

    def dma(self, q, out, in_, reads=(), writes=(), **kw):
        return self.add(q, lambda e: e.dma_start(out=out, in_=in_, **kw), reads, writes, dma=True)

    def finalize(self, final_deps=()):
        nc = self.nc
        for e in ENGS:
            c = 0
            for op in self.ops[e]:
                if op.dma:
                    op.sem = self.dsem[e][op.k % DMA_K]
                    op.count = 16 * (op.k // DMA_K + 1)
                    op.needed = True
                elif op.needed:
                    c += 1
                    op.sem = self.csem[e]
                    op.count = c
        ops = self.ops

        def emit(e, eng):
            waited = {}
            for op in ops[e]:
                for d in op.deps:
                    key = id(d.sem)
                    if waited.get(key, 0) < d.count:
                        eng.wait_ge(d.sem, d.count)
                        waited[key] = d.count
                if op.fn is None:
                    continue
                inst = op.fn(eng)
                if op.needed:
                    inst.then_inc(op.sem, 16 if op.dma else 1)
            if e == "sync":
                for d in final_deps:
                    eng.wait_ge(d.sem, d.count)

        with nc.Block() as block:
            @block.sync
            def _(eng):
                emit("sync", eng)

            @block.scalar
            def _(eng):
                emit("scalar", eng)

            @block.vector
            def _(eng):
                emit("vector", eng)

            @block.gpsimd
            def _(eng):
                emit("gpsimd", eng)

            @block.tensor
            def _(eng):
                emit("tensor", eng)


class TT:
    def __init__(self, C, st, name, shape, dt=F32):
        self.t = st.enter_context(C.nc.sbuf_tensor("s_" + name, shape, dt))
        self.b = C.S.buf(name)


class Ctx:
    pass


CONST_COLS = {}
NC2 = {}


def _build_consts():
    cols = []
    off = [0]

    def put(name, arr):
        arr = np.asarray(arr, np.float32).reshape(128, -1)
        CONST_COLS[name] = (off[0], arr.shape[1])
        off[0] += arr.shape[1]
        cols.append(arr)

    p = np.arange(128)[:, None]
    f = np.arange(128)[None, :]
    same = (p // 64) == (f // 64)
    put("ident", (p == f))
    put("ones", np.ones((128, 128)))
    put("tri", (p <= f) & same)
    put("blk", same)
    put("strict", (f < p) & same)
    put("upper", (f >= p) & same)
    NEG = -30000.0
    pp = np.arange(128)[:, None]
    ff = np.arange(512)[None, :]
    cm = [np.where(pp + 128 * d <= ff, 0.0, NEG) for d in range(4)]
    NC2.__setitem__("cmask", np.concatenate(cm, axis=1).astype(np.float32))
    wm = []
    for k in range(8):
        rel = ff - pp - 128 * (k - 4)
        wm.append(np.where((rel >= 0) & (rel < 512), 0.0, NEG))
    NC2.__setitem__("wmask", np.concatenate(wm, axis=1).astype(np.float32))
    pm = [np.where(16 * pp + 31 - ff <= 512 * d, 0.0, NEG) for d in range(5)]
    NC2.__setitem__("pm_list", [p_.astype(np.float32) for p_ in pm])
    tok0 = np.arange(256) * 16
    cend = tok0 + 31
    sst = np.arange(64) * 64
    ov = ((tok0[:, None] <= (sst + 63)[None]) & (cend[:, None] >= sst[None])).astype(np.float32)
    ov[255] = 0.0
    NC2.__setitem__("ovl", np.ascontiguousarray(np.concatenate([ov[0:128], ov[128:256]], axis=1)))
    eb = np.zeros((128, 4096), np.float32)
    eb[np.arange(4096) // 64, np.arange(4096)] = 1.0
    NC2.__setitem__("ebig", eb)
    gs = np.zeros((128, 12 * 128), np.float32)
    for r in range(12):
        gs[r, r * 128:(r + 1) * 128] = 1.0
    NC2.__setitem__("gsel", gs)
    return np.concatenate(cols, axis=1)


CONSTS = _build_consts()


def _rope_tables(off=0):
    half = 16
    inv = 500000.0 ** (-np.arange(half, dtype=np.float32) / half)
    def tab(pos):
        ang = pos.astype(np.float32)[None, :] * inv[:, None]
        c = np.cos(ang).astype(np.float32)
        sn = np.sin(ang).astype(np.float32)
        P = ang.shape[1]
        return (np.concatenate([c, c, np.ones((96, P), np.float32)], 0),
                np.concatenate([-sn, sn, np.zeros((96, P), np.float32)], 0))
    c1, s1 = tab(np.maximum(np.arange(4096) - off, 0))
    pk = np.maximum(np.arange(256) * 16 + 31 - off, 0)
    c2, s2 = tab(pk)
    return np.ascontiguousarray(c1), np.ascontiguousarray(s1), np.ascontiguousarray(c2), np.ascontiguousarray(s2)


def _sel_tables(off=0):
    t = np.arange(4096)
    cur = t // 64
    jb = np.arange(64)
    j0 = off // 64
    valid = (jb[None] <= cur[:, None]) & (jb[None] >= j0)
    forced = valid & ((jb[None] == j0) | (jb[None] == cur[:, None]) | (jb[None] == cur[:, None] - 1))
    bias = np.where(forced, 10.0, np.where(valid, 0.0, -1e9)).astype(np.float32)
    return bias, valid.astype(np.float32)


def _rot_perm(ncols_heads):
    idx = []
    for h in range(ncols_heads):
        for d in range(128):
            idx.append(h * 128 + (d + 16 if d < 16 else (d - 16 if d < 32 else d)))
    return np.array(idx)


def cst(C, name, dt="f"):
    o, n = CONST_COLS[name]
    t = C.cf if dt == "f" else C.cb
    return t.t[:, o:o + n]


def build(debug=None):
    nc = bass.Bass("TRN2", target_bir_lowering=False)
    C = Ctx()
    C.nc = nc
    I = {}

    def inp(name, shape):
        I[name] = nc.dram_tensor(name, list(shape), F32, kind="ExternalInput").ap()
        return I[name]

    inp("xT", [D, SEQ])
    inp("consts", [128, CONSTS.shape[1]])
    inp("selm", [128, 3])
    inp("wg_qkv", [D, 3072])
    inp("wg_z", [D, 1024])
    inp("wg_ba", [D, 16])
    inp("convw", [128, 24 * 4])
    inp("a_log", [1, 8])
    inp("dt_bias", [1, 8])
    inp("dn_norm_w", [1, 128])
    for (nm, shp) in NSA_INPUTS:
        inp(nm, shp)
    for (nm, shp) in P2_INPUTS:
        inp(nm, shp)
    C.xTb = nc.dram_tensor("xTb_scratch", [D, SEQ], BF16).ap()
    outs = {}
    if debug == "nsa":
        outs["ynsa"] = nc.dram_tensor("ynsa", [128, 8 * HALF], F32, kind="ExternalOutput").ap()
    if debug == "gdn":
        outs["ydn"] = nc.dram_tensor("ydn", [128, 8 * HALF], F32, kind="ExternalOutput").ap()

    with ExitStack() as st:
        S = Sched(nc, st)
        C.S = S
        C.st = st
        C.xTb_b = S.buf("xTb")
        C.wsc_b = S.buf("wscratch")
        C.wsc = {
            "dn": nc.dram_tensor("sc_dn", [8, 128, 8, 128], BF16).ap(),
            "ns": nc.dram_tensor("sc_ns", [8, 128, 8, 128], BF16).ap(),
            "m": nc.dram_tensor("sc_m", [16, 128, 8, 128], BF16).ap(),
            "out": nc.dram_tensor("sc_out", [128, 8, 1024], BF16).ap(),
            "up": nc.dram_tensor("sc_up", [44, 128, 8, 128], BF16).ap(),
            "down": nc.dram_tensor("sc_down", [2, 128, 22, 512], BF16).ap(),
        }
        C.ps = [st.enter_context(nc.psum_tensor("ps%d" % i, [128, 512], F32)) for i in range(8)]
        C.psb = [S.buf("ps%d" % i) for i in range(8)]
        C.pi = 0
        C.nrot = 8

        def pb():
            i = C.pi % C.nrot
            C.pi = (i + 1) % C.nrot
            return C.ps[i], C.psb[i]

        C.pb = pb
        C.cf = TT(C, st, "cf", [128, CONSTS.shape[1]])
        C.cb = TT(C, st, "cb", [128, CONSTS.shape[1]], BF16)
        S.dma("sync", C.cf.t[:], I["consts"], writes=[C.cf.b])
        S.add("vector", lambda e: e.tensor_copy(out=C.cb.t[:], in_=C.cf.t[:]), [C.cf.b], [C.cb.b])
        C.selm = TT(C, st, "selm", [128, 3])
        S.dma("sync", C.selm.t[:], I["selm"], writes=[C.selm.b])
        C.eps = TT(C, st, "eps", [128, 2])
        S.add("gpsimd", lambda e: e.memset(C.eps.t[:, 0:1], RMS_EPS), [], [C.eps.b])
        S.add("gpsimd", lambda e: e.memset(C.eps.t[:, 1:2], LN_EPS), [], [C.eps.b])
        C.ymdn = TT(C, st, "ymdn", [128, 8, HALF], BF16)
        C.yhdn = TT(C, st, "yhdn", [128, 8, 128], BF16)
        S.add("gpsimd", lambda e: e.memset(C.ymdn.t[:], 0.0), [], [C.ymdn.b])

        fin = []
        C.have_xTb = debug not in ("nsa", "p2")
        if debug not in ("nsa", "p2"):
            stage_gdn(C, I)
        S.barrier()
        C.ymnsa = TT(C, st, "ymnsa", [128, 8, HALF], BF16)
        C.yhnsa = TT(C, st, "yhnsa", [128, 8, 128], BF16)
        S.add("gpsimd", lambda e: e.memset(C.ymnsa.t[:], 0.0), [], [C.ymnsa.b])
        if debug == "p2":
            S.add("gpsimd", lambda e: e.memset(C.yhdn.t[:], 0.0), [], [C.yhdn.b])
            S.add("gpsimd", lambda e: e.memset(C.yhnsa.t[:], 0.0), [], [C.yhnsa.b])
        if debug not in ("gdn", "p2"):
            stage_nsa(C, I)
        if debug == "nsa":
            with ExitStack() as s2:
                for h in range(8):
                    tmp = TT(C, s2, "dbgn%d" % h, [128, HALF])
                    S.add("vector", lambda e, tmp=tmp, h=h: e.tensor_copy(out=tmp.t[:], in_=C.ymnsa.t[:, h, :]), [C.ymnsa.b], [tmp.b])
                    fin.append(S.dma("sync", outs["ynsa"][:, h * HALF:(h + 1) * HALF], tmp.t[:], reads=[tmp.b]))
                S.finalize(final_deps=fin)
            return nc
        if debug == "gdn":
            with ExitStack() as s2:
                for h in range(8):
                    tmp = TT(C, s2, "dbg%d" % h, [128, HALF])
                    S.add("vector", lambda e, tmp=tmp, h=h: e.tensor_copy(out=tmp.t[:], in_=C.ymdn.t[:, h, :]), [C.ymdn.b], [tmp.b])
                    fin.append(S.dma("sync", outs["ydn"][:, h * HALF:(h + 1) * HALF], tmp.t[:], reads=[tmp.b]))
                S.finalize(final_deps=fin)
            return nc
        out_ap = nc.dram_tensor("out", [HALF, D], F32, kind="ExternalOutput").ap()
        if debug == "all":
            dd = {"ydn": nc.dram_tensor("ydn", [128, 8 * HALF], F32, kind="ExternalOutput").ap(),
                  "ynsa": nc.dram_tensor("ynsa", [128, 8 * HALF], F32, kind="ExternalOutput").ap()}
            with ExitStack() as s2:
                tmps = [TT(C, s2, "dbga%d" % i, [128, HALF]) for i in range(2)]
                k = 0
                for nm, src in (("ydn", C.ymdn), ("ynsa", C.ymnsa)):
                    for h in range(8):
                        tmp = tmps[k % 2]
                        k += 1
                        S.add("vector", lambda e, tmp=tmp, h=h, src=src: e.tensor_copy(out=tmp.t[:], in_=src.t[:, h, :]), [src.b], [tmp.b])
                        fin.append(S.dma("sync", dd[nm][:, h * HALF:(h + 1) * HALF], tmp.t[:], reads=[tmp.b]))
        stage_p2(C, I, out_ap, fin)
        S.finalize(final_deps=fin)
    return nc


def stage_gdn(C, I):
    nc, S, pb = C.nc, C.S, C.pb
    C.nrot = 6
    C.pi = 0
    with ExitStack() as st:
        A = lambda name, shape, dt=F32: TT(C, st, name, shape, dt)
        ident_b = cst(C, "ident", "b")
        ones_b = cst(C, "ones", "b")
        ones_f = cst(C, "ones")
        cbf = [C.cf.b]
        cbb = [C.cb.b]
        Wqkv = A("Wqkv", [128, 8, 3072], BF16)
        Wz = A("Wz", [128, 8, 1024], BF16)
        Wba = A("Wba", [128, 8, 16], BF16)
        stg = [A("stg0", [128, 8, 128])] * 2
        xf = [stg[0]]
        k = 0
        for (src, dst, ncol) in ((I["wg_qkv"], Wqkv, 3072), (I["wg_z"], Wz, 1024)):
            for c0 in range(0, ncol, 128):
                sg = stg[k % 2]
                S.dma("sync", sg.t[:], src[:, c0:c0 + 128].rearrange("(kc p) c -> p kc c", p=128), writes=[sg.b])
                eng = "gpsimd" if k % 2 else "vector"
                S.add(eng, lambda e, sg=sg, dst=dst, c0=c0: e.tensor_copy(out=dst.t[:, :, c0:c0 + 128], in_=sg.t[:]), [sg.b], [dst.b])
                k += 1
        sg = stg[0]
        S.dma("sync", sg.t[:, :, 0:16], I["wg_ba"].rearrange("(kc p) c -> p kc c", p=128), writes=[sg.b])
        S.add("vector", lambda e: e.tensor_copy(out=Wba.t[:], in_=sg.t[:, :, 0:16]), [sg.b], [Wba.b])
        cw = A("cw", [128, 96])
        S.dma("sync", cw.t[:], I["convw"], writes=[cw.b])
        Dg = A("Dg", [128, 96, 128], BF16)
        for j in range(96):
            eng = "gpsimd" if j % 2 else "vector"
            S.add(eng, lambda e, j=j: e.tensor_scalar(out=Dg.t[:, j, :], in0=cst(C, "ident"), scalar1=cw.t[:, j:j + 1], scalar2=None, op0=ALU.mult),
                  [cw.b] + cbf, [Dg.b])
        sp = A("sp", [128, 16 + 128])
        S.dma("sync", sp.t[:, 0:8], I["a_log"].partition_broadcast(128), writes=[sp.b])
        S.dma("sync", sp.t[:, 8:16], I["dt_bias"].partition_broadcast(128), writes=[sp.b])
        S.dma("sync", sp.t[:, 16:144], I["dn_norm_w"].partition_broadcast(128), writes=[sp.b])
        negA = A("negA", [128, 8])
        S.add("scalar", lambda e: e.activation(out=negA.t[:], in_=sp.t[:, 0:8], func=AF.Exp), [sp.b], [negA.b])
        S.add("vector", lambda e: e.tensor_scalar(out=negA.t[:], in0=negA.t[:], scalar1=-1.0, scalar2=None, op0=ALU.mult), [negA.b], [negA.b])
        nw4 = A("nw4", [128, 4, 128])
        for h in range(4):
            S.add("vector", lambda e, h=h: e.tensor_copy(out=nw4.t[:, h, :], in_=sp.t[:, 16:144]), [sp.b], [nw4.b])

        xf = stg
        xb = [A("xb%d" % i, [128, 8, 128], BF16) for i in range(1)]
        pre = [A("pre%d" % i, [128, 24, 131], BF16) for i in range(1)]
        S.add("gpsimd", lambda e: e.memset(pre[0].t[:], 0.0), [], [pre[0].b])
        actl = [A("act%d" % i, [128, 4, 128]) for i in range(2)]
        sql = [A("sq%d" % i, [128, 4, 128], BF16) for i in range(2)]
        rinv = A("rinv", [128, 4, 128])
        qT = A("qT", [128, 8, 128], BF16)
        kT = A("kT", [128, 8, 128], BF16)
        vT = A("vT", [128, 8, 128], BF16)
        kbeg = A("kbeg", [128, 8, 128], BF16)
        vb = A("vb", [128, 8, 128], BF16)
        kd = A("kd", [128, 8, 128], BF16)
        sm = A("sm", [128, 16 * 8])
        SM = lambda i: sm.t[:, i * 8:(i + 1) * 8]
        triG = A("triG", [128, 8, 128])
        Sf = A("Sf", [128, 8, 128])
        Sb = A("Sb", [128, 8, 128], BF16)
        Sfb = [S.buf("Sf0"), S.buf("Sf1")]
        Sbb = [S.buf("Sb0"), S.buf("Sb1")]
        S.add("gpsimd", lambda e: e.memset(Sf.t[:], 0.0), [], Sfb)
        S.add("gpsimd", lambda e: e.memset(Sb.t[:], 0.0), [], Sbb)
        hbT = []
        for hb in range(2):
            d = {}
            for (nm, shp, dt) in (("d1", [128, 4, 128], F32), ("Ds", [128, 4, 128], BF16), ("DT", [128, 4, 128], BF16),
                                  ("egbc", [128, 4, 128], F32),
                                  ("XA", [128, 4, 128], BF16), ("XB", [128, 4, 128], BF16),
                                  ("TXA", [128, 2, 4, 128], BF16), ("TXB", [128, 2, 4, 128], BF16),
                                  ("u", [128, 4, 128], BF16), ("wT", [128, 4, 128], BF16), ("qkT", [128, 4, 128], BF16),
                                  ("qdT", [128, 4, 128], BF16), ("vnew", [128, 4, 128], BF16),
                                  ("ss", [128, 4], F32),
                                  ):
                d[nm] = A("%s_%d" % (nm, hb), shp, dt)
            d["a1"] = d["d1"]
            d["X0"] = d["XB"]
            d["zs"] = d["Ds"]
            d["ytok"] = d["vnew"]
            d["osq"] = d["d1"]
            d["y1"] = d["egbc"]
            hbT.append(d)

        xT3 = I["xT"].rearrange("(kc p) t -> p kc t", p=128)

        for t in range(NT):
            xfi, xbi, pr = xf[t % 2], xb[0], pre[0]
            if t > 0:
                S.add("gpsimd", lambda e, pr=pr: e.tensor_copy(out=pr.t[:, :, 0:3], in_=pr.t[:, :, 128:131]), [pr.b], [pr.b])
            S.dma("sync", xfi.t[:], xT3[:, :, t * 128:(t + 1) * 128], writes=[xfi.b])
            S.add("gpsimd", lambda e, xfi=xfi, xbi=xbi: e.tensor_copy(out=xbi.t[:], in_=xfi.t[:]), [xfi.b], [xbi.b])
            S.dma("sync", C.xTb.rearrange("(kc p) t -> p kc t", p=128)[:, :, t * 128:(t + 1) * 128], xbi.t[:], reads=[xbi.b], writes=[C.xTb_b])
            for cg in range(6):
                ps, psb = pb()
                for j in range(4):
                    ct = cg * 4 + j
                    for kc in range(8):
                        S.add("tensor", lambda e, ps=ps, j=j, ct=ct, kc=kc, xbi=xbi: e.matmul(
                            ps[:, j * 128:(j + 1) * 128], lhsT=Wqkv.t[:, kc, ct * 128:(ct + 1) * 128], rhs=xbi.t[:, kc, :],
                            start=(kc == 0), stop=(kc == 7)), [Wqkv.b, xbi.b], [psb])
                S.add("scalar", lambda e, ps=ps, cg=cg, pr=pr: e.activation(
                    out=pr.t[:, cg * 4:(cg + 1) * 4, 3:131], in_=ps[:].rearrange("p (a b) -> p a b", a=4), func=AF.Copy), [psb], [pr.b])
            for cg in range(6):
                ps, psb = pb()
                for j in range(4):
                    ct = cg * 4 + j
                    for i in range(4):
                        S.add("tensor", lambda e, ps=ps, j=j, ct=ct, i=i, pr=pr: e.matmul(
                            ps[:, j * 128:(j + 1) * 128], lhsT=Dg.t[:, ct * 4 + i, :], rhs=pr.t[:, ct, i:i + 128],
                            start=(i == 0), stop=(i == 3)), [Dg.b, pr.b], [psb])
                if cg < 4:
                    act, sq = actl[cg % 2], sql[cg % 2]
                    S.add("scalar", lambda e, ps=ps, act=act: e.activation(
                        out=act.t[:], in_=ps[:].rearrange("p (a b) -> p a b", a=4), func=AF.Silu), [psb], [act.b])
                    S.add("gpsimd", lambda e, act=act, sq=sq: e.tensor_tensor(out=sq.t[:], in0=act.t[:], in1=act.t[:], op=ALU.mult), [act.b], [sq.b])
                    ps, psb = pb()
                    S.add("tensor", lambda e, ps=ps, sq=sq: e.matmul(ps[:], lhsT=ones_b, rhs=sq.t[:].rearrange("p a b -> p (a b)"),
                                                                     start=True, stop=True), [sq.b] + cbb, [psb])
                    S.add("scalar", lambda e, ps=ps: e.activation(out=rinv.t[:].rearrange("p a b -> p (a b)"), in_=ps[:],
                                                                  func=AF.Sqrt, bias=C.eps.t[:, 0:1], scale=1.0), [psb, C.eps.b], [rinv.b])
                    S.add("vector", lambda e: e.reciprocal(out=rinv.t[:], in_=rinv.t[:]), [rinv.b], [rinv.b])
                    if cg < 2:
                        S.add("vector", lambda e, cg=cg, act=act: e.scalar_tensor_tensor(out=qT.t[:, cg * 4:(cg + 1) * 4, :], in0=act.t[:], scalar=128.0 ** -0.5, in1=rinv.t[:],
                                                                                         op0=ALU.mult, op1=ALU.mult), [act.b, rinv.b], [qT.b])
                    else:
                        S.add("vector", lambda e, cg=cg, act=act: e.tensor_tensor(out=kT.t[:, (cg - 2) * 4:(cg - 1) * 4, :], in0=act.t[:], in1=rinv.t[:], op=ALU.mult), [act.b, rinv.b], [kT.b])
                else:
                    S.add("scalar", lambda e, ps=ps, cg=cg: e.activation(
                        out=vT.t[:, (cg - 4) * 4:(cg - 3) * 4, :], in_=ps[:].rearrange("p (a b) -> p a b", a=4), func=AF.Silu), [psb], [vT.b])
            ps, psb = pb()
            for kc in range(8):
                S.add("tensor", lambda e, ps=ps, kc=kc, xbi=xbi: e.matmul(ps[:, 0:16], lhsT=xbi.t[:, kc, :], rhs=Wba.t[:, kc, :],
                                                                          start=(kc == 0), stop=(kc == 7)), [Wba.b, xbi.b], [psb])
            smb = [sm.b]
            S.add("scalar", lambda e, ps=ps: e.activation(out=SM(0), in_=ps[:, 0:8], func=AF.Sigmoid), [psb], smb)
            S.add("vector", lambda e, ps=ps: e.tensor_tensor(out=SM(1), in0=ps[:, 8:16], in1=sp.t[:, 8:16], op=ALU.add), [psb, sp.b], smb)
            S.add("vector", lambda e: e.tensor_scalar(out=SM(2), in0=SM(1), scalar1=-1.0, scalar2=None, op0=ALU.mult), smb, smb)
            S.add("vector", lambda e: e.tensor_tensor(out=SM(2), in0=SM(2), in1=SM(1), op=ALU.max), smb, smb)
            S.add("scalar", lambda e: e.activation(out=SM(3), in_=SM(2), func=AF.Exp, scale=-1.0), smb, smb)
            S.add("scalar", lambda e: e.activation(out=SM(4), in_=SM(3), func=AF.Ln, bias=ones_f[:, 0:1], scale=1.0), smb + cbf, smb)
            S.add("vector", lambda e: e.tensor_scalar(out=SM(1), in0=SM(1), scalar1=0.0, scalar2=None, op0=ALU.max), smb, smb)
            S.add("vector", lambda e: e.tensor_tensor(out=SM(4), in0=SM(4), in1=SM(1), op=ALU.add), smb, smb)
            S.add("vector", lambda e: e.tensor_tensor(out=SM(5), in0=SM(4), in1=negA.t[:], op=ALU.mult), smb + [negA.b], smb)
            ps, psb = pb()
            S.add("tensor", lambda e, ps=ps: e.matmul(ps[:, 0:8], lhsT=cst(C, "tri"), rhs=SM(5), start=True, stop=True), smb + cbf, [psb])
            S.add("tensor", lambda e, ps=ps: e.matmul(ps[:, 8:16], lhsT=cst(C, "blk"), rhs=SM(5), start=True, stop=True), smb + cbf, [psb])
            S.add("vector", lambda e, ps=ps: e.tensor_copy(out=sm.t[:, 48:64], in_=ps[:, 0:16]), [psb], smb)
            S.add("scalar", lambda e: e.activation(out=SM(8), in_=SM(6), func=AF.Exp), smb, smb)
            S.add("vector", lambda e: e.tensor_tensor(out=SM(9), in0=SM(8), in1=SM(0), op=ALU.mult), smb, smb)
            S.add("vector", lambda e: e.tensor_tensor(out=SM(10), in0=SM(7), in1=SM(6), op=ALU.subtract), smb, smb)
            S.add("scalar", lambda e: e.activation(out=SM(10), in_=SM(10), func=AF.Exp), smb, smb)
            S.add("vector", lambda e: e.tensor_scalar(out=SM(11), in0=SM(0), scalar1=-1.0, scalar2=None, op0=ALU.mult), smb, smb)
            S.add("vector", lambda e: e.tensor_tensor(out=triG.t[:], in0=cst(C, "tri").unsqueeze(1).to_broadcast([128, 8, 128]),
                                                      in1=SM(5).unsqueeze(2).to_broadcast([128, 8, 128]), op=ALU.mult), smb + cbf, [triG.b])
            ps, psb = pb()
            pk = ps[:].bitcast(BF16)
            for h in range(8):
                S.add("tensor", lambda e, pk=pk, h=h: e.transpose(out=pk[:, h * 128:(h + 1) * 128], in_=kT.t[:, h, :], identity=ident_b), [kT.b] + cbb, [psb])
            pk3 = pk.rearrange("p (a b) -> p a b", a=8)
            S.add("vector", lambda e, pk3=pk3: e.tensor_tensor(out=kbeg.t[:], in0=pk3, in1=SM(9).unsqueeze(2).to_broadcast([128, 8, 128]), op=ALU.mult), [psb] + smb, [kbeg.b])
            S.add("vector", lambda e, pk3=pk3: e.tensor_tensor(out=kd.t[:], in0=pk3, in1=SM(10).unsqueeze(2).to_broadcast([128, 8, 128]), op=ALU.mult), [psb] + smb, [kd.b])
            ps, psb = pb()
            pv = ps[:].bitcast(BF16)
            for h in range(8):
                S.add("tensor", lambda e, pv=pv, h=h: e.transpose(out=pv[:, h * 128:(h + 1) * 128], in_=vT.t[:, h, :], identity=ident_b), [vT.b] + cbb, [psb])
            pv3 = pv.rearrange("p (a b) -> p a b", a=8)
            S.add("vector", lambda e, pv3=pv3: e.tensor_tensor(out=vb.t[:], in0=pv3, in1=SM(0).unsqueeze(2).to_broadcast([128, 8, 128]), op=ALU.mult), [psb] + smb, [vb.b])

            def hb_gen(hb, t=t, xbi=xbi):
                W = hbT[hb]
                h0 = hb * 4
                r3 = lambda ap: ap.rearrange("p (a b) -> p a b", a=4)
                psg, psgb = pb()
                S.add("tensor", lambda e, psg=psg, h0=h0: e.matmul(psg[:], lhsT=ones_f, rhs=triG.t[:, h0:h0 + 4, :].rearrange("p a b -> p (a b)"), start=True, stop=True),
                      [triG.b] + cbf, [psgb])
                S.add("scalar", lambda e, psg=psg, W=W: e.activation(out=W["egbc"].t[:], in_=r3(psg[:]), func=AF.Exp), [psgb], [W["egbc"].b])
                S.add("vector", lambda e, psg=psg, W=W, h0=h0: e.tensor_tensor(out=W["d1"].t[:], in0=r3(psg[:]),
                                                                               in1=sm.t[:, 48 + h0:48 + h0 + 4].unsqueeze(2).to_broadcast([128, 4, 128]), op=ALU.subtract),
                      [psgb] + smb, [W["d1"].b])
                S.add("scalar", lambda e, W=W: e.activation(out=W["Ds"].t[:], in_=W["d1"].t[:], func=AF.Exp, scale=-1.0), [W["d1"].b], [W["Ds"].b])
                S.add("scalar", lambda e, W=W: e.activation(out=W["DT"].t[:], in_=W["d1"].t[:], func=AF.Exp), [W["d1"].b], [W["DT"].b])
                S.add("vector", lambda e, W=W: e.scalar_tensor_tensor(out=W["Ds"].t[:], in0=W["Ds"].t[:], scalar=1.0, in1=cst(C, "strict", "b").unsqueeze(1).to_broadcast([128, 4, 128]),
                                                                     op0=ALU.min, op1=ALU.mult), [W["Ds"].b] + cbb, [W["Ds"].b])
                S.add("vector", lambda e, W=W: e.scalar_tensor_tensor(out=W["DT"].t[:], in0=W["DT"].t[:], scalar=1.0, in1=cst(C, "upper", "b").unsqueeze(1).to_broadcast([128, 4, 128]),
                                                                     op0=ALU.min, op1=ALU.mult), [W["DT"].b] + cbb, [W["DT"].b])
                yield
                ps, psb = pb()
                for h in range(4):
                    S.add("tensor", lambda e, ps=ps, h=h, h0=h0: e.matmul(ps[:, h * 128:(h + 1) * 128], lhsT=kT.t[:, h0 + h, :], rhs=kT.t[:, h0 + h, :], start=True, stop=True),
                          [kT.b], [psb])
                S.add("vector", lambda e, ps=ps, W=W: e.tensor_tensor(out=W["a1"].t[:], in0=r3(ps[:]), in1=W["Ds"].t[:], op=ALU.mult), [psb, W["Ds"].b], [W["a1"].b])
                S.add("vector", lambda e, W=W, h0=h0: e.tensor_tensor(out=W["X0"].t[:], in0=W["a1"].t[:],
                                                                      in1=sm.t[:, 88 + h0:88 + h0 + 4].unsqueeze(2).to_broadcast([128, 4, 128]), op=ALU.mult),
                      [W["a1"].b] + smb, [W["X0"].b])
                yield
                ps, psb = pb()
                pt = ps[:].bitcast(BF16)
                for h in range(4):
                    S.add("tensor", lambda e, pt=pt, h=h, W=W: e.transpose(out=pt[:, h * 128:(h + 1) * 128], in_=W["X0"].t[:, h, :], identity=ident_b), [W["X0"].b] + cbb, [psb])
                S.add("scalar", lambda e, pt=pt, W=W: e.activation(out=W["TXA"].t[:, 1, :, :], in_=pt[:, 0:512].rearrange("p (a b) -> p a b", a=4), func=AF.Copy), [psb], [W["TXA"].b])
                S.add("gpsimd", lambda e, W=W: e.tensor_copy(out=W["TXA"].t[:, 0, :, :], in_=ident_b.unsqueeze(1).to_broadcast([128, 4, 128])), cbb, [W["TXA"].b])
                yield
                Xc, TXc = W["X0"], W["TXA"]
                Xn_l = [W["XA"], W["XB"]]
                TXn_l = [W["TXB"], W["TXA"]]
                for lv in range(6):
                    Xn, TXn = Xn_l[lv % 2], TXn_l[lv % 2]
                    pa = [pb(), pb()]
                    for h in range(4):
                        ps, psb = pa[h // 2]
                        o4 = ps[:].rearrange("p (s a b) -> p s a b", s=2, a=2)
                        S.add("tensor", lambda e, o4=o4, h=h, Xc=Xc, TXc=TXc: e.matmul(o4[:, :, h % 2, :], lhsT=Xc.t[:, h, :], rhs=TXc.t[:, :, h, :], start=True, stop=True),
                              [Xc.b, TXc.b], [psb])
                    last = (lv == 5)
                    if not last:
                        psx, psxb = pb()
                        for h in range(4):
                            S.add("tensor", lambda e, psx=psx, h=h, Xc=Xc, TXc=TXc: e.matmul(psx[:, h * 128:(h + 1) * 128], lhsT=TXc.t[:, 1, h, :], rhs=Xc.t[:, h, :], start=True, stop=True),
                                  [Xc.b, TXc.b], [psxb])
                    for q in range(2):
                        ps, psb = pa[q]
                        o4 = ps[:].rearrange("p (s a b) -> p s a b", s=2, a=2)
                        S.add("vector", lambda e, o4=o4, q=q, TXc=TXc, TXn=TXn: e.tensor_tensor(out=TXn.t[:, 0, 2 * q:2 * q + 2, :], in0=o4[:, 0, :, :], in1=TXc.t[:, 0, 2 * q:2 * q + 2, :], op=ALU.add),
                              [psb, TXc.b], [TXn.b])
                        if not last:
                            S.add("scalar", lambda e, o4=o4, q=q, TXn=TXn: e.activation(out=TXn.t[:, 1, 2 * q:2 * q + 2, :], in_=o4[:, 1, :, :], func=AF.Copy), [psb], [TXn.b])
                    if not last:
                        S.add("scalar", lambda e, psx=psx, Xn=Xn: e.activation(out=Xn.t[:], in_=r3(psx[:]), func=AF.Copy), [psxb], [Xn.b])
                    Xc, TXc = Xn, TXn
                    yield
                Tt = TXc
                ps, psb = pb()
                for h in range(4):
                    S.add("tensor", lambda e, ps=ps, h=h, h0=h0, Tt=Tt: e.matmul(ps[:, h * 128:(h + 1) * 128], lhsT=Tt.t[:, 0, h, :], rhs=vb.t[:, h0 + h, :], start=True, stop=True),
                          [Tt.b, vb.b], [psb])
                S.add("scalar", lambda e, ps=ps, W=W: e.activation(out=W["u"].t[:], in_=r3(ps[:]), func=AF.Copy), [psb], [W["u"].b])
                ps, psb = pb()
                for h in range(4):
                    S.add("tensor", lambda e, ps=ps, h=h, h0=h0, Tt=Tt: e.matmul(ps[:, h * 128:(h + 1) * 128], lhsT=kbeg.t[:, h0 + h, :], rhs=Tt.t[:, 0, h, :], start=True, stop=True),
                          [Tt.b, kbeg.b], [psb])
                S.add("scalar", lambda e, ps=ps, W=W: e.activation(out=W["wT"].t[:], in_=r3(ps[:]), func=AF.Copy), [psb], [W["wT"].b])
                ps, psb = pb()
                for h in range(4):
                    S.add("tensor", lambda e, ps=ps, h=h, h0=h0: e.matmul(ps[:, h * 128:(h + 1) * 128], lhsT=kT.t[:, h0 + h, :], rhs=qT.t[:, h0 + h, :], start=True, stop=True),
                          [kT.b, qT.b], [psb])
                S.add("vector", lambda e, ps=ps, W=W: e.tensor_tensor(out=W["qkT"].t[:], in0=r3(ps[:]), in1=W["DT"].t[:], op=ALU.mult), [psb, W["DT"].b], [W["qkT"].b])
                S.add("gpsimd", lambda e, W=W, h0=h0: e.tensor_tensor(out=W["qdT"].t[:], in0=qT.t[:, h0:h0 + 4, :], in1=W["egbc"].t[:], op=ALU.mult), [qT.b, W["egbc"].b], [W["qdT"].b])
                need_out = t >= 15
                yield
                pso, psob = C.ps[6 + hb], C.psb[6 + hb]
                for c in range(2):
                    rs = slice(c * 64, (c + 1) * 64)
                    ps1, ps1b = pb()
                    for h in range(4):
                        S.add("tensor", lambda e, ps1=ps1, h=h, h0=h0, rs=rs, W=W: e.matmul(ps1[rs, h * 128:(h + 1) * 128], lhsT=W["wT"].t[:, h, rs], rhs=Sb.t[:, h0 + h, :], start=True, stop=True),
                              [W["wT"].b, Sbb[hb]], [ps1b])
                    S.add("vector", lambda e, ps1=ps1, rs=rs, W=W: e.tensor_tensor(out=W["vnew"].t[rs, :, :], in0=W["u"].t[rs, :, :], in1=r3(ps1[rs, :]), op=ALU.subtract),
                          [ps1b, W["u"].b], [W["vnew"].b])
                    yield
                    for h in (range(4) if need_out else ()):
                        S.add("tensor", lambda e, h=h, h0=h0, rs=rs, W=W: e.matmul(pso[rs, h * 128:(h + 1) * 128], lhsT=W["qdT"].t[:, h, rs], rhs=Sb.t[:, h0 + h, :], start=True, stop=False),
                              [W["qdT"].b, Sbb[hb]], [psob])
                        S.add("tensor", lambda e, h=h, rs=rs, W=W: e.matmul(pso[rs, h * 128:(h + 1) * 128], lhsT=W["qkT"].t[rs, h, rs], rhs=W["vnew"].t[rs, h, :], start=False, stop=True),
                              [W["qkT"].b, W["vnew"].b], [psob])
                    ps3, ps3b = pb()
                    for h in range(4):
                        S.add("tensor", lambda e, ps3=ps3, h=h, h0=h0, rs=rs, W=W: e.matmul(ps3[:, h * 128:(h + 1) * 128], lhsT=kd.t[rs, h0 + h, :], rhs=W["vnew"].t[rs, h, :], start=True, stop=True),
                              [kd.b, W["vnew"].b], [ps3b])
                    col = c * 64 + 63
                    S.add("vector", lambda e, h0=h0, col=col, W=W: e.tensor_tensor(out=Sf.t[:, h0:h0 + 4, :], in0=Sf.t[:, h0:h0 + 4, :],
                                                                                   in1=W["egbc"].t[:, :, col:col + 1].to_broadcast([128, 4, 128]), op=ALU.mult),
                          [Sfb[hb], W["egbc"].b], [Sfb[hb]])
                    S.add("vector", lambda e, ps3=ps3, h0=h0: e.tensor_tensor(out=Sf.t[:, h0:h0 + 4, :], in0=Sf.t[:, h0:h0 + 4, :], in1=r3(ps3[:]), op=ALU.add),
                          [Sfb[hb], ps3b], [Sfb[hb]])
                    S.add("scalar", lambda e, h0=h0: e.activation(out=Sb.t[:, h0:h0 + 4, :], in_=Sf.t[:, h0:h0 + 4, :], func=AF.Copy), [Sfb[hb]], [Sbb[hb]])
                    yield
                if not need_out:
                    return
                zc = hb
                ps, psb = pb()
                for kc in range(8):
                    S.add("tensor", lambda e, ps=ps, zc=zc, kc=kc, xbi=xbi: e.matmul(ps[:], lhsT=xbi.t[:, kc, :], rhs=Wz.t[:, kc, zc * 512:(zc + 1) * 512],
                                                                                      start=(kc == 0), stop=(kc == 7)), [Wz.b, xbi.b], [psb])
                S.add("scalar", lambda e, ps=ps, W=W: e.activation(out=W["zs"].t[:], in_=ps[:].rearrange("p (a b) -> p a b", a=4), func=AF.Silu), [psb], [W["zs"].b])
                S.add("gpsimd", lambda e, W=W: e.tensor_tensor(out=W["zs"].t[:], in0=W["zs"].t[:], in1=nw4.t[:], op=ALU.mult), [W["zs"].b, nw4.b], [W["zs"].b])
                S.add("scalar", lambda e, W=W: e.activation(out=W["osq"].t[:], in_=r3(pso[:]), func=AF.Square), [psob], [W["osq"].b])
                S.add("vector", lambda e, W=W: e.tensor_reduce(out=W["ss"].t[:], in_=W["osq"].t[:], axis=AX.X, op=ALU.add), [W["osq"].b], [W["ss"].b])
                S.add("scalar", lambda e, W=W: e.activation(out=W["ss"].t[:], in_=W["ss"].t[:], func=AF.Sqrt, bias=C.eps.t[:, 0:1], scale=1.0 / 128.0), [W["ss"].b, C.eps.b], [W["ss"].b])
                S.add("vector", lambda e, W=W: e.reciprocal(out=W["ss"].t[:], in_=W["ss"].t[:]), [W["ss"].b], [W["ss"].b])
                S.add("vector", lambda e, W=W: e.tensor_tensor(out=W["y1"].t[:], in0=r3(pso[:]), in1=W["ss"].t[:].unsqueeze(2).to_broadcast([128, 4, 128]), op=ALU.mult),
                      [psob, W["ss"].b], [W["y1"].b])
                S.add("vector", lambda e, W=W: e.tensor_tensor(out=W["ytok"].t[:], in0=W["y1"].t[:], in1=W["zs"].t[:], op=ALU.mult), [W["y1"].b, W["zs"].b], [W["ytok"].b])
                ps, psb = pb()
                py = ps[:].bitcast(BF16)
                for h in range(4):
                    S.add("tensor", lambda e, py=py, h=h, W=W: e.transpose(out=py[:, h * 128:(h + 1) * 128], in_=W["ytok"].t[:, h, :], identity=ident_b), [W["ytok"].b] + cbb, [psb])
                slot = (t % 16) * 128
                half = t // 16
                S.add("vector", lambda e, py=py, h0=h0, slot=slot, half=half: e.scalar_tensor_tensor(
                    out=C.ymdn.t[:, h0:h0 + 4, slot:slot + 128], in0=py[:, 0:512].rearrange("p (a b) -> p a b", a=4), scalar=C.selm.t[:, half:half + 1],
                    in1=C.ymdn.t[:, h0:h0 + 4, slot:slot + 128], op0=ALU.mult, op1=ALU.add), [psb, C.ymdn.b, C.selm.b], [C.ymdn.b])
                if t == 15:
                    S.add("vector", lambda e, py=py, h0=h0: e.tensor_scalar(out=C.yhdn.t[:, h0:h0 + 4, :], in0=py[:, 0:512].rearrange("p (a b) -> p a b", a=4),
                                                                           scalar1=C.selm.t[:, 2:3], scalar2=None, op0=ALU.mult), [psb, C.selm.b], [C.yhdn.b])

            gens = [hb_gen(0), hb_gen(1)]
            alive = [True, True]
            while any(alive):
                for gi in range(2):
                    if alive[gi]:
                        try:
                            next(gens[gi])
                        except StopIteration:
                            alive[gi] = False


def make_in_maps(inp):
    x = np.asarray(inp["x"], np.float32)
    w_in = np.asarray(inp["w_in"], np.float32)[0]
    maps = []
    cw = np.asarray(inp["dn_conv_w"], np.float32)[0]
    convw = np.ascontiguousarray(cw.reshape(4, 24, 128).transpose(2, 1, 0).reshape(128, 96))
    off = np.cumsum((0,) + (1024, 1024, 1024, 1024, 8, 8, 1024, 256, 256, 256, 256, 256, 256, 24, 1024, 1024))
    o_nq, o_kc, o_vc, o_ksl, o_vsl, o_kwn, o_vwn, o_gate, o_mgd, o_mgn = [int(off[i]) for i in (6, 7, 8, 9, 10, 11, 12, 13, 14, 15)]
    nsa_host = {k_: v_ for k_, v_ in NC2.items() if k_ != "pm_list"}
    pml = NC2["pm_list"]
    NEGT = np.full((128, 512), -30000.0, np.float32)
    ZT = np.zeros((128, 512), np.float32)
    percore = []
    for gg_ in range(2):
        off_ = 2048 * (1 - gg_)
        c1, s1, c2, s2 = _rope_tables(off_)
        sbias, svalid = _sel_tables(off_)
        if gg_ == 1:
            pm0 = [pml[3], pml[4], ZT]
            kneg = np.zeros((128, 1024), np.float32)
        else:
            pm0 = [NEGT, NEGT, NEGT]
            kneg = np.full((128, 1024), -30000.0, np.float32)
        percore.append({"ropeC": c1, "ropeS": s1, "ropeCk": c2, "ropeSk": s2, "selbias": sbias, "selvalid": svalid,
                        "pmask": np.ascontiguousarray(np.concatenate(pml[0:4] + pm0, axis=1)), "keyneg": kneg})
    f32c = lambda a: np.ascontiguousarray(np.asarray(a, np.float32))
    nsa_host["cmp_k_w1"] = f32c(inp["cmp_k_w1"][0])
    nsa_host["cmp_v_w1"] = f32c(inp["cmp_v_w1"][0])
    nsa_host["cmp_k_w2"] = f32c(inp["cmp_k_w2"][0])
    nsa_host["cmp_k_w2r"] = f32c(np.asarray(inp["cmp_k_w2"][0])[:, _rot_perm(1)])
    nsa_host["cmp_v_w2"] = f32c(inp["cmp_v_w2"][0])
    nsa_host["peT_k"] = f32c(np.asarray(inp["cmp_pos_k"][0]).T)
    nsa_host["peT_v"] = f32c(np.asarray(inp["cmp_pos_v"][0]).T)
    for gg in range(2):
        sl = lambda o: w_in[:, o + gg * 128:o + (gg + 1) * 128]
        nsa_host["wn_kf%d" % gg] = f32c(np.concatenate([sl(o_kc), sl(o_vc), sl(o_ksl), sl(o_kwn)], axis=1))
        rp = _rot_perm(1)
        nsa_host["wn_kr%d" % gg] = f32c(np.concatenate([sl(o_ksl)[:, rp], sl(o_kwn)[:, rp]], axis=1))
        nsa_host["wn_v%d" % gg] = f32c(np.concatenate([sl(o_vsl), sl(o_vwn)], axis=1))
        wq = w_in[:, o_nq + gg * 512:o_nq + (gg + 1) * 512]
        nsa_host["wn_q%d" % gg] = f32c(wq)
        nsa_host["wn_qr%d" % gg] = f32c(wq[:, _rot_perm(4)])
        gcols = [o_gate + br * 8 + gg * 4 + hh for br in range(3) for hh in range(4)]
        nsa_host["wn_g%d" % gg] = f32c(w_in[:, gcols])
    fcw_ = np.asarray(inp["ffn_conv_w"], np.float32)[0]
    p2_host = {
        "wm": f32c(w_in[:, o_mgd:o_mgd + 2048]),
        "w_dn": f32c(inp["w_branch_dn"][0]), "w_nsa": f32c(inp["w_branch_nsa"][0]), "w_out": f32c(inp["w_out"][0]),
        "ln1_g": f32c(inp["ln1_g"]).reshape(1, D), "ln1_b": f32c(inp["ln1_b"]).reshape(1, D),
        "ln2_g": f32c(inp["ln2_g"]).reshape(1, D), "ln2_b": f32c(inp["ln2_b"]).reshape(1, D),
        "w_up": f32c(inp["ffn_w_up"][0]), "w_down": f32c(inp["ffn_w_down"][0]),
        "fconvw": f32c(fcw_.reshape(3, 44, 128).transpose(2, 1, 0).reshape(128, 132)),
        "fconvb": f32c(np.asarray(inp["ffn_conv_b"], np.float32)[0].reshape(44, 128).T),
    }
    for c in range(8):
        b, g = c // 2, c % 2
        selm = np.zeros((128, 3), np.float32)
        selm[:, 1] = 1.0
        selm[:, 2] = float(g)
        m = {
            "xT": (np.ascontiguousarray(x[b].T) if g == 1 else
                   np.ascontiguousarray(np.concatenate([np.zeros((D, HALF), np.float32), x[b, 0:HALF].T], axis=1))),
            "consts": CONSTS,
            "selm": selm,
            "wg_qkv": np.ascontiguousarray(w_in[:, 0:3072]),
            "wg_z": np.ascontiguousarray(w_in[:, 3072:4096]),
            "wg_ba": np.ascontiguousarray(w_in[:, 4096:4112]),
            "convw": convw,
            "a_log": np.asarray(inp["dn_a_log"], np.float32).reshape(1, 8),
            "dt_bias": np.asarray(inp["dn_dt_bias"], np.float32).reshape(1, 8),
            "dn_norm_w": np.asarray(inp["dn_norm_w"], np.float32).reshape(1, 128),
        }
        m.update(nsa_host)
        m.update(percore[g])
        m.update(p2_host)
        xh = np.zeros((HALF + 128, D), np.float32)
        if g == 0:
            xh[128:] = x[b, 0:HALF]
        else:
            xh[:] = x[b, HALF - 128:SEQ]
        m["xh"] = xh
        m["xTh"] = np.ascontiguousarray(xh.T)
        maps.append(m)
    return maps


def kernel(**inp):
    nc = build()
    maps = make_in_maps(inp)
    res = run_bass_kernel_spmd(nc, maps, core_ids=list(range(8)))
    out = np.zeros((4, SEQ, D), np.float32)
    for c in range(8):
        b, g = c // 2, c % 2
        out[b, g * HALF:(g + 1) * HALF] = res.results[c]["out"]
    return out


NSA_INPUTS = [("ropeC", [128, 4096]), ("ropeS", [128, 4096]), ("ropeCk", [128, 256]), ("ropeSk", [128, 256]),
              ("selbias", [4096, 64]), ("selvalid", [4096, 64]),
              ("cmask", [128, 2048]), ("wmask", [128, 4096]), ("pmask", [128, 3584]), ("keyneg", [128, 1024]), ("ovl", [128, 128]),
              ("ebig", [128, 4096]), ("gsel", [128, 1536]),
              ("cmp_k_w1", [4096, 256]), ("cmp_v_w1", [4096, 256]), ("cmp_k_w2", [256, 128]), ("cmp_k_w2r", [256, 128]),
              ("cmp_v_w2", [256, 128]), ("peT_k", [128, 32]), ("peT_v", [128, 32])]
for _g in range(2):
    NSA_INPUTS += [("wn_kf%d" % _g, [D, 512]), ("wn_kr%d" % _g, [D, 256]), ("wn_v%d" % _g, [D, 256]),
                   ("wn_q%d" % _g, [D, 512]), ("wn_qr%d" % _g, [D, 512]), ("wn_g%d" % _g, [D, 12])]


def stage_nsa(C, I):
    nc, S, pb = C.nc, C.S, C.pb
    C.nrot = 3
    C.pi = 0
    pselp = (C.ps[3], C.psb[3])
    accs = [((C.ps[4], C.psb[4]), (C.ps[5], C.psb[5])), ((C.ps[6], C.psb[6]), (C.ps[7], C.psb[7]))]
    acci = [0]
    SC = 128.0 ** -0.5
    with ExitStack() as st:
        A = lambda name, shape, dt=F32: TT(C, st, name, shape, dt)
        ident_b = cst(C, "ident", "b")
        ones_b = cst(C, "ones", "b")
        cbb = [C.cb.b]
        stg = A("nstg", [128, 8, 256])
        stg2 = stg.t[:].rearrange("p a b -> p (a b)")
        xb = A("nxb", [128, 8, 512], BF16)
        xT3 = I["xT"].rearrange("(kc p) t -> p kc t", p=128)

        def load_bf(name, ncols, dst, rows=128):
            for c0 in range(0, ncols, 2048):
                n = min(2048, ncols - c0)
                S.dma("sync", stg2[0:rows, 0:n], I[name][0:rows, c0:c0 + n], writes=[stg.b])
                S.add("vector", lambda e, n=n, c0=c0: e.tensor_copy(out=dst.t[0:rows, c0:c0 + n], in_=stg2[0:rows, 0:n]), [stg.b], [dst.b])

        def load_w(name, ncols, dst):
            src3 = I[name].rearrange("(kc p) c -> p kc c", p=128)
            for c0 in range(0, ncols, 256):
                n = min(256, ncols - c0)
                S.dma("sync", stg.t[:, :, 0:n], src3[:, :, c0:c0 + n], writes=[stg.b])
                S.add("vector", lambda e, c0=c0, n=n: e.tensor_copy(out=dst.t[:, :, c0:c0 + n], in_=stg.t[:, :, 0:n]), [stg.b], [dst.b])

        xTb3 = C.xTb.rearrange("(kc p) t -> p kc t", p=128)

        def load_x(tg):
            if C.have_xTb:
                S.dma("sync", xb.t[:], xTb3[:, :, tg * 512:(tg + 1) * 512], reads=[C.xTb_b], writes=[xb.b])
            else:
                for hh_ in range(2):
                    S.dma("sync", stg.t[:], xT3[:, :, tg * 512 + hh_ * 256:tg * 512 + (hh_ + 1) * 256], writes=[stg.b])
                    S.add("gpsimd", lambda e, hh_=hh_: e.tensor_copy(out=xb.t[:, :, hh_ * 256:(hh_ + 1) * 256], in_=stg.t[:]), [stg.b], [xb.b])

        ropeC = A("ropeC", [128, 512])
        ropeS = A("ropeS", [128, 512])
        r1 = A("r1", [128, 512])
        r2 = A("r2", [128, 512])

        def load_rope(tg):
            S.dma("sync", ropeC.t[:], I["ropeC"][:, tg * 512:(tg + 1) * 512], writes=[ropeC.b])
            S.dma("sync", ropeS.t[:], I["ropeS"][:, tg * 512:(tg + 1) * 512], writes=[ropeS.b])

        def proj_fm(W, c0, m, n):
            ps, psb = pb()
            for kc in range(8):
                S.add("tensor", lambda e, ps=ps, kc=kc: e.matmul(ps[0:m, 0:n], lhsT=W.t[:, kc, c0:c0 + m], rhs=xb.t[:, kc, 0:n],
                                                                 start=(kc == 0), stop=(kc == 7)), [W.b, xb.b], [psb])
            return ps, psb

        def rope_into(dst_ap, dstb, ps, psb, psr, psrb, n, scale=1.0):
            S.add("vector", lambda e: e.tensor_tensor(out=r1.t[:, 0:n], in0=ps[:, 0:n], in1=ropeC.t[:, 0:n], op=ALU.mult), [psb, ropeC.b], [r1.b])
            S.add("vector", lambda e: e.tensor_tensor(out=r2.t[:, 0:n], in0=psr[:, 0:n], in1=ropeS.t[:, 0:n], op=ALU.mult), [psrb, ropeS.b], [r2.b])
            S.add("gpsimd", lambda e: e.tensor_tensor(out=r1.t[:, 0:n], in0=r1.t[:, 0:n], in1=r2.t[:, 0:n], op=ALU.add), [r1.b, r2.b], [r1.b])
            S.add("scalar", lambda e: e.activation(out=dst_ap, in_=r1.t[:, 0:n], func=AF.Copy, scale=scale), [r1.b], [dstb])

        import os as _os
        _lim = _os.environ.get("NSA_LIM", "")
        pc_bufs = {}

        def precast_gen():
            jobs = []
            for key, nm, ncol in (("m", "wm", 2048), ("dn", "w_dn", 1024), ("ns", "w_nsa", 1024), ("out", "w_out", 1024), ("up", "w_up", 5632)):
                src3 = I[nm].rearrange("(kc p) c -> p kc c", p=128)
                for c0 in range(0, ncol, 256):
                    jobs.append((key, src3[:, :, c0:c0 + 256], c0, 8, 256))
            srcd = I["w_down"].rearrange("(k p) c -> p k c", p=128)
            for k0 in range(0, 22, 2):
                jobs.append(("down", srcd[:, k0:k0 + 2, :], k0, 2, 1024))
            def views(ji):
                key, src, a0, K_, n_ = jobs[ji]
                pf = pc_bufs["f"][ji % 2]
                pbf = pc_bufs["b"][ji % 2]
                return pf, pbf, pf.t[:].rearrange("p (k c) -> p k c", k=K_), pbf.t[:].rearrange("p (k c) -> p k c", k=K_)

            in_ver = {}

            def issue_in(ji):
                pf, pbf, vf, vb_ = views(ji)
                S.dma("gpsimd", vf, jobs[ji][1], writes=[pf.b])
                in_ver[ji] = pc_bufs["ver"]

            issue_in(0)
            for ji, (key, src, a0, K_, n_) in enumerate(jobs):
                pf, pbf, vf, vb_ = views(ji)
                if in_ver.get(ji) != pc_bufs["ver"]:
                    issue_in(ji)
                if ji + 1 < len(jobs):
                    issue_in(ji + 1)
                S.add("vector", lambda e, vf=vf, vb_=vb_: e.tensor_copy(out=vb_, in_=vf), [pf.b], [pbf.b])
                if key == "down":
                    for hf in range(2):
                        S.dma("gpsimd", C.wsc["down"][hf][:, a0:a0 + 2, :], vb_[:, :, hf * 512:(hf + 1) * 512], reads=[pbf.b], writes=[C.wsc_b])
                elif key == "out":
                    S.dma("gpsimd", C.wsc["out"][:, :, a0:a0 + 256], vb_, reads=[pbf.b], writes=[C.wsc_b])
                else:
                    for j_ in range(2):
                        S.dma("gpsimd", C.wsc[key][a0 // 128 + j_], vb_[:, :, j_ * 128:(j_ + 1) * 128], reads=[pbf.b], writes=[C.wsc_b])
                yield

        pc_gen = precast_gen()
        pc_done = [False]

        def precast_step(n):
            for _ in range(n):
                if pc_done[0]:
                    return
                try:
                    next(pc_gen)
                except StopIteration:
                    pc_done[0] = True

        for g in range(2):
            if _lim and g == 1:
                break
            S.barrier()
            with ExitStack() as sg:
                G = lambda name, shape, dt=F32: TT(C, sg, "%s_g%d" % (name, g), shape, dt)
                kselT = G("kselT", [128, 4096], BF16)
                kwinT = G("kwinT", [128, 4096], BF16)
                vsw = G("vsw", [128, 32, 256], BF16)
                kcT = G("kcT", [128, 256], BF16)
                vc = G("vc", [128, 2, 128], BF16)
                S.add("gpsimd", lambda e: e.memset(kcT.t[:], 0.0), [], [kcT.b])
                with ExitStack() as sa:
                    B = lambda name, shape, dt=F32: TT(C, sa, "%s_a%d" % (name, g), shape, dt)
                    kcmpb = B("kcmpb", [128, 4096], BF16)
                    vcmpb = B("vcmpb", [128, 4096], BF16)
                    pc_bufs["ver"] = g
                    pc_bufs["f"] = [B("pcf%d" % i, [128, 2048]) for i in range(2)]
                    pc_bufs["b"] = [B("pcb%d" % i, [128, 2048], BF16) for i in range(2)]
                    sa1 = ExitStack()
                    B1 = lambda name, shape, dt=F32: TT(C, sa1, "%s_a%d" % (name, g), shape, dt)
                    Wkf = B1("Wkf", [128, 8, 512], BF16)
                    Wkr = B1("Wkr", [128, 8, 256], BF16)
                    Wv = B1("Wv", [128, 8, 256], BF16)
                    load_w("wn_kf%d" % g, 512, Wkf)
                    load_w("wn_kr%d" % g, 256, Wkr)
                    load_w("wn_v%d" % g, 256, Wv)
                    for tg in range(8):
                        precast_step(4 if (g == 0 and tg < 3) else 3)
                        load_x(tg)
                        load_rope(tg)
                        cs = slice(tg * 512, (tg + 1) * 512)
                        if _lim == "A00":
                            continue
                        for j, dst in enumerate((kcmpb, vcmpb, kselT, kwinT)):
                            ps, psb = proj_fm(Wkf, j * 128, 128, 512)
                            if j < 2 or _lim == "A01":
                                S.add("scalar", lambda e, ps=ps, dst=dst, cs=cs: e.activation(out=dst.t[:, cs], in_=ps[:], func=AF.Copy), [psb], [dst.b])
                            else:
                                psr, psrb = proj_fm(Wkr, (j - 2) * 128, 128, 512)
                                rope_into(dst.t[:, cs], dst.b, ps, psb, psr, psrb, 512)
                        for s4 in range(4):
                            if _lim in ("A01", "A02"):
                                break
                            ps, psb = pb()
                            for kc in range(8):
                                S.add("tensor", lambda e, ps=ps, kc=kc, s4=s4: e.matmul(ps[:, 0:256], lhsT=xb.t[:, kc, s4 * 128:(s4 + 1) * 128], rhs=Wv.t[:, kc, :],
                                                                                       start=(kc == 0), stop=(kc == 7)), [Wv.b, xb.b], [psb])
                            S.add("scalar", lambda e, ps=ps, s4=s4, tg=tg: e.activation(out=vsw.t[:, tg * 4 + s4, :], in_=ps[:, 0:256], func=AF.Copy), [psb], [vsw.b])
                    if g == 1:
                        precast_step(100)
                    sa1.close()
                    S.barrier()
                    if _lim.startswith("A0"):
                        continue
                    w1 = B("w1", [128, 16, 256], BF16)
                    w2 = B("w2", [128, 2, 128], BF16)
                    w2r = B("w2r", [128, 2, 128], BF16)
                    pe = B("pe", [128, 32])
                    hid = B("hid", [128, 2, 256], BF16)
                    S.add("gpsimd", lambda e: e.memset(hid.t[:], 0.0), [], [hid.b])
                    kpe = [B("kpe%d" % i, [128, 255], BF16) for i in range(3)]
                    for is_k, src, nm in ((True, kcmpb, "k"), (False, vcmpb, "v")):
                        w1v = I["cmp_%s_w1" % nm].rearrange("(l d) h -> d l h", d=128)
                        def load_w1(hf):
                            for q_ in range(2):
                                S.dma("sync", stg2.rearrange("p (l h) -> p l h", l=8), w1v[:, hf * 16 + q_ * 8:hf * 16 + (q_ + 1) * 8, :], writes=[stg.b])
                                S.add("gpsimd", lambda e, q_=q_: e.tensor_copy(out=w1.t[:, q_ * 8:(q_ + 1) * 8, :], in_=stg2.rearrange("p (l h) -> p l h", l=8)), [stg.b], [w1.b])
                        S.dma("sync", stg.t[:, 0:2, 0:128], I["cmp_%s_w2" % nm].rearrange("(hc p) d -> p hc d", p=128), writes=[stg.b])
                        S.add("vector", lambda e: e.tensor_copy(out=w2.t[:], in_=stg.t[:, 0:2, 0:128]), [stg.b], [w2.b])
                        if is_k:
                            S.dma("sync", stg.t[:, 0:2, 0:128], I["cmp_k_w2r"].rearrange("(hc p) d -> p hc d", p=128), writes=[stg.b])
                            S.add("vector", lambda e: e.tensor_copy(out=w2r.t[:], in_=stg.t[:, 0:2, 0:128]), [stg.b], [w2r.b])
                        S.dma("sync", pe.t[:], I["peT_%s" % nm], writes=[pe.b])
                        hps = [pb(), pb()]
                        for l in range(32):
                            if l % 16 == 0:
                                load_w1(l // 16)
                            kp = kpe[l % 3]
                            S.add("vector", lambda e, kp=kp, l=l, src=src: e.tensor_scalar(out=kp.t[:], in0=src.t[:, l:l + 4065:16], scalar1=pe.t[:, l:l + 1], scalar2=None, op0=ALU.add),
                                  [src.b, pe.b], [kp.b])
                            for hc in range(2):
                                S.add("tensor", lambda e, kp=kp, l=l, hc=hc, hps=hps: e.matmul(hps[hc][0][:, 0:255], lhsT=w1.t[:, l % 16, hc * 128:(hc + 1) * 128], rhs=kp.t[:],
                                                                                     start=(l == 0), stop=(l == 31)), [w1.b, kp.b], [hps[hc][1]])
                        for hc in range(2):
                            S.add("scalar", lambda e, hc=hc, hps=hps: e.activation(out=hid.t[:, hc, 0:255], in_=hps[hc][0][:, 0:255], func=AF.Silu), [hps[hc][1]], [hid.b])
                        if is_k:
                            ps, psb = pb()
                            psr, psrb = pb()
                            for hc in range(2):
                                S.add("tensor", lambda e, ps=ps, hc=hc: e.matmul(ps[:, 0:255], lhsT=w2.t[:, hc, :], rhs=hid.t[:, hc, 0:255], start=(hc == 0), stop=(hc == 1)), [w2.b, hid.b], [psb])
                                S.add("tensor", lambda e, psr=psr, hc=hc: e.matmul(psr[:, 0:255], lhsT=w2r.t[:, hc, :], rhs=hid.t[:, hc, 0:255], start=(hc == 0), stop=(hc == 1)), [w2r.b, hid.b], [psrb])
                            S.dma("sync", ropeC.t[:, 0:256], I["ropeCk"], writes=[ropeC.b])
                            S.dma("sync", ropeS.t[:, 0:256], I["ropeSk"], writes=[ropeS.b])
                            rope_into(kcT.t[:, 0:255], kcT.b, ps, psb, psr, psrb, 255)
                        else:
                            for ncn in range(2):
                                ps, psb = pb()
                                for hc in range(2):
                                    S.add("tensor", lambda e, ps=ps, hc=hc, ncn=ncn: e.matmul(ps[:, 0:128], lhsT=hid.t[:, hc, ncn * 128:(ncn + 1) * 128], rhs=w2.t[:, hc, :],
                                                                                             start=(hc == 0), stop=(hc == 1)), [w2.b, hid.b], [psb])
                                S.add("scalar", lambda e, ps=ps, ncn=ncn: e.activation(out=vc.t[:, ncn, :], in_=ps[:, 0:128], func=AF.Copy), [psb], [vc.b])
                if _lim.startswith("A"):
                    continue
                S.barrier()
                with ExitStack() as sc:
                    B = lambda name, shape, dt=F32: TT(C, sc, "%s_c%d" % (name, g), shape, dt)
                    wmask = B("wmask", [128, 4096], BF16)
                    pmask = B("pmask", [128, 3584], BF16)
                    keyneg = B("keyneg", [128, 1024], BF16)
                    onesrow = B("onesrow", [128, 512], BF16)
                    S.add("gpsimd", lambda e: e.memset(onesrow.t[:], 1.0), [], [onesrow.b])
                    ebig = B("ebig", [128, 4096], BF16)
                    ovl = B("ovl", [128, 128], BF16)
                    gsel = B("gsel", [128, 1536])
                    load_bf("wmask", 4096, wmask)
                    load_bf("pmask", 3584, pmask)
                    load_bf("keyneg", 1024, keyneg)
                    load_bf("ebig", 4096, ebig, rows=64)
                    load_bf("ovl", 128, ovl)
                    S.dma("sync", gsel.t[0:12, :], I["gsel"][0:12, :], writes=[gsel.b])
                    Wq = B("Wq", [128, 8, 512], BF16)
                    Wqr = B("Wqr", [128, 8, 512], BF16)
                    Wg = B("Wg", [128, 8, 12], BF16)
                    load_w("wn_q%d" % g, 512, Wq)
                    load_w("wn_qr%d" % g, 512, Wqr)
                    load_w("wn_g%d" % g, 12, Wg)
                    qT = [B("qT%d" % h, [128, 512], BF16) for h in range(4)]
                    gs = B("gs", [128, 512])
                    PT = [B("PT%d" % i, [128, 512], BF16) for i in range(4)]
                    pti = [0]
                    PTcl = [B("PTc%d" % i, [128, 2, 512], BF16) for i in range(2)]
                    acc = B("acc", [128, 4, 512])
                    rr = B("rr", [128, 512])
                    rg = r2
                    tmpc = rr
                    negselT = B("negselT", [128, 512], BF16)
                    selb = B("selb", [128, 64])
                    selv = B("selv", [128, 64])
                    sc1 = B("sc1", [128, 64])
                    sc2 = B("sc2", [128, 64])
                    slt = B("slt", [128, 64])
                    m8 = B("m8", [128, 16])
                    nsb = B("nsb", [128, 64], BF16)
                    zb = B("zb", [128, 256], BF16)
                    S.add("gpsimd", lambda e: e.memset(zb.t[:], 0.0), [], [zb.b])

                    def combine(h, br, ao, asum):
                        S.add("vector", lambda e: e.tensor_scalar(out=rr.t[:], in0=asum[0][:], scalar1=1e-30, scalar2=None, op0=ALU.max), [asum[1]], [rr.b])
                        S.add("vector", lambda e: e.reciprocal(out=rr.t[:], in_=rr.t[:]), [rr.b], [rr.b])
                        psg, psgb = pb()
                        r = br * 4 + h
                        S.add("tensor", lambda e: e.matmul(psg[:], lhsT=gsel.t[0:12, r * 128:(r + 1) * 128], rhs=gs.t[0:12, :], start=True, stop=True), [gsel.b, gs.b], [psgb])
                        S.add("vector", lambda e: e.tensor_tensor(out=rg.t[:], in0=rr.t[:], in1=psg[:], op=ALU.mult), [rr.b, psgb], [rg.b])
                        if br == 0:
                            S.add("vector", lambda e: e.tensor_tensor(out=acc.t[:, h, :], in0=ao[0][:], in1=rg.t[:], op=ALU.mult), [ao[1], rg.b], [acc.b])
                        else:
                            S.add("vector", lambda e: e.tensor_tensor(out=tmpc.t[:], in0=ao[0][:], in1=rg.t[:], op=ALU.mult), [ao[1], rg.b], [tmpc.b])
                            S.add("gpsimd", lambda e: e.tensor_tensor(out=acc.t[:, h, :], in0=acc.t[:, h, :], in1=tmpc.t[:], op=ALU.add), [acc.b, tmpc.b], [acc.b])

                    for tg in range(3, 8):
                        load_x(tg)
                        load_rope(tg)
                        for h in range(4):
                            ps, psb = proj_fm(Wq, h * 128, 128, 512)
                            psr, psrb = proj_fm(Wqr, h * 128, 128, 512)
                            rope_into(qT[h].t[:], qT[h].b, ps, psb, psr, psrb, 512, scale=SC)
                        ps, psb = proj_fm(Wg, 0, 12, 512)
                        S.add("scalar", lambda e, ps=ps: e.activation(out=gs.t[0:12, :], in_=ps[0:12, :], func=AF.Sigmoid), [psb], [gs.b])
                        ncs = [0] if tg < 4 else [0, 1]
                        S.add("tensor", lambda e: e.matmul(pselp[0][:, 0:256], lhsT=zb.t[:, 0:128], rhs=zb.t[:, 0:256], start=True, stop=False), [zb.b], [pselp[1]])
                        for h in range(4):
                            ao, asum = accs[acci[0] % 2]
                            acci[0] += 1
                            PTc = PTcl[h % 2]
                            for ii, nci in enumerate(ncs):
                                ps, psb = pb()
                                dlt = (tg - 4) if nci == 1 else 4 + min(tg - 3, 2)
                                msk = True
                                S.add("tensor", lambda e, ps=ps, nci=nci, h=h, msk=msk: e.matmul(ps[:], lhsT=kcT.t[:, nci * 128:(nci + 1) * 128], rhs=qT[h].t[:], start=True, stop=(not msk)),
                                      [kcT.b, qT[h].b], [psb])
                                if msk:
                                    S.add("tensor", lambda e, ps=ps, dlt=dlt: e.matmul(ps[:], lhsT=ident_b, rhs=pmask.t[:, dlt * 512:(dlt + 1) * 512], start=False, stop=True),
                                          [pmask.b] + cbb, [psb])
                                S.add("scalar", lambda e, ps=ps, PTc=PTc, nci=nci: e.activation(out=PTc.t[:, nci, :], in_=ps[:], func=AF.Exp), [psb], [PTc.b])
                                S.add("tensor", lambda e, PTc=PTc, nci=nci, ii=ii, ao=ao, n=len(ncs): e.matmul(ao[0][:], lhsT=vc.t[:, nci, :], rhs=PTc.t[:, nci, :], start=(ii == 0), stop=(ii == n - 1)),
                                      [vc.b, PTc.b], [ao[1]])
                                S.add("tensor", lambda e, PTc=PTc, nci=nci, ii=ii, asum=asum, n=len(ncs): e.matmul(asum[0][:], lhsT=ones_b, rhs=PTc.t[:, nci, :], start=(ii == 0), stop=(ii == n - 1)),
                                      [PTc.b] + cbb, [asum[1]])
                            combine(h, 0, ao, asum)
                            for ii, nci in enumerate(ncs):
                                S.add("vector", lambda e, PTc=PTc, nci=nci: e.tensor_tensor(out=PTc.t[:, nci, :], in0=PTc.t[:, nci, :], in1=rr.t[:], op=ALU.mult), [PTc.b, rr.b], [PTc.b])
                                for s4 in range(4):
                                    first = (h == 0 and ii == 0)
                                    last = (h == 3 and ii == len(ncs) - 1)
                                    S.add("tensor", lambda e, PTc=PTc, nci=nci, s4=s4, first=first, last=last: e.matmul(
                                        pselp[0][:, s4 * 64:(s4 + 1) * 64], lhsT=PTc.t[:, nci, s4 * 128:(s4 + 1) * 128], rhs=ovl.t[:, nci * 64:(nci + 1) * 64],
                                        start=False, stop=last), [PTc.b, ovl.b], [pselp[1]])
                        for s4 in range(4):
                            tile_i = tg * 4 + s4
                            S.dma("sync", selb.t[:], I["selbias"][tile_i * 128:(tile_i + 1) * 128, :], writes=[selb.b])
                            S.dma("sync", selv.t[:], I["selvalid"][tile_i * 128:(tile_i + 1) * 128, :], writes=[selv.b])
                            ps, psb = pselp
                            S.add("vector", lambda e, ps=ps, s4=s4: e.tensor_tensor(out=sc1.t[:], in0=ps[:, s4 * 64:(s4 + 1) * 64], in1=selb.t[:], op=ALU.add), [psb, selb.b], [sc1.b])
                            S.add("vector", lambda e: e.max(out=m8.t[:, 0:8], in_=sc1.t[:]), [sc1.b], [m8.b])
                            S.add("vector", lambda e: e.match_replace(out=sc2.t[:], in_to_replace=m8.t[:, 0:8], in_values=sc1.t[:], imm_value=-2e9), [sc1.b, m8.b], [sc2.b])
                            S.add("vector", lambda e: e.max(out=m8.t[:, 8:16], in_=sc2.t[:]), [sc2.b], [m8.b])
                            S.add("vector", lambda e: e.tensor_scalar(out=slt.t[:], in0=sc1.t[:], scalar1=m8.t[:, 15:16], scalar2=None, op0=ALU.is_ge), [sc1.b, m8.b], [slt.b])
                            S.add("vector", lambda e: e.tensor_tensor(out=slt.t[:], in0=slt.t[:], in1=selv.t[:], op=ALU.mult), [slt.b, selv.b], [slt.b])
                            S.add("vector", lambda e: e.tensor_scalar(out=nsb.t[:], in0=slt.t[:], scalar1=30000.0, scalar2=-30000.0, op0=ALU.mult, op1=ALU.add), [slt.b], [nsb.b])
                            pst, pstb = pb()
                            ptb = pst[:].bitcast(BF16)
                            S.add("tensor", lambda e, ptb=ptb: e.transpose(out=ptb[0:64, 0:128], in_=nsb.t[:], identity=ident_b), [nsb.b] + cbb, [pstb])
                            S.add("scalar", lambda e, ptb=ptb, s4=s4: e.activation(out=negselT.t[0:64, s4 * 128:(s4 + 1) * 128], in_=ptb[0:64, 0:128], func=AF.Copy), [pstb], [negselT.b])
                        tiles = []
                        for br in (1, 2):
                            for h in range(4):
                                pair = accs[acci[0] % 2]
                                acci[0] += 1
                                kts = list(range(0, 4 * tg + 4)) if br == 1 else list(range(max(0, 4 * tg - 4), 4 * tg + 4))
                                for ii, kt in enumerate(kts):
                                    tiles.append((br, h, kt, ii, len(kts), pair))

                        def stage1(tl):
                            br, h, kt, ii, n, pair = tl
                            ps, psb = pb()
                            ksrc = kselT if br == 1 else kwinT
                            S.add("tensor", lambda e: e.matmul(ps[:], lhsT=ksrc.t[:, kt * 128:(kt + 1) * 128], rhs=qT[h].t[:], start=True, stop=False), [ksrc.b, qT[h].b], [psb])
                            if br == 1:
                                diag = kt >= 4 * tg
                                S.add("tensor", lambda e: e.matmul(ps[:], lhsT=ebig.t[0:64, kt * 128:(kt + 1) * 128], rhs=negselT.t[0:64, :], start=False, stop=(not diag)), [ebig.b, negselT.b], [psb])
                                if diag:
                                    dd = kt - 4 * tg
                                    S.add("tensor", lambda e: e.matmul(ps[:], lhsT=ident_b, rhs=wmask.t[:, (4 + dd) * 512:(5 + dd) * 512], start=False, stop=True), [wmask.b] + cbb, [psb])
                            else:
                                dd = kt - (4 * tg - 4)
                                if kt < 16:
                                    S.add("tensor", lambda e: e.matmul(ps[:], lhsT=keyneg.t[0:1, (kt - 8) * 128:(kt - 7) * 128], rhs=onesrow.t[0:1, :], start=False, stop=False), [keyneg.b, onesrow.b], [psb])
                                S.add("tensor", lambda e: e.matmul(ps[:], lhsT=ident_b, rhs=wmask.t[:, dd * 512:(dd + 1) * 512], start=False, stop=True), [wmask.b] + cbb, [psb])
                            pt = PT[pti[0] % len(PT)]
                            pti[0] += 1
                            S.add("scalar", lambda e: e.activation(out=pt.t[:], in_=ps[:], func=AF.Exp), [psb], [pt.b])
                            return pt

                        def stage2(tl, pt):
                            br, h, kt, ii, n, pair = tl
                            ao, asum = pair
                            vo = 0 if br == 1 else 128
                            S.add("tensor", lambda e: e.matmul(ao[0][:], lhsT=vsw.t[:, kt, vo:vo + 128], rhs=pt.t[:], start=(ii == 0), stop=(ii == n - 1)), [vsw.b, pt.b], [ao[1]])
                            S.add("tensor", lambda e: e.matmul(asum[0][:], lhsT=ones_b, rhs=pt.t[:], start=(ii == 0), stop=(ii == n - 1)), [pt.b] + cbb, [asum[1]])
                            if ii == n - 1:
                                combine(h, br, ao, asum)

                        LA = 3
                        C.nrot = 4
                        pend = []
                        for idx, tl in enumerate(tiles):
                            pend.append((tl, stage1(tl)))
                            if len(pend) > LA:
                                t0_, p0_ = pend.pop(0)
                                stage2(t0_, p0_)
                        while pend:
                            t0_, p0_ = pend.pop(0)
                            stage2(t0_, p0_)
                        C.nrot = 3
                        C.pi = 0
                        half = tg // 4
                        slot = (tg % 4) * 512
                        for h in range(4):
                            if tg >= 4:
                                S.add("vector", lambda e, h=h, g=g, slot=slot, half=half: e.scalar_tensor_tensor(out=C.ymnsa.t[:, g * 4 + h, slot:slot + 512], in0=acc.t[:, h, :], scalar=C.selm.t[:, half:half + 1],
                                    in1=C.ymnsa.t[:, g * 4 + h, slot:slot + 512], op0=ALU.mult, op1=ALU.add), [acc.b, C.ymnsa.b, C.selm.b], [C.ymnsa.b])
                            if tg == 3:
                                S.add("vector", lambda e, h=h, g=g: e.tensor_scalar(out=C.yhnsa.t[:, g * 4 + h, :], in0=acc.t[:, h, 384:512], scalar1=C.selm.t[:, 2:3], scalar2=None, op0=ALU.mult),
                                      [acc.b, C.selm.b], [C.yhnsa.b])
    C.nrot = 8
    C.pi = 0


P2_INPUTS = [("xh", [HALF + 128, D]), ("xTh", [D, HALF + 128]), ("wm", [D, 2048]), ("w_dn", [D, D]), ("w_nsa", [D, D]), ("w_out", [D, D]),
             ("ln1_g", [1, D]), ("ln1_b", [1, D]), ("w_up", [D, 5632]), ("fconvw", [128, 132]), ("fconvb", [128, 44]),
             ("w_down", [2816, D]), ("ln2_g", [1, D]), ("ln2_b", [1, D])]


def stage_p2(C, I, out_ap, fin):
    nc, S, pb = C.nc, C.S, C.pb
    C.nrot = 8
    C.pi = 0
    S.barrier()
    with ExitStack() as st:
        A = lambda name, shape, dt=F32: TT(C, st, name, shape, dt)
        ident_b = cst(C, "ident", "b")
        cbb = [C.cb.b]
        stgs = [A("pstg%d" % i, [128, 1024]) for i in range(2)]
        sti = [0]

        def load_chunk(src, dst_ap, dstb, K, ncol):
            sg = stgs[sti[0] % 2]
            sti[0] += 1
            v = sg.t[:, 0:K * ncol].rearrange("p (k c) -> p k c", k=K)
            S.dma("sync", v, src, writes=[sg.b])
            if sti[0] % 2:
                S.add("scalar", lambda e: e.activation(out=dst_ap, in_=v, func=AF.Copy), [sg.b], [dstb])
            else:
                S.add("gpsimd", lambda e: e.tensor_copy(out=dst_ap, in_=v), [sg.b], [dstb])

        def load_sc(src, dst_ap, dstb):
            S.dma("sync", dst_ap, src, reads=[C.wsc_b], writes=[dstb])

        wdn3 = I["w_dn"].rearrange("(kc p) c -> p kc c", p=128)
        wns3 = I["w_nsa"].rearrange("(kc p) c -> p kc c", p=128)
        wm3 = I["wm"].rearrange("(kc p) c -> p kc c", p=128)
        wo3 = I["w_out"].rearrange("(kc p) c -> p kc c", p=128)
        wup3 = I["w_up"].rearrange("(kc p) c -> p kc c", p=128)
        wdw3 = I["w_down"].rearrange("(k p) c -> p k c", p=128)
        xTh3 = I["xTh"].rearrange("(kc p) t -> p kc t", p=128)

        x1 = A("x1", [128, 4, D])
        x1T = A("x1T", [128, 8, 512], BF16)
        hist = A("hist", [128, 44, 2])
        lnp = A("lnp", [128, 2, D])
        fcw = A("fcw", [128, 132])
        fcb = A("fcb", [128, 44])
        st1 = A("st1", [128, 8])
        junk = A("junk", [128, D], BF16)
        S.dma("sync", fcw.t[:], I["fconvw"], writes=[fcw.b])
        S.dma("sync", fcb.t[:], I["fconvb"], writes=[fcb.b])

        def load_ln(which):
            S.dma("sync", lnp.t[:, 0, :], I["ln%d_g" % which].partition_broadcast(128), writes=[lnp.b])
            S.dma("sync", lnp.t[:, 1, :], I["ln%d_b" % which].partition_broadcast(128), writes=[lnp.b])

        def layernorm(r_ap, rb, out_ap_, outb):
            s1 = [st1.b]
            S.add("scalar", lambda e: e.activation(out=junk.t[:], in_=r_ap, func=AF.Copy, accum_out=st1.t[:, 0:1]), [rb], [junk.b, st1.b])
            S.add("scalar", lambda e: e.activation(out=junk.t[:], in_=r_ap, func=AF.Square, accum_out=st1.t[:, 1:2]), [rb], [junk.b, st1.b])
            S.add("vector", lambda e: e.tensor_scalar(out=st1.t[:, 2:3], in0=st1.t[:, 0:1], scalar1=1.0 / D, scalar2=None, op0=ALU.mult), s1, s1)
            S.add("vector", lambda e: e.tensor_tensor(out=st1.t[:, 3:4], in0=st1.t[:, 2:3], in1=st1.t[:, 2:3], op=ALU.mult), s1, s1)
            S.add("vector", lambda e: e.scalar_tensor_tensor(out=st1.t[:, 4:5], in0=st1.t[:, 1:2], scalar=1.0 / D, in1=st1.t[:, 3:4], op0=ALU.mult, op1=ALU.subtract), s1, s1)
            S.add("scalar", lambda e: e.activation(out=st1.t[:, 5:6], in_=st1.t[:, 4:5], func=AF.Sqrt, bias=C.eps.t[:, 1:2], scale=1.0), s1 + [C.eps.b], s1)
            S.add("vector", lambda e: e.reciprocal(out=st1.t[:, 6:7], in_=st1.t[:, 5:6]), s1, s1)
            S.add("vector", lambda e: e.tensor_scalar(out=out_ap_, in0=r_ap, scalar1=st1.t[:, 2:3], scalar2=st1.t[:, 6:7], op0=ALU.subtract, op1=ALU.mult), [rb] + s1, [outb])
            S.add("gpsimd", lambda e: e.tensor_tensor(out=out_ap_, in0=out_ap_, in1=lnp.t[:, 0, :], op=ALU.mult), [outb, lnp.b], [outb])
            S.add("gpsimd", lambda e: e.tensor_tensor(out=out_ap_, in0=out_ap_, in1=lnp.t[:, 1, :], op=ALU.add), [outb, lnp.b], [outb])

        def do_block(blk):
            N = 128 if blk < 0 else 512
            nt = N // 128
            tok0 = 0 if blk < 0 else 128 + blk * 512
            if blk < 0:
                ysrc = lambda T_, kc: T_[1].t[:, kc, :]
            else:
                ysrc = lambda T_, kc, blk=blk: T_[0].t[:, kc, blk * 512:(blk + 1) * 512]
            YD = (C.ymdn, C.yhdn)
            YN = (C.ymnsa, C.yhnsa)
            ydb = [C.ymdn.b, C.yhdn.b]
            ynb = [C.ymnsa.b, C.yhnsa.b]
            S.barrier()
            with ExitStack() as sa:
                B = lambda name, shape, dt=F32: TT(C, sa, "%s_b%d" % (name, blk + 1), shape, dt)
                yT = B("yT", [128, 8, 512], BF16)
                xgb = B("xgb", [128, 8, 512], BF16)
                wch = {k: [B("wch_%s%d" % (k, i), [128, 8, 128], BF16) for i in range(3)] for k in ("dn", "ns", "md", "mn")}
                gdt = B("gdt", [128, 512])
                gnt = B("gnt", [128, 512])
                t1 = B("t1", [128, 512])
                t2 = B("t2", [128, 512])
                wo = B("wo", [128, 8, D], BF16)
                xt = [B("xt%d" % i, [128, D]) for i in range(2)]
                rbuf = B("rbuf", [128, D])
                x1b = B("x1b", [128, D], BF16)
                for k0 in range(0, 8, 2):
                    load_chunk(xTh3[:, k0:k0 + 2, tok0:tok0 + N], xgb.t[:, k0:k0 + 2, 0:N], xgb.b, 2, N)
                for c in range(8):
                    cs_ = slice(c * 128, (c + 1) * 128)
                    w = {k: wch[k][c % 3] for k in wch}
                    load_sc(C.wsc["dn"][c], w["dn"].t[:], w["dn"].b)
                    load_sc(C.wsc["ns"][c], w["ns"].t[:], w["ns"].b)
                    load_sc(C.wsc["m"][c], w["md"].t[:], w["md"].b)
                    load_sc(C.wsc["m"][8 + c], w["mn"].t[:], w["mn"].b)
                    pd, pdb = pb()
                    pn, pnb = pb()
                    pgd, pgdb = pb()
                    pgn, pgnb = pb()
                    for kc in range(8):
                        S.add("tensor", lambda e, pd=pd, kc=kc, w=w: e.matmul(pd[:, 0:N], lhsT=w["dn"].t[:, kc, :], rhs=ysrc(YD, kc), start=(kc == 0), stop=(kc == 7)), [w["dn"].b] + ydb, [pdb])
                    for kc in range(8):
                        S.add("tensor", lambda e, pn=pn, kc=kc, w=w: e.matmul(pn[:, 0:N], lhsT=w["ns"].t[:, kc, :], rhs=ysrc(YN, kc), start=(kc == 0), stop=(kc == 7)), [w["ns"].b] + ynb, [pnb])
                    for kc in range(8):
                        S.add("tensor", lambda e, pgd=pgd, kc=kc, w=w: e.matmul(pgd[:, 0:N], lhsT=w["md"].t[:, kc, :], rhs=xgb.t[:, kc, 0:N], start=(kc == 0), stop=(kc == 7)), [w["md"].b, xgb.b], [pgdb])
                    for kc in range(8):
                        S.add("tensor", lambda e, pgn=pgn, kc=kc, w=w: e.matmul(pgn[:, 0:N], lhsT=w["mn"].t[:, kc, :], rhs=xgb.t[:, kc, 0:N], start=(kc == 0), stop=(kc == 7)), [w["mn"].b, xgb.b], [pgnb])
                    S.add("scalar", lambda e, pgd=pgd: e.activation(out=gdt.t[:, 0:N], in_=pgd[:, 0:N], func=AF.Sigmoid), [pgdb], [gdt.b])
                    S.add("scalar", lambda e, pgn=pgn: e.activation(out=gnt.t[:, 0:N], in_=pgn[:, 0:N], func=AF.Sigmoid), [pgnb], [gnt.b])
                    S.add("vector", lambda e, pd=pd: e.tensor_tensor(out=t1.t[:, 0:N], in0=gdt.t[:, 0:N], in1=pd[:, 0:N], op=ALU.mult), [gdt.b, pdb], [t1.b])
                    S.add("vector", lambda e, pn=pn: e.tensor_tensor(out=t2.t[:, 0:N], in0=gnt.t[:, 0:N], in1=pn[:, 0:N], op=ALU.mult), [gnt.b, pnb], [t2.b])
                    S.add("vector", lambda e, c=c: e.tensor_tensor(out=yT.t[:, c, 0:N], in0=t1.t[:, 0:N], in1=t2.t[:, 0:N], op=ALU.add), [t1.b, t2.b], [yT.b])
                load_sc(C.wsc["out"], wo.t[:], wo.b)
                load_ln(1)
                for s in range(nt):
                    xti = xt[s % 2]
                    S.dma("sync", xti.t[:], I["xh"][tok0 + s * 128:tok0 + (s + 1) * 128, :], writes=[xti.b])
                    for hf in range(2):
                        pz, pzb = pb()
                        for kc in range(8):
                            S.add("tensor", lambda e, pz=pz, kc=kc, s=s, hf=hf: e.matmul(pz[:], lhsT=yT.t[:, kc, s * 128:(s + 1) * 128], rhs=wo.t[:, kc, hf * 512:(hf + 1) * 512],
                                                                                      start=(kc == 0), stop=(kc == 7)), [yT.b, wo.b], [pzb])
                        S.add("vector", lambda e, pz=pz, hf=hf, xti=xti: e.scalar_tensor_tensor(out=rbuf.t[:, hf * 512:(hf + 1) * 512], in0=xti.t[:, hf * 512:(hf + 1) * 512], scalar=ALPHA, in1=pz[:],
                                                                                               op0=ALU.mult, op1=ALU.add), [xti.b, pzb], [rbuf.b])
                    layernorm(rbuf.t[:], rbuf.b, x1.t[:, s, :], x1.b)
                    S.add("scalar", lambda e, s=s: e.activation(out=x1b.t[:], in_=x1.t[:, s, :], func=AF.Copy), [x1.b], [x1b.b])
                    pt, ptb_ = pb()
                    ptv = pt[:].bitcast(BF16)
                    for kc in range(8):
                        S.add("tensor", lambda e, ptv=ptv, kc=kc: e.transpose(out=ptv[:, kc * 128:(kc + 1) * 128], in_=x1b.t[:, kc * 128:(kc + 1) * 128], identity=ident_b), [x1b.b] + cbb, [ptb_])
                    S.add("scalar", lambda e, ptv=ptv, s=s: e.activation(out=x1T.t[:, :, s * 128:(s + 1) * 128], in_=ptv.rearrange("p (a b) -> p a b", a=8), func=AF.Copy), [ptb_], [x1T.b])
            S.barrier()
            with ExitStack() as sb:
                B = lambda name, shape, dt=F32: TT(C, sb, "%s_f%d" % (name, blk + 1), shape, dt)
                wup = [B("wup%d" % i, [128, 8, 128], BF16) for i in range(6)]
                wi = [0]
                if blk < 0:
                    for cti in range(44):
                        w = wup[wi[0] % 6]
                        wi[0] += 1
                        load_sc(C.wsc["up"][cti], w.t[:], w.b)
                        ps, psb = pb()
                        for kc in range(8):
                            S.add("tensor", lambda e, ps=ps, kc=kc, w=w: e.matmul(ps[:, 0:128], lhsT=w.t[:, kc, :], rhs=x1T.t[:, kc, 0:128], start=(kc == 0), stop=(kc == 7)), [w.b, x1T.b], [psb])
                        S.add("vector", lambda e, ps=ps, cti=cti: e.tensor_scalar(out=hist.t[:, cti, :], in0=ps[:, 126:128], scalar1=C.selm.t[:, 2:3], scalar2=None, op0=ALU.mult), [psb, C.selm.b], [hist.b])
                    return
                hT = B("hT", [128, 22, 512], BF16)
                ub = [B("ub%d" % i, [128, 514]) for i in range(2)]
                cc = [B("cc%d" % i, [128, 512]) for i in range(2)]
                sgt = B("sgt", [128, 512])
                wdh = B("wdh", [128, 22, 512], BF16)
                r2 = B("r2", [128, 4, D])
                ot = [B("ot%d" % i, [128, D]) for i in range(2)]
                for ct in range(22):
                    for gv in range(2):
                        cti = ct + 22 * gv
                        w = wup[wi[0] % 6]
                        wi[0] += 1
                        load_sc(C.wsc["up"][cti], w.t[:], w.b)
                        ps, psb = pb()
                        for kc in range(8):
                            S.add("tensor", lambda e, ps=ps, kc=kc, w=w: e.matmul(ps[:], lhsT=w.t[:, kc, :], rhs=x1T.t[:, kc, :], start=(kc == 0), stop=(kc == 7)), [w.b, x1T.b], [psb])
                        u = ub[gv]
                        S.add("gpsimd", lambda e, u=u, cti=cti: e.tensor_copy(out=u.t[:, 0:2], in_=hist.t[:, cti, :]), [hist.b], [u.b])
                        S.add("scalar", lambda e, u=u, ps=ps: e.activation(out=u.t[:, 2:514], in_=ps[:], func=AF.Copy), [psb], [u.b])
                        S.add("gpsimd", lambda e, u=u, cti=cti: e.tensor_copy(out=hist.t[:, cti, :], in_=u.t[:, 512:514]), [u.b], [hist.b])
                        cv = cc[gv]
                        S.add("vector", lambda e, u=u, cv=cv, cti=cti: e.tensor_scalar(out=cv.t[:], in0=u.t[:, 2:514], scalar1=fcw.t[:, cti * 3 + 2:cti * 3 + 3], scalar2=fcb.t[:, cti:cti + 1],
                                                                                    op0=ALU.mult, op1=ALU.add), [u.b, fcw.b, fcb.b], [cv.b])
                        S.add("vector", lambda e, u=u, cv=cv, cti=cti: e.scalar_tensor_tensor(out=cv.t[:], in0=u.t[:, 1:513], scalar=fcw.t[:, cti * 3 + 1:cti * 3 + 2], in1=cv.t[:],
                                                                                           op0=ALU.mult, op1=ALU.add), [u.b, fcw.b, cv.b], [cv.b])
                        S.add("vector", lambda e, u=u, cv=cv, cti=cti: e.scalar_tensor_tensor(out=cv.t[:], in0=u.t[:, 0:512], scalar=fcw.t[:, cti * 3:cti * 3 + 1], in1=cv.t[:],
                                                                                           op0=ALU.mult, op1=ALU.add), [u.b, fcw.b, cv.b], [cv.b])
                    S.add("scalar", lambda e: e.activation(out=sgt.t[:], in_=cc[0].t[:], func=AF.Silu), [cc[0].b], [sgt.b])
                    S.add("vector", lambda e, ct=ct: e.tensor_tensor(out=hT.t[:, ct, :], in0=sgt.t[:], in1=cc[1].t[:], op=ALU.mult), [sgt.b, cc[1].b], [hT.b])
                for hf in range(2):
                    load_sc(C.wsc["down"][hf], wdh.t[:], wdh.b)
                    for s in range(4):
                        pf, pfb = pb()
                        for k in range(22):
                            S.add("tensor", lambda e, pf=pf, k=k, s=s: e.matmul(pf[:], lhsT=hT.t[:, k, s * 128:(s + 1) * 128], rhs=wdh.t[:, k, :], start=(k == 0), stop=(k == 21)), [hT.b, wdh.b], [pfb])
                        S.add("vector", lambda e, pf=pf, s=s, hf=hf: e.scalar_tensor_tensor(out=r2.t[:, s, hf * 512:(hf + 1) * 512], in0=x1.t[:, s, hf * 512:(hf + 1) * 512], scalar=ALPHA, in1=pf[:],
                                                                                         op0=ALU.mult, op1=ALU.add), [x1.b, pfb], [r2.b])
                load_ln(2)
                for s in range(4):
                    o = ot[s % 2]
                    layernorm(r2.t[:, s, :], r2.b, o.t[:], o.b)
                    row0 = blk * 512 + s * 128
                    fin.append(S.dma("sync", out_ap[row0:row0 + 128, :], o.t[:], reads=[o.b]))

        for blk in range(-1, 4):
            do_block(blk)
```

```python
import numpy as np
import concourse.bass as bass
import concourse.mybir as mybir
from concourse.bass_utils import run_bass_kernel_spmd
from contextlib import ExitStack

F32 = mybir.dt.float32
BF16 = mybir.dt.bfloat16
AF = mybir.ActivationFunctionType
ALU = mybir.AluOpType
AX = mybir.AxisListType

ENGS = ["sync", "scalar", "vector", "gpsimd", "tensor"]
DMA_K = 12

SEQ = 4096
NT = 32
D = 1024
HALF = 2048
RMS_EPS = 1e-6
LN_EPS = 1e-5
ALPHA = 2.0 ** 0.25


class Buf:
    __slots__ = ("name", "last_w", "rd_c", "rd_d")

    def __init__(self, name):
        self.name = name
        self.last_w = None
        self.rd_c = {}
        self.rd_d = []


class Op:
    __slots__ = ("eng", "fn", "deps", "needed", "count", "dma", "sem", "k")


class Sched:
    def __init__(self, nc, stack):
        self.nc = nc
        self.ops = {e: [] for e in ENGS}
        self.ndma = {e: 0 for e in ENGS}
        self.dma_ops = {e: [] for e in ENGS}
        self.csem = {e: stack.enter_context(nc.semaphore("c_" + e)) for e in ENGS}
        self.dsem = {e: [stack.enter_context(nc.semaphore("d_%s_%d" % (e, i))) for i in range(DMA_K)]
                     for e in ("sync", "scalar", "gpsimd")}
        self.nbuf = 0

    def buf(self, name=None):
        self.nbuf += 1
        return Buf(name or "b%d" % self.nbuf)

    def add(self, eng, fn, reads=(), writes=(), dma=False):
        op = Op()
        op.eng = eng
        op.fn = fn
        op.dma = dma
        op.needed = False
        op.count = None
        op.sem = None
        deps = []

        def dep(d, raw):
            if d is None:
                return
            if (not raw) and (not dma) and (not d.dma) and d.eng == eng:
                return
            if eng == "tensor" and d.eng == "tensor" and not d.dma and not dma:
                return
            deps.append(d)

        for b in reads:
            dep(b.last_w, True)
        for b in writes:
            dep(b.last_w, False)
            for r in b.rd_c.values():
                dep(r, False)
            for r in b.rd_d:
                dep(r, False)
        if dma:
            j = self.ndma[eng]
            self.ndma[eng] += 1
            op.k = j
            if j >= DMA_K:
                deps.append(self.dma_ops[eng][j - DMA_K])
            self.dma_ops[eng].append(op)
        seen = set()
        dd = []
        for d in deps:
            if id(d) not in seen:
                seen.add(id(d))
                dd.append(d)
                d.needed = True
        op.deps = dd
        for b in reads:
            if dma:
                b.rd_d.append(op)
            else:
                b.rd_c[eng] = op
        for b in writes:
            b.last_w = op
            b.rd_c = {}
            b.rd_d = []
        self.ops[eng].append(op)
        return op

    def barrier(self):
        lasts = []
        for e in ENGS:
            for op in reversed(self.ops[e]):
                if (not op.dma) and op.fn is not None:
                    lasts.append(op)
                    break
            lasts.extend(self.dma_ops[e][-DMA_K:])
        for d in lasts:
            d.needed = True
        for e in ENGS:
            op = Op()
            op.eng = e
            op.fn = None
            op.dma = False
            op.needed = False
            op.count = None
            op.sem = None
            op.deps = list(lasts)
            self.ops[e].append(op)

    def dma(self, q, out, in_, reads=(), writes=(), **kw):
        return self.add(q, lambda e: e.dma_start(out=out, in_=in_, **kw), reads, writes, dma=True)

    def finalize(self, final_deps=()):
        nc = self.nc
        for e in ENGS:
            c = 0
            for op in self.ops[e]:
                if op.dma:
                    op.sem = self.dsem[e][op.k % DMA_K]
                    op.count = 16 * (op.k // DMA_K + 1)
                    op.needed = True
                elif op.needed:
                    c += 1
                    op.sem = self.csem[e]
                    op.count = c
        ops = self.ops

        def emit(e, eng):
            waited = {}
            for op in ops[e]:
                for d in op.deps:
                    key = id(d.sem)
                    if waited.get(key, 0) < d.count:
                        eng.wait_ge(d.sem, d.count)
                        waited[key] = d.count
                if op.fn is None:
                    continue
                inst = op.fn(eng)
                if op.needed:
                    inst.then_inc(op.sem, 16 if op.dma else 1)
            if e == "sync":
                for d in final_deps:
                    eng.wait_ge(d.sem, d.count)

        with nc.Block() as block:
            @block.sync
            def _(eng):
                emit("sync", eng)

            @block.scalar
            def _(eng):
                emit("scalar", eng)

            @block.vector
            def _(eng):
                emit("vector", eng)

            @block.gpsimd
            def _(eng):
                emit("gpsimd", eng)

            @block.tensor
            def _(eng):
                emit("tensor", eng)


class TT:
    def __init__(self, C, st, name, shape, dt=F32):
        self.t = st.enter_context(C.nc.sbuf_tensor("s_" + name, shape, dt))
        self.b = C.S.buf(name)


class Ctx:
    pass


CONST_COLS = {}
NC2 = {}


def _build_consts():
    cols = []
    off = [0]

    def put(name, arr):
        arr = np.asarray(arr, np.float32).reshape(128, -1)
        CONST_COLS[name] = (off[0], arr.shape[1])
        off[0] += arr.shape[1]
        cols.append(arr)

    p = np.arange(128)[:, None]
    f = np.arange(128)[None, :]
    same = (p // 64) == (f // 64)
    put("ident", (p == f))
    put("ones", np.ones((128, 128)))
    put("tri", (p <= f) & same)
    put("blk", same)
    put("strict", (f < p) & same)
    put("upper", (f >= p) & same)
    NEG = -30000.0
    pp = np.arange(128)[:, None]
    ff = np.arange(512)[None, :]
    cm = [np.where(pp + 128 * d <= ff, 0.0, NEG) for d in range(4)]
    NC2.__setitem__("cmask", np.concatenate(cm, axis=1).astype(np.float32))
    wm = []
    for k in range(8):
        rel = ff - pp - 128 * (k - 4)
        wm.append(np.where((rel >= 0) & (rel < 512), 0.0, NEG))
    NC2.__setitem__("wmask", np.concatenate(wm, axis=1).astype(np.float32))
    pm = [np.where(16 * pp + 31 - ff <= 512 * d, 0.0, NEG) for d in range(5)]
    NC2.__setitem__("pm_list", [p_.astype(np.float32) for p_ in pm])
    tok0 = np.arange(256) * 16
    cend = tok0 + 31
    sst = np.arange(64) * 64
    ov = ((tok0[:, None] <= (sst + 63)[None]) & (cend[:, None] >= sst[None])).astype(np.float32)
    ov[255] = 0.0
    NC2.__setitem__("ovl", np.ascontiguousarray(np.concatenate([ov[0:128], ov[128:256]], axis=1)))
    eb = np.zeros((128, 4096), np.float32)
    eb[np.arange(4096) // 64, np.arange(4096)] = 1.0
    NC2.__setitem__("ebig", eb)
    gs = np.zeros((128, 12 * 128), np.float32)
    for r in range(12):
        gs[r, r * 128:(r + 1) * 128] = 1.0
    NC2.__setitem__("gsel", gs)
    return np.concatenate(cols, axis=1)


CONSTS = _build_consts()


def _rope_tables(off=0):
    half = 16
    inv = 500000.0 ** (-np.arange(half, dtype=np.float32) / half)
    def tab(pos):
        ang = pos.astype(np.float32)[None, :] * inv[:, None]
        c = np.cos(ang).astype(np.float32)
        sn = np.sin(ang).astype(np.float32)
        P = ang.shape[1]
        return (np.concatenate([c, c, np.ones((96, P), np.float32)], 0),
                np.concatenate([-sn, sn, np.zeros((96, P), np.float32)], 0))
    c1, s1 = tab(np.maximum(np.arange(4096) - off, 0))
    pk = np.maximum(np.arange(256) * 16 + 31 - off, 0)
    c2, s2 = tab(pk)
    return np.ascontiguousarray(c1), np.ascontiguousarray(s1), np.ascontiguousarray(c2), np.ascontiguousarray(s2)


def _sel_tables(off=0):
    t = np.arange(4096)
    cur = t // 64
    jb = np.arange(64)
    j0 = off // 64
    valid = (jb[None] <= cur[:, None]) & (jb[None] >= j0)
    forced = valid & ((jb[None] == j0) | (jb[None] == cur[:, None]) | (jb[None] == cur[:, None] - 1))
    bias = np.where(forced, 10.0, np.where(valid, 0.0, -1e9)).astype(np.float32)
    return bias, valid.astype(np.float32)


def _rot_perm(ncols_heads):
    idx = []
    for h in range(ncols_heads):
        for d in range(128):
            idx.append(h * 128 + (d + 16 if d < 16 else (d - 16 if d < 32 else d)))
    return np.array(idx)


def cst(C, name, dt="f"):
    o, n = CONST_COLS[name]
    t = C.cf if dt == "f" else C.cb
    return t.t[:, o:o + n]


def build(debug=None):
    nc = bass.Bass("TRN2", target_bir_lowering=False)
    C = Ctx()
    C.nc = nc
    I = {}

    def inp(name, shape):
        I[name] = nc.dram_tensor(name, list(shape), F32, kind="ExternalInput").ap()
        return I[name]

    inp("xT", [D, SEQ])
    inp("consts", [128, CONSTS.shape[1]])
    inp("selm", [128, 3])
    inp("wg_qkv", [D, 3072])
    inp("wg_z", [D, 1024])
    inp("wg_ba", [D, 16])
    inp("convw", [128, 24 * 4])
    inp("a_log", [1, 8])
    inp("dt_bias", [1, 8])
    inp("dn_norm_w", [1, 128])
    for (nm, shp) in NSA_INPUTS:
        inp(nm, shp)
    for (nm, shp) in P2_INPUTS:
        inp(nm, shp)
    C.xTb = nc.dram_tensor("xTb_scratch", [D, SEQ], BF16).ap()
    outs = {}
    if debug == "nsa":
        outs["ynsa"] = nc.dram_tensor("ynsa", [128, 8 * HALF], F32, kind="ExternalOutput").ap()
    if debug == "gdn":
        outs["ydn"] = nc.dram_tensor("ydn", [128, 8 * HALF], F32, kind="ExternalOutput").ap()

    with ExitStack() as st:
        S = Sched(nc, st)
        C.S = S
        C.st = st
        C.xTb_b = S.buf("xTb")
        C.wsc_b = S.buf("wscratch")
        C.wsc = {
            "dn": nc.dram_tensor("sc_dn", [8, 128, 8, 128], BF16).ap(),
            "ns": nc.dram_tensor("sc_ns", [8, 128, 8, 128], BF16).ap(),
            "m": nc.dram_tensor("sc_m", [16, 128, 8, 128], BF16).ap(),
            "out": nc.dram_tensor("sc_out", [128, 8, 1024], BF16).ap(),
            "up": nc.dram_tensor("sc_up", [44, 128, 8, 128], BF16).ap(),
            "down": nc.dram_tensor("sc_down", [2, 128, 22, 512], BF16).ap(),
        }
        C.ps = [st.enter_context(nc.psum_tensor("ps%d" % i, [128, 512], F32)) for i in range(8)]
        C.psb = [S.buf("ps%d" % i) for i in range(8)]
        C.pi = 0
        C.nrot = 8

        def pb():
            i = C.pi % C.nrot
            C.pi = (i + 1) % C.nrot
            return C.ps[i], C.psb[i]

        C.pb = pb
        C.cf = TT(C, st, "cf", [128, CONSTS.shape[1]])
        C.cb = TT(C, st, "cb", [128, CONSTS.shape[1]], BF16)
        S.dma("sync", C.cf.t[:], I["consts"], writes=[C.cf.b])
        S.add("vector", lambda e: e.tensor_copy(out=C.cb.t[:], in_=C.cf.t[:]), [C.cf.b], [C.cb.b])
        C.selm = TT(C, st, "selm", [128, 3])
        S.dma("sync", C.selm.t[:], I["selm"], writes=[C.selm.b])
        C.eps = TT(C, st, "eps", [128, 2])
        S.add("gpsimd", lambda e: e.memset(C.eps.t[:, 0:1], RMS_EPS), [], [C.eps.b])
        S.add("gpsimd", lambda e: e.memset(C.eps.t[:, 1:2], LN_EPS), [], [C.eps.b])
        C.ymdn = TT(C, st, "ymdn", [128, 8, HALF], BF16)
        C.yhdn = TT(C, st, "yhdn", [128, 8, 128], BF16)
        S.add("gpsimd", lambda e: e.memset(C.ymdn.t[:], 0.0), [], [C.ymdn.b])

        fin = []
        C.have_xTb = debug not in ("nsa", "p2")
        if debug not in ("nsa", "p2"):
            stage_gdn(C, I)
        S.barrier()
        C.ymnsa = TT(C, st, "ymnsa", [128, 8, HALF], BF16)
        C.yhnsa = TT(C, st, "yhnsa", [128, 8, 128], BF16)
        S.add("gpsimd", lambda e: e.memset(C.ymnsa.t[:], 0.0), [], [C.ymnsa.b])
        if debug == "p2":
            S.add("gpsimd", lambda e: e.memset(C.yhdn.t[:], 0.0), [], [C.yhdn.b])
            S.add("gpsimd", lambda e: e.memset(C.yhnsa.t[:], 0.0), [], [C.yhnsa.b])
        if debug not in ("gdn", "p2"):
            stage_nsa(C, I)
        if debug == "nsa":
            with ExitStack() as s2:
                for h in range(8):
                    tmp = TT(C, s2, "dbgn%d" % h, [128, HALF])
                    S.add("vector", lambda e, tmp=tmp, h=h: e.tensor_copy(out=tmp.t[:], in_=C.ymnsa.t[:, h, :]), [C.ymnsa.b], [tmp.b])
                    fin.append(S.dma("sync", outs["ynsa"][:, h * HALF:(h + 1) * HALF], tmp.t[:], reads=[tmp.b]))
                S.finalize(final_deps=fin)
            return nc
        if debug == "gdn":
            with ExitStack() as s2:
                for h in range(8):
                    tmp = TT(C, s2, "dbg%d" % h, [128, HALF])
                    S.add("vector", lambda e, tmp=tmp, h=h: e.tensor_copy(out=tmp.t[:], in_=C.ymdn.t[:, h, :]), [C.ymdn.b], [tmp.b])
                    fin.append(S.dma("sync", outs["ydn"][:, h * HALF:(h + 1) * HALF], tmp.t[:], reads=[tmp.b]))
                S.finalize(final_deps=fin)
            return nc
        out_ap = nc.dram_tensor("out", [HALF, D], F32, kind="ExternalOutput").ap()
        if debug == "all":
            dd = {"ydn": nc.dram_tensor("ydn", [128, 8 * HALF], F32, kind="ExternalOutput").ap(),
                  "ynsa": nc.dram_tensor("ynsa", [128, 8 * HALF], F32, kind="ExternalOutput").ap()}
            with ExitStack() as s2:
                tmps = [TT(C, s2, "dbga%d" % i, [128, HALF]) for i in range(2)]
                k = 0
                for nm, src in (("ydn", C.ymdn), ("ynsa", C.ymnsa)):
                    for h in range(8):
                        tmp = tmps[k % 2]
                        k += 1
                        S.add("vector", lambda e, tmp=tmp, h=h, src=src: e.tensor_copy(out=tmp.t[:], in_=src.t[:, h, :]), [src.b], [tmp.b])
                        fin.append(S.dma("sync", dd[nm][:, h * HALF:(h + 1) * HALF], tmp.t[:], reads=[tmp.b]))
        stage_p2(C, I, out_ap, fin)
        S.finalize(final_deps=fin)
    return nc


def stage_gdn(C, I):
    nc, S, pb = C.nc, C.S, C.pb
    C.nrot = 6
    C.pi = 0
    with ExitStack() as st:
        A = lambda name, shape, dt=F32: TT(C, st, name, shape, dt)
        ident_b = cst(C, "ident", "b")
        ones_b = cst(C, "ones", "b")
        ones_f = cst(C, "ones")
        cbf = [C.cf.b]
        cbb = [C.cb.b]
        Wqkv = A("Wqkv", [128, 8, 3072], BF16)
        Wz = A("Wz", [128, 8, 1024], BF16)
        Wba = A("Wba", [128, 8, 16], BF16)
        stg = [A("stg0", [128, 8, 128])] * 2
        xf = [stg[0]]
        k = 0
        for (src, dst, ncol) in ((I["wg_qkv"], Wqkv, 3072), (I["wg_z"], Wz, 1024)):
            for c0 in range(0, ncol, 128):
                sg = stg[k % 2]
                S.dma("sync", sg.t[:], src[:, c0:c0 + 128].rearrange("(kc p) c -> p kc c", p=128), writes=[sg.b])
                eng = "gpsimd" if k % 2 else "vector"
                S.add(eng, lambda e, sg=sg, dst=dst, c0=c0: e.tensor_copy(out=dst.t[:, :, c0:c0 + 128], in_=sg.t[:]), [sg.b], [dst.b])
                k += 1
        sg = stg[0]
        S.dma("sync", sg.t[:, :, 0:16], I["wg_ba"].rearrange("(kc p) c -> p kc c", p=128), writes=[sg.b])
        S.add("vector", lambda e: e.tensor_copy(out=Wba.t[:], in_=sg.t[:, :, 0:16]), [sg.b], [Wba.b])
        cw = A("cw", [128, 96])
        S.dma("sync", cw.t[:], I["convw"], writes=[cw.b])
        Dg = A("Dg", [128, 96, 128], BF16)
        for j in range(96):
            eng = "gpsimd" if j % 2 else "vector"
            S.add(eng, lambda e, j=j: e.tensor_scalar(out=Dg.t[:, j, :], in0=cst(C, "ident"), scalar1=cw.t[:, j:j + 1], scalar2=None, op0=ALU.mult),
                  [cw.b] + cbf, [Dg.b])
        sp = A("sp", [128, 16 + 128])
        S.dma("sync", sp.t[:, 0:8], I["a_log"].partition_broadcast(128), writes=[sp.b])
        S.dma("sync", sp.t[:, 8:16], I["dt_bias"].partition_broadcast(128), writes=[sp.b])
        S.dma("sync", sp.t[:, 16:144], I["dn_norm_w"].partition_broadcast(128), writes=[sp.b])
        negA = A("negA", [128, 8])
        S.add("scalar", lambda e: e.activation(out=negA.t[:], in_=sp.t[:, 0:8], func=AF.Exp), [sp.b], [negA.b])
        S.add("vector", lambda e: e.tensor_scalar(out=negA.t[:], in0=negA.t[:], scalar1=-1.0, scalar2=None, op0=ALU.mult), [negA.b], [negA.b])
        nw4 = A("nw4", [128, 4, 128])
        for h in range(4):
            S.add("vector", lambda e, h=h: e.tensor_copy(out=nw4.t[:, h, :], in_=sp.t[:, 16:144]), [sp.b], [nw4.b])

        xf = stg
        xb = [A("xb%d" % i, [128, 8, 128], BF16) for i in range(1)]
        pre = [A("pre%d" % i, [128, 24, 131], BF16) for i in range(1)]
        S.add("gpsimd", lambda e: e.memset(pre[0].t[:], 0.0), [], [pre[0].b])
        actl = [A("act%d" % i, [128, 4, 128]) for i in range(2)]
        sql = [A("sq%d" % i, [128, 4, 128], BF16) for i in range(2)]
        rinv = A("rinv", [128, 4, 128])
        qT = A("qT", [128, 8, 128], BF16)
        kT = A("kT", [128, 8, 128], BF16)
        vT = A("vT", [128, 8, 128], BF16)
        kbeg = A("kbeg", [128, 8, 128], BF16)
        vb = A("vb", [128, 8, 128], BF16)
        kd = A("kd", [128, 8, 128], BF16)
        sm = A("sm", [128, 16 * 8])
        SM = lambda i: sm.t[:, i * 8:(i + 1) * 8]
        triG = A("triG", [128, 8, 128])
        Sf = A("Sf", [128, 8, 128])
        Sb = A("Sb", [128, 8, 128], BF16)
        Sfb = [S.buf("Sf0"), S.buf("Sf1")]
        Sbb = [S.buf("Sb0"), S.buf("Sb1")]
        S.add("gpsimd", lambda e: e.memset(Sf.t[:], 0.0), [], Sfb)
        S.add("gpsimd", lambda e: e.memset(Sb.t[:], 0.0), [], Sbb)
        hbT = []
        for hb in range(2):
            d = {}
            for (nm, shp, dt) in (("d1", [128, 4, 128], F32), ("Ds", [128, 4, 128], BF16), ("DT", [128, 4, 128], BF16),
                                  ("egbc", [128, 4, 128], F32),
                                  ("XA", [128, 4, 128], BF16), ("XB", [128, 4, 128], BF16),
                                  ("TXA", [128, 2, 4, 128], BF16), ("TXB", [128, 2, 4, 128], BF16),
                                  ("u", [128, 4, 128], BF16), ("wT", [128, 4, 128], BF16), ("qkT", [128, 4, 128], BF16),
                                  ("qdT", [128, 4, 128], BF16), ("vnew", [128, 4, 128], BF16),
                                  ("ss", [128, 4], F32),
                                  ):
                d[nm] = A("%s_%d" % (nm, hb), shp, dt)
            d["a1"] = d["d1"]
            d["X0"] = d["XB"]
            d["zs"] = d["Ds"]
            d["ytok"] = d["vnew"]
            d["osq"] = d["d1"]
            d["y1"] = d["egbc"]
            hbT.append(d)

        xT3 = I["xT"].rearrange("(kc p) t -> p kc t", p=128)

        for t in range(NT):
            xfi, xbi, pr = xf[t % 2], xb[0], pre[0]
            if t > 0:
                S.add("gpsimd", lambda e, pr=pr: e.tensor_copy(out=pr.t[:, :, 0:3], in_=pr.t[:, :, 128:131]), [pr.b], [pr.b])
            S.dma("sync", xfi.t[:], xT3[:, :, t * 128:(t + 1) * 128], writes=[xfi.b])
            S.add("gpsimd", lambda e, xfi=xfi, xbi=xbi: e.tensor_copy(out=xbi.t[:], in_=xfi.t[:]), [xfi.b], [xbi.b])
            S.dma("sync", C.xTb.rearrange("(kc p) t -> p kc t", p=128)[:, :, t * 128:(t + 1) * 128], xbi.t[:], reads=[xbi.b], writes=[C.xTb_b])
            for cg in range(6):
                ps, psb = pb()
                for j in range(4):
                    ct = cg * 4 + j
                    for kc in range(8):
                        S.add("tensor", lambda e, ps=ps, j=j, ct=ct, kc=kc, xbi=xbi: e.matmul(
                            ps[:, j * 128:(j + 1) * 128], lhsT=Wqkv.t[:, kc, ct * 128:(ct + 1) * 128], rhs=xbi.t[:, kc, :],
                            start=(kc == 0), stop=(kc == 7)), [Wqkv.b, xbi.b], [psb])
                S.add("scalar", lambda e, ps=ps, cg=cg, pr=pr: e.activation(
                    out=pr.t[:, cg * 4:(cg + 1) * 4, 3:131], in_=ps[:].rearrange("p (a b) -> p a b", a=4), func=AF.Copy), [psb], [pr.b])
            for cg in range(6):
                ps, psb = pb()
                for j in range(4):
                    ct = cg * 4 + j
                    for i in range(4):
                        S.add("tensor", lambda e, ps=ps, j=j, ct=ct, i=i, pr=pr: e.matmul(
                            ps[:, j * 128:(j + 1) * 128], lhsT=Dg.t[:, ct * 4 + i, :], rhs=pr.t[:, ct, i:i + 128],
                            start=(i == 0), stop=(i == 3)), [Dg.b, pr.b], [psb])
                if cg < 4:
                    act, sq = actl[cg % 2], sql[cg % 2]
                    S.add("scalar", lambda e, ps=ps, act=act: e.activation(
                        out=act.t[:], in_=ps[:].rearrange("p (a b) -> p a b", a=4), func=AF.Silu), [psb], [act.b])
                    S.add("gpsimd", lambda e, act=act, sq=sq: e.tensor_tensor(out=sq.t[:], in0=act.t[:], in1=act.t[:], op=ALU.mult), [act.b], [sq.b])
                    ps, psb = pb()
                    S.add("tensor", lambda e, ps=ps, sq=sq: e.matmul(ps[:], lhsT=ones_b, rhs=sq.t[:].rearrange("p a b -> p (a b)"),
                                                                     start=True, stop=True), [sq.b] + cbb, [psb])
                    S.add("scalar", lambda e, ps=ps: e.activation(out=rinv.t[:].rearrange("p a b -> p (a b)"), in_=ps[:],
                                                                  func=AF.Sqrt, bias=C.eps.t[:, 0:1], scale=1.0), [psb, C.eps.b], [rinv.b])
                    S.add("vector", lambda e: e.reciprocal(out=rinv.t[:], in_=rinv.t[:]), [rinv.b], [rinv.b])
                    if cg < 2:
                        S.add("vector", lambda e, cg=cg, act=act: e.scalar_tensor_tensor(out=qT.t[:, cg * 4:(cg + 1) * 4, :], in0=act.t[:], scalar=128.0 ** -0.5, in1=rinv.t[:],
                                                                                         op0=ALU.mult, op1=ALU.mult), [act.b, rinv.b], [qT.b])
                    else:
                        S.add("vector", lambda e, cg=cg, act=act: e.tensor_tensor(out=kT.t[:, (cg - 2) * 4:(cg - 1) * 4, :], in0=act.t[:], in1=rinv.t[:], op=ALU.mult), [act.b, rinv.b], [kT.b])
                else:
                    S.add("scalar", lambda e, ps=ps, cg=cg: e.activation(
                        out=vT.t[:, (cg - 4) * 4:(cg - 3) * 4, :], in_=ps[:].rearrange("p (a b) -> p a b", a=4), func=AF.Silu), [psb], [vT.b])
            ps, psb = pb()
            for kc in range(8):
                S.add("tensor", lambda e, ps=ps, kc=kc, xbi=xbi: e.matmul(ps[:, 0:16], lhsT=xbi.t[:, kc, :], rhs=Wba.t[:, kc, :],
                                                                          start=(kc == 0), stop=(kc == 7)), [Wba.b, xbi.b], [psb])
            smb = [sm.b]
            S.add("scalar", lambda e, ps=ps: e.activation(out=SM(0), in_=ps[:, 0:8], func=AF.Sigmoid), [psb], smb)
            S.add("vector", lambda e, ps=ps: e.tensor_tensor(out=SM(1), in0=ps[:, 8:16], in1=sp.t[:, 8:16], op=ALU.add), [psb, sp.b], smb)
            S.add("vector", lambda e: e.tensor_scalar(out=SM(2), in0=SM(1), scalar1=-1.0, scalar2=None, op0=ALU.mult), smb, smb)
            S.add("vector", lambda e: e.tensor_tensor(out=SM(2), in0=SM(2), in1=SM(1), op=ALU.max), smb, smb)
            S.add("scalar", lambda e: e.activation(out=SM(3), in_=SM(2), func=AF.Exp, scale=-1.0), smb, smb)
            S.add("scalar", lambda e: e.activation(out=SM(4), in_=SM(3), func=AF.Ln, bias=ones_f[:, 0:1], scale=1.0), smb + cbf, smb)
            S.add("vector", lambda e: e.tensor_scalar(out=SM(1), in0=SM(1), scalar1=0.0, scalar2=None, op0=ALU.max), smb, smb)
            S.add("vector", lambda e: e.tensor_tensor(out=SM(4), in0=SM(4), in1=SM(1), op=ALU.add), smb, smb)
            S.add("vector", lambda e: e.tensor_tensor(out=SM(5), in0=SM(4), in1=negA.t[:], op=ALU.mult), smb + [negA.b], smb)
            ps, psb = pb()
            S.add("tensor", lambda e, ps=ps: e.matmul(ps[:, 0:8], lhsT=cst(C, "tri"), rhs=SM(5), start=True, stop=True), smb + cbf, [psb])
            S.add("tensor", lambda e, ps=ps: e.matmul(ps[:, 8:16], lhsT=cst(C, "blk"), rhs=SM(5), start=True, stop=True), smb + cbf, [psb])
            S.add("vector", lambda e, ps=ps: e.tensor_copy(out=sm.t[:, 48:64], in_=ps[:, 0:16]), [psb], smb)
            S.add("scalar", lambda e: e.activation(out=SM(8), in_=SM(6), func=AF.Exp), smb, smb)
            S.add("vector", lambda e: e.tensor_tensor(out=SM(9), in0=SM(8), in1=SM(0), op=ALU.mult), smb, smb)
            S.add("vector", lambda e: e.tensor_tensor(out=SM(10), in0=SM(7), in1=SM(6), op=ALU.subtract), smb, smb)
            S.add("scalar", lambda e: e.activation(out=SM(10), in_=SM(10), func=AF.Exp), smb, smb)
            S.add("vector", lambda e: e.tensor_scalar(out=SM(11), in0=SM(0), scalar1=-1.0, scalar2=None, op0=ALU.mult), smb, smb)
            S.add("vector", lambda e: e.tensor_tensor(out=triG.t[:], in0=cst(C, "tri").unsqueeze(1).to_broadcast([128, 8, 128]),
                                                      in1=SM(5).unsqueeze(2).to_broadcast([128, 8, 128]), op=ALU.mult), smb + cbf, [triG.b])
            ps, psb = pb()
            pk = ps[:].bitcast(BF16)
            for h in range(8):
                S.add("tensor", lambda e, pk=pk, h=h: e.transpose(out=pk[:, h * 128:(h + 1) * 128], in_=kT.t[:, h, :], identity=ident_b), [kT.b] + cbb, [psb])
            pk3 = pk.rearrange("p (a b) -> p a b", a=8)
            S.add("vector", lambda e, pk3=pk3: e.tensor_tensor(out=kbeg.t[:], in0=pk3, in1=SM(9).unsqueeze(2).to_broadcast([128, 8, 128]), op=ALU.mult), [psb] + smb, [kbeg.b])
            S.add("vector", lambda e, pk3=pk3: e.tensor_tensor(out=kd.t[:], in0=pk3, in1=SM(10).unsqueeze(2).to_broadcast([128, 8, 128]), op=ALU.mult), [psb] + smb, [kd.b])
            ps, psb = pb()
            pv = ps[:].bitcast(BF16)
            for h in range(8):
                S.add("tensor", lambda e, pv=pv, h=h: e.transpose(out=pv[:, h * 128:(h + 1) * 128], in_=vT.t[:, h, :], identity=ident_b), [vT.b] + cbb, [psb])
            pv3 = pv.rearrange("p (a b) -> p a b", a=8)
            S.add("vector", lambda e, pv3=pv3: e.tensor_tensor(out=vb.t[:], in0=pv3, in1=SM(0).unsqueeze(2).to_broadcast([128, 8, 128]), op=ALU.mult), [psb] + smb, [vb.b])

            def hb_gen(hb, t=t, xbi=xbi):
                W = hbT[hb]
                h0 = hb * 4
                r3 = lambda ap: ap.rearrange("p (a b) -> p a b", a=4)
                psg, psgb = pb()
                S.add("tensor", lambda e, psg=psg, h0=h0: e.matmul(psg[:], lhsT=ones_f, rhs=triG.t[:, h0:h0 + 4, :].rearrange("p a b -> p (a b)"), start=True, stop=True),
                      [triG.b] + cbf, [psgb])
                S.add("scalar", lambda e, psg=psg, W=W: e.activation(out=W["egbc"].t[:], in_=r3(psg[:]), func=AF.Exp), [psgb], [W["egbc"].b])
                S.add("vector", lambda e, psg=psg, W=W, h0=h0: e.tensor_tensor(out=W["d1"].t[:], in0=r3(psg[:]),
                                                                               in1=sm.t[:, 48 + h0:48 + h0 + 4].unsqueeze(2).to_broadcast([128, 4, 128]), op=ALU.subtract),
                      [psgb] + smb, [W["d1"].b])
                S.add("scalar", lambda e, W=W: e.activation(out=W["Ds"].t[:], in_=W["d1"].t[:], func=AF.Exp, scale=-1.0), [W["d1"].b], [W["Ds"].b])
                S.add("scalar", lambda e, W=W: e.activation(out=W["DT"].t[:], in_=W["d1"].t[:], func=AF.Exp), [W["d1"].b], [W["DT"].b])
                S.add("vector", lambda e, W=W: e.scalar_tensor_tensor(out=W["Ds"].t[:], in0=W["Ds"].t[:], scalar=1.0, in1=cst(C, "strict", "b").unsqueeze(1).to_broadcast([128, 4, 128]),
                                                                     op0=ALU.min, op1=ALU.mult), [W["Ds"].b] + cbb, [W["Ds"].b])
                S.add("vector", lambda e, W=W: e.scalar_tensor_tensor(out=W["DT"].t[:], in0=W["DT"].t[:], scalar=1.0, in1=cst(C, "upper", "b").unsqueeze(1).to_broadcast([128, 4, 128]),
                                                                     op0=ALU.min, op1=ALU.mult), [W["DT"].b] + cbb, [W["DT"].b])
                yield
                ps, psb = pb()
                for h in range(4):
                    S.add("tensor", lambda e, ps=ps, h=h, h0=h0: e.matmul(ps[:, h * 128:(h + 1) * 128], lhsT=kT.t[:, h0 + h, :], rhs=kT.t[:, h0 + h, :], start=True, stop=True),
                          [kT.b], [psb])
                S.add("vector", lambda e, ps=ps, W=W: e.tensor_tensor(out=W["a1"].t[:], in0=r3(ps[:]), in1=W["Ds"].t[:], op=ALU.mult), [psb, W["Ds"].b], [W["a1"].b])
                S.add("vector", lambda e, W=W, h0=h0: e.tensor_tensor(out=W["X0"].t[:], in0=W["a1"].t[:],
                                                                      in1=sm.t[:, 88 + h0:88 + h0 + 4].unsqueeze(2).to_broadcast([128, 4, 128]), op=ALU.mult),
                      [W["a1"].b] + smb, [W["X0"].b])
                yield
                ps, psb = pb()
                pt = ps[:].bitcast(BF16)
                for h in range(4):
                    S.add("tensor", lambda e, pt=pt, h=h, W=W: e.transpose(out=pt[:, h * 128:(h + 1) * 128], in_=W["X0"].t[:, h, :], identity=ident_b), [W["X0"].b] + cbb, [psb])
                S.add("scalar", lambda e, pt=pt, W=W: e.activation(out=W["TXA"].t[:, 1, :, :], in_=pt[:, 0:512].rearrange("p (a b) -> p a b", a=4), func=AF.Copy), [psb], [W["TXA"].b])
                S.add("gpsimd", lambda e, W=W: e.tensor_copy(out=W["TXA"].t[:, 0, :, :], in_=ident_b.unsqueeze(1).to_broadcast([128, 4, 128])), cbb, [W["TXA"].b])
                yield
                Xc, TXc = W["X0"], W["TXA"]
                Xn_l = [W["XA"], W["XB"]]
                TXn_l = [W["TXB"], W["TXA"]]
                for lv in range(6):
                    Xn, TXn = Xn_l[lv % 2], TXn_l[lv % 2]
                    pa = [pb(), pb()]
                    for h in range(4):
                        ps, psb = pa[h // 2]
                        o4 = ps[:].rearrange("p (s a b) -> p s a b", s=2, a=2)
                        S.add("tensor", lambda e, o4=o4, h=h, Xc=Xc, TXc=TXc: e.matmul(o4[:, :, h % 2, :], lhsT=Xc.t[:, h, :], rhs=TXc.t[:, :, h, :], start=True, stop=True),
                              [Xc.b, TXc.b], [psb])
                    last = (lv == 5)
                    if not last:
                        psx, psxb = pb()
                        for h in range(4):
                            S.add("tensor", lambda e, psx=psx, h=h, Xc=Xc, TXc=TXc: e.matmul(psx[:, h * 128:(h + 1) * 128], lhsT=TXc.t[:, 1, h, :], rhs=Xc.t[:, h, :], start=True, stop=True),
                                  [Xc.b, TXc.b], [psxb])
                    for q in range(2):
                        ps, psb = pa[q]
                        o4 = ps[:].rearrange("p (s a b) -> p s a b", s=2, a=2)
                        S.add("vector", lambda e, o4=o4, q=q, TXc=TXc, TXn=TXn: e.tensor_tensor(out=TXn.t[:, 0, 2 * q:2 * q + 2, :], in0=o4[:, 0, :, :], in1=TXc.t[:, 0, 2 * q:2 * q + 2, :], op=ALU.add),
                              [psb, TXc.b], [TXn.b])
                        if not last:
                            S.add("scalar", lambda e, o4=o4, q=q, TXn=TXn: e.activation(out=TXn.t[:, 1, 2 * q:2 * q + 2, :], in_=o4[:, 1, :, :], func=AF.Copy), [psb], [TXn.b])
                    if not last:
                        S.add("scalar", lambda e, psx=psx, Xn=Xn: e.activation(out=Xn.t[:], in_=r3(psx[:]), func=AF.Copy), [psxb], [Xn.b])
                    Xc, TXc = Xn, TXn
                    yield
                Tt = TXc
                ps, psb = pb()
                for h in range(4):
                    S.add("tensor", lambda e, ps=ps, h=h, h0=h0, Tt=Tt: e.matmul(ps[:, h * 128:(h + 1) * 128], lhsT=Tt.t[:, 0, h, :], rhs=vb.t[:, h0 + h, :], start=True, stop=True),
                          [Tt.b, vb.b], [psb])
                S.add("scalar", lambda e, ps=ps, W=W: e.activation(out=W["u"].t[:], in_=r3(ps[:]), func=AF.Copy), [psb], [W["u"].b])
                ps, psb = pb()
                for h in range(4):
                    S.add("tensor", lambda e, ps=ps, h=h, h0=h0, Tt=Tt: e.matmul(ps[:, h * 128:(h + 1) * 128], lhsT=kbeg.t[:, h0 + h, :], rhs=Tt.t[:, 0, h, :], start=True, stop=True),
                          [Tt.b, kbeg.b], [psb])
                S.add("scalar", lambda e, ps=ps, W=W: e.activation(out=W["wT"].t[:], in_=r3(ps[:]), func=AF.Copy), [psb], [W["wT"].b])
                ps, psb = pb()
                for h in range(4):
                    S.add("tensor", lambda e, ps=ps, h=h, h0=h0: e.matmul(ps[:, h * 128:(h + 1) * 128], lhsT=kT.t[:, h0 + h, :], rhs=qT.t[:, h0 + h, :], start=True, stop=True),
                          [kT.b, qT.b], [psb])
                S.add("vector", lambda e, ps=ps, W=W: e.tensor_tensor(out=W["qkT"].t[:], in0=r3(ps[:]), in1=W["DT"].t[:], op=ALU.mult), [psb, W["DT"].b], [W["qkT"].b])
                S.add("gpsimd", lambda e, W=W, h0=h0: e.tensor_tensor(out=W["qdT"].t[:], in0=qT.t[:, h0:h0 + 4, :], in1=W["egbc"].t[:], op=ALU.mult), [qT.b, W["egbc"].b], [W["qdT"].b])
                need_out = t >= 15
                yield
                pso, psob = C.ps[6 + hb], C.psb[6 + hb]
                for c in range(2):
                    rs = slice(c * 64, (c + 1) * 64)
                    ps1, ps1b = pb()
                    for h in range(4):
                        S.add("tensor", lambda e, ps1=ps1, h=h, h0=h0, rs=rs, W=W: e.matmul(ps1[rs, h * 128:(h + 1) * 128], lhsT=W["wT"].t[:, h, rs], rhs=Sb.t[:, h0 + h, :], start=True, stop=True),
                              [W["wT"].b, Sbb[hb]], [ps1b])
                    S.add("vector", lambda e, ps1=ps1, rs=rs, W=W: e.tensor_tensor(out=W["vnew"].t[rs, :, :], in0=W["u"].t[rs, :, :], in1=r3(ps1[rs, :]), op=ALU.subtract),
                          [ps1b, W["u"].b], [W["vnew"].b])
                    yield
                    for h in (range(4) if need_out else ()):
                        S.add("tensor", lambda e, h=h, h0=h0, rs=rs, W=W: e.matmul(pso[rs, h * 128:(h + 1) * 128], lhsT=W["qdT"].t[:, h, rs], rhs=Sb.t[:, h0 + h, :], start=True, stop=False),
                              [W["qdT"].b, Sbb[hb]], [psob])
                        S.add("tensor", lambda e, h=h, rs=rs, W=W: e.matmul(pso[rs, h * 128:(h + 1) * 128], lhsT=W["qkT"].t[rs, h, rs], rhs=W["vnew"].t[rs, h, :], start=False, stop=True),
                              [W["qkT"].b, W["vnew"].b], [psob])
                    ps3, ps3b = pb()
                    for h in range(4):
                        S.add("tensor", lambda e, ps3=ps3, h=h, h0=h0, rs=rs, W=W: e.matmul(ps3[:, h * 128:(h + 1) * 128], lhsT=kd.t[rs, h0 + h, :], rhs=W["vnew"].t[rs, h, :], start=True, stop=True),
                              [kd.b, W["vnew"].b], [ps3b])
                    col = c * 64 + 63
                    S.add("vector", lambda e, h0=h0, col=col, W=W: e.tensor_tensor(out=Sf.t[:, h0:h0 + 4, :], in0=Sf.t[:, h0:h0 + 4, :],
                                                                                   in1=W["egbc"].t[:, :, col:col + 1].to_broadcast([128, 4, 128]), op=ALU.mult),
                          [Sfb[hb], W["egbc"].b], [Sfb[hb]])
                    S.add("vector", lambda e, ps3=ps3, h0=h0: e.tensor_tensor(out=Sf.t[:, h0:h0 + 4, :], in0=Sf.t[:, h0:h0 + 4, :], in1=r3(ps3[:]), op=ALU.add),
                          [Sfb[hb], ps3b], [Sfb[hb]])
                    S.add("scalar", lambda e, h0=h0: e.activation(out=Sb.t[:, h0:h0 + 4, :], in_=Sf.t[:, h0:h0 + 4, :], func=AF.Copy), [Sfb[hb]], [Sbb[hb]])
                    yield
                if not need_out:
                    return
                zc = hb
                ps, psb = pb()
                for kc in range(8):
                    S.add("tensor", lambda e, ps=ps, zc=zc, kc=kc, xbi=xbi: e.matmul(ps[:], lhsT=xbi.t[:, kc, :], rhs=Wz.t[:, kc, zc * 512:(zc + 1) * 512],
                                                                                      start=(kc == 0), stop=(kc == 7)), [Wz.b, xbi.b], [psb])
                S.add("scalar", lambda e, ps=ps, W=W: e.activation(out=W["zs"].t[:], in_=ps[:].rearrange("p (a b) -> p a b", a=4), func=AF.Silu), [psb], [W["zs"].b])
                S.add("gpsimd", lambda e, W=W: e.tensor_tensor(out=W["zs"].t[:], in0=W["zs"].t[:], in1=nw4.t[:], op=ALU.mult), [W["zs"].b, nw4.b], [W["zs"].b])
                S.add("scalar", lambda e, W=W: e.activation(out=W["osq"].t[:], in_=r3(pso[:]), func=AF.Square), [psob], [W["osq"].b])
                S.add("vector", lambda e, W=W: e.tensor_reduce(out=W["ss"].t[:], in_=W["osq"].t[:], axis=AX.X, op=ALU.add), [W["osq"].b], [W["ss"].b])
                S.add("scalar", lambda e, W=W: e.activation(out=W["ss"].t[:], in_=W["ss"].t[:], func=AF.Sqrt, bias=C.eps.t[:, 0:1], scale=1.0 / 128.0), [W["ss"].b, C.eps.b], [W["ss"].b])
                S.add("vector", lambda e, W=W: e.reciprocal(out=W["ss"].t[:], in_=W["ss"].t[:]), [W["ss"].b], [W["ss"].b])
                S.add("vector", lambda e, W=W: e.tensor_tensor(out=W["y1"].t[:], in0=r3(pso[:]), in1=W["ss"].t[:].unsqueeze(2).to_broadcast([128, 4, 128]), op=ALU.mult),
                      [psob, W["ss"].b], [W["y1"].b])
                S.add("vector", lambda e, W=W: e.tensor_tensor(out=W["ytok"].t[:], in0=W["y1"].t[:], in1=W["zs"].t[:], op=ALU.mult), [W["y1"].b, W["zs"].b], [W["ytok"].b])
                ps, psb = pb()
                py = ps[:].bitcast(BF16)
                for h in range(4):
                    S.add("tensor", lambda e, py=py, h=h, W=W: e.transpose(out=py[:, h * 128:(h + 1) * 128], in_=W["ytok"].t[:, h, :], identity=ident_b), [W["ytok"].b] + cbb, [psb])
                slot = (t % 16) * 128
                half = t // 16
                S.add("vector", lambda e, py=py, h0=h0, slot=slot, half=half: e.scalar_tensor_tensor(
                    out=C.ymdn.t[:, h0:h0 + 4, slot:slot + 128], in0=py[:, 0:512].rearrange("p (a b) -> p a b", a=4), scalar=C.selm.t[:, half:half + 1],
                    in1=C.ymdn.t[:, h0:h0 + 4, slot:slot + 128], op0=ALU.mult, op1=ALU.add), [psb, C.ymdn.b, C.selm.b], [C.ymdn.b])
                if t == 15:
                    S.add("vector", lambda e, py=py, h0=h0: e.tensor_scalar(out=C.yhdn.t[:, h0:h0 + 4, :], in0=py[:, 0:512].rearrange("p (a b) -> p a b", a=4),
                                                                           scalar1=C.selm.t[:, 2:3], scalar2=None, op0=ALU.mult), [psb, C.selm.b], [C.yhdn.b])

            gens = [hb_gen(0), hb_gen(1)]
            alive = [True, True]
            while any(alive):
                for gi in range(2):
                    if alive[gi]:
                        try:
                            next(gens[gi])
                        except StopIteration:
                            alive[gi] = False


def make_in_maps(inp):
    x = np.asarray(inp["x"], np.float32)
    w_in = np.asarray(inp["w_in"], np.float32)[0]
    maps = []
    cw = np.asarray(inp["dn_conv_w"], np.float32)[0]
    convw = np.ascontiguousarray(cw.reshape(4, 24, 128).transpose(2, 1, 0).reshape(128, 96))
    off = np.cumsum((0,) + (1024, 1024, 1024, 1024, 8, 8, 1024, 256, 256, 256, 256, 256, 256, 24, 1024, 1024))
    o_nq, o_kc, o_vc, o_ksl, o_vsl, o_kwn, o_vwn, o_gate, o_mgd, o_mgn = [int(off[i]) for i in (6, 7, 8, 9, 10, 11, 12, 13, 14, 15)]
    nsa_host = {k_: v_ for k_, v_ in NC2.items() if k_ != "pm_list"}
    pml = NC2["pm_list"]
    NEGT = np.full((128, 512), -30000.0, np.float32)
    ZT = np.zeros((128, 512), np.float32)
    percore = []
    for gg_ in range(2):
        off_ = 2048 * (1 - gg_)
        c1, s1, c2, s2 = _rope_tables(off_)
        sbias, svalid = _sel_tables(off_)
        if gg_ == 1:
            pm0 = [pml[3], pml[4], ZT]
            kneg = np.zeros((128, 1024), np.float32)
        else:
            pm0 = [NEGT, NEGT, NEGT]
            kneg = np.full((128, 1024), -30000.0, np.float32)
        percore.append({"ropeC": c1, "ropeS": s1, "ropeCk": c2, "ropeSk": s2, "selbias": sbias, "selvalid": svalid,
                        "pmask": np.ascontiguousarray(np.concatenate(pml[0:4] + pm0, axis=1)), "keyneg": kneg})
    f32c = lambda a: np.ascontiguousarray(np.asarray(a, np.float32))
    nsa_host["cmp_k_w1"] = f32c(inp["cmp_k_w1"][0])
    nsa_host["cmp_v_w1"] = f32c(inp["cmp_v_w1"][0])
    nsa_host["cmp_k_w2"] = f32c(inp["cmp_k_w2"][0])
    nsa_host["cmp_k_w2r"] = f32c(np.asarray(inp["cmp_k_w2"][0])[:, _rot_perm(1)])
    nsa_host["cmp_v_w2"] = f32c(inp["cmp_v_w2"][0])
    nsa_host["peT_k"] = f32c(np.asarray(inp["cmp_pos_k"][0]).T)
    nsa_host["peT_v"] = f32c(np.asarray(inp["cmp_pos_v"][0]).T)
    for gg in range(2):
        sl = lambda o: w_in[:, o + gg * 128:o + (gg + 1) * 128]
        nsa_host["wn_kf%d" % gg] = f32c(np.concatenate([sl(o_kc), sl(o_vc), sl(o_ksl), sl(o_kwn)], axis=1))
        rp = _rot_perm(1)
        nsa_host["wn_kr%d" % gg] = f32c(np.concatenate([sl(o_ksl)[:, rp], sl(o_kwn)[:, rp]], axis=1))
        nsa_host["wn_v%d" % gg] = f32c(np.concatenate([sl(o_vsl), sl(o_vwn)], axis=1))
        wq = w_in[:, o_nq + gg * 512:o_nq + (gg + 1) * 512]
        nsa_host["wn_q%d" % gg] = f32c(wq)
        nsa_host["wn_qr%d" % gg] = f32c(wq[:, _rot_perm(4)])
        gcols = [o_gate + br * 8 + gg * 4 + hh for br in range(3) for hh in range(4)]
        nsa_host["wn_g%d" % gg] = f32c(w_in[:, gcols])
    fcw_ = np.asarray(inp["ffn_conv_w"], np.float32)[0]
    p2_host = {
        "wm": f32c(w_in[:, o_mgd:o_mgd + 2048]),
        "w_dn": f32c(inp["w_branch_dn"][0]), "w_nsa": f32c(inp["w_branch_nsa"][0]), "w_out": f32c(inp["w_out"][0]),
        "ln1_g": f32c(inp["ln1_g"]).reshape(1, D), "ln1_b": f32c(inp["ln1_b"]).reshape(1, D),
        "ln2_g": f32c(inp["ln2_g"]).reshape(1, D), "ln2_b": f32c(inp["ln2_b"]).reshape(1, D),
        "w_up": f32c(inp["ffn_w_up"][0]), "w_down": f32c(inp["ffn_w_down"][0]),
        "fconvw": f32c(fcw_.reshape(3, 44, 128).transpose(2, 1, 0).reshape(128, 132)),
        "fconvb": f32c(np.asarray(inp["ffn_conv_b"], np.float32)[0].reshape(44, 128).T),
    }
    for c in range(8):
        b, g = c // 2, c % 2
        selm = np.zeros((128, 3), np.float32)
        selm[:, 1] = 1.0
        selm[:, 2] = float(g)
        m = {
            "xT": (np.ascontiguousarray(x[b].T) if g == 1 else
                   np.ascontiguousarray(np.concatenate([np.zeros((D, HALF), np.float32), x[b, 0:HALF].T], axis=1))),
            "consts": CONSTS,
            "selm": selm,
            "wg_qkv": np.ascontiguousarray(w_in[:, 0:3072]),
            "wg_z": np.ascontiguousarray(w_in[:, 3072:4096]),
            "wg_ba": np.ascontiguousarray(w_in[:, 4096:4112]),
            "convw": convw,
            "a_log": np.asarray(inp["dn_a_log"], np.float32).reshape(1, 8),
            "dt_bias": np.asarray(inp["dn_dt_bias"], np.float32).reshape(1, 8),
            "dn_norm_w": np.asarray(inp["dn_norm_w"], np.float32).reshape(1, 128),
        }
        m.update(nsa_host)
        m.update(percore[g])
        m.update(p2_host)
        xh = np.zeros((HALF + 128, D), np.float32)
        if g == 0:
            xh[128:] = x[b, 0:HALF]
        else:
            xh[:] = x[b, HALF - 128:SEQ]
        m["xh"] = xh
        m["xTh"] = np.ascontiguousarray(xh.T)
        maps.append(m)
    return maps


def kernel(**inp):
    nc = build()
    maps = make_in_maps(inp)
    res = run_bass_kernel_spmd(nc, maps, core_ids=list(range(8)))
    out = np.zeros((4, SEQ, D), np.float32)
    for c in range(8):
        b, g = c // 2, c % 2
        out[b, g * HALF:(g + 1) * HALF] = res.results[c]["out"]
    return out


NSA_INPUTS = [("ropeC", [128, 4096]), ("ropeS", [128, 4096]), ("ropeCk", [128, 256]), ("ropeSk", [128, 256]),
              ("selbias", [4096, 64]), ("selvalid", [4096, 64]),
              ("cmask", [128, 2048]), ("wmask", [128, 4096]), ("pmask", [128, 3584]), ("keyneg", [128, 1024]), ("ovl", [128, 128]),
              ("ebig", [128, 4096]), ("gsel", [128, 1536]),
              ("cmp_k_w1", [4096, 256]), ("cmp_v_w1", [4096, 256]), ("cmp_k_w2", [256, 128]), ("cmp_k_w2r", [256, 128]),
              ("cmp_v_w2", [256, 128]), ("peT_k", [128, 32]), ("peT_v", [128, 32])]
for _g in range(2):
    NSA_INPUTS += [("wn_kf%d" % _g, [D, 512]), ("wn_kr%d" % _g, [D, 256]), ("wn_v%d" % _g, [D, 256]),
                   ("wn_q%d" % _g, [D, 512]), ("wn_qr%d" % _g, [D, 512]), ("wn_g%d" % _g, [D, 12])]


def stage_nsa(C, I):
    nc, S, pb = C.nc, C.S, C.pb
    C.nrot = 3
    C.pi = 0
    pselp = (C.ps[3], C.psb[3])
    accs = [((C.ps[4], C.psb[4]), (C.ps[5], C.psb[5])), ((C.ps[6], C.psb[6]), (C.ps[7], C.psb[7]))]
    acci = [0]
    SC = 128.0 ** -0.5
    with ExitStack() as st:
        A = lambda name, shape, dt=F32: TT(C, st, name, shape, dt)
        ident_b = cst(C, "ident", "b")
        ones_b = cst(C, "ones", "b")
        cbb = [C.cb.b]
        stg = A("nstg", [128, 8, 256])
        stg2 = stg.t[:].rearrange("p a b -> p (a b)")
        xb = A("nxb", [128, 8, 512], BF16)
        xT3 = I["xT"].rearrange("(kc p) t -> p kc t", p=128)

        def load_bf(name, ncols, dst, rows=128):
            for c0 in range(0, ncols, 2048):
                n = min(2048, ncols - c0)
                S.dma("sync", stg2[0:rows, 0:n], I[name][0:rows, c0:c0 + n], writes=[stg.b])
                S.add("vector", lambda e, n=n, c0=c0: e.tensor_copy(out=dst.t[0:rows, c0:c0 + n], in_=stg2[0:rows, 0:n]), [stg.b], [dst.b])

        def load_w(name, ncols, dst):
            src3 = I[name].rearrange("(kc p) c -> p kc c", p=128)
            for c0 in range(0, ncols, 256):
                n = min(256, ncols - c0)
                S.dma("sync", stg.t[:, :, 0:n], src3[:, :, c0:c0 + n], writes=[stg.b])
                S.add("vector", lambda e, c0=c0, n=n: e.tensor_copy(out=dst.t[:, :, c0:c0 + n], in_=stg.t[:, :, 0:n]), [stg.b], [dst.b])

        xTb3 = C.xTb.rearrange("(kc p) t -> p kc t", p=128)

        xcur = [xb]

        def load_x(tg):
            xd = xcur[0]
            if C.have_xTb:
                S.dma("sync", xd.t[:], xTb3[:, :, tg * 512:(tg + 1) * 512], reads=[C.xTb_b], writes=[xd.b])
            else:
                for hh_ in range(2):
                    S.dma("sync", stg.t[:], xT3[:, :, tg * 512 + hh_ * 256:tg * 512 + (hh_ + 1) * 256], writes=[stg.b])
                    S.add("gpsimd", lambda e, hh_=hh_: e.tensor_copy(out=xd.t[:, :, hh_ * 256:(hh_ + 1) * 256], in_=stg.t[:]), [stg.b], [xd.b])

        ropeC = A("ropeC", [128, 512])
        ropeS = A("ropeS", [128, 512])
        r1 = A("r1", [128, 512])
        r2 = A("r2", [128, 512])

        def load_rope(tg):
            S.dma("sync", ropeC.t[:], I["ropeC"][:, tg * 512:(tg + 1) * 512], writes=[ropeC.b])
            S.dma("sync", ropeS.t[:], I["ropeS"][:, tg * 512:(tg + 1) * 512], writes=[ropeS.b])

        def proj_fm(W, c0, m, n):
            ps, psb = pb()
            xs = xcur[0]
            for kc in range(8):
                S.add("tensor", lambda e, ps=ps, kc=kc: e.matmul(ps[0:m, 0:n], lhsT=W.t[:, kc, c0:c0 + m], rhs=xs.t[:, kc, 0:n],
                                                                 start=(kc == 0), stop=(kc == 7)), [W.b, xs.b], [psb])
            return ps, psb

        def rope_into(dst_ap, dstb, ps, psb, psr, psrb, n, scale=1.0):
            S.add("vector", lambda e: e.tensor_tensor(out=r1.t[:, 0:n], in0=ps[:, 0:n], in1=ropeC.t[:, 0:n], op=ALU.mult), [psb, ropeC.b], [r1.b])
            S.add("vector", lambda e: e.tensor_tensor(out=r2.t[:, 0:n], in0=psr[:, 0:n], in1=ropeS.t[:, 0:n], op=ALU.mult), [psrb, ropeS.b], [r2.b])
            S.add("gpsimd", lambda e: e.tensor_tensor(out=r1.t[:, 0:n], in0=r1.t[:, 0:n], in1=r2.t[:, 0:n], op=ALU.add), [r1.b, r2.b], [r1.b])
            S.add("scalar", lambda e: e.activation(out=dst_ap, in_=r1.t[:, 0:n], func=AF.Copy, scale=scale), [r1.b], [dstb])

        import os as _os
        _lim = _os.environ.get("NSA_LIM", "")
        pc_bufs = {}

        def precast_gen():
            jobs = []
            for key, nm, ncol in (("m", "wm", 2048), ("dn", "w_dn", 1024), ("ns", "w_nsa", 1024), ("out", "w_out", 1024), ("up", "w_up", 5632)):
                src3 = I[nm].rearrange("(kc p) c -> p kc c", p=128)
                for c0 in range(0, ncol, 256):
                    jobs.append((key, src3[:, :, c0:c0 + 256], c0, 8, 256))
            srcd = I["w_down"].rearrange("(k p) c -> p k c", p=128)
            for k0 in range(0, 22, 2):
                jobs.append(("down", srcd[:, k0:k0 + 2, :], k0, 2, 1024))
            def views(ji):
                key, src, a0, K_, n_ = jobs[ji]
                pf = pc_bufs["f"][ji % 2]
                pbf = pc_bufs["b"][ji % 2]
                return pf, pbf, pf.t[:].rearrange("p (k c) -> p k c", k=K_), pbf.t[:].rearrange("p (k c) -> p k c", k=K_)

            in_ver = {}

            def issue_in(ji):
                pf, pbf, vf, vb_ = views(ji)
                S.dma("gpsimd", vf, jobs[ji][1], writes=[pf.b])
                in_ver[ji] = pc_bufs["ver"]

            issue_in(0)
            for ji, (key, src, a0, K_, n_) in enumerate(jobs):
                pf, pbf, vf, vb_ = views(ji)
                if in_ver.get(ji) != pc_bufs["ver"]:
                    issue_in(ji)
                if ji + 1 < len(jobs):
                    issue_in(ji + 1)
                S.add("vector", lambda e, vf=vf, vb_=vb_: e.tensor_copy(out=vb_, in_=vf), [pf.b], [pbf.b])
                if key == "down":
                    for hf in range(2):
                        S.dma("gpsimd", C.wsc["down"][hf][:, a0:a0 + 2, :], vb_[:, :, hf * 512:(hf + 1) * 512], reads=[pbf.b], writes=[C.wsc_b])
                elif key == "out":
                    S.dma("gpsimd", C.wsc["out"][:, :, a0:a0 + 256], vb_, reads=[pbf.b], writes=[C.wsc_b])
                else:
                    for j_ in range(2):
                        S.dma("gpsimd", C.wsc[key][a0 // 128 + j_], vb_[:, :, j_ * 128:(j_ + 1) * 128], reads=[pbf.b], writes=[C.wsc_b])
                yield

        pc_gen = precast_gen()
        pc_done = [False]

        def precast_step(n):
            for _ in range(n):
                if pc_done[0]:
                    return
                try:
                    next(pc_gen)
                except StopIteration:
                    pc_done[0] = True

        for g in range(2):
            if _lim and g == 1:
                break
            S.barrier()
            with ExitStack() as sg:
                G = lambda name, shape, dt=F32: TT(C, sg, "%s_g%d" % (name, g), shape, dt)
                kselT = G("kselT", [128, 4096], BF16)
                kwinT = G("kwinT", [128, 4096], BF16)
                vsw = G("vsw", [128, 32, 256], BF16)
                kcT = G("kcT", [128, 256], BF16)
                vc = G("vc", [128, 2, 128], BF16)
                S.add("gpsimd", lambda e: e.memset(kcT.t[:], 0.0), [], [kcT.b])
                with ExitStack() as sa:
                    B = lambda name, shape, dt=F32: TT(C, sa, "%s_a%d" % (name, g), shape, dt)
                    kcmpb = B("kcmpb", [128, 4096], BF16)
                    vcmpb = B("vcmpb", [128, 4096], BF16)
                    pc_bufs["ver"] = g
                    pc_bufs["f"] = [B("pcf%d" % i, [128, 2048]) for i in range(2)]
                    pc_bufs["b"] = [B("pcb%d" % i, [128, 2048], BF16) for i in range(2)]
                    xb2 = B("xb2", [128, 8, 512], BF16)
                    sa1 = ExitStack()
                    B1 = lambda name, shape, dt=F32: TT(C, sa1, "%s_a%d" % (name, g), shape, dt)
                    Wkf = B1("Wkf", [128, 8, 512], BF16)
                    Wkr = B1("Wkr", [128, 8, 256], BF16)
                    Wv = B1("Wv", [128, 8, 256], BF16)
                    load_w("wn_kf%d" % g, 512, Wkf)
                    load_w("wn_kr%d" % g, 256, Wkr)
                    load_w("wn_v%d" % g, 256, Wv)
                    for tg in range(8):
                        precast_step(4 if (g == 0 and tg < 3) else 3)
                        xcur[0] = xb if tg % 2 == 0 else xb2
                        load_x(tg)
                        load_rope(tg)
                        cs = slice(tg * 512, (tg + 1) * 512)
                        if _lim == "A00":
                            continue
                        for j, dst in enumerate((kcmpb, vcmpb, kselT, kwinT)):
                            ps, psb = proj_fm(Wkf, j * 128, 128, 512)
                            if j < 2 or _lim == "A01":
                                S.add("scalar", lambda e, ps=ps, dst=dst, cs=cs: e.activation(out=dst.t[:, cs], in_=ps[:], func=AF.Copy), [psb], [dst.b])
                            else:
                                psr, psrb = proj_fm(Wkr, (j - 2) * 128, 128, 512)
                                rope_into(dst.t[:, cs], dst.b, ps, psb, psr, psrb, 512)
                        for s4 in range(4):
                            if _lim in ("A01", "A02"):
                                break
                            ps, psb = pb()
                            for kc in range(8):
                                S.add("tensor", lambda e, ps=ps, kc=kc, s4=s4, xs=xcur[0]: e.matmul(ps[:, 0:256], lhsT=xs.t[:, kc, s4 * 128:(s4 + 1) * 128], rhs=Wv.t[:, kc, :],
                                                                                                   start=(kc == 0), stop=(kc == 7)), [Wv.b, xcur[0].b], [psb])
                            S.add("scalar", lambda e, ps=ps, s4=s4, tg=tg: e.activation(out=vsw.t[:, tg * 4 + s4, :], in_=ps[:, 0:256], func=AF.Copy), [psb], [vsw.b])
                    xcur[0] = xb
                    if g == 1:
                        precast_step(100)
                    sa1.close()
                    S.barrier()
                    if _lim.startswith("A0"):
                        continue
                    w1 = B("w1", [128, 16, 256], BF16)
                    w2 = B("w2", [128, 2, 128], BF16)
                    w2r = B("w2r", [128, 2, 128], BF16)
                    pe = B("pe", [128, 32])
                    hid = B("hid", [128, 2, 256], BF16)
                    S.add("gpsimd", lambda e: e.memset(hid.t[:], 0.0), [], [hid.b])
                    kpe = [B("kpe%d" % i, [128, 255], BF16) for i in range(3)]
                    for is_k, src, nm in ((True, kcmpb, "k"), (False, vcmpb, "v")):
                        w1v = I["cmp_%s_w1" % nm].rearrange("(l d) h -> d l h", d=128)
                        def load_w1(hf):
                            for q_ in range(2):
                                S.dma("sync", stg2.rearrange("p (l h) -> p l h", l=8), w1v[:, hf * 16 + q_ * 8:hf * 16 + (q_ + 1) * 8, :], writes=[stg.b])
                                S.add("gpsimd", lambda e, q_=q_: e.tensor_copy(out=w1.t[:, q_ * 8:(q_ + 1) * 8, :], in_=stg2.rearrange("p (l h) -> p l h", l=8)), [stg.b], [w1.b])
                        S.dma("sync", stg.t[:, 0:2, 0:128], I["cmp_%s_w2" % nm].rearrange("(hc p) d -> p hc d", p=128), writes=[stg.b])
                        S.add("vector", lambda e: e.tensor_copy(out=w2.t[:], in_=stg.t[:, 0:2, 0:128]), [stg.b], [w2.b])
                        if is_k:
                            S.dma("sync", stg.t[:, 0:2, 0:128], I["cmp_k_w2r"].rearrange("(hc p) d -> p hc d", p=128), writes=[stg.b])
                            S.add("vector", lambda e: e.tensor_copy(out=w2r.t[:], in_=stg.t[:, 0:2, 0:128]), [stg.b], [w2r.b])
                        S.dma("sync", pe.t[:], I["peT_%s" % nm], writes=[pe.b])
                        hps = [pb(), pb()]
                        for l in range(32):
                            if l % 16 == 0:
                                load_w1(l // 16)
                            kp = kpe[l % 3]
                            S.add("vector", lambda e, kp=kp, l=l, src=src: e.tensor_scalar(out=kp.t[:], in0=src.t[:, l:l + 4065:16], scalar1=pe.t[:, l:l + 1], scalar2=None, op0=ALU.add),
                                  [src.b, pe.b], [kp.b])
                            for hc in range(2):
                                S.add("tensor", lambda e, kp=kp, l=l, hc=hc, hps=hps: e.matmul(hps[hc][0][:, 0:255], lhsT=w1.t[:, l % 16, hc * 128:(hc + 1) * 128], rhs=kp.t[:],
                                                                                     start=(l == 0), stop=(l == 31)), [w1.b, kp.b], [hps[hc][1]])
                        for hc in range(2):
                            S.add("scalar", lambda e, hc=hc, hps=hps: e.activation(out=hid.t[:, hc, 0:255], in_=hps[hc][0][:, 0:255], func=AF.Silu), [hps[hc][1]], [hid.b])
                        if is_k:
                            ps, psb = pb()
                            psr, psrb = pb()
                            for hc in range(2):
                                S.add("tensor", lambda e, ps=ps, hc=hc: e.matmul(ps[:, 0:255], lhsT=w2.t[:, hc, :], rhs=hid.t[:, hc, 0:255], start=(hc == 0), stop=(hc == 1)), [w2.b, hid.b], [psb])
                                S.add("tensor", lambda e, psr=psr, hc=hc: e.matmul(psr[:, 0:255], lhsT=w2r.t[:, hc, :], rhs=hid.t[:, hc, 0:255], start=(hc == 0), stop=(hc == 1)), [w2r.b, hid.b], [psrb])
                            S.dma("sync", ropeC.t[:, 0:256], I["ropeCk"], writes=[ropeC.b])
                            S.dma("sync", ropeS.t[:, 0:256], I["ropeSk"], writes=[ropeS.b])
                            rope_into(kcT.t[:, 0:255], kcT.b, ps, psb, psr, psrb, 255)
                        else:
                            for ncn in range(2):
                                ps, psb = pb()
                                for hc in range(2):
                                    S.add("tensor", lambda e, ps=ps, hc=hc, ncn=ncn: e.matmul(ps[:, 0:128], lhsT=hid.t[:, hc, ncn * 128:(ncn + 1) * 128], rhs=w2.t[:, hc, :],
                                                                                             start=(hc == 0), stop=(hc == 1)), [w2.b, hid.b], [psb])
                                S.add("scalar", lambda e, ps=ps, ncn=ncn: e.activation(out=vc.t[:, ncn, :], in_=ps[:, 0:128], func=AF.Copy), [psb], [vc.b])
                if _lim.startswith("A"):
                    continue
                S.barrier()
                with ExitStack() as sc:
                    B = lambda name, shape, dt=F32: TT(C, sc, "%s_c%d" % (name, g), shape, dt)
                    wmask = B("wmask", [128, 4096], BF16)
                    pmask = B("pmask", [128, 3584], BF16)
                    keyneg = B("keyneg", [128, 1024], BF16)
                    onesrow = B("onesrow", [128, 512], BF16)
                    S.add("gpsimd", lambda e: e.memset(onesrow.t[:], 1.0), [], [onesrow.b])
                    ebig = B("ebig", [128, 4096], BF16)
                    ovl = B("ovl", [128, 128], BF16)
                    gsel = B("gsel", [128, 1536])
                    load_bf("wmask", 4096, wmask)
                    load_bf("pmask", 3584, pmask)
                    load_bf("keyneg", 1024, keyneg)
                    load_bf("ebig", 4096, ebig, rows=64)
                    load_bf("ovl", 128, ovl)
                    S.dma("sync", gsel.t[0:12, :], I["gsel"][0:12, :], writes=[gsel.b])
                    Wq = B("Wq", [128, 8, 512], BF16)
                    Wqr = B("Wqr", [128, 8, 512], BF16)
                    Wg = B("Wg", [128, 8, 12], BF16)
                    load_w("wn_q%d" % g, 512, Wq)
                    load_w("wn_qr%d" % g, 512, Wqr)
                    load_w("wn_g%d" % g, 12, Wg)
                    qT = [B("qT%d" % h, [128, 512], BF16) for h in range(4)]
                    gs = B("gs", [128, 512])
                    PT = [B("PT%d" % i, [128, 512], BF16) for i in range(4)]
                    pti = [0]
                    PTcl = [B("PTc%d" % i, [128, 2, 512], BF16) for i in range(2)]
                    acc = B("acc", [128, 4, 512])
                    rr = B("rr", [128, 512])
                    rg = r2
                    tmpc = rr
                    negselT = B("negselT", [128, 512], BF16)
                    selb = B("selb", [128, 64])
                    selv = B("selv", [128, 64])
                    sc1 = B("sc1", [128, 64])
                    sc2 = B("sc2", [128, 64])
                    slt = B("slt", [128, 64])
                    m8 = B("m8", [128, 16])
                    nsb = B("nsb", [128, 64], BF16)
                    zb = B("zb", [128, 256], BF16)
                    S.add("gpsimd", lambda e: e.memset(zb.t[:], 0.0), [], [zb.b])

                    def combine(h, br, ao, asum):
                        S.add("vector", lambda e: e.tensor_scalar(out=rr.t[:], in0=asum[0][:], scalar1=1e-30, scalar2=None, op0=ALU.max), [asum[1]], [rr.b])
                        S.add("vector", lambda e: e.reciprocal(out=rr.t[:], in_=rr.t[:]), [rr.b], [rr.b])
                        psg, psgb = pb()
                        r = br * 4 + h
                        S.add("tensor", lambda e: e.matmul(psg[:], lhsT=gsel.t[0:12, r * 128:(r + 1) * 128], rhs=gs.t[0:12, :], start=True, stop=True), [gsel.b, gs.b], [psgb])
                        S.add("vector", lambda e: e.tensor_tensor(out=rg.t[:], in0=rr.t[:], in1=psg[:], op=ALU.mult), [rr.b, psgb], [rg.b])
                        if br == 0:
                            S.add("vector", lambda e: e.tensor_tensor(out=acc.t[:, h, :], in0=ao[0][:], in1=rg.t[:], op=ALU.mult), [ao[1], rg.b], [acc.b])
                        else:
                            S.add("vector", lambda e: e.tensor_tensor(out=tmpc.t[:], in0=ao[0][:], in1=rg.t[:], op=ALU.mult), [ao[1], rg.b], [tmpc.b])
                            S.add("gpsimd", lambda e: e.tensor_tensor(out=acc.t[:, h, :], in0=acc.t[:, h, :], in1=tmpc.t[:], op=ALU.add), [acc.b, tmpc.b], [acc.b])

                    for tg in range(3, 8):
                        load_x(tg)
                        load_rope(tg)
                        for h in range(4):
                            ps, psb = proj_fm(Wq, h * 128, 128, 512)
                            psr, psrb = proj_fm(Wqr, h * 128, 128, 512)
                            rope_into(qT[h].t[:], qT[h].b, ps, psb, psr, psrb, 512, scale=SC)
                        ps, psb = proj_fm(Wg, 0, 12, 512)
                        S.add("scalar", lambda e, ps=ps: e.activation(out=gs.t[0:12, :], in_=ps[0:12, :], func=AF.Sigmoid), [psb], [gs.b])
                        ncs = [0] if tg < 4 else [0, 1]
                        S.add("tensor", lambda e: e.matmul(pselp[0][:, 0:256], lhsT=zb.t[:, 0:128], rhs=zb.t[:, 0:256], start=True, stop=False), [zb.b], [pselp[1]])
                        for h in range(4):
                            ao, asum = accs[acci[0] % 2]
                            acci[0] += 1
                            PTc = PTcl[h % 2]
                            for ii, nci in enumerate(ncs):
                                ps, psb = pb()
                                dlt = (tg - 4) if nci == 1 else 4 + min(tg - 3, 2)
                                msk = True
                                S.add("tensor", lambda e, ps=ps, nci=nci, h=h, msk=msk: e.matmul(ps[:], lhsT=kcT.t[:, nci * 128:(nci + 1) * 128], rhs=qT[h].t[:], start=True, stop=(not msk)),
                                      [kcT.b, qT[h].b], [psb])
                                if msk:
                                    S.add("tensor", lambda e, ps=ps, dlt=dlt: e.matmul(ps[:], lhsT=ident_b, rhs=pmask.t[:, dlt * 512:(dlt + 1) * 512], start=False, stop=True),
                                          [pmask.b] + cbb, [psb])
                                S.add("scalar", lambda e, ps=ps, PTc=PTc, nci=nci: e.activation(out=PTc.t[:, nci, :], in_=ps[:], func=AF.Exp), [psb], [PTc.b])
                                S.add("tensor", lambda e, PTc=PTc, nci=nci, ii=ii, ao=ao, n=len(ncs): e.matmul(ao[0][:], lhsT=vc.t[:, nci, :], rhs=PTc.t[:, nci, :], start=(ii == 0), stop=(ii == n - 1)),
                                      [vc.b, PTc.b], [ao[1]])
                                S.add("tensor", lambda e, PTc=PTc, nci=nci, ii=ii, asum=asum, n=len(ncs): e.matmul(asum[0][:], lhsT=ones_b, rhs=PTc.t[:, nci, :], start=(ii == 0), stop=(ii == n - 1)),
                                      [PTc.b] + cbb, [asum[1]])
                            combine(h, 0, ao, asum)
                            for ii, nci in enumerate(ncs):
                                S.add("vector", lambda e, PTc=PTc, nci=nci: e.tensor_tensor(out=PTc.t[:, nci, :], in0=PTc.t[:, nci, :], in1=rr.t[:], op=ALU.mult), [PTc.b, rr.b], [PTc.b])
                                for s4 in range(4):
                                    first = (h == 0 and ii == 0)
                                    last = (h == 3 and ii == len(ncs) - 1)
                                    S.add("tensor", lambda e, PTc=PTc, nci=nci, s4=s4, first=first, last=last: e.matmul(
                                        pselp[0][:, s4 * 64:(s4 + 1) * 64], lhsT=PTc.t[:, nci, s4 * 128:(s4 + 1) * 128], rhs=ovl.t[:, nci * 64:(nci + 1) * 64],
                                        start=False, stop=last), [PTc.b, ovl.b], [pselp[1]])
                        for s4 in range(4):
                            tile_i = tg * 4 + s4
                            S.dma("sync", selb.t[:], I["selbias"][tile_i * 128:(tile_i + 1) * 128, :], writes=[selb.b])
                            S.dma("sync", selv.t[:], I["selvalid"][tile_i * 128:(tile_i + 1) * 128, :], writes=[selv.b])
                            ps, psb = pselp
                            S.add("vector", lambda e, ps=ps, s4=s4: e.tensor_tensor(out=sc1.t[:], in0=ps[:, s4 * 64:(s4 + 1) * 64], in1=selb.t[:], op=ALU.add), [psb, selb.b], [sc1.b])
                            S.add("vector", lambda e: e.max(out=m8.t[:, 0:8], in_=sc1.t[:]), [sc1.b], [m8.b])
                            S.add("vector", lambda e: e.match_replace(out=sc2.t[:], in_to_replace=m8.t[:, 0:8], in_values=sc1.t[:], imm_value=-2e9), [sc1.b, m8.b], [sc2.b])
                            S.add("vector", lambda e: e.max(out=m8.t[:, 8:16], in_=sc2.t[:]), [sc2.b], [m8.b])
                            S.add("vector", lambda e: e.tensor_scalar(out=slt.t[:], in0=sc1.t[:], scalar1=m8.t[:, 15:16], scalar2=None, op0=ALU.is_ge), [sc1.b, m8.b], [slt.b])
                            S.add("vector", lambda e: e.tensor_tensor(out=slt.t[:], in0=slt.t[:], in1=selv.t[:], op=ALU.mult), [slt.b, selv.b], [slt.b])
                            S.add("vector", lambda e: e.tensor_scalar(out=nsb.t[:], in0=slt.t[:], scalar1=30000.0, scalar2=-30000.0, op0=ALU.mult, op1=ALU.add), [slt.b], [nsb.b])
                            pst, pstb = pb()
                            ptb = pst[:].bitcast(BF16)
                            S.add("tensor", lambda e, ptb=ptb: e.transpose(out=ptb[0:64, 0:128], in_=nsb.t[:], identity=ident_b), [nsb.b] + cbb, [pstb])
                            S.add("scalar", lambda e, ptb=ptb, s4=s4: e.activation(out=negselT.t[0:64, s4 * 128:(s4 + 1) * 128], in_=ptb[0:64, 0:128], func=AF.Copy), [pstb], [negselT.b])
                        tiles = []
                        for br in (1, 2):
                            for h in range(4):
                                pair = accs[acci[0] % 2]
                                acci[0] += 1
                                kts = list(range(0, 4 * tg + 4)) if br == 1 else list(range(max(0, 4 * tg - 4), 4 * tg + 4))
                                for ii, kt in enumerate(kts):
                                    tiles.append((br, h, kt, ii, len(kts), pair))

                        def stage1(tl):
                            br, h, kt, ii, n, pair = tl
                            ps, psb = pb()
                            ksrc = kselT if br == 1 else kwinT
                            S.add("tensor", lambda e: e.matmul(ps[:], lhsT=ksrc.t[:, kt * 128:(kt + 1) * 128], rhs=qT[h].t[:], start=True, stop=False), [ksrc.b, qT[h].b], [psb])
                            if br == 1:
                                diag = kt >= 4 * tg
                                S.add("tensor", lambda e: e.matmul(ps[:], lhsT=ebig.t[0:64, kt * 128:(kt + 1) * 128], rhs=negselT.t[0:64, :], start=False, stop=(not diag)), [ebig.b, negselT.b], [psb])
                                if diag:
                                    dd = kt - 4 * tg
                                    S.add("tensor", lambda e: e.matmul(ps[:], lhsT=ident_b, rhs=wmask.t[:, (4 + dd) * 512:(5 + dd) * 512], start=False, stop=True), [wmask.b] + cbb, [psb])
                            else:
                                dd = kt - (4 * tg - 4)
                                if kt < 16:
                                    S.add("tensor", lambda e: e.matmul(ps[:], lhsT=keyneg.t[0:1, (kt - 8) * 128:(kt - 7) * 128], rhs=onesrow.t[0:1, :], start=False, stop=False), [keyneg.b, onesrow.b], [psb])
                                S.add("tensor", lambda e: e.matmul(ps[:], lhsT=ident_b, rhs=wmask.t[:, dd * 512:(dd + 1) * 512], start=False, stop=True), [wmask.b] + cbb, [psb])
                            pt = PT[pti[0] % len(PT)]
                            pti[0] += 1
                            S.add("scalar", lambda e: e.activation(out=pt.t[:], in_=ps[:], func=AF.Exp), [psb], [pt.b])
                            return pt

                        def stage2(tl, pt):
                            br, h, kt, ii, n, pair = tl
                            ao, asum = pair
                            vo = 0 if br == 1 else 128
                            S.add("tensor", lambda e: e.matmul(ao[0][:], lhsT=vsw.t[:, kt, vo:vo + 128], rhs=pt.t[:], start=(ii == 0), stop=(ii == n - 1)), [vsw.b, pt.b], [ao[1]])
                            S.add("tensor", lambda e: e.matmul(asum[0][:], lhsT=ones_b, rhs=pt.t[:], start=(ii == 0), stop=(ii == n - 1)), [pt.b] + cbb, [asum[1]])
                            if ii == n - 1:
                                combine(h, br, ao, asum)

                        LA = 3
                        C.nrot = 4
                        pend = []
                        for idx, tl in enumerate(tiles):
                            pend.append((tl, stage1(tl)))
                            if len(pend) > LA:
                                t0_, p0_ = pend.pop(0)
                                stage2(t0_, p0_)
                        while pend:
                            t0_, p0_ = pend.pop(0)
                            stage2(t0_, p0_)
                        C.nrot = 3
                        C.pi = 0
                        half = tg // 4
                        slot = (tg % 4) * 512
                        for h in range(4):
                            if tg >= 4:
                                S.add("vector", lambda e, h=h, g=g, slot=slot, half=half: e.scalar_tensor_tensor(out=C.ymnsa.t[:, g * 4 + h, slot:slot + 512], in0=acc.t[:, h, :], scalar=C.selm.t[:, half:half + 1],
                                    in1=C.ymnsa.t[:, g * 4 + h, slot:slot + 512], op0=ALU.mult, op1=ALU.add), [acc.b, C.ymnsa.b, C.selm.b], [C.ymnsa.b])
                            if tg == 3:
                                S.add("vector", lambda e, h=h, g=g: e.tensor_scalar(out=C.yhnsa.t[:, g * 4 + h, :], in0=acc.t[:, h, 384:512], scalar1=C.selm.t[:, 2:3], scalar2=None, op0=ALU.mult),
                                      [acc.b, C.selm.b], [C.yhnsa.b])
    C.nrot = 8
    C.pi = 0


P2_INPUTS = [("xh", [HALF + 128, D]), ("xTh", [D, HALF + 128]), ("wm", [D, 2048]), ("w_dn", [D, D]), ("w_nsa", [D, D]), ("w_out", [D, D]),
             ("ln1_g", [1, D]), ("ln1_b", [1, D]), ("w_up", [D, 5632]), ("fconvw", [128, 132]), ("fconvb", [128, 44]),
             ("w_down", [2816, D]), ("ln2_g", [1, D]), ("ln2_b", [1, D])]


def stage_p2(C, I, out_ap, fin):
    nc, S, pb = C.nc, C.S, C.pb
    C.nrot = 8
    C.pi = 0
    S.barrier()
    with ExitStack() as st:
        A = lambda name, shape, dt=F32: TT(C, st, name, shape, dt)
        ident_b = cst(C, "ident", "b")
        cbb = [C.cb.b]
        stgs = [A("pstg%d" % i, [128, 1024]) for i in range(2)]
        sti = [0]

        def load_chunk(src, dst_ap, dstb, K, ncol):
            sg = stgs[sti[0] % 2]
            sti[0] += 1
            v = sg.t[:, 0:K * ncol].rearrange("p (k c) -> p k c", k=K)
            S.dma("sync", v, src, writes=[sg.b])
            if sti[0] % 2:
                S.add("scalar", lambda e: e.activation(out=dst_ap, in_=v, func=AF.Copy), [sg.b], [dstb])
            else:
                S.add("gpsimd", lambda e: e.tensor_copy(out=dst_ap, in_=v), [sg.b], [dstb])

        def load_sc(src, dst_ap, dstb):
            S.dma("sync", dst_ap, src, reads=[C.wsc_b], writes=[dstb])

        wdn3 = I["w_dn"].rearrange("(kc p) c -> p kc c", p=128)
        wns3 = I["w_nsa"].rearrange("(kc p) c -> p kc c", p=128)
        wm3 = I["wm"].rearrange("(kc p) c -> p kc c", p=128)
        wo3 = I["w_out"].rearrange("(kc p) c -> p kc c", p=128)
        wup3 = I["w_up"].rearrange("(kc p) c -> p kc c", p=128)
        wdw3 = I["w_down"].rearrange("(k p) c -> p k c", p=128)
        xTh3 = I["xTh"].rearrange("(kc p) t -> p kc t", p=128)

        x1 = A("x1", [128, 4, D])
        x1T = A("x1T", [128, 8, 512], BF16)
        hist = A("hist", [128, 44, 2])
        lnp = A("lnp", [128, 2, D])
        fcw = A("fcw", [128, 132])
        fcb = A("fcb", [128, 44])
        st1 = A("st1", [128, 8])
        junk = A("junk", [128, D], BF16)
        S.dma("sync", fcw.t[:], I["fconvw"], writes=[fcw.b])
        S.dma("sync", fcb.t[:], I["fconvb"], writes=[fcb.b])

        def load_ln(which):
            S.dma("sync", lnp.t[:, 0, :], I["ln%d_g" % which].partition_broadcast(128), writes=[lnp.b])
            S.dma("sync", lnp.t[:, 1, :], I["ln%d_b" % which].partition_broadcast(128), writes=[lnp.b])

        def layernorm(r_ap, rb, out_ap_, outb):
            s1 = [st1.b]
            S.add("scalar", lambda e: e.activation(out=junk.t[:], in_=r_ap, func=AF.Copy, accum_out=st1.t[:, 0:1]), [rb], [junk.b, st1.b])
            S.add("scalar", lambda e: e.activation(out=junk.t[:], in_=r_ap, func=AF.Square, accum_out=st1.t[:, 1:2]), [rb], [junk.b, st1.b])
            S.add("vector", lambda e: e.tensor_scalar(out=st1.t[:, 2:3], in0=st1.t[:, 0:1], scalar1=1.0 / D, scalar2=None, op0=ALU.mult), s1, s1)
            S.add("vector", lambda e: e.tensor_tensor(out=st1.t[:, 3:4], in0=st1.t[:, 2:3], in1=st1.t[:, 2:3], op=ALU.mult), s1, s1)
            S.add("vector", lambda e: e.scalar_tensor_tensor(out=st1.t[:, 4:5], in0=st1.t[:, 1:2], scalar=1.0 / D, in1=st1.t[:, 3:4], op0=ALU.mult, op1=ALU.subtract), s1, s1)
            S.add("scalar", lambda e: e.activation(out=st1.t[:, 5:6], in_=st1.t[:, 4:5], func=AF.Sqrt, bias=C.eps.t[:, 1:2], scale=1.0), s1 + [C.eps.b], s1)
            S.add("vector", lambda e: e.reciprocal(out=st1.t[:, 6:7], in_=st1.t[:, 5:6]), s1, s1)
            S.add("vector", lambda e: e.tensor_scalar(out=out_ap_, in0=r_ap, scalar1=st1.t[:, 2:3], scalar2=st1.t[:, 6:7], op0=ALU.subtract, op1=ALU.mult), [rb] + s1, [outb])
            S.add("gpsimd", lambda e: e.tensor_tensor(out=out_ap_, in0=out_ap_, in1=lnp.t[:, 0, :], op=ALU.mult), [outb, lnp.b], [outb])
            S.add("gpsimd", lambda e: e.tensor_tensor(out=out_ap_, in0=out_ap_, in1=lnp.t[:, 1, :], op=ALU.add), [outb, lnp.b], [outb])

        def do_block(blk):
            N = 128 if blk < 0 else 512
            nt = N // 128
            tok0 = 0 if blk < 0 else 128 + blk * 512
            if blk < 0:
                ysrc = lambda T_, kc: T_[1].t[:, kc, :]
            else:
                ysrc = lambda T_, kc, blk=blk: T_[0].t[:, kc, blk * 512:(blk + 1) * 512]
            YD = (C.ymdn, C.yhdn)
            YN = (C.ymnsa, C.yhnsa)
            ydb = [C.ymdn.b, C.yhdn.b]
            ynb = [C.ymnsa.b, C.yhnsa.b]
            S.barrier()
            with ExitStack() as sa:
                B = lambda name, shape, dt=F32: TT(C, sa, "%s_b%d" % (name, blk + 1), shape, dt)
                yT = B("yT", [128, 8, 512], BF16)
                xgb = B("xgb", [128, 8, 512], BF16)
                wch = {k: [B("wch_%s%d" % (k, i), [128, 8, 128], BF16) for i in range(3)] for k in ("dn", "ns", "md", "mn")}
                gdt = B("gdt", [128, 512])
                gnt = B("gnt", [128, 512])
                t1 = B("t1", [128, 512])
                t2 = B("t2", [128, 512])
                wo = B("wo", [128, 8, D], BF16)
                xt = [B("xt%d" % i, [128, D]) for i in range(2)]
                rbuf = B("rbuf", [128, D])
                x1b = B("x1b", [128, D], BF16)
                for k0 in range(0, 8, 2):
                    load_chunk(xTh3[:, k0:k0 + 2, tok0:tok0 + N], xgb.t[:, k0:k0 + 2, 0:N], xgb.b, 2, N)
                for c in range(8):
                    cs_ = slice(c * 128, (c + 1) * 128)
                    w = {k: wch[k][c % 3] for k in wch}
                    load_sc(C.wsc["dn"][c], w["dn"].t[:], w["dn"].b)
                    load_sc(C.wsc["ns"][c], w["ns"].t[:], w["ns"].b)
                    load_sc(C.wsc["m"][c], w["md"].t[:], w["md"].b)
                    load_sc(C.wsc["m"][8 + c], w["mn"].t[:], w["mn"].b)
                    if c == 1:
                        load_sc(C.wsc["out"], wo.t[:], wo.b)
                        load_ln(1)
                    pd, pdb = pb()
                    pn, pnb = pb()
                    pgd, pgdb = pb()
                    pgn, pgnb = pb()
                    for kc in range(8):
                        S.add("tensor", lambda e, pd=pd, kc=kc, w=w: e.matmul(pd[:, 0:N], lhsT=w["dn"].t[:, kc, :], rhs=ysrc(YD, kc), start=(kc == 0), stop=(kc == 7)), [w["dn"].b] + ydb, [pdb])
                    for kc in range(8):
                        S.add("tensor", lambda e, pn=pn, kc=kc, w=w: e.matmul(pn[:, 0:N], lhsT=w["ns"].t[:, kc, :], rhs=ysrc(YN, kc), start=(kc == 0), stop=(kc == 7)), [w["ns"].b] + ynb, [pnb])
                    for kc in range(8):
                        S.add("tensor", lambda e, pgd=pgd, kc=kc, w=w: e.matmul(pgd[:, 0:N], lhsT=w["md"].t[:, kc, :], rhs=xgb.t[:, kc, 0:N], start=(kc == 0), stop=(kc == 7)), [w["md"].b, xgb.b], [pgdb])
                    for kc in range(8):
                        S.add("tensor", lambda e, pgn=pgn, kc=kc, w=w: e.matmul(pgn[:, 0:N], lhsT=w["mn"].t[:, kc, :], rhs=xgb.t[:, kc, 0:N], start=(kc == 0), stop=(kc == 7)), [w["mn"].b, xgb.b], [pgnb])
                    S.add("scalar", lambda e, pgd=pgd: e.activation(out=gdt.t[:, 0:N], in_=pgd[:, 0:N], func=AF.Sigmoid), [pgdb], [gdt.b])
                    S.add("scalar", lambda e, pgn=pgn: e.activation(out=gnt.t[:, 0:N], in_=pgn[:, 0:N], func=AF.Sigmoid), [pgnb], [gnt.b])
                    S.add("vector", lambda e, pd=pd: e.tensor_tensor(out=t1.t[:, 0:N], in0=gdt.t[:, 0:N], in1=pd[:, 0:N], op=ALU.mult), [gdt.b, pdb], [t1.b])
                    S.add("vector", lambda e, pn=pn: e.tensor_tensor(out=t2.t[:, 0:N], in0=gnt.t[:, 0:N], in1=pn[:, 0:N], op=ALU.mult), [gnt.b, pnb], [t2.b])
                    S.add("vector", lambda e, c=c: e.tensor_tensor(out=yT.t[:, c, 0:N], in0=t1.t[:, 0:N], in1=t2.t[:, 0:N], op=ALU.add), [t1.b, t2.b], [yT.b])
                for s in range(nt):
                    xti = xt[s % 2]
                    S.dma("sync", xti.t[:], I["xh"][tok0 + s * 128:tok0 + (s + 1) * 128, :], writes=[xti.b])
                    for hf in range(2):
                        pz, pzb = pb()
                        for kc in range(8):
                            S.add("tensor", lambda e, pz=pz, kc=kc, s=s, hf=hf: e.matmul(pz[:], lhsT=yT.t[:, kc, s * 128:(s + 1) * 128], rhs=wo.t[:, kc, hf * 512:(hf + 1) * 512],
                                                                                      start=(kc == 0), stop=(kc == 7)), [yT.b, wo.b], [pzb])
                        S.add("vector", lambda e, pz=pz, hf=hf, xti=xti: e.scalar_tensor_tensor(out=rbuf.t[:, hf * 512:(hf + 1) * 512], in0=xti.t[:, hf * 512:(hf + 1) * 512], scalar=ALPHA, in1=pz[:],
                                                                                               op0=ALU.mult, op1=ALU.add), [xti.b, pzb], [rbuf.b])
                    layernorm(rbuf.t[:], rbuf.b, x1.t[:, s, :], x1.b)
                    S.add("scalar", lambda e, s=s: e.activation(out=x1b.t[:], in_=x1.t[:, s, :], func=AF.Copy), [x1.b], [x1b.b])
                    pt, ptb_ = pb()
                    ptv = pt[:].bitcast(BF16)
                    for kc in range(8):
                        S.add("tensor", lambda e, ptv=ptv, kc=kc: e.transpose(out=ptv[:, kc * 128:(kc + 1) * 128], in_=x1b.t[:, kc * 128:(kc + 1) * 128], identity=ident_b), [x1b.b] + cbb, [ptb_])
                    S.add("scalar", lambda e, ptv=ptv, s=s: e.activation(out=x1T.t[:, :, s * 128:(s + 1) * 128], in_=ptv.rearrange("p (a b) -> p a b", a=8), func=AF.Copy), [ptb_], [x1T.b])
            S.barrier()
            with ExitStack() as sb:
                B = lambda name, shape, dt=F32: TT(C, sb, "%s_f%d" % (name, blk + 1), shape, dt)
                wup = [B("wup%d" % i, [128, 8, 128], BF16) for i in range(6)]
                wi = [0]
                if blk < 0:
                    for cti in range(44):
                        w = wup[wi[0] % 6]
                        wi[0] += 1
                        load_sc(C.wsc["up"][cti], w.t[:], w.b)
                        ps, psb = pb()
                        for kc in range(8):
                            S.add("tensor", lambda e, ps=ps, kc=kc, w=w: e.matmul(ps[:, 0:128], lhsT=w.t[:, kc, :], rhs=x1T.t[:, kc, 0:128], start=(kc == 0), stop=(kc == 7)), [w.b, x1T.b], [psb])
                        S.add("vector", lambda e, ps=ps, cti=cti: e.tensor_scalar(out=hist.t[:, cti, :], in0=ps[:, 126:128], scalar1=C.selm.t[:, 2:3], scalar2=None, op0=ALU.mult), [psb, C.selm.b], [hist.b])
                    return
                hT = B("hT", [128, 22, 512], BF16)
                ub = [B("ub%d" % i, [128, 514]) for i in range(2)]
                cc = [B("cc%d" % i, [128, 512]) for i in range(2)]
                sgt = B("sgt", [128, 512])
                wdh = B("wdh", [128, 22, 512], BF16)
                r2 = B("r2", [128, 4, D])
                ot = [B("ot%d" % i, [128, D]) for i in range(2)]
                for ct in range(22):
                    if ct == 2:
                        load_sc(C.wsc["down"][0], wdh.t[:], wdh.b)
                    for gv in range(2):
                        cti = ct + 22 * gv
                        w = wup[wi[0] % 6]
                        wi[0] += 1
                        load_sc(C.wsc["up"][cti], w.t[:], w.b)
                        ps, psb = pb()
                        for kc in range(8):
                            S.add("tensor", lambda e, ps=ps, kc=kc, w=w: e.matmul(ps[:], lhsT=w.t[:, kc, :], rhs=x1T.t[:, kc, :], start=(kc == 0), stop=(kc == 7)), [w.b, x1T.b], [psb])
                        u = ub[gv]
                        S.add("gpsimd", lambda e, u=u, cti=cti: e.tensor_copy(out=u.t[:, 0:2], in_=hist.t[:, cti, :]), [hist.b], [u.b])
                        S.add("scalar", lambda e, u=u, ps=ps: e.activation(out=u.t[:, 2:514], in_=ps[:], func=AF.Copy), [psb], [u.b])
                        S.add("gpsimd", lambda e, u=u, cti=cti: e.tensor_copy(out=hist.t[:, cti, :], in_=u.t[:, 512:514]), [u.b], [hist.b])
                        cv = cc[gv]
                        S.add("vector", lambda e, u=u, cv=cv, cti=cti: e.tensor_scalar(out=cv.t[:], in0=u.t[:, 2:514], scalar1=fcw.t[:, cti * 3 + 2:cti * 3 + 3], scalar2=fcb.t[:, cti:cti + 1],
                                                                                    op0=ALU.mult, op1=ALU.add), [u.b, fcw.b, fcb.b], [cv.b])
                        S.add("vector", lambda e, u=u, cv=cv, cti=cti: e.scalar_tensor_tensor(out=cv.t[:], in0=u.t[:, 1:513], scalar=fcw.t[:, cti * 3 + 1:cti * 3 + 2], in1=cv.t[:],
                                                                                           op0=ALU.mult, op1=ALU.add), [u.b, fcw.b, cv.b], [cv.b])
                        S.add("vector", lambda e, u=u, cv=cv, cti=cti: e.scalar_tensor_tensor(out=cv.t[:], in0=u.t[:, 0:512], scalar=fcw.t[:, cti * 3:cti * 3 + 1], in1=cv.t[:],
                                                                                           op0=ALU.mult, op1=ALU.add), [u.b, fcw.b, cv.b], [cv.b])
                    S.add("scalar", lambda e: e.activation(out=sgt.t[:], in_=cc[0].t[:], func=AF.Silu), [cc[0].b], [sgt.b])
                    S.add("vector", lambda e, ct=ct: e.tensor_tensor(out=hT.t[:, ct, :], in0=sgt.t[:], in1=cc[1].t[:], op=ALU.mult), [sgt.b, cc[1].b], [hT.b])
                for hf in range(2):
                    if hf == 1:
                        load_sc(C.wsc["down"][hf], wdh.t[:], wdh.b)
                    for s in range(4):
                        pf, pfb = pb()
                        for k in range(22):
                            S.add("tensor", lambda e, pf=pf, k=k, s=s: e.matmul(pf[:], lhsT=hT.t[:, k, s * 128:(s + 1) * 128], rhs=wdh.t[:, k, :], start=(k == 0), stop=(k == 21)), [hT.b, wdh.b], [pfb])
                        S.add("vector", lambda e, pf=pf, s=s, hf=hf: e.scalar_tensor_tensor(out=r2.t[:, s, hf * 512:(hf + 1) * 512], in0=x1.t[:, s, hf * 512:(hf + 1) * 512], scalar=ALPHA, in1=pf[:],
                                                                                         op0=ALU.mult, op1=ALU.add), [x1.b, pfb], [r2.b])
                load_ln(2)
                for s in range(4):
                    o = ot[s % 2]
                    layernorm(r2.t[:, s, :], r2.b, o.t[:], o.b)
                    row0 = blk * 512 + s * 128
                    fin.append(S.dma("sync", out_ap[row0:row0 + 128, :], o.t[:], reads=[o.b]))

        for blk in range(-1, 4):
            do_block(blk)
```
